# Optimizing a Trainium2 kernel written in Bass

```python
import jax, jax.numpy as jnp
from jax import lax
import numpy as np

D_MODEL = 1024
BATCH = 4
SEQ = 4096
DEPTH = 1
DEC_BATCH = 32
DEC_SEQ = 64
PAST_LEN = 2048

CHUNK = 64
Q_BLOCK = 128
EPS = 1e-6
MLA_HEADS = 4
Q_LORA = 384
KV_LORA = 256
QK_NOPE = 128
QK_ROPE = 64
QK_DIM = QK_NOPE + QK_ROPE
V_DIM = 128
ROPE_THETA = 10000.0
MLA_WIDTH = MLA_HEADS * V_DIM
GLA_HEADS = 4
GLA_DK = 64
GLA_DV = 128
GLA_GATE_RANK = 16
GLA_TAU = 16.0
GLA_WIDTH = GLA_HEADS * GLA_DV
MIX_WIDTH = MLA_WIDTH + GLA_WIDTH
IN_SPLITS = (Q_LORA, KV_LORA, QK_ROPE, GLA_HEADS * GLA_DK, GLA_HEADS * GLA_DK, GLA_WIDTH, GLA_GATE_RANK, GLA_WIDTH)
IN_COLS = Q_LORA + KV_LORA + QK_ROPE + 2 * GLA_HEADS * GLA_DK + GLA_WIDTH + GLA_GATE_RANK + GLA_WIDTH
D_FF = 4 * D_MODEL

kernel_name = 'hymba_mla_gla_adaln_stream_step'


def rms_norm(x, g):
    xf = x.astype(jnp.float32)
    y = xf * lax.rsqrt(jnp.mean(xf * xf, axis=-1, keepdims=True) + EPS)
    return (y * g.astype(jnp.float32)).astype(x.dtype)


def rope(x, pos):
    half = QK_ROPE // 2
    inv = jnp.power(ROPE_THETA, -jnp.arange(half, dtype=jnp.float32) / half)
    ang = pos.astype(jnp.float32)[:, None] * inv[None, :]
    cos = jnp.cos(ang)[:, None, :]
    sin = jnp.sin(ang)[:, None, :]
    xf = x.astype(jnp.float32)
    x1, x2 = xf[..., :half], xf[..., half:]
    return jnp.concatenate([x1 * cos - x2 * sin, x2 * cos + x1 * sin], axis=-1).astype(x.dtype)


def chunk_causal_attention(q, k, v, q_pos, k_pos):
    B, T, H, D = q.shape
    qb = min(Q_BLOCK, T)
    nb = T // qb
    q_blocks = q.reshape(B, nb, qb, H, D).swapaxes(0, 1)
    pos_blocks = q_pos.reshape(nb, qb)
    k_chunk = k_pos // CHUNK
    scale = QK_DIM ** -0.5

    def attend(args):
        qblk, pblk = args
        s = jnp.einsum('bqhd,bkhd->bhqk', qblk, k).astype(jnp.float32) * scale
        visible = k_chunk[None, :] <= (pblk // CHUNK)[:, None]
        p = jax.nn.softmax(jnp.where(visible, s, -jnp.inf), axis=-1)
        return jnp.einsum('bhqk,bkhd->bqhd', p.astype(v.dtype), v)

    o = lax.map(attend, (q_blocks, pos_blocks))
    return o.swapaxes(0, 1).reshape(B, T, H, v.shape[-1])


def mla_mixer(c_q, c_kv, k_rope_raw, past_latent, past_krope,
              g_q_lora, w_uq, g_kv_lora, w_ukv, g_q_head, g_k_head):
    B, T, _ = c_q.shape
    P = past_latent.shape[1]
    S = P + T
    q = (rms_norm(c_q, g_q_lora) @ w_uq).reshape(B, T, MLA_HEADS, QK_DIM)
    latent_new = rms_norm(c_kv, g_kv_lora)
    latent = jnp.concatenate([past_latent.astype(latent_new.dtype), latent_new], axis=1)
    krope = jnp.concatenate([past_krope.astype(k_rope_raw.dtype), k_rope_raw], axis=1)
    kv = (latent @ w_ukv).reshape(B, S, MLA_HEADS, QK_NOPE + V_DIM)
    k_nope, v = kv[..., :QK_NOPE], kv[..., QK_NOPE:]
    k = jnp.concatenate([k_nope, jnp.broadcast_to(krope[:, :, None, :], (B, S, MLA_HEADS, QK_ROPE))], axis=-1)
    q = rms_norm(q, g_q_head)
    k = rms_norm(k, g_k_head)
    q_pos = P + jnp.arange(T)
    k_pos = jnp.arange(S)
    q = jnp.concatenate([q[..., :QK_NOPE], rope(q[..., QK_NOPE:], q_pos)], axis=-1)
    k = jnp.concatenate([k[..., :QK_NOPE], rope(k[..., QK_NOPE:], k_pos)], axis=-1)
    o = chunk_causal_attention(q, k, v, q_pos, k_pos)
    return o.reshape(B, T, MLA_WIDTH), latent_new, k_rope_raw


def gla_recurrence(q, k, v, log_a, s0):
    B, T, H, DK = q.shape
    DV = v.shape[-1]
    L = min(CHUNK, T)
    n = T // L

    def chunks(a):
        return a.astype(jnp.float32).reshape(B, n, L, *a.shape[2:]).swapaxes(0, 1)

    causal = jnp.tril(jnp.ones((L, L), dtype=bool))[None, :, :, None, None]

    def step(state, xs):
        qc, kc, vc, ac = xs
        b = jnp.cumsum(ac, axis=1)
        decay = jnp.exp(jnp.where(causal, b[:, :, None] - b[:, None, :], -jnp.inf))
        scores = jnp.einsum('bthk,bshk,btshk->bths', qc, kc, decay)
        o = (jnp.einsum('bths,bshv->bthv', scores, vc)
             + jnp.einsum('bthk,bhkv->bthv', qc * jnp.exp(b), state))
        b_end = b[:, -1]
        state = (state * jnp.exp(b_end)[..., None]
                 + jnp.einsum('bshk,bshv->bhkv', kc * jnp.exp(b_end[:, None] - b), vc))
        return state, o

    s_final, o = lax.scan(step, s0.astype(jnp.float32), (chunks(q), chunks(k), chunks(v), chunks(log_a)))
    return o.swapaxes(0, 1).reshape(B, T, H, DV), s_final.astype(s0.dtype)


def gla_mixer(q, k, v, gate_lr, r, s0, w_gate_up, b_gate_up, g_gla_out):
    B, T, _ = q.shape
    q = q.reshape(B, T, GLA_HEADS, GLA_DK) * (GLA_DK ** -0.5)
    k = k.reshape(B, T, GLA_HEADS, GLA_DK)
    v = v.reshape(B, T, GLA_HEADS, GLA_DV)
    log_a = jax.nn.log_sigmoid((gate_lr @ w_gate_up + b_gate_up).astype(jnp.float32)) / GLA_TAU
    log_a = log_a.reshape(B, T, GLA_HEADS, GLA_DK)
    o, s_new = gla_recurrence(q, k, v, log_a, s0)
    o = rms_norm(o.astype(r.dtype), g_gla_out.reshape(GLA_HEADS, GLA_DV))
    return o.reshape(B, T, GLA_WIDTH) * jax.nn.silu(r), s_new


def encoder_layer(x, c, past_latent, past_krope, gla_s0,
                  w_ada, b_ada, g_norm1, w_in, g_q_lora, w_uq, g_kv_lora, w_ukv, g_q_head, g_k_head,
                  w_gate_up, b_gate_up, g_gla_out, w_out, g_norm2, w_up, w_down):
    mod = (jax.nn.silu(c.astype(jnp.float32)) @ w_ada.astype(jnp.float32)
           + b_ada.astype(jnp.float32)).astype(x.dtype)
    shift1, scale1, gate1, shift2, scale2, gate2 = [m[:, None, :] for m in jnp.split(mod, 6, axis=-1)]
    h = rms_norm(x, g_norm1) * (1 + scale1) + shift1
    proj = h @ w_in
    split_points = [int(i) for i in np.cumsum(IN_SPLITS)[:-1]]
    c_q, c_kv, k_rope_raw, gq, gk, gv, g_lr, g_r = jnp.split(proj, split_points, axis=-1)
    a_out, latent_new, krope_new = mla_mixer(c_q, c_kv, k_rope_raw, past_latent, past_krope,
                                             g_q_lora, w_uq, g_kv_lora, w_ukv, g_q_head, g_k_head)
    b_out, s_new = gla_mixer(gq, gk, gv, g_lr, g_r, gla_s0, w_gate_up, b_gate_up, g_gla_out)
    x = x + gate1 * (jnp.concatenate([a_out, b_out], axis=-1) @ w_out)
    h2 = rms_norm(x, g_norm2) * (1 + scale2) + shift2
    x = x + gate2 * (jnp.square(jax.nn.relu(h2 @ w_up)) @ w_down)
    return x, latent_new, krope_new, s_new


def setup_inputs(seed: int = 0) -> dict:
    key = jax.random.key(seed)
    ks = jax.random.split(key, 28)

    def nrm(k, shape, s):
        return jax.random.normal(k, shape, jnp.float32) * s

    def gain(k, shape):
        return 1.0 + 0.02 * jax.random.normal(k, shape, jnp.float32)

    return {
        'x_prompt': nrm(ks[0], (BATCH, SEQ, D_MODEL), 1.0),
        'x_sample': nrm(ks[1], (DEC_BATCH, DEC_SEQ, D_MODEL), 1.0),
        'cache_mla_latent': nrm(ks[2], (DEPTH, DEC_BATCH, PAST_LEN, KV_LORA), 1.0),
        'cache_mla_krope': nrm(ks[3], (DEPTH, DEC_BATCH, PAST_LEN, QK_ROPE), 1.0),
        'state_gla': nrm(ks[4], (DEPTH, DEC_BATCH, GLA_HEADS, GLA_DK, GLA_DV), 1.0),
        'c_prompt': nrm(ks[5], (BATCH, D_MODEL), 1.0),
        'c_sample': nrm(ks[6], (DEC_BATCH, D_MODEL), 1.0),
        'w_ada': nrm(ks[7], (DEPTH, D_MODEL, 6 * D_MODEL), D_MODEL ** -0.5),
        'b_ada': nrm(ks[8], (DEPTH, 6 * D_MODEL), 0.02),
        'g_norm1': gain(ks[9], (DEPTH, D_MODEL)),
        'w_in': nrm(ks[10], (DEPTH, D_MODEL, IN_COLS), D_MODEL ** -0.5),
        'g_q_lora': gain(ks[11], (DEPTH, Q_LORA)),
        'w_uq': nrm(ks[12], (DEPTH, Q_LORA, MLA_HEADS * QK_DIM), Q_LORA ** -0.5),
        'g_kv_lora': gain(ks[13], (DEPTH, KV_LORA)),
        'w_ukv': nrm(ks[14], (DEPTH, KV_LORA, MLA_HEADS * (QK_NOPE + V_DIM)), KV_LORA ** -0.5),
        'g_q_head': gain(ks[15], (DEPTH, QK_DIM)),
        'g_k_head': gain(ks[16], (DEPTH, QK_DIM)),
        'w_gate_up': nrm(ks[17], (DEPTH, GLA_GATE_RANK, GLA_HEADS * GLA_DK), GLA_GATE_RANK ** -0.5),
        'b_gate_up': nrm(ks[18], (DEPTH, GLA_HEADS * GLA_DK), 0.1),
        'g_gla_out': gain(ks[19], (DEPTH, GLA_WIDTH)),
        'w_out': nrm(ks[20], (DEPTH, MIX_WIDTH, D_MODEL), MIX_WIDTH ** -0.5),
        'g_norm2': gain(ks[21], (DEPTH, D_MODEL)),
        'w_up': nrm(ks[22], (DEPTH, D_MODEL, D_FF), D_MODEL ** -0.5),
        'w_down': nrm(ks[23], (DEPTH, D_FF, D_MODEL), D_FF ** -0.5),
    }


def reference(x_prompt, x_sample, cache_mla_latent, cache_mla_krope, state_gla, c_prompt, c_sample,
              w_ada, b_ada, g_norm1, w_in, g_q_lora, w_uq, g_kv_lora, w_ukv, g_q_head, g_k_head,
              w_gate_up, b_gate_up, g_gla_out, w_out, g_norm2, w_up, w_down):
    B = x_prompt.shape[0]
    y_p, y_s = x_prompt, x_sample
    lat_p, kr_p, st_p, lat_s, kr_s, st_s = [], [], [], [], [], []
    for l in range(DEPTH):
        params = (w_ada[l], b_ada[l], g_norm1[l], w_in[l], g_q_lora[l], w_uq[l], g_kv_lora[l], w_ukv[l],
                  g_q_head[l], g_k_head[l], w_gate_up[l], b_gate_up[l], g_gla_out[l], w_out[l],
                  g_norm2[l], w_up[l], w_down[l])
        empty_lat = jnp.zeros((B, 0, KV_LORA), x_prompt.dtype)
        empty_kr = jnp.zeros((B, 0, QK_ROPE), x_prompt.dtype)
        zero_state = jnp.zeros((B, GLA_HEADS, GLA_DK, GLA_DV), x_prompt.dtype)
        y_p, a, b, s = encoder_layer(y_p, c_prompt, empty_lat, empty_kr, zero_state, *params)
        lat_p.append(a); kr_p.append(b); st_p.append(s)
        y_s, a, b, s = encoder_layer(y_s, c_sample, cache_mla_latent[l], cache_mla_krope[l], state_gla[l], *params)
        lat_s.append(a); kr_s.append(b); st_s.append(s)
    return (y_p, y_s, jnp.stack(lat_p), jnp.stack(kr_p), jnp.stack(st_p),
            jnp.stack(lat_s), jnp.stack(kr_s), jnp.stack(st_s))
```

```python
import contextlib
import numpy as np
import concourse.bass as bass
import concourse.mybir as mybir
from concourse.bass_utils import run_bass_kernel_spmd

F32 = mybir.dt.float32
BF16 = mybir.dt.bfloat16
AF = mybir.ActivationFunctionType
ALU = mybir.AluOpType
EPS = 1e-6
NT = 256
NB = NT // 128
NCH = NT // 64
NEG = -30000.0


class Sched:
    ENGS = ("pe", "act", "dve", "pool", "sp")

    def __init__(self, nc):
        self.nc = nc
        self.ops = {e: [] for e in self.ENGS}
        self.last_w = {}
        self.readers = {}
        self.dma_cnt = {}
        self.pending = {e: None for e in self.ENGS}
        self.stopped = False

    def barrier(self):
        if self.stopped:
            return
        toks = []
        for e in self.ENGS:
            for i in range(len(self.ops[e]) - 1, -1, -1):
                if self.ops[e][i]["dma"] is None:
                    toks.append(("c", e, i))
                    break
        for k, c in self.dma_cnt.items():
            toks.append(("d", k, c))
        for e in self.ENGS:
            self.pending[e] = list(toks)

    def add(self, eng, fn, r=(), w=(), dma=None):
        if self.stopped:
            return None
        ops = self.ops[eng]
        idx = len(ops)
        deps = set()
        for k in r:
            t = self.last_w.get(k)
            if t is not None:
                deps.add(t)
        for k in w:
            t = self.last_w.get(k)
            if t is not None:
                deps.add(t)
            for t in self.readers.get(k, ()):
                deps.add(t)
        if self.pending[eng] is not None:
            deps.update(self.pending[eng])
            self.pending[eng] = None
        if dma is not None:
            c = self.dma_cnt.get(dma, 0) + 16
            self.dma_cnt[dma] = c
            tok = ("d", dma, c)
        else:
            tok = ("c", eng, idx)
        fdeps = []
        for t in deps:
            if t[0] == "c":
                if t[1] == eng and eng == "pe":
                    continue
                if t[1] == eng and t[2] == idx:
                    continue
                self.ops[t[1]][t[2]]["sig"] = True
            fdeps.append(t)
        ops.append(dict(fn=fn, deps=fdeps, sig=False, dma=dma))
        for k in w:
            self.last_w[k] = tok
            self.readers[k] = []
        for k in r:
            self.readers.setdefault(k, []).append(tok)
        return tok

    def emit(self, final_eng="sp"):
        nc = self.nc
        with contextlib.ExitStack() as es:
            esem = {e: es.enter_context(nc.semaphore("s_" + e)) for e in self.ENGS}
            dsem = {}
            for i, k in enumerate(self.dma_cnt):
                dsem[k] = es.enter_context(nc.semaphore("d%d" % i))
            sigidx = {}
            for e in self.ENGS:
                c = 0
                arr = []
                for op in self.ops[e]:
                    if op["sig"]:
                        c += 1
                    arr.append(c)
                sigidx[e] = arr
            block = es.enter_context(nc.Block())

            def run(e, h):
                waited = {}
                for op in self.ops[e]:
                    for t in op["deps"]:
                        if t[0] == "c":
                            key = ("c", t[1])
                            val = sigidx[t[1]][t[2]]
                            sem = esem[t[1]]
                        else:
                            key = ("d", t[1])
                            val = t[2]
                            sem = dsem[t[1]]
                        if waited.get(key, 0) >= val:
                            continue
                        waited[key] = val
                        h.wait_ge(sem, val)
                    ins = op["fn"](h)
                    if op["dma"] is not None:
                        ins.then_inc(dsem[op["dma"]], 16)
                    elif op["sig"]:
                        ins.then_inc(esem[e], 1)
                if e == final_eng:
                    for k, c in self.dma_cnt.items():
                        if waited.get(("d", k), 0) < c:
                            h.wait_ge(dsem[k], c)

            @block.tensor
            def _(h):
                run("pe", h)

            @block.scalar
            def _(h):
                run("act", h)

            @block.vector
            def _(h):
                run("dve", h)

            @block.gpsimd
            def _(h):
                run("pool", h)

            @block.sync
            def _(h):
                run("sp", h)


class Rot:
    def __init__(self, items):
        self.items = items
        self.i = 0

    def get(self):
        it = self.items[self.i % len(self.items)]
        self.i += 1
        return it


def OP(meth, *a, **kw):
    return lambda e: getattr(e, meth)(*a, **kw)


def seq(fns):
    def f(e):
        ins = None
        for g in fns:
            ins = g(e)
        return ins
    return f


def mm(out, lhsT, rhs, start=True, stop=True):
    return OP("matmul", out, lhsT, rhs, start=start, stop=stop)


def act(out, in_, func, **kw):
    return OP("activation", out=out, in_=in_, func=func, **kw)


CQ0, CKV0, KR0, GQ0, GK0, GV0, GLR0, GR0 = 0, 384, 640, 704, 960, 1216, 1728, 1744


class _Stop(Exception):
    pass


def build(dbg=False, stop=None):
    def ck(tag):
        if stop == tag:
            S.stopped = True
    nc = bass.Bass("TRN2", target_bir_lowering=False)
    S = Sched(nc)

    def din(name, shape):
        return nc.dram_tensor(name, list(shape), F32, kind="ExternalInput").ap()

    def dout(name, shape):
        return nc.dram_tensor(name, list(shape), F32, kind="ExternalOutput").ap()

    xpre = din("xpre", [2048, 1024])
    xown = din("xown", [2048, 1024])
    xsm = din("xsm", [256, 1024])
    latc = din("latc", [4, 2048, 256])
    krc = din("krc", [4, 2048, 64])
    stc = din("stc", [4, 2, 128, 128])
    cT = din("cT", [1024, 5])
    flag = din("flag", [128, 2])
    w_ada = din("w_ada", [1024, 6144])
    b_ada_row = din("b_ada_row", [1, 6144])
    b_adaT = din("b_adaT", [128, 48])
    g1T = din("g1T", [128, 8])
    g2T = din("g2T", [128, 8])
    gqlT = din("gqlT", [128, 3])
    gqn = din("gqn", [128, 1])
    gkn = din("gkn", [128, 1])
    gqr = din("gqr", [128, 1])
    gkr_row = din("gkr_row", [1, 64])
    gkv_row = din("gkv_row", [1, 256])
    ggoT = din("ggoT", [128, 4])
    w_in = din("w_in", [1024, 2256])
    w_uq_ext = din("w_uq_ext", [384, 1024])
    w_ukv_p = din("w_ukv_p", [256, 1024])
    w_gu_aug = din("w_gu_aug", [32, 256])
    w_out = din("w_out", [1024, 1024])
    w_up = din("w_up", [1024, 4096])
    w_down = din("w_down", [4096, 1024])
    kcs_pre = din("kcs_pre", [128, 16, 128])
    kcs_own = din("kcs_own", [128, 16, 128])
    kcs_sn = din("kcs_sn", [128, 2, 128])
    qcs_own = din("qcs_own", [128, 2048])
    qcs_s = din("qcs_s", [128, 256])
    c_ident = din("c_ident", [128, 128])
    c_tri = din("c_tri", [128, 128])
    c_ut = din("c_ut", [128, 128])
    c_mask = din("c_mask", [128, 256])
    c_sel = din("c_sel", [5, 3, 128])
    c_comb = din("c_comb", [128, 128])
    c_hsel = din("c_hsel", [128, 2])

    y_own = dout("y_own", [2048, 1024])
    y_s = dout("y_s", [256, 1024])
    lat_own = dout("lat_own", [2048, 256])
    kr_own = dout("kr_own", [2048, 64])
    st_p = dout("st_p", [2, 128, 128])
    lat_s = dout("lat_s", [256, 256])
    kr_s = dout("kr_s", [256, 64])
    st_s = dout("st_s", [4, 2, 128, 128])
    x1s = nc.dram_tensor("x1s", [2304, 1024], F32).ap()

    P = [nc.alloc_psum_tensor("P%d" % i, [128, 512], F32) for i in range(8)]

    def PK(i):
        return "P%d" % i

    with contextlib.ExitStack() as glob:
        def gsb(name, shape, dt=F32):
            return glob.enter_context(nc.sbuf_tensor(name, list(shape), dt))

        ident = gsb("ident", [128, 128])
        tri = gsb("tri", [128, 128])
        ut = gsb("ut", [128, 128])
        maskr = gsb("maskr", [128, 256])
        sel = gsb("sel", [5, 3, 128])
        comb = gsb("comb", [128, 128])
        flg = gsb("flg", [128, 2])
        hsel = gsb("hsel", [128, 2])
        ones_bf = gsb("ones_bf", [128, 128], BF16)
        ones_f = gsb("ones_f", [1, 128])
        gm1 = gsb("gm1", [128, 8, 5])
        sh1 = gsb("sh1", [128, 8, 5])
        gm2 = gsb("gm2", [128, 8, 5])
        sh2 = gsb("sh2", [128, 8, 5])
        gater = gsb("gater", [5, 2, 1024])
        g2c = gsb("g2c", [128, 8])
        for (t, d) in ((ident, c_ident), (tri, c_tri), (ut, c_ut), (maskr, c_mask), (comb, c_comb), (flg, flag), (g2c, g2T), (hsel, c_hsel)):
            S.add("sp", (OP("dma_start", out=t[:, :], in_=d[:, :])), w=[t.name], dma=t.name)
        S.add("sp", OP("dma_start", out=sel[:, :, :], in_=c_sel[:, :, :]), w=["sel"], dma="sel")
        S.add("dve", OP("memset", ones_bf[:, :], 1.0), w=["ones_bf"])
        S.add("dve", OP("memset", ones_f[:, :], 1.0), w=["ones_f"])

        GEN_OUT = [5, 6, 0, 1, 4]
        GEN_IN = [5, 6]
        gen4 = Rot(list(GEN_OUT))
        SCB = [0, 1, 4]

        def set_gen(lst):
            gen4.items = list(lst)
            gen4.i = 0

        def gp():
            i = gen4.get()
            return P[i], PK(i)

        try:
            with contextlib.ExitStack() as ph1:
                def psb(name, shape, dt=F32):
                    return ph1.enter_context(nc.sbuf_tensor(name, list(shape), dt))

                win = psb("win", [128, 8, 2256], BF16)
                wuq = psb("wuq", [128, 3, 1024], BF16)
                wukv = psb("wukv", [128, 2, 1024], BF16)
                wout = psb("wout", [128, 8, 1024], BF16)
                wgu = psb("wgu", [32, 256], BF16)
                g1c = psb("g1c", [128, 8])
                gqk = psb("gqk", [128, 1])
                gkn_t = psb("gkn_t", [128, 1])
                gqr_t = psb("gqr_t", [128, 1])
                ggo = psb("ggo", [128, 4])
                gkr_bc = psb("gkr_bc", [128, NB, 64])
                gkv_bc = psb("gkv_bc", [128, 256])
                g1bc = psb("g1bc", [128, 1024])
                zcol = psb("zcol", [128, 1])
                glrT = psb("glrT", [32, NT], BF16)
                Sst = psb("Sst", [128, 2, 128])
                S.add("dve", OP("memset", zcol[:, :], 0.0), w=["zcol"])
                S.add("dve", OP("memset", glrT[:, :], 1.0), w=["glrT"])
                S.add("dve", OP("memset", Sst[:, :, :], 0.0), w=[("Sst", 0), ("Sst", 1)])
                for (t, d) in ((g1c, g1T), (gkn_t, gkn), (gqr_t, gqr), (ggo, ggoT)):
                    S.add("sp", (OP("dma_start", out=t[:, :], in_=d[:, :])), w=[t.name], dma=t.name)
                S.add("sp", OP("dma_start", out=gqk[:, :], in_=gqn[:, :]), w=["gqk"], dma="gqk")
                S.add("dve", OP("tensor_tensor", out=gqk[:, :], in0=gqk[:, :], in1=gkn_t[:, :], op=ALU.mult), r=["gkn_t", "gqk"], w=["gqk"])

                with contextlib.ExitStack() as ph0:
                    def zsb(name, shape, dt=F32):
                        return ph0.enter_context(nc.sbuf_tensor(name, list(shape), dt))
                    wa = [zsb("wa%d" % i, [128, 8, 512], BF16) for i in range(2)]
                    wuq_f = zsb("wuq_f", [128, 3, 1024])
                    ctile = zsb("ctile", [128, 8, 5])
                    c_e = zsb("c_e", [128, 8, 5])
                    siluT = zsb("siluT", [128, 8, 5], BF16)
                    badT = zsb("badT", [128, 48])
                    badr = zsb("badr", [1, 6144])
                    badr_bf = zsb("badr_bf", [1, 6144], BF16)
                    ones5 = zsb("ones5", [1, 8], BF16)
                    gql = zsb("gql", [128, 3])
                    rowt = zsb("rowt", [1, 320])
                    modT = zsb("modT", [128, 32, 5])

                    S.add("sp", OP("dma_start", out=ctile[:, :, :], in_=cT.rearrange("(kc p) b -> p kc b", p=128)), w=["ctile"], dma="ctile")
                    S.add("sp", OP("dma_start", out=badT[:, :], in_=b_adaT[:, :]), w=["badT"], dma="badT")
                    S.add("sp", OP("dma_start", out=badr[:, :], in_=b_ada_row[:, :]), w=["badr"], dma="badr")
                    S.add("sp", OP("dma_start", out=gql[:, :], in_=gqlT[:, :]), w=["gql"], dma="gql")
                    S.add("sp", OP("dma_start", out=rowt[:, 0:256], in_=gkv_row[:, :]), w=["rowt0"], dma="rowt0")
                    S.add("sp", OP("dma_start", out=rowt[:, 256:320], in_=gkr_row[:, :]), w=["rowt1"], dma="rowt1")
                    S.add("sp", OP("dma_start", out=wuq_f[:, :, :], in_=w_uq_ext.rearrange("(kc p) n -> p kc n", p=128)), w=["wuq_f"], dma="wuq_f")
                    S.add("dve", OP("memset", ones5[:, :], 1.0), w=["ones5"])
                    S.add("dve", OP("tensor_copy", out=badr_bf[:, :], in_=badr[:, :]), r=["badr"], w=["badr_bf"])
                    S.add("act", act(c_e[:, :, :], ctile[:, :, :], AF.Exp, scale=-1.0), r=["ctile"], w=["c_e"])
                    S.add("dve", OP("tensor_scalar", out=c_e[:, :, :], in0=c_e[:, :, :], scalar1=1.0, scalar2=None, op0=ALU.add), r=["c_e"], w=["c_e"])
                    S.add("dve", OP("reciprocal", out=c_e[:, :, :], in_=c_e[:, :, :]), r=["c_e"], w=["c_e"])
                    S.add("dve", OP("tensor_tensor", out=siluT[:, :, :], in0=ctile[:, :, :], in1=c_e[:, :, :], op=ALU.mult), r=["c_e", "ctile"], w=["siluT"])

                    def wa_load(ct, slot):
                        S.add("pool", OP("dma_start", out=wa[slot][:, :, :], in_=w_ada[:, ct * 512:(ct + 1) * 512].rearrange("(kc p) n -> p kc n", p=128)),
                              w=["wa%d" % slot], dma="wa%d" % slot)
                    order = [0, 1, 2, 3, 4, 5, 6, 7, 8, 9, 10, 11]
                    wa_load(order[0], 0)
                    fidx = {}
                    fi = 0
                    for ct in order:
                        if ct in (0, 1, 2, 3, 6, 7, 8, 9):
                            for cc in range(4):
                                fidx[(ct, cc)] = fi
                                fi += 1
                    PF, PFk = P[4], PK(4)
                    for n, ct in enumerate(order):
                        slot = n % 2
                        if n + 1 < len(order):
                            wa_load(order[n + 1], (n + 1) % 2)
                        if ct in (0, 1, 2, 3, 6, 7, 8, 9):
                            fns = []
                            for cc in range(4):
                                f = fidx[(ct, cc)]
                                for kc in range(8):
                                    fns.append(mm(PF[:, f * 8:f * 8 + 5], wa[slot][:, kc, cc * 128:(cc + 1) * 128], siluT[:, kc, :], start=(kc == 0), stop=(kc == 7)))
                            S.add("pe", seq(fns), r=["wa%d" % slot, "siluT"], w=[PFk])
                        else:
                            gi = 0 if ct in (4, 5) else 1
                            hf = ct % 2
                            Pr, Prk = P[5 + (n % 2)], PK(5 + (n % 2))
                            fns = [mm(Pr[0:5, :], siluT[:, kc, :], wa[slot][:, kc, :], start=(kc == 0), stop=False) for kc in range(8)]
                            fns.append(mm(Pr[0:5, :], ones5[0:1, 0:5], badr_bf[0:1, ct * 512:(ct + 1) * 512], start=False, stop=True))
                            S.add("pe", seq(fns), r=["wa%d" % slot, "siluT", "ones5", "badr_bf"], w=[Prk])
                            S.add("act", act(gater[:, gi, hf * 512:(hf + 1) * 512], Pr[0:5, :], AF.Copy), r=[Prk], w=["gater"])
                    for (ct, cc), f in fidx.items():
                        chunk = ct * 4 + cc
                        S.add("dve", (OP("tensor_scalar", out=modT[:, f, :], in0=PF[:, f * 8:f * 8 + 5], scalar1=badT[:, chunk:chunk + 1], scalar2=None, op0=ALU.add)),
                              r=[PFk, "badT"], w=["modT"])
                    for kc in range(8):
                        S.add("dve", (OP("tensor_scalar", out=gm1[:, kc, :], in0=modT[:, 8 + kc, :], scalar1=1.0, scalar2=g1c[:, kc:kc + 1], op0=ALU.add, op1=ALU.mult)),
                              r=["modT", "g1c"], w=["gm1"])
                        S.add("dve", (OP("tensor_scalar", out=gm2[:, kc, :], in0=modT[:, 24 + kc, :], scalar1=1.0, scalar2=g2c[:, kc:kc + 1], op0=ALU.add, op1=ALU.mult)),
                              r=["modT", "g2c"], w=["gm2"])
                    S.add("dve", OP("tensor_copy", out=sh1[:, :, :], in_=modT[:, 0:8, :]), r=["modT"], w=["sh1"])
                    S.add("dve", OP("tensor_copy", out=sh2[:, :, :], in_=modT[:, 16:24, :]), r=["modT"], w=["sh2"])

                    for hf in range(2):
                        S.add("pool", (OP("dma_start", out=win[:, :, hf * 1128:(hf + 1) * 1128], in_=w_in[:, hf * 1128:(hf + 1) * 1128].rearrange("(kc p) n -> p kc n", p=128))),
                              w=["win%d" % hf], dma="win%d" % hf)
                    S.add("pool", OP("dma_start", out=wukv[:, :, :], in_=w_ukv_p.rearrange("(kc p) n -> p kc n", p=128)), w=["wukv"], dma="wukv")
                    S.add("pool", OP("dma_start", out=wgu[:, :], in_=w_gu_aug[:, :]), w=["wgu"], dma="wgu")
                    S.add("pool", OP("dma_start", out=wout[:, :, :], in_=w_out.rearrange("(kc p) n -> p kc n", p=128)), w=["wout"], dma="wout")
                    for kc in range(3):
                        S.add("dve", (OP("tensor_scalar", out=wuq[:, kc, :], in0=wuq_f[:, kc, :], scalar1=gql[:, kc:kc + 1], scalar2=None, op0=ALU.mult)),
                              r=["wuq_f", "gql"], w=["wuq"])
                        for h in range(4):
                            c0 = 768 + h * 64
                            S.add("dve", (OP("tensor_scalar", out=wuq[:, kc, c0:c0 + 32], in0=wuq[:, kc, c0:c0 + 32], scalar1=-1.0, scalar2=None, op0=ALU.mult)),
                                  r=["wuq"], w=["wuq"])
                    Pb, Pbk = P[7], PK(7)
                    S.add("pe", mm(Pb[:, 0:320], ones_f[0:1, :], rowt[0:1, 0:320]), r=["ones_f", "rowt0", "rowt1"], w=[Pbk])
                    S.add("act", act(gkv_bc[:, :], Pb[:, 0:256], AF.Copy), r=[Pbk], w=["gkv_bc"])
                    for b in range(NB):
                        S.add("act", (OP("activation", out=gkr_bc[:, b, :], in_=Pb[:, 256:320], func=AF.Copy)), r=[Pbk], w=["gkr_bc"])
                S.barrier()
                ck('p0')

                Kn = psb("Kn", [128, 4, 4096], BF16)
                Kr = psb("Kr", [128, 2048], BF16)
                Vs = psb("Vs", [128, 32, 512], BF16)
                sclK = psb("sclK", [128, 32, 4])
                Sbf = psb("Sbf", [128, 2, 2, 128], BF16)
                xt = psb("xt", [128, 1024])
                hT = psb("hT", [128, 8, NT], BF16)
                lat_tm = psb("lat_tm", [128, NB, 256])
                kr_tm = psb("kr_tm", [128, NB, 64])
                latT = psb("latT", [128, 2, NT], BF16)
                gk_tm = psb("gk_tm", [128, NB, 256])
                gv_tm = psb("gv_tm", [128, NB, 512], BF16)
                l_tm = psb("l_tm", [128, NB, 256])
                khat = psb("khat", [128, NB, 256], BF16)
                cqT = psb("cqT", [128, 3, NT], BF16)
                qn = psb("qn", [128, 4, NT], BF16)
                qr = psb("qr", [128, 4, 2, NT], BF16)
                mixT = psb("mixT", [128, 8, NT], BF16)
                xr = xt
                kcs = psb("kcs", [128, NB, 128])
                qcs = psb("qcs", [128, NT])
                epst = psb("epst", [128, NT])
                stat = psb("stat", [128, 64])
                dec = psb("dec", [128, 2, NCH])
                TFl = [psb("TF%d" % i, [128, NT]) for i in range(10)]
                TBl = [psb("TB%d" % i, [128, NT], BF16) for i in range(11)]
                PTb = [psb("PT%d" % i, [128, NT], BF16) for i in range(4)]
                TFg = Rot([(t, t.name) for t in TFl[0:5]])
                TFq = Rot([(t, t.name) for t in TFl[5:8]])
                TFk = Rot([(t, t.name) for t in TFl[8:10]])
                TBg = Rot([(t, t.name) for t in TBl[0:6]])
                TBq = Rot([(t, t.name) for t in TBl[6:11]])
                PT = Rot([(t, t.name) for t in PTb])
                lat2 = xt[:, 0:512].rearrange("p (b d) -> p b d", b=NB)
                kr2 = xt[:, 512:640].rearrange("p (b d) -> p b d", b=NB)
                kcs2 = xt[:, 640:896].rearrange("p (b d) -> p b d", b=NB)
                latT2 = hT[:, 0:2, :]
                BS0 = dict(lat=lat_tm[:, :, :], latk="lat_tm", kr=kr_tm[:, :, :], krk="kr_tm", kcs=kcs[:, :, :], kcsk="kcs", latT=latT[:, :, :], latTk="latT")
                BS1 = dict(lat=lat2, latk="xt_lat", kr=kr2, krk="xt_kr", kcs=kcs2, kcsk="xt_kcs", latT=latT2, latTk="hT")
                XTK = ["xt", "xt_lat", "xt_kr", "xt_kcs"]


                WIN = ["win0", "win1"]
                S.add("dve", OP("memset", Kr[:, :], 0.0), w=[("Kr", kt_) for kt_ in range(2048 // NT)])
                S.add("dve", OP("memset", qr[:, :, :, :], 0.0), w=[("qr", h) for h in range(4)])

                def load_g1bc(v):
                    for fh in range(2):
                        Pg, Pgk = gp()
                        S.add("pe", (OP("matmul", Pg[:, :], sel[:, v, :], gater[:, 0, fh * 512:(fh + 1) * 512], start=True, stop=True)),
                              r=["sel", "gater"], w=[Pgk])
                        S.add("act", (OP("activation", out=g1bc[:, fh * 512:(fh + 1) * 512], in_=Pg[:, :], func=AF.Copy)), r=[Pgk], w=["g1bc"])

                def front(xsrc, r0, segs, hTb=None, hTk="hT", TFp=None):
                    hTb = hT if hTb is None else hTb
                    TFp = TFg if TFp is None else TFp
                    for b in range(NB):
                        S.add("sp", (OP("dma_start", out=xt[:, :], in_=xsrc[r0 + b * 128:r0 + (b + 1) * 128, :])), w=XTK, dma="xt")
                        jt, jk = TFp.get()
                        for q4 in range(4):
                            S.add("act", (OP("activation", out=jt[:, :], in_=xt[:, q4 * 256:(q4 + 1) * 256], func=AF.Square, accum_out=stat[:, q4:q4 + 1])),
                                  r=["xt"], w=[jk, "stat"])
                        S.add("dve", OP("tensor_reduce", out=stat[:, 4:5], in_=stat[:, 0:4], axis=mybir.AxisListType.X, op=ALU.add), r=["stat"], w=["stat"])
                        S.add("act", act(stat[:, 5:6], stat[:, 4:5], AF.Ln, scale=1.0 / 1024, bias=EPS), r=["stat"], w=["stat"])
                        S.add("act", act(stat[:, 6:7], stat[:, 5:6], AF.Exp, scale=-0.5), r=["stat"], w=["stat"])
                        S.add("dve", OP("tensor_scalar", out=xt[:, :], in0=xt[:, :], scalar1=stat[:, 6:7], scalar2=None, op0=ALU.mult), r=["xt", "stat"], w=["xt"])
                        yield
                        for k2 in range(2):
                            Pt, Ptk = gp()
                            S.add("pe", seq([(OP("transpose", Pt[:, kk * 128:(kk + 1) * 128], xt[:, (k2 * 4 + kk) * 128:(k2 * 4 + kk + 1) * 128], ident[:, :])) for kk in range(4)]),
                                  r=["xt", "ident"], w=[Ptk])
                            for kk in range(4):
                                kc = k2 * 4 + kk
                                for (c0, ncol, m) in segs:
                                    lo = max(c0, b * 128)
                                    hi = min(c0 + ncol, (b + 1) * 128)
                                    if lo >= hi:
                                        continue
                                    S.add("act", (OP("activation",
                                        out=hTb[:, kc, lo:hi], in_=Pt[:, kk * 128 + lo - b * 128:kk * 128 + hi - b * 128], func=AF.Identity,
                                        scale=gm1[:, kc, m:m + 1], bias=sh1[:, kc, m:m + 1])), r=[Ptk, "gm1", "sh1"], w=[hTk])
                        yield

                def kvproj(lat_dst, kr_dst, r0, hTb=None, hTk="hT", TFp=None):
                    hTb = hT if hTb is None else hTb
                    TFp = TFg if TFp is None else TFp
                    for b in range(NB):
                        Pq, Pqk = gp()
                        S.add("pe", seq([mm(Pq[:, 0:320], hTb[:, kc, b * 128:(b + 1) * 128], win[:, kc, CKV0:CKV0 + 320], start=(kc == 0), stop=(kc == 7)) for kc in range(8)]),
                              r=[hTk] + WIN, w=[Pqk])
                        jt, jk = TFp.get()
                        S.add("act", (OP("activation", out=jt[:, :], in_=Pq[:, 0:256], func=AF.Square, accum_out=stat[:, 8 + b:9 + b])), r=[Pqk], w=[jk, "stat"])
                        S.add("act", (OP("activation", out=stat[:, 12 + b:13 + b], in_=stat[:, 8 + b:9 + b], func=AF.Ln, scale=1.0 / 256, bias=EPS)), r=["stat"], w=["stat"])
                        S.add("act", (OP("activation", out=stat[:, 16 + b:17 + b], in_=stat[:, 12 + b:13 + b], func=AF.Exp, scale=-0.5)), r=["stat"], w=["stat"])
                        S.add("dve", (OP("scalar_tensor_tensor", out=lat_tm[:, b, :], in0=Pq[:, 0:256], scalar=stat[:, 16 + b:17 + b], in1=gkv_bc[:, :], op0=ALU.mult, op1=ALU.mult)),
                              r=[Pqk, "stat", "gkv_bc"], w=["lat_tm"])
                        S.add("act", (OP("activation", out=kr_tm[:, b, :], in_=Pq[:, 256:320], func=AF.Copy)), r=[Pqk], w=["kr_tm"])
                        yield
                    if lat_dst is not None:
                        S.add("pool", OP("dma_start", out=lat_dst[r0:r0 + NT, :].rearrange("(b p) d -> p b d", p=128), in_=lat_tm[:, :, :]), r=["lat_tm"], dma="lat_o")
                        S.add("pool", OP("dma_start", out=kr_dst[r0:r0 + NT, :].rearrange("(b p) d -> p b d", p=128), in_=kr_tm[:, :, :]), r=["kr_tm"], dma="kr_o")

                def lat_transpose(bs=None):
                    bs = BS0 if bs is None else bs
                    for lc in range(2):
                        Pt, Ptk = gp()
                        S.add("pe", seq([(OP("transpose", Pt[:, b * 128:(b + 1) * 128], bs["lat"][:, b, lc * 128:(lc + 1) * 128], ident[:, :])) for b in range(NB)]),
                              r=[bs["latk"], "ident"], w=[Ptk])
                        S.add("dve", (OP("tensor_copy", out=bs["latT"][:, lc, :], in_=Pt[:, 0:NT])), r=[Ptk], w=[bs["latTk"]])
                        yield

                def kside(key0, cs_src, cs_b0, bs=None, TFp=None, load_cs=True):
                    kb0 = key0 // 128
                    bs = BS0 if bs is None else bs
                    TFp = TFg if TFp is None else TFp
                    latT_, latTk_, kr_, krk_, kcs_, kcsk_ = bs["latT"], bs["latTk"], bs["kr"], bs["krk"], bs["kcs"], bs["kcsk"]
                    if load_cs:
                        S.add("sp", OP("dma_start", out=kcs_, in_=cs_src[:, cs_b0:cs_b0 + NB, :]), w=[kcsk_], dma=kcsk_)
                    for h in range(4):
                        Pq, Pqk = gp()
                        S.add("pe", seq([mm(Pq[:, 0:NT], wukv[:, lc, h * 128:(h + 1) * 128], latT_[:, lc, :], start=(lc == 0), stop=(lc == 1)) for lc in range(2)]),
                              r=["wukv", latTk_], w=[Pqk])
                        S.add("act", (OP("activation", out=Kn[:, h, key0:key0 + NT], in_=Pq[:, 0:NT], func=AF.Copy)), r=[Pqk], w=[("Kn", key0 // NT)])
                        yield
                    for b in range(NB):
                        Pq, Pqk = gp()
                        S.add("pe", seq([mm(Pq[:, :], latT_[:, lc, b * 128:(b + 1) * 128], wukv[:, lc, 0:512], start=(lc == 0), stop=(lc == 1)) for lc in range(2)]),
                              r=["wukv", latTk_], w=[Pqk])
                        jt, jk = TFp.get()
                        for h in range(4):
                            S.add("act", (OP("activation", out=jt[:, 0:128], in_=Pq[:, h * 128:(h + 1) * 128], func=AF.Square, accum_out=stat[:, 20 + b * 4 + h:21 + b * 4 + h])),
                                  r=[Pqk], w=[jk, "stat"])
                        S.add("act", (OP("activation", out=jt[:, 128:192], in_=kr_[:, b, :], func=AF.Square, accum_out=stat[:, 28 + b:29 + b])), r=[krk_], w=[jk, "stat"])
                        S.add("dve", (OP("tensor_scalar", out=stat[:, 32 + b * 4:36 + b * 4], in0=stat[:, 20 + b * 4:24 + b * 4], scalar1=stat[:, 28 + b:29 + b], scalar2=None, op0=ALU.add)),
                              r=["stat"], w=["stat"])
                        S.add("act", (OP("activation", out=stat[:, 40 + b * 4:44 + b * 4], in_=stat[:, 32 + b * 4:36 + b * 4], func=AF.Ln, scale=1.0 / 192, bias=EPS)), r=["stat"], w=["stat"])
                        S.add("act", (OP("activation", out=sclK[:, kb0 + b, :], in_=stat[:, 40 + b * 4:44 + b * 4], func=AF.Exp, scale=-0.5, bias=float(-0.5 * np.log(192.0)))),
                              r=["stat"], w=[("sclK", key0 // NT)])
                        yield
                        Pv, Pvk = gp()
                        S.add("pe", seq([mm(Pv[:, :], latT_[:, lc, b * 128:(b + 1) * 128], wukv[:, lc, 512:1024], start=(lc == 0), stop=(lc == 1)) for lc in range(2)]),
                              r=["wukv", latTk_], w=[Pvk])
                        S.add("dve", (OP("tensor_copy", out=Vs[:, kb0 + b, :], in_=Pv[:, :])), r=[Pvk], w=[("Vs", key0 // NT)])
                        yield
                    t1, t1k = TFp.get()
                    t2, t2k = TFp.get()
                    t3, t3k = TFp.get()
                    v1 = t1[:, 0:NB * 64].rearrange("p (b d) -> p b d", b=NB)
                    v2 = t2[:, 0:NB * 64].rearrange("p (b d) -> p b d", b=NB)
                    v3 = t3[:, 0:NB * 64].rearrange("p (b d) -> p b d", b=NB)
                    S.add("dve", OP("tensor_tensor", out=v1, in0=kr_, in1=gkr_bc[:, :, :], op=ALU.mult), r=[krk_, "gkr_bc"], w=[t1k])
                    S.add("dve", OP("tensor_tensor", out=v2, in0=v1, in1=kcs_[:, :, 0:64], op=ALU.mult), r=[t1k, kcsk_], w=[t2k])
                    S.add("dve", OP("tensor_tensor", out=v3[:, :, 0:32], in0=v1[:, :, 32:64], in1=kcs_[:, :, 64:96], op=ALU.mult), r=[t1k, kcsk_], w=[t3k])
                    S.add("dve", OP("tensor_tensor", out=v3[:, :, 32:64], in0=v1[:, :, 0:32], in1=kcs_[:, :, 96:128], op=ALU.mult), r=[t1k, kcsk_], w=[t3k])
                    S.add("dve", OP("tensor_tensor", out=v2, in0=v2, in1=v3, op=ALU.add), r=[t2k, t3k], w=[t2k])
                    yield
                    Pt, Ptk = gp()
                    S.add("pe", seq([(OP("transpose", Pt[0:64, b * 128:(b + 1) * 128], v2[:, b, :], ident[:, :])) for b in range(NB)]),
                          r=[t2k, "ident"], w=[Ptk])
                    hb = 0 if key0 < 2048 else 64
                    kk0 = key0 % 2048
                    if hb == 0:
                        S.add("act", OP("activation", out=Kr[0:64, kk0:kk0 + NT], in_=Pt[0:64, 0:NT], func=AF.Copy), r=[Ptk], w=[("Kr", (key0 % 2048) // NT)])
                    else:
                        sh, shk = TFp.get()
                        S.add("act", OP("activation", out=sh[0:64, :], in_=Pt[0:64, 0:NT], func=AF.Copy), r=[Ptk], w=[shk])
                        P2, P2k = gp()
                        S.add("pe", OP("matmul", P2[:, 0:NT], comb[0:64, :], sh[0:64, :], start=True, stop=True), r=[shk, "comb"], w=[P2k])
                        S.add("act", OP("activation", out=Kr[64:128, kk0:kk0 + NT], in_=P2[64:128, 0:NT], func=AF.Copy), r=[P2k], w=[("Kr", (key0 % 2048) // NT)])
                    yield

                def gla_common(own, hTb=None, hTk="hT", TFp=None):
                    hTb = hT if hTb is None else hTb
                    TFp = TFg if TFp is None else TFp
                    Pl, Plk = gp()
                    S.add("pe", seq([mm(Pl[0:16, 0:NT], win[:, kc, GLR0:GLR0 + 16], hTb[:, kc, :], start=(kc == 0), stop=(kc == 7)) for kc in range(8)]),
                          r=[hTk] + WIN, w=[Plk])
                    S.add("act", OP("activation", out=glrT[0:16, :], in_=Pl[0:16, 0:NT], func=AF.Copy), r=[Plk], w=["glrT"])
                    yield
                    for b in range(NB):
                        Pk_, Pkk = gp()
                        S.add("pe", seq([mm(Pk_[:, 0:256], hTb[:, kc, b * 128:(b + 1) * 128], win[:, kc, GK0:GK0 + 256], start=(kc == 0), stop=(kc == 7)) for kc in range(8)]),
                              r=[hTk] + WIN, w=[Pkk])
                        S.add("act", (OP("activation", out=gk_tm[:, b, :], in_=Pk_[:, 0:256], func=AF.Copy)), r=[Pkk], w=["gk_tm"])
                        yield
                        Pv, Pvk = gp()
                        S.add("pe", seq([mm(Pv[:, :], hTb[:, kc, b * 128:(b + 1) * 128], win[:, kc, GV0:GV0 + 512], start=(kc == 0), stop=(kc == 7)) for kc in range(8)]),
                              r=[hTk] + WIN, w=[Pvk])
                        S.add("dve", (OP("tensor_copy", out=gv_tm[:, b, :], in_=Pv[:, :])), r=[Pvk], w=["gv_tm"])
                        yield
                        Pz, Pzk = gp()
                        S.add("pe", (OP("matmul", Pz[:, 0:256], glrT[:, b * 128:(b + 1) * 128], wgu[:, :], start=True, stop=True)), r=["glrT", "wgu"], w=[Pzk])
                        S.add("act", (OP("activation", out=l_tm[:, b, :], in_=Pz[:, 0:256], func=AF.Exp, scale=-1.0)), r=[Pzk], w=["l_tm"])
                        S.add("act", (OP("activation", out=l_tm[:, b, :], in_=l_tm[:, b, :], func=AF.Ln, bias=1.0)), r=["l_tm"], w=["l_tm"])
                        yield
                        Pc, Pck = gp()
                        S.add("pe", (OP("matmul", Pc[:, 0:256], ut[:, :], l_tm[:, b, :], start=True, stop=True)), r=["ut", "l_tm"], w=[Pck])
                        et, etk = TFp.get()
                        S.add("act", (OP("activation", out=et[:, :], in_=Pc[:, 0:256], func=AF.Exp)), r=[Pck], w=[etk])
                        S.add("dve", (OP("tensor_tensor", out=khat[:, b, :], in0=gk_tm[:, b, :], in1=et[:, :], op=ALU.mult)), r=[etk, "gk_tm"], w=["khat"])
                        yield
                    return None

                def bt_step(hp, TFp, need_e):
                    Pb_, Pbk_ = gp()
                    S.add("pe", seq([OP("matmul", Pb_[:, b * 128:(b + 1) * 128], l_tm[:, b, hp * 128:(hp + 1) * 128], tri[:, :], start=True, stop=True) for b in range(NB)]),
                          r=["l_tm", "tri"], w=[Pbk_])
                    S.add("act", OP("activation", out=dec[:, hp, :], in_=Pb_[:, 63:NT:64], func=AF.Exp), r=[Pbk_], w=[("dec", hp)])
                    if not need_e:
                        return None
                    eb, ebk = TFp.get()
                    enb, enbk = TFp.get()
                    S.add("act", OP("activation", out=eb[:, :], in_=Pb_[:, 0:NT], func=AF.Exp), r=[Pbk_], w=[ebk])
                    S.add("act", OP("activation", out=enb[:, :], in_=Pb_[:, 0:NT], func=AF.Exp, scale=-1.0), r=[Pbk_], w=[enbk])
                    return eb, ebk, enb, enbk

                def state_update(hp, ch, Pu, Puk):
                    b, par = ch // 2, ch % 2
                    fns = []
                    for hh in range(2):
                        h = hp * 2 + hh
                        fns.append(mm(Pu[hh * 64:(hh + 1) * 64, 0:128], khat[par * 64:(par + 1) * 64, b, h * 64:(h + 1) * 64], gv_tm[par * 64:(par + 1) * 64, b, h * 128:(h + 1) * 128]))
                    S.add("pe", seq(fns), r=["khat", "gv_tm"], w=[Puk])

                def gla_prefix(hTb=None, hTk="hT", TFp=None):
                    yield from gla_common(False, hTb, hTk, TFp)
                    for hp in range(2):
                        bt_step(hp, TFp, False)
                        yield
                        for ch in range(NCH):
                            Pu, Puk = gp()
                            state_update(hp, ch, Pu, Puk)
                            S.add("dve", (OP("scalar_tensor_tensor", out=Sst[:, hp, :], in0=Sst[:, hp, :], scalar=dec[:, hp, ch:ch + 1], in1=Pu[:, 0:128], op0=ALU.mult, op1=ALU.add)),
                                  r=[Puk, ("dec", hp), ("Sst", hp)], w=[("Sst", hp)])
                            yield

                def gla_own(per_chunk_state, st_dst, TFp=None, TBp=None):
                    TFp = TFg if TFp is None else TFp
                    TBp = TBg if TBp is None else TBp
                    yield from gla_common(True, None, "hT", TFp)
                    for hp in range(2):
                        eb, ebk, enb, enbk = bt_step(hp, TFp, True)
                        yield
                        Pq, Pqk = gp()
                        S.add("pe", seq([mm(Pq[:, 0:NT], win[:, kc, GQ0 + hp * 128:GQ0 + (hp + 1) * 128], hT[:, kc, :], start=(kc == 0), stop=(kc == 7)) for kc in range(8)]),
                              r=["hT"] + WIN, w=[Pqk])
                        qtls = []
                        for hh in range(2):
                            qtl, qtlk = TBp.get()
                            S.add("dve", OP("scalar_tensor_tensor", out=qtl[:, :], in0=Pq[:, 0:NT], scalar=hsel[:, hh:hh + 1], in1=eb[:, :], op0=ALU.mult, op1=ALU.mult),
                                  r=[Pqk, ebk, "hsel"], w=[qtlk])
                            qtls.append((qtl, qtlk))
                        yield
                        Pk2, Pk2k = gp()
                        S.add("pe", seq([mm(Pk2[:, 0:NT], win[:, kc, GK0 + hp * 128:GK0 + (hp + 1) * 128], hT[:, kc, :], start=(kc == 0), stop=(kc == 7)) for kc in range(8)]),
                              r=["hT"] + WIN, w=[Pk2k])
                        ktl, ktlk = TBp.get()
                        S.add("dve", OP("tensor_tensor", out=ktl[:, :], in0=Pk2[:, 0:NT], in1=enb[:, :], op=ALU.mult), r=[Pk2k, enbk], w=[ktlk])
                        yield
                        Ps, Psk = gp()
                        fns = []
                        for hh in range(2):
                            for b in range(NB):
                                co = hh * NT + b * 128
                                fns.append(mm(Ps[:, co:co + 128], ktl[:, b * 128:(b + 1) * 128], qtls[hh][0][:, b * 128:(b + 1) * 128]))
                        S.add("pe", seq(fns), r=[ktlk, qtls[0][1], qtls[1][1]], w=[Psk])
                        mks = []
                        for hh in range(2):
                            msk, mskk = TBp.get()
                            S.add("dve", OP("tensor_tensor", out=msk[:, :], in0=Ps[:, hh * NT:(hh + 1) * NT], in1=maskr[:, :], op=ALU.mult), r=[Psk, "maskr"], w=[mskk])
                            mks.append((msk, mskk))
                        yield
                        Po, Pok = P[7], PK(7)
                        for ch in range(NCH):
                            b, par = ch // 2, ch % 2
                            sb_par = ch % 2
                            if per_chunk_state:
                                S.add("sp", OP("dma_start", out=Sst[:, hp, :], in_=stc[ch, hp, :, :]), w=[("Sst", hp)], dma=("Sst_in", hp))
                            if per_chunk_state or ch == 0:
                                S.add("act", OP("activation", out=Sbf[:, hp, sb_par, :], in_=Sst[:, hp, :], func=AF.Copy), r=[("Sst", hp)], w=[("Sbf", hp, sb_par)])
                            fns = []
                            for hh in range(2):
                                h = hp * 2 + hh
                                oc = hh * NT + ch * 64
                                if par == 0:
                                    ob = hh * NT + b * 128
                                    fns.append(OP("matmul", Po[:, ob:ob + 128], gv_tm[:, b, h * 128:(h + 1) * 128], mks[hh][0][:, b * 128:(b + 1) * 128], start=(ch == 0 and hh == 0), stop=False, skip_group_check=True))
                                fns.append(OP("matmul", Po[:, oc:oc + 64], Sbf[:, hp, sb_par, :], qtls[hh][0][:, ch * 64:(ch + 1) * 64], start=False, stop=(par == 1), skip_group_check=True))
                            S.add("pe", seq(fns), r=["gv_tm", mks[0][1], mks[1][1], ("Sbf", hp, sb_par), qtls[0][1], qtls[1][1]], w=[Pok])
                            Pu, Puk = gp()
                            state_update(hp, ch, Pu, Puk)
                            S.add("dve", OP("scalar_tensor_tensor", out=Sst[:, hp, :], in0=Sst[:, hp, :], scalar=dec[:, hp, ch:ch + 1], in1=Pu[:, 0:128], op0=ALU.mult, op1=ALU.add),
                                  r=[Puk, ("dec", hp), ("Sst", hp)], w=[("Sst", hp)])
                            if per_chunk_state:
                                S.add("pool", OP("dma_start", out=st_dst[ch, hp, :, :], in_=Sst[:, hp, :]), r=[("Sst", hp)], dma=("Sst_out", hp))
                            elif ch + 1 < NCH:
                                np_ = (ch + 1) % 2
                                S.add("act", OP("activation", out=Sbf[:, hp, np_, :], in_=Sst[:, hp, :], func=AF.Copy), r=[("Sst", hp)], w=[("Sbf", hp, np_)])
                            yield
                        yield
                        for hh in range(2):
                            h = hp * 2 + hh
                            oc = hh * NT
                            sq, sqk = TBp.get()
                            S.add("act", (OP("activation", out=sq[:, :], in_=Po[:, oc:oc + NT], func=AF.Square)), r=[Pok], w=[sqk])
                            Pn, Pnk = gp()
                            S.add("pe", (OP("matmul", Pn[:, 0:NT], ones_bf[:, :], sq[:, :], start=True, stop=True)), r=[sqk, "ones_bf"], w=[Pnk])
                            rs, rsk = TFp.get()
                            S.add("act", (OP("activation", out=rs[:, :], in_=Pn[:, 0:NT], func=AF.Ln, scale=1.0 / 128, bias=EPS)), r=[Pnk], w=[rsk])
                            yield
                            S.add("act", (OP("activation", out=rs[:, :], in_=rs[:, :], func=AF.Exp, scale=-0.5)), r=[rsk], w=[rsk])
                            on, onk = TFp.get()
                            S.add("dve", (OP("tensor_tensor", out=on[:, :], in0=Po[:, oc:oc + NT], in1=rs[:, :], op=ALU.mult)), r=[Pok, rsk], w=[onk])
                            Pr, Prk = gp()
                            S.add("pe", seq([mm(Pr[:, 0:NT], win[:, kc, GR0 + h * 128:GR0 + (h + 1) * 128], hT[:, kc, :], start=(kc == 0), stop=(kc == 7)) for kc in range(8)]),
                                  r=["hT"] + WIN, w=[Prk])
                            sg, sgk = TFp.get()
                            S.add("act", (OP("activation", out=sg[:, :], in_=Pr[:, 0:NT], func=AF.Exp, scale=-1.0)), r=[Prk], w=[sgk])
                            S.add("act", (OP("activation", out=sg[:, :], in_=sg[:, :], func=AF.Ln, bias=1.0)), r=[sgk], w=[sgk])
                            S.add("act", (OP("activation", out=sg[:, :], in_=sg[:, :], func=AF.Exp, scale=-1.0)), r=[sgk], w=[sgk])
                            S.add("dve", (OP("tensor_tensor", out=sg[:, :], in0=Pr[:, 0:NT], in1=sg[:, :], op=ALU.mult)), r=[Prk, sgk], w=[sgk])
                            S.add("dve", (OP("scalar_tensor_tensor", out=mixT[:, 4 + h, :], in0=on[:, :], scalar=ggo[:, h:h + 1], in1=sg[:, :], op0=ALU.mult, op1=ALU.mult)),
                                  r=[onk, sgk, "ggo"], w=[("mixT", 4 + h), "hTalt"])
                            yield

                def qpath(qcs_src, c0, TFp=None, TBp=None):
                    TFp = TFq if TFp is None else TFp
                    TBp = TBq if TBp is None else TBp
                    S.add("sp", OP("dma_start", out=qcs[:, :], in_=qcs_src[:, c0:c0 + NT]), w=["qcs"], dma="qcs")
                    sqs = []
                    for kc3 in range(3):
                        Pq, Pqk = gp()
                        S.add("pe", seq([mm(Pq[:, 0:NT], win[:, kc, CQ0 + kc3 * 128:CQ0 + (kc3 + 1) * 128], hT[:, kc, :], start=(kc == 0), stop=(kc == 7)) for kc in range(8)]),
                              r=["hT"] + WIN, w=[Pqk])
                        S.add("act", (OP("activation", out=cqT[:, kc3, :], in_=Pq[:, 0:NT], func=AF.Copy)), r=[Pqk], w=["cqT"])
                        sq, sqk = TBp.get()
                        S.add("act", (OP("activation", out=sq[:, :], in_=Pq[:, 0:NT], func=AF.Square)), r=[Pqk], w=[sqk])
                        sqs.append((sq, sqk))
                        yield
                    Pss, Pssk = gp()
                    S.add("pe", seq([mm(Pss[:, 0:NT], ones_bf[:, :], sqs[i][0][:, :], start=(i == 0), stop=(i == 2)) for i in range(3)]),
                          r=[s[1] for s in sqs] + ["ones_bf"], w=[Pssk])
                    epstk = "epst"
                    S.add("act", (OP("activation", out=epst[:, :], in_=Pss[:, 0:NT], func=AF.Identity, scale=EPS / 384.0, bias=EPS * EPS)), r=[Pssk], w=[epstk])
                    yield
                    for h in range(4):
                        Pn_, Pnk_ = gp()
                        S.add("pe", seq([mm(Pn_[:, 0:NT], wuq[:, kc3, h * 192:h * 192 + 128], cqT[:, kc3, :], start=(kc3 == 0), stop=(kc3 == 2)) for kc3 in range(3)]),
                              r=["wuq", "cqT"], w=[Pnk_])
                        Pab, Pabk = gp()
                        fns = [mm(Pab[0:64, 0:NT], wuq[:, kc3, h * 192 + 128:h * 192 + 192], cqT[:, kc3, :], start=(kc3 == 0), stop=(kc3 == 2)) for kc3 in range(3)]
                        fns += [mm(Pab[64:128, 0:NT], wuq[:, kc3, 768 + h * 64:768 + (h + 1) * 64], cqT[:, kc3, :], start=(kc3 == 0), stop=(kc3 == 2)) for kc3 in range(3)]
                        S.add("pe", seq(fns), r=["wuq", "cqT"], w=[Pabk])
                        s1, s1k = TBp.get()
                        s2, s2k = TBp.get()
                        S.add("act", (OP("activation", out=s1[:, :], in_=Pn_[:, 0:NT], func=AF.Square)), r=[Pnk_], w=[s1k])
                        S.add("act", (OP("activation", out=s2[0:64, :], in_=Pab[0:64, 0:NT], func=AF.Square)), r=[Pabk], w=[s2k])
                        Ph, Phk = gp()
                        S.add("pe", seq([mm(Ph[:, 0:NT], ones_bf[:, :], s1[:, :], start=True, stop=False), mm(Ph[:, 0:NT], ones_bf[0:64, :], s2[0:64, :], start=False, stop=True)]),
                              r=[s1k, s2k, "ones_bf"], w=[Phk])
                        rq, rqk = TFp.get()
                        S.add("dve", (OP("scalar_tensor_tensor", out=rq[:, :], in0=Ph[:, 0:NT], scalar=1.0 / 192, in1=epst[:, :], op0=ALU.mult, op1=ALU.add)), r=[Phk, epstk], w=[rqk])
                        S.add("act", (OP("activation", out=rq[:, :], in_=rq[:, :], func=AF.Ln)), r=[rqk], w=[rqk])
                        S.add("act", (OP("activation", out=rq[:, :], in_=rq[:, :], func=AF.Exp, scale=-0.5)), r=[rqk], w=[rqk])
                        S.add("dve", (OP("scalar_tensor_tensor", out=qn[:, h, :], in0=Pn_[:, 0:NT], scalar=gqk[:, 0:1], in1=rq[:, :], op0=ALU.mult, op1=ALU.mult)),
                              r=[Pnk_, rqk, "gqk"], w=[("qn", h)])
                        ab, abk = TFp.get()
                        S.add("dve", (OP("scalar_tensor_tensor", out=ab[:, :], in0=Pab[:, 0:NT], scalar=gqr_t[:, 0:1], in1=rq[:, :], op0=ALU.mult, op1=ALU.mult)),
                              r=[Pabk, rqk, "gqr_t"], w=[abk])
                        S.add("dve", (OP("tensor_tensor", out=ab[:, :], in0=ab[:, :], in1=qcs[:, :], op=ALU.mult)), r=[abk, "qcs"], w=[abk])
                        Pc_, Pck_ = gp()
                        S.add("pe", (OP("matmul", Pc_[:, 0:NT], comb[:, :], ab[:, :], start=True, stop=True)), r=[abk, "comb"], w=[Pck_])
                        S.add("act", (OP("activation", out=qr[0:64, h, 0, :], in_=Pc_[0:64, 0:NT], func=AF.Copy)), r=[Pck_], w=[("qr", h)])
                        S.add("act", (OP("activation", out=qr[64:128, h, 1, :], in_=Pc_[64:128, 0:NT], func=AF.Copy)), r=[Pck_], w=[("qr", h)])
                        yield

                rlbuf = psb("rlbuf", [128, NT])
                att = dict(cnt=0)

                def blk(h, kb, ncols, q0, first, last, bias_ap, zero_tri=False, zero_rows=None, finish=None):
                    return dict(h=h, kb=kb, ncols=ncols, q0=q0, first=first, last=last, bias=bias_ap, zero_tri=zero_tri, zero_rows=zero_rows, finish=finish)

                def emit_qk(B):
                    h, kb, ncols, q0 = B["h"], B["kb"], B["ncols"], B["q0"]
                    key0 = kb * 128
                    hb = 0 if key0 < 2048 else 64
                    kk0 = key0 % 2048
                    si = SCB[att["cnt"] % 3]
                    att["cnt"] += 1
                    Ps, Psk = P[si], PK(si)
                    B["Ps"], B["Psk"] = Ps, Psk
                    kt = key0 // NT
                    S.add("pe", seq([
                        mm(Ps[:, 0:ncols], Kn[:, h, key0:key0 + 128], qn[:, h, q0:q0 + ncols], start=True, stop=False),
                        mm(Ps[:, 0:ncols], Kr[:, kk0:kk0 + 128], qr[:, h, hb // 64, q0:q0 + ncols], start=False, stop=True)]),
                        r=[("Kn", kt), ("Kr", kk0 // NT), ("qn", h), ("qr", h)], w=[Psk])

                def emit_rest(B):
                    h, kb, ncols, q0 = B["h"], B["kb"], B["ncols"], B["q0"]
                    Ps, Psk = B["Ps"], B["Psk"]
                    kt = kb * 128 // NT
                    pt, ptk = PT.get()
                    if B["bias"] is None:
                        S.add("act", OP("activation", out=pt[:, 0:ncols], in_=Ps[:, 0:ncols], func=AF.Exp, scale=sclK[:, kb, h:h + 1]),
                              r=[Psk, ("sclK", kt)], w=[ptk])
                    else:
                        S.add("act", OP("activation", out=pt[:, 0:ncols], in_=Ps[:, 0:ncols], func=AF.Exp, scale=sclK[:, kb, h:h + 1], bias=B["bias"]),
                              r=[Psk, ("sclK", kt), "flg"], w=[ptk])
                    if B["zero_tri"]:
                        S.add("pool", OP("memset", pt[64:128, 0:64], 0.0), r=[ptk], w=[ptk])
                    if B["zero_rows"] is not None:
                        zr = B["zero_rows"]
                        S.add("pool", OP("memset", pt[zr[0]:zr[1], 0:ncols], 0.0), r=[ptk], w=[ptk])
                    ab_ = 2 + (h % 2)
                    Pa, Pak = P[ab_], PK(ab_)
                    S.add("pe", seq([
                        OP("matmul", Pa[:, q0:q0 + ncols], Vs[:, kb, h * 128:(h + 1) * 128], pt[:, 0:ncols], start=B["first"], stop=B["last"], skip_group_check=True),
                        OP("matmul", Pa[:, NT + q0:NT + q0 + ncols], ones_bf[:, :], pt[:, 0:ncols], start=False, stop=B["last"], skip_group_check=True)]),
                        r=[("Vs", kt), ptk, "ones_bf"], w=[Pak])
                    if B["finish"] is not None:
                        fh_, fq0, fn = B["finish"]
                        S.add("dve", OP("reciprocal", out=rlbuf[:, 0:fn], in_=Pa[:, NT + fq0:NT + fq0 + fn]), r=[Pak], w=["rlbuf"])
                        S.add("dve", OP("tensor_tensor", out=mixT[:, fh_, fq0:fq0 + fn], in0=Pa[:, fq0:fq0 + fn], in1=rlbuf[:, 0:fn], op=ALU.mult), r=[Pak, "rlbuf"], w=[("mixT", fh_), "hTalt"])

                def attn_run(blocks, hook=None):
                    set_gen(GEN_IN)
                    try:
                        _attn_run(blocks, hook)
                    finally:
                        set_gen(GEN_OUT)

                def _attn_run(blocks, hook=None):
                    n = len(blocks)
                    emit_qk(blocks[0])
                    if n > 1:
                        emit_qk(blocks[1])
                    for k in range(n):
                        if k + 2 < n:
                            emit_qk(blocks[k + 2])
                        emit_rest(blocks[k])
                        if hook is not None:
                            hook()

                def prompt_blocks(p):
                    out = []
                    for h in range(4):
                        nown = NB * p + NB
                        for kb in range(16):
                            out.append(blk(h, kb, NT, 0, kb == 0, False, flg[:, 1:2]))
                        for j in range(nown):
                            kb = 16 + j
                            dj = j - NB * p
                            if dj < 0:
                                out.append(blk(h, kb, NT, 0, False, False, None))
                            else:
                                lastb = (j == nown - 1)
                                out.append(blk(h, kb, NT - 128 * dj, 128 * dj, False, lastb, None, zero_tri=True, finish=((h, 0, NT) if lastb else None)))
                    return out

                def sample_blocks(i):
                    q0 = i * 64
                    par = i % 2
                    out = []
                    for h in range(4):
                        for kb in range(16):
                            out.append(blk(h, kb, 64, q0, kb == 0, False, None))
                        out.append(blk(h, 16 + i // 2, 64, q0, False, True, None, zero_rows=((1 - par) * 64, (1 - par) * 64 + 64), finish=(h, q0, 64)))
                    return out

                def back(xsrc, r0, x1row0, blocks_g1, TFp=None):
                    TFp = TFk if TFp is None else TFp
                    for b in range(NB):
                        if blocks_g1 is not None:
                            load_g1bc(blocks_g1[b])
                        S.add("sp", (OP("dma_start", out=xr[:, :], in_=xsrc[r0 + b * 128:r0 + (b + 1) * 128, :])), w=XTK, dma="xt")
                        for fh in range(2):
                            Po, Pok = gp()
                            S.add("pe", seq([mm(Po[:, :], mixT[:, k, b * 128:(b + 1) * 128], wout[:, k, fh * 512:(fh + 1) * 512], start=(k == 0), stop=(k == 7)) for k in range(8)]),
                                  r=[("mixT", k) for k in range(8)] + ["wout"], w=[Pok])
                            tt, ttk = TFp.get()
                            tt2, tt2k = TFp.get()
                            S.add("dve", (OP("tensor_tensor", out=tt[:, :], in0=Po[:, 0:256], in1=g1bc[:, fh * 512:fh * 512 + 256], op=ALU.mult)), r=[Pok, "g1bc"], w=[ttk])
                            S.add("dve", (OP("tensor_tensor", out=tt2[:, :], in0=Po[:, 256:512], in1=g1bc[:, fh * 512 + 256:fh * 512 + 512], op=ALU.mult)), r=[Pok, "g1bc"], w=[tt2k])
                            S.add("dve", (OP("tensor_tensor", out=xr[:, fh * 512:fh * 512 + 256], in0=xr[:, fh * 512:fh * 512 + 256], in1=tt[:, :], op=ALU.add)), r=[ttk, "xt"], w=["xt"])
                            S.add("dve", (OP("tensor_tensor", out=xr[:, fh * 512 + 256:fh * 512 + 512], in0=xr[:, fh * 512 + 256:fh * 512 + 512], in1=tt2[:, :], op=ALU.add)), r=[tt2k, "xt"], w=["xt"])
                            yield
                        S.add("pool", (OP("dma_start", out=x1s[x1row0 + b * 128:x1row0 + (b + 1) * 128, :], in_=xr[:, :])), r=["xt"], w=[("x1s", x1row0 // 128 + b)], dma="x1s_w")

                def run(g):
                    for _ in g:
                        pass

                def chain(*gens):
                    for g in gens:
                        yield from g

                def hook_of(g, every=1):
                    st = dict(n=0)

                    def hk():
                        st["n"] += 1
                        if st["n"] % every == 0:
                            next(g, None)
                    return hk

                def inter(gens):
                    active = dict(gens)
                    while active:
                        for name in list(active):
                            g = active.get(name)
                            if g is None:
                                continue
                            try:
                                tok = next(g)
                            except StopIteration:
                                del active[name]
                                continue
                            if isinstance(tok, str) and tok.startswith("need:"):
                                dep = tok[5:]
                                if dep in active:
                                    for _ in active[dep]:
                                        pass
                                    del active[dep]

                load_g1bc(0)
                NPRE = 2048 // NT
                hbufs = [(hT, "hT"), (mixT, "hTalt")]
                run(front(xpre, 0, [(0, NT, 0)], *hbufs[0]))
                for t in range(NPRE):
                    hb_, hk_ = hbufs[t % 2]
                    gens = dict(k=chain(kvproj(None, None, 0, hb_, hk_), lat_transpose(), kside(t * NT, kcs_pre, t * NB)),
                                g=gla_prefix(hb_, hk_, TFq))
                    if t + 1 < NPRE:
                        gens["f"] = front(xpre, (t + 1) * NT, [(0, NT, 0)], hbufs[(t + 1) % 2][0], hbufs[(t + 1) % 2][1], TFk)
                    inter(gens)
                    ck('pre%d' % t)
                for hp in range(2):
                    S.add("dve", (OP("tensor_scalar", out=Sst[:, hp, :], in0=Sst[:, hp, :], scalar1=flg[:, 0:1], scalar2=None, op0=ALU.mult)), r=[("Sst", hp), "flg"], w=[("Sst", hp)])
                NOWN = 2048 // NT

                def pre_own(p):
                    return chain(front(xown, p * NT, [(0, NT, 0)]), kvproj(lat_own, kr_own, p * NT), lat_transpose(), kside(2048 + p * NT, kcs_own, p * NB))
                run(pre_own(0))
                run(qpath(qcs_own, 0))
                for p in range(NOWN):
                    hk_chain = chain(gla_own(False, None), pre_own(p + 1) if p + 1 < NOWN else iter(()))
                    attn_run(prompt_blocks(p), hook_of(hk_chain, 1))
                    set_gen(GEN_IN)
                    run(hk_chain)
                    set_gen(GEN_OUT)
                    gens = dict(back=back(xown, p * NT, p * NT, None))
                    if p + 1 < NOWN:
                        gens["q"] = qpath(qcs_own, (p + 1) * NT)
                    inter(gens)
                    ck('own%d' % p)
                for hp in range(2):
                    S.add("pool", (OP("dma_start", out=st_p[hp, :, :], in_=Sst[:, hp, :])), r=[("Sst", hp)], dma=("st_p", hp))
                run(front(xsm, 0, [(i * 64, 64, 1 + i) for i in range(4)]))
                run(kvproj(lat_s, kr_s, 0))
                run(lat_transpose())
                run(kside(2048, kcs_sn, 0))
                inter(dict(q=qpath(qcs_s, 0), g=gla_own(True, st_s)))
                NPT = 2048 // NT
                BSS = [BS0, BS1]

                def stage_a(i, t, bs):
                    S.add("sp", OP("dma_start", out=bs["lat"], in_=latc[i, t * NT:(t + 1) * NT, :].rearrange("(b p) d -> p b d", p=128)), w=[bs["latk"]], dma=bs["latk"])
                    S.add("sp", OP("dma_start", out=bs["kr"], in_=krc[i, t * NT:(t + 1) * NT, :].rearrange("(b p) d -> p b d", p=128)), w=[bs["krk"]], dma=bs["krk"])
                    S.add("sp", OP("dma_start", out=bs["kcs"], in_=kcs_pre[:, t * NB:(t + 1) * NB, :]), w=[bs["kcsk"]], dma=bs["kcsk"])
                    yield
                    yield from lat_transpose(bs)

                tiles = [(i, t) for i in range(4) for t in range(NPT)]
                run(stage_a(0, 0, BSS[0]))
                for n, (i, t) in enumerate(tiles):
                    bs = BSS[n % 2]
                    nxtA = stage_a(tiles[n + 1][0], tiles[n + 1][1], BSS[(n + 1) % 2]) if n + 1 < len(tiles) else iter(())
                    if t < NPT - 1:
                        inter(dict(b=kside(t * NT, kcs_pre, t * NB, bs, None, False), a=nxtA))
                    else:
                        run(kside(t * NT, kcs_pre, t * NB, bs, None, False))
                        attn_run(sample_blocks(i), hook_of(nxtA, 4))
                        run(nxtA)
                run(back(xsm, 0, 2048, [1, 2]))
                ck('samp')
            S.barrier()

            with contextlib.ExitStack() as ph2:
                def msb(name, shape, dt=F32):
                    return ph2.enter_context(nc.sbuf_tensor(name, list(shape), dt))
                wup = msb("wup", [128, 8, 4096], BF16)
                wdn = msb("wdn", [128, 32, 1024], BF16)
                g2bc = msb("g2bc", [128, 1024])
                x1t = [msb("x1t%d" % i, [128, NB, 1024]) for i in range(2)]
                xw = msb("xw", [128, 1024])
                h2T = [msb("h2T%d" % i, [128, 8, NT], BF16) for i in range(2)]
                uT = msb("uT", [128, 32, NT], BF16)
                st2 = msb("st2", [128, 8])
                RFb = [msb("RF%d" % i, [128, NT]) for i in range(6)]
                RFa = Rot([(t, t.name) for t in RFb[0:2]])
                RF = Rot([(t, t.name) for t in RFb[2:6]])
                gen8 = Rot(list(range(8)))

                def gp8():
                    i = gen8.get()
                    return P[i], PK(i)

                for jb in range(8):
                    S.add("pool", OP("dma_start", out=wup[:, :, jb * 512:(jb + 1) * 512], in_=w_up[:, jb * 512:(jb + 1) * 512].rearrange("(kc p) n -> p kc n", p=128)),
                          w=[("wup", jb)], dma=("wup", jb))
                for j4 in range(8):
                    S.add("pool", OP("dma_start", out=wdn[:, j4 * 4:(j4 + 1) * 4, :], in_=w_down[j4 * 512:(j4 + 1) * 512, :].rearrange("(j p) n -> p j n", p=128)),
                          w=[("wdn", j4)], dma=("wdn", j4))
                WDN = [("wdn", j4) for j4 in range(8)]

                def load_g2bc(v):
                    for fh in range(2):
                        Pg, Pgk = gp8()
                        S.add("pe", OP("matmul", Pg[:, :], sel[:, v, :], gater[:, 1, fh * 512:(fh + 1) * 512], start=True, stop=True),
                              r=["sel", "gater"], w=[Pgk])
                        S.add("act", OP("activation", out=g2bc[:, fh * 512:(fh + 1) * 512], in_=Pg[:, :], func=AF.Copy), r=[Pgk], w=["g2bc"])

                def mlp_front(ti, row0, segs):
                    xb, hb = x1t[ti % 2], h2T[ti % 2]
                    xk, hk = "x1t%d" % (ti % 2), "h2T%d" % (ti % 2)
                    for b in range(NB):
                        S.add("sp", OP("dma_start", out=xb[:, b, :], in_=x1s[row0 + b * 128:row0 + (b + 1) * 128, :]), r=[("x1s", row0 // 128 + b)], w=[(xk, b)], dma=(xk, b))
                        jt, jk = RFa.get()
                        for q4 in range(4):
                            S.add("act", OP("activation", out=jt[:, :], in_=xb[:, b, q4 * 256:(q4 + 1) * 256], func=AF.Square, accum_out=st2[:, q4:q4 + 1]),
                                  r=[(xk, b)], w=[jk, "st2"])
                        S.add("dve", OP("tensor_reduce", out=st2[:, 4:5], in_=st2[:, 0:4], axis=mybir.AxisListType.X, op=ALU.add), r=["st2"], w=["st2"])
                        S.add("act", act(st2[:, 5:6], st2[:, 4:5], AF.Ln, scale=1.0 / 1024, bias=EPS), r=["st2"], w=["st2"])
                        S.add("act", act(st2[:, 6:7], st2[:, 5:6], AF.Exp, scale=-0.5), r=["st2"], w=["st2"])
                        S.add("dve", OP("tensor_scalar", out=xw[:, :], in0=xb[:, b, :], scalar1=st2[:, 6:7], scalar2=None, op0=ALU.mult), r=[(xk, b), "st2"], w=["xw"])
                        yield
                        for k2 in range(2):
                            Pt, Ptk = gp8()
                            S.add("pe", seq([OP("transpose", Pt[:, kk * 128:(kk + 1) * 128], xw[:, (k2 * 4 + kk) * 128:(k2 * 4 + kk + 1) * 128], ident[:, :]) for kk in range(4)]),
                                  r=["xw", "ident"], w=[Ptk])
                            for kk in range(4):
                                kc = k2 * 4 + kk
                                for (c0, ncol, m) in segs:
                                    lo = max(c0, b * 128)
                                    hi = min(c0 + ncol, (b + 1) * 128)
                                    if lo >= hi:
                                        continue
                                    S.add("act", OP("activation", out=hb[:, kc, lo:hi], in_=Pt[:, kk * 128 + lo - b * 128:kk * 128 + hi - b * 128], func=AF.Identity,
                                                    scale=gm2[:, kc, m:m + 1], bias=sh2[:, kc, m:m + 1]), r=[Ptk, "gm2", "sh2"], w=[hk])
                            yield

                def mlp_main(ti, ydst, yrow0, blocks_g2, hook):
                    xb, hb = x1t[ti % 2], h2T[ti % 2]
                    xk, hk = "x1t%d" % (ti % 2), "h2T%d" % (ti % 2)
                    for j in range(32):
                        Pu, Puk = gp8()
                        S.add("pe", seq([mm(Pu[:, 0:NT], wup[:, kc, j * 128:(j + 1) * 128], hb[:, kc, :], start=(kc == 0), stop=(kc == 7)) for kc in range(8)]),
                              r=[hk, ("wup", j // 4)], w=[Puk])
                        rt, rtk = RF.get()
                        S.add("act", OP("activation", out=rt[:, :], in_=Pu[:, 0:NT], func=AF.Relu), r=[Puk], w=[rtk])
                        S.add("dve", OP("tensor_tensor", out=uT[:, j, :], in0=Pu[:, 0:NT], in1=rt[:, :], op=ALU.mult), r=[Puk, rtk], w=[("uT", j)])
                        if hook is not None and j % 2 == 1:
                            hook()
                    UT = [("uT", j) for j in range(32)]
                    for b in range(NB):
                        if blocks_g2 is not None:
                            load_g2bc(blocks_g2[b])
                        for fh in range(2):
                            Po, Pok = gp8()
                            S.add("pe", seq([mm(Po[:, :], uT[:, j, b * 128:(b + 1) * 128], wdn[:, j, fh * 512:(fh + 1) * 512], start=(j == 0), stop=(j == 31)) for j in range(32)]),
                                  r=UT + WDN, w=[Pok])
                            for q2 in range(2):
                                tt, ttk = RF.get()
                                c0 = fh * 512 + q2 * 256
                                S.add("dve", OP("tensor_tensor", out=tt[:, :], in0=Po[:, q2 * 256:(q2 + 1) * 256], in1=g2bc[:, c0:c0 + 256], op=ALU.mult), r=[Pok, "g2bc"], w=[ttk])
                                S.add("dve", OP("tensor_tensor", out=xb[:, b, c0:c0 + 256], in0=xb[:, b, c0:c0 + 256], in1=tt[:, :], op=ALU.add), r=[ttk, (xk, b)], w=[(xk, b)])
                        S.add("pool", OP("dma_start", out=ydst[yrow0 + b * 128:yrow0 + (b + 1) * 128, :], in_=xb[:, b, :]), r=[(xk, b)], dma=("y_o", ti % 2, b))

                load_g2bc(0)
                NMT = 2048 // NT
                tiles2 = [(p * NT, y_own, p * NT, [(0, NT, 0)], None) for p in range(NMT)] + [(2048, y_s, 0, [(i * 64, 64, 1 + i) for i in range(4)], [1, 2])]
                for _ in mlp_front(0, tiles2[0][0], tiles2[0][3]):
                    pass
                for ti, (row0, ydst, yrow0, segs, bg2) in enumerate(tiles2):
                    if ti + 1 < len(tiles2):
                        nx = mlp_front(ti + 1, tiles2[ti + 1][0], tiles2[ti + 1][3])
                    else:
                        nx = iter(())
                    mlp_main(ti, ydst, yrow0, bg2, (lambda nx=nx: next(nx, None)))
                    for _ in nx:
                        pass

        except _Stop:
            pass
        S.emit()
    return nc


def _rope_tables(pos):
    half = 32
    inv = np.power(np.float32(10000.0), -np.arange(half, dtype=np.float32) / np.float32(half)).astype(np.float32)
    ang = pos.astype(np.float32)[:, None] * inv[None, :]
    return np.cos(ang).astype(np.float32), np.sin(ang).astype(np.float32)


def _kcs(pos):
    c, s = _rope_tables(pos)
    t = np.concatenate([c, c, -s, s], axis=1)
    n = pos.shape[0]
    return np.ascontiguousarray(t.reshape(n // 128, 128, 128).transpose(1, 0, 2))


def _qcs(pos):
    c, s = _rope_tables(pos)
    return np.ascontiguousarray(np.concatenate([c.T, c.T, s.T, s.T], axis=0))


_NC_CACHE = {}


def _prep(x_prompt, x_sample, cache_mla_latent, cache_mla_krope, state_gla, c_prompt, c_sample,
           w_ada, b_ada, g_norm1, w_in, g_q_lora, w_uq, g_kv_lora, w_ukv, g_q_head, g_k_head,
           w_gate_up, b_gate_up, g_gla_out, w_out, g_norm2, w_up, w_down):
    f = lambda a: np.ascontiguousarray(np.asarray(a, dtype=np.float32))
    x_prompt, x_sample = f(x_prompt), f(x_sample)
    latc_all, krc_all, st_all = f(cache_mla_latent)[0], f(cache_mla_krope)[0], f(state_gla)[0]
    c_prompt, c_sample = f(c_prompt), f(c_sample)
    w_ada, b_ada, g_norm1, w_in = f(w_ada)[0], f(b_ada)[0], f(g_norm1)[0], f(w_in)[0]
    g_q_lora, w_uq, g_kv_lora, w_ukv = f(g_q_lora)[0], f(w_uq)[0], f(g_kv_lora)[0], f(w_ukv)[0]
    g_q_head, g_k_head = f(g_q_head)[0], f(g_k_head)[0]
    w_gate_up, b_gate_up, g_gla_out = f(w_gate_up)[0], f(b_gate_up)[0], f(g_gla_out)[0]
    w_out, g_norm2, w_up, w_down = f(w_out)[0], f(g_norm2)[0], f(w_up)[0], f(w_down)[0]

    rot_cols = []
    for h in range(4):
        base = h * 192 + 128
        rot_cols += list(range(base + 32, base + 64)) + list(range(base, base + 32))
    w_uq_ext = np.ascontiguousarray(np.concatenate([w_uq, w_uq[:, rot_cols]], axis=1))
    kn_cols, v_cols = [], []
    for h in range(4):
        kn_cols += list(range(h * 256, h * 256 + 128))
        v_cols += list(range(h * 256 + 128, h * 256 + 256))
    w_ukv_p = np.ascontiguousarray(w_ukv[:, kn_cols + v_cols])
    w_gu_aug = np.zeros((32, 256), np.float32)
    w_gu_aug[0:16] = w_gate_up
    w_gu_aug[16] = b_gate_up
    colT = lambda v, n: np.ascontiguousarray(v.reshape(n, 128).T)
    gqr_col = np.concatenate([g_q_head[128:192], np.roll(g_q_head[128:192], -32)])[:, None]
    ident = np.eye(128, dtype=np.float32)
    ii = np.arange(128)
    same = (ii[:, None] // 64) == (ii[None, :] // 64)
    tri = np.where(same & (ii[:, None] <= ii[None, :]), np.float32(-1.0 / 16), np.float32(0)).astype(np.float32)
    utm = np.where(same & (ii[:, None] > ii[None, :]), np.float32(-1.0 / 16), np.float32(0)).astype(np.float32)
    mblk = (same & (ii[:, None] <= ii[None, :])).astype(np.float32)
    maskr = np.ascontiguousarray(np.tile(mblk, (1, NB)))
    hsel = np.zeros((128, 2), np.float32)
    hsel[0:64, 0] = 0.125
    hsel[64:128, 1] = 0.125
    sel = np.zeros((5, 3, 128), np.float32)
    sel[0, 0, :] = 1
    sel[1, 1, 0:64] = 1
    sel[2, 1, 64:128] = 1
    sel[3, 2, 0:64] = 1
    sel[4, 2, 64:128] = 1
    comb = ((ii[:, None] % 64) == (ii[None, :] % 64)).astype(np.float32)
    kcs_pre = _kcs(np.arange(2048))
    kcs_sn = _kcs(2048 + (np.arange(256) % 64))
    qcs_s = _qcs(2048 + (np.arange(256) % 64))
    shared = dict(
        w_ada=w_ada, b_ada_row=b_ada[None, :], b_adaT=colT(b_ada, 48),
        g1T=colT(g_norm1, 8), g2T=colT(g_norm2, 8), gqlT=colT(g_q_lora, 3),
        gqn=np.ascontiguousarray(g_q_head[0:128, None]), gkn=np.ascontiguousarray(g_k_head[0:128, None]),
        gqr=np.ascontiguousarray(gqr_col), gkr_row=np.ascontiguousarray(g_k_head[None, 128:192]),
        gkv_row=g_kv_lora[None, :], ggoT=colT(g_gla_out, 4),
        w_in=w_in, w_uq_ext=w_uq_ext, w_ukv_p=w_ukv_p, w_gu_aug=w_gu_aug, w_out=w_out, w_up=w_up, w_down=w_down,
        kcs_pre=kcs_pre, kcs_sn=kcs_sn, qcs_s=qcs_s,
        c_ident=ident, c_tri=tri, c_ut=utm, c_mask=maskr, c_sel=sel, c_comb=comb, c_hsel=hsel,
    )
    in_maps = []
    for c in range(8):
        pb, half = c // 2, c % 2
        pos_own = half * 2048 + np.arange(2048)
        flag = np.zeros((128, 2), np.float32)
        flag[:, 0] = float(half)
        flag[:, 1] = 0.0 if half == 1 else NEG
        cvec = np.concatenate([c_prompt[pb:pb + 1], c_sample[4 * c:4 * c + 4]], axis=0)
        m = dict(shared)
        m.update(
            xpre=np.ascontiguousarray(x_prompt[pb, 0:2048]),
            xown=np.ascontiguousarray(x_prompt[pb, half * 2048:(half + 1) * 2048]),
            xsm=np.ascontiguousarray(x_sample[4 * c:4 * c + 4].reshape(256, 1024)),
            latc=np.ascontiguousarray(latc_all[4 * c:4 * c + 4]),
            krc=np.ascontiguousarray(krc_all[4 * c:4 * c + 4]),
            stc=np.ascontiguousarray(st_all[4 * c:4 * c + 4].reshape(4, 2, 128, 128)),
            cT=np.ascontiguousarray(cvec.T), flag=flag,
            kcs_own=_kcs(pos_own), qcs_own=_qcs(pos_own),
        )
        in_maps.append(m)

    return in_maps


def kernel(**inputs):
    in_maps = _prep(**inputs)
    if "nc" not in _NC_CACHE:
        _NC_CACHE["nc"] = build()
    res = run_bass_kernel_spmd(_NC_CACHE["nc"], in_maps, core_ids=list(range(8)))
    return _assemble(res.results)


def _assemble(R):
    y_p = np.zeros((4, 4096, 1024), np.float32)
    lat_p = np.zeros((1, 4, 4096, 256), np.float32)
    kr_p = np.zeros((1, 4, 4096, 64), np.float32)
    st_pp = np.zeros((1, 4, 4, 64, 128), np.float32)
    y_s = np.zeros((32, 64, 1024), np.float32)
    lat_s = np.zeros((1, 32, 64, 256), np.float32)
    kr_s = np.zeros((1, 32, 64, 64), np.float32)
    st_s = np.zeros((1, 32, 4, 64, 128), np.float32)
    for c in range(8):
        pb, half = c // 2, c % 2
        sl = slice(half * 2048, (half + 1) * 2048)
        y_p[pb, sl] = R[c]["y_own"]
        lat_p[0, pb, sl] = R[c]["lat_own"]
        kr_p[0, pb, sl] = R[c]["kr_own"]
        if half == 1:
            st_pp[0, pb] = R[c]["st_p"].reshape(4, 64, 128)
        y_s[4 * c:4 * c + 4] = R[c]["y_s"].reshape(4, 64, 1024)
        lat_s[0, 4 * c:4 * c + 4] = R[c]["lat_s"].reshape(4, 64, 256)
        kr_s[0, 4 * c:4 * c + 4] = R[c]["kr_s"].reshape(4, 64, 64)
        st_s[0, 4 * c:4 * c + 4] = R[c]["st_s"].reshape(4, 4, 64, 128)
    return (y_p, y_s, lat_p, kr_p, st_pp, lat_s, kr_s, st_s)
```

```python
import contextlib
import numpy as np
import concourse.bass as bass
import concourse.mybir as mybir
from concourse.bass_utils import run_bass_kernel_spmd

F32 = mybir.dt.float32
BF16 = mybir.dt.bfloat16
AF = mybir.ActivationFunctionType
ALU = mybir.AluOpType
EPS = 1e-6
NT = 256
NB = NT // 128
NCH = NT // 64
NEG = -30000.0


class Sched:
    ENGS = ("pe", "act", "dve", "pool", "sp")

    def __init__(self, nc):
        self.nc = nc
        self.ops = {e: [] for e in self.ENGS}
        self.last_w = {}
        self.readers = {}
        self.dma_cnt = {}
        self.pending = {e: None for e in self.ENGS}
        self.stopped = False

    def barrier(self):
        if self.stopped:
            return
        toks = []
        for e in self.ENGS:
            for i in range(len(self.ops[e]) - 1, -1, -1):
                if self.ops[e][i]["dma"] is None:
                    toks.append(("c", e, i))
                    break
        for k, c in self.dma_cnt.items():
            toks.append(("d", k, c))
        for e in self.ENGS:
            self.pending[e] = list(toks)

    def add(self, eng, fn, r=(), w=(), dma=None):
        if self.stopped:
            return None
        ops = self.ops[eng]
        idx = len(ops)
        deps = set()
        for k in r:
            t = self.last_w.get(k)
            if t is not None:
                deps.add(t)
        for k in w:
            t = self.last_w.get(k)
            if t is not None:
                deps.add(t)
            for t in self.readers.get(k, ()):
                deps.add(t)
        if self.pending[eng] is not None:
            deps.update(self.pending[eng])
            self.pending[eng] = None
        if dma is not None:
            c = self.dma_cnt.get(dma, 0) + 16
            self.dma_cnt[dma] = c
            tok = ("d", dma, c)
        else:
            tok = ("c", eng, idx)
        fdeps = []
        for t in deps:
            if t[0] == "c":
                if t[1] == eng and eng == "pe":
                    continue
                if t[1] == eng and t[2] == idx:
                    continue
                self.ops[t[1]][t[2]]["sig"] = True
            fdeps.append(t)
        ops.append(dict(fn=fn, deps=fdeps, sig=False, dma=dma))
        for k in w:
            self.last_w[k] = tok
            self.readers[k] = []
        for k in r:
            self.readers.setdefault(k, []).append(tok)
        return tok

    def emit(self, final_eng="sp"):
        nc = self.nc
        with contextlib.ExitStack() as es:
            esem = {e: es.enter_context(nc.semaphore("s_" + e)) for e in self.ENGS}
            dsem = {}
            for i, k in enumerate(self.dma_cnt):
                dsem[k] = es.enter_context(nc.semaphore("d%d" % i))
            sigidx = {}
            for e in self.ENGS:
                c = 0
                arr = []
                for op in self.ops[e]:
                    if op["sig"]:
                        c += 1
                    arr.append(c)
                sigidx[e] = arr
            block = es.enter_context(nc.Block())

            def run(e, h):
                waited = {}
                for op in self.ops[e]:
                    for t in op["deps"]:
                        if t[0] == "c":
                            key = ("c", t[1])
                            val = sigidx[t[1]][t[2]]
                            sem = esem[t[1]]
                        else:
                            key = ("d", t[1])
                            val = t[2]
                            sem = dsem[t[1]]
                        if waited.get(key, 0) >= val:
                            continue
                        waited[key] = val
                        h.wait_ge(sem, val)
                    ins = op["fn"](h)
                    if op["dma"] is not None:
                        ins.then_inc(dsem[op["dma"]], 16)
                    elif op["sig"]:
                        ins.then_inc(esem[e], 1)
                if e == final_eng:
                    for k, c in self.dma_cnt.items():
                        if waited.get(("d", k), 0) < c:
                            h.wait_ge(dsem[k], c)

            @block.tensor
            def _(h):
                run("pe", h)

            @block.scalar
            def _(h):
                run("act", h)

            @block.vector
            def _(h):
                run("dve", h)

            @block.gpsimd
            def _(h):
                run("pool", h)

            @block.sync
            def _(h):
                run("sp", h)


class Rot:
    def __init__(self, items):
        self.items = items
        self.i = 0

    def get(self):
        it = self.items[self.i % len(self.items)]
        self.i += 1
        return it


def OP(meth, *a, **kw):
    return lambda e: getattr(e, meth)(*a, **kw)


def seq(fns):
    def f(e):
        ins = None
        for g in fns:
            ins = g(e)
        return ins
    return f


def mm(out, lhsT, rhs, start=True, stop=True):
    return OP("matmul", out, lhsT, rhs, start=start, stop=stop)


def act(out, in_, func, **kw):
    return OP("activation", out=out, in_=in_, func=func, **kw)


CQ0, CKV0, KR0, GQ0, GK0, GV0, GLR0, GR0 = 0, 384, 640, 704, 960, 1216, 1728, 1744


class _Stop(Exception):
    pass


def build(dbg=False, stop=None):
    def ck(tag):
        if stop == tag:
            S.stopped = True
    nc = bass.Bass("TRN2", target_bir_lowering=False)
    S = Sched(nc)

    def din(name, shape):
        return nc.dram_tensor(name, list(shape), F32, kind="ExternalInput").ap()

    def dout(name, shape):
        return nc.dram_tensor(name, list(shape), F32, kind="ExternalOutput").ap()

    xpre = din("xpre", [2048, 1024])
    xown = din("xown", [2048, 1024])
    xsm = din("xsm", [256, 1024])
    latc = din("latc", [4, 2048, 256])
    krc = din("krc", [4, 2048, 64])
    stc = din("stc", [4, 2, 128, 128])
    cT = din("cT", [1024, 5])
    flag = din("flag", [128, 2])
    w_ada = din("w_ada", [1024, 6144])
    b_ada_row = din("b_ada_row", [1, 6144])
    b_adaT = din("b_adaT", [128, 48])
    g1T = din("g1T", [128, 8])
    g2T = din("g2T", [128, 8])
    gqlT = din("gqlT", [128, 3])
    gqn = din("gqn", [128, 1])
    gkn = din("gkn", [128, 1])
    gqr = din("gqr", [128, 1])
    gkr_row = din("gkr_row", [1, 64])
    gkv_row = din("gkv_row", [1, 256])
    ggoT = din("ggoT", [128, 4])
    w_in = din("w_in", [1024, 2256])
    w_uq_ext = din("w_uq_ext", [384, 1024])
    w_ukv_p = din("w_ukv_p", [256, 1024])
    w_gu_aug = din("w_gu_aug", [32, 256])
    w_out = din("w_out", [1024, 1024])
    w_up = din("w_up", [1024, 4096])
    w_down = din("w_down", [4096, 1024])
    kcs_pre = din("kcs_pre", [128, 16, 128])
    kcs_own = din("kcs_own", [128, 16, 128])
    kcs_sn = din("kcs_sn", [128, 2, 128])
    qcs_own = din("qcs_own", [128, 2048])
    qcs_s = din("qcs_s", [128, 256])
    c_ident = din("c_ident", [128, 128])
    c_tri = din("c_tri", [128, 128])
    c_ut = din("c_ut", [128, 128])
    c_mask = din("c_mask", [128, 256])
    c_sel = din("c_sel", [5, 3, 128])
    c_comb = din("c_comb", [128, 128])
    c_hsel = din("c_hsel", [128, 2])

    y_own = dout("y_own", [2048, 1024])
    y_s = dout("y_s", [256, 1024])
    lat_own = dout("lat_own", [2048, 256])
    kr_own = dout("kr_own", [2048, 64])
    st_p = dout("st_p", [2, 128, 128])
    lat_s = dout("lat_s", [256, 256])
    kr_s = dout("kr_s", [256, 64])
    st_s = dout("st_s", [4, 2, 128, 128])
    x1s = nc.dram_tensor("x1s", [2304, 1024], F32).ap()

    P = [nc.alloc_psum_tensor("P%d" % i, [128, 512], F32) for i in range(8)]

    def PK(i):
        return "P%d" % i

    with contextlib.ExitStack() as glob:
        def gsb(name, shape, dt=F32):
            return glob.enter_context(nc.sbuf_tensor(name, list(shape), dt))

        ident = gsb("ident", [128, 128])
        tri = gsb("tri", [128, 128])
        ut = gsb("ut", [128, 128])
        maskr = gsb("maskr", [128, 256])
        sel = gsb("sel", [5, 3, 128])
        comb = gsb("comb", [128, 128])
        flg = gsb("flg", [128, 2])
        hsel = gsb("hsel", [128, 2])
        ones_bf = gsb("ones_bf", [128, 128], BF16)
        ones_f = gsb("ones_f", [1, 128])
        gm1 = gsb("gm1", [128, 8, 5])
        sh1 = gsb("sh1", [128, 8, 5])
        gm2 = gsb("gm2", [128, 8, 5])
        sh2 = gsb("sh2", [128, 8, 5])
        gater = gsb("gater", [5, 2, 1024])
        g2c = gsb("g2c", [128, 8])
        for (t, d) in ((ident, c_ident), (tri, c_tri), (ut, c_ut), (maskr, c_mask), (comb, c_comb), (flg, flag), (g2c, g2T), (hsel, c_hsel)):
            S.add("sp", (OP("dma_start", out=t[:, :], in_=d[:, :])), w=[t.name], dma=t.name)
        S.add("sp", OP("dma_start", out=sel[:, :, :], in_=c_sel[:, :, :]), w=["sel"], dma="sel")
        S.add("dve", OP("memset", ones_bf[:, :], 1.0), w=["ones_bf"])
        S.add("dve", OP("memset", ones_f[:, :], 1.0), w=["ones_f"])

        GEN_OUT = [5, 6, 0, 1, 4]
        GEN_IN = [5, 6]
        gen4 = Rot(list(GEN_OUT))
        SCB = [0, 1, 4]

        def set_gen(lst):
            gen4.items = list(lst)
            gen4.i = 0

        def gp():
            i = gen4.get()
            return P[i], PK(i)

        try:
            with contextlib.ExitStack() as ph1:
                def psb(name, shape, dt=F32):
                    return ph1.enter_context(nc.sbuf_tensor(name, list(shape), dt))

                win = psb("win", [128, 8, 2256], BF16)
                wuq = psb("wuq", [128, 3, 1024], BF16)
                wukv = psb("wukv", [128, 2, 1024], BF16)
                wout = psb("wout", [128, 8, 1024], BF16)
                wgu = psb("wgu", [32, 256], BF16)
                g1c = psb("g1c", [128, 8])
                gqk = psb("gqk", [128, 1])
                gkn_t = psb("gkn_t", [128, 1])
                gqr_t = psb("gqr_t", [128, 1])
                ggo = psb("ggo", [128, 4])
                gkr_bc = psb("gkr_bc", [128, NB, 64])
                gkv_bc = psb("gkv_bc", [128, 256])
                g1bc = psb("g1bc", [128, 1024])
                zcol = psb("zcol", [128, 1])
                glrT = psb("glrT", [32, NT], BF16)
                Sst = psb("Sst", [128, 2, 128])
                S.add("dve", OP("memset", zcol[:, :], 0.0), w=["zcol"])
                S.add("dve", OP("memset", glrT[:, :], 1.0), w=["glrT"])
                S.add("dve", OP("memset", Sst[:, :, :], 0.0), w=[("Sst", 0), ("Sst", 1)])
                for (t, d) in ((g1c, g1T), (gkn_t, gkn), (gqr_t, gqr), (ggo, ggoT)):
                    S.add("sp", (OP("dma_start", out=t[:, :], in_=d[:, :])), w=[t.name], dma=t.name)
                S.add("sp", OP("dma_start", out=gqk[:, :], in_=gqn[:, :]), w=["gqk"], dma="gqk")
                S.add("dve", OP("tensor_tensor", out=gqk[:, :], in0=gqk[:, :], in1=gkn_t[:, :], op=ALU.mult), r=["gkn_t", "gqk"], w=["gqk"])

                with contextlib.ExitStack() as ph0:
                    def zsb(name, shape, dt=F32):
                        return ph0.enter_context(nc.sbuf_tensor(name, list(shape), dt))
                    wa = [zsb("wa%d" % i, [128, 8, 512], BF16) for i in range(2)]
                    wuq_f = zsb("wuq_f", [128, 3, 1024])
                    ctile = zsb("ctile", [128, 8, 5])
                    c_e = zsb("c_e", [128, 8, 5])
                    siluT = zsb("siluT", [128, 8, 5], BF16)
                    badT = zsb("badT", [128, 48])
                    badr = zsb("badr", [1, 6144])
                    badr_bf = zsb("badr_bf", [1, 6144], BF16)
                    ones5 = zsb("ones5", [1, 8], BF16)
                    gql = zsb("gql", [128, 3])
                    rowt = zsb("rowt", [1, 320])
                    modT = zsb("modT", [128, 32, 5])

                    S.add("sp", OP("dma_start", out=ctile[:, :, :], in_=cT.rearrange("(kc p) b -> p kc b", p=128)), w=["ctile"], dma="ctile")
                    S.add("sp", OP("dma_start", out=badT[:, :], in_=b_adaT[:, :]), w=["badT"], dma="badT")
                    S.add("sp", OP("dma_start", out=badr[:, :], in_=b_ada_row[:, :]), w=["badr"], dma="badr")
                    S.add("sp", OP("dma_start", out=gql[:, :], in_=gqlT[:, :]), w=["gql"], dma="gql")
                    S.add("sp", OP("dma_start", out=rowt[:, 0:256], in_=gkv_row[:, :]), w=["rowt0"], dma="rowt0")
                    S.add("sp", OP("dma_start", out=rowt[:, 256:320], in_=gkr_row[:, :]), w=["rowt1"], dma="rowt1")
                    S.add("sp", OP("dma_start", out=wuq_f[:, :, :], in_=w_uq_ext.rearrange("(kc p) n -> p kc n", p=128)), w=["wuq_f"], dma="wuq_f")
                    S.add("dve", OP("memset", ones5[:, :], 1.0), w=["ones5"])
                    S.add("dve", OP("tensor_copy", out=badr_bf[:, :], in_=badr[:, :]), r=["badr"], w=["badr_bf"])
                    S.add("act", act(c_e[:, :, :], ctile[:, :, :], AF.Exp, scale=-1.0), r=["ctile"], w=["c_e"])
                    S.add("dve", OP("tensor_scalar", out=c_e[:, :, :], in0=c_e[:, :, :], scalar1=1.0, scalar2=None, op0=ALU.add), r=["c_e"], w=["c_e"])
                    S.add("dve", OP("reciprocal", out=c_e[:, :, :], in_=c_e[:, :, :]), r=["c_e"], w=["c_e"])
                    S.add("dve", OP("tensor_tensor", out=siluT[:, :, :], in0=ctile[:, :, :], in1=c_e[:, :, :], op=ALU.mult), r=["c_e", "ctile"], w=["siluT"])

                    def wa_load(ct, slot):
                        S.add("pool", OP("dma_start", out=wa[slot][:, :, :], in_=w_ada[:, ct * 512:(ct + 1) * 512].rearrange("(kc p) n -> p kc n", p=128)),
                              w=["wa%d" % slot], dma="wa%d" % slot)
                    order = [0, 1, 2, 3, 4, 5, 6, 7, 8, 9, 10, 11]
                    wa_load(order[0], 0)
                    fidx = {}
                    fi = 0
                    for ct in order:
                        if ct in (0, 1, 2, 3, 6, 7, 8, 9):
                            for cc in range(4):
                                fidx[(ct, cc)] = fi
                                fi += 1
                    PF, PFk = P[4], PK(4)
                    for n, ct in enumerate(order):
                        slot = n % 2
                        if n + 1 < len(order):
                            wa_load(order[n + 1], (n + 1) % 2)
                        if ct in (0, 1, 2, 3, 6, 7, 8, 9):
                            fns = []
                            for cc in range(4):
                                f = fidx[(ct, cc)]
                                for kc in range(8):
                                    fns.append(mm(PF[:, f * 8:f * 8 + 5], wa[slot][:, kc, cc * 128:(cc + 1) * 128], siluT[:, kc, :], start=(kc == 0), stop=(kc == 7)))
                            S.add("pe", seq(fns), r=["wa%d" % slot, "siluT"], w=[PFk])
                        else:
                            gi = 0 if ct in (4, 5) else 1
                            hf = ct % 2
                            Pr, Prk = P[5 + (n % 2)], PK(5 + (n % 2))
                            fns = [mm(Pr[0:5, :], siluT[:, kc, :], wa[slot][:, kc, :], start=(kc == 0), stop=False) for kc in range(8)]
                            fns.append(mm(Pr[0:5, :], ones5[0:1, 0:5], badr_bf[0:1, ct * 512:(ct + 1) * 512], start=False, stop=True))
                            S.add("pe", seq(fns), r=["wa%d" % slot, "siluT", "ones5", "badr_bf"], w=[Prk])
                            S.add("act", act(gater[:, gi, hf * 512:(hf + 1) * 512], Pr[0:5, :], AF.Copy), r=[Prk], w=["gater"])
                    for (ct, cc), f in fidx.items():
                        chunk = ct * 4 + cc
                        S.add("dve", (OP("tensor_scalar", out=modT[:, f, :], in0=PF[:, f * 8:f * 8 + 5], scalar1=badT[:, chunk:chunk + 1], scalar2=None, op0=ALU.add)),
                              r=[PFk, "badT"], w=["modT"])
                    for kc in range(8):
                        S.add("dve", (OP("tensor_scalar", out=gm1[:, kc, :], in0=modT[:, 8 + kc, :], scalar1=1.0, scalar2=g1c[:, kc:kc + 1], op0=ALU.add, op1=ALU.mult)),
                              r=["modT", "g1c"], w=["gm1"])
                        S.add("dve", (OP("tensor_scalar", out=gm2[:, kc, :], in0=modT[:, 24 + kc, :], scalar1=1.0, scalar2=g2c[:, kc:kc + 1], op0=ALU.add, op1=ALU.mult)),
                              r=["modT", "g2c"], w=["gm2"])
                    S.add("dve", OP("tensor_copy", out=sh1[:, :, :], in_=modT[:, 0:8, :]), r=["modT"], w=["sh1"])
                    S.add("dve", OP("tensor_copy", out=sh2[:, :, :], in_=modT[:, 16:24, :]), r=["modT"], w=["sh2"])

                    for hf in range(2):
                        S.add("pool", (OP("dma_start", out=win[:, :, hf * 1128:(hf + 1) * 1128], in_=w_in[:, hf * 1128:(hf + 1) * 1128].rearrange("(kc p) n -> p kc n", p=128))),
                              w=["win%d" % hf], dma="win%d" % hf)
                    S.add("pool", OP("dma_start", out=wukv[:, :, :], in_=w_ukv_p.rearrange("(kc p) n -> p kc n", p=128)), w=["wukv"], dma="wukv")
                    S.add("pool", OP("dma_start", out=wgu[:, :], in_=w_gu_aug[:, :]), w=["wgu"], dma="wgu")
                    S.add("pool", OP("dma_start", out=wout[:, :, :], in_=w_out.rearrange("(kc p) n -> p kc n", p=128)), w=["wout"], dma="wout")
                    for kc in range(3):
                        S.add("dve", (OP("tensor_scalar", out=wuq[:, kc, :], in0=wuq_f[:, kc, :], scalar1=gql[:, kc:kc + 1], scalar2=None, op0=ALU.mult)),
                              r=["wuq_f", "gql"], w=["wuq"])
                        for h in range(4):
                            c0 = 768 + h * 64
                            S.add("dve", (OP("tensor_scalar", out=wuq[:, kc, c0:c0 + 32], in0=wuq[:, kc, c0:c0 + 32], scalar1=-1.0, scalar2=None, op0=ALU.mult)),
                                  r=["wuq"], w=["wuq"])
                    Pb, Pbk = P[7], PK(7)
                    S.add("pe", mm(Pb[:, 0:320], ones_f[0:1, :], rowt[0:1, 0:320]), r=["ones_f", "rowt0", "rowt1"], w=[Pbk])
                    S.add("act", act(gkv_bc[:, :], Pb[:, 0:256], AF.Copy), r=[Pbk], w=["gkv_bc"])
                    for b in range(NB):
                        S.add("act", (OP("activation", out=gkr_bc[:, b, :], in_=Pb[:, 256:320], func=AF.Copy)), r=[Pbk], w=["gkr_bc"])
                S.barrier()
                ck('p0')

                Kn = psb("Kn", [128, 4, 4096], BF16)
                Kr = psb("Kr", [128, 2048], BF16)
                Vs = psb("Vs", [128, 32, 512], BF16)
                sclK = psb("sclK", [128, 32, 4])
                Sbf = psb("Sbf", [128, 2, 2, 128], BF16)
                xt = psb("xt", [128, 1024])
                hT = psb("hT", [128, 8, NT], BF16)
                lat_tm = psb("lat_tm", [128, NB, 256])
                kr_tm = psb("kr_tm", [128, NB, 64])
                latT = psb("latT", [128, 2, NT], BF16)
                gk_tm = psb("gk_tm", [128, NB, 256])
                gv_tm = psb("gv_tm", [128, NB, 512], BF16)
                l_tm = psb("l_tm", [128, NB, 256])
                khat = psb("khat", [128, NB, 256], BF16)
                cqT = psb("cqT", [128, 3, NT], BF16)
                qn = psb("qn", [128, 4, NT], BF16)
                qr = psb("qr", [128, 4, 2, NT], BF16)
                mixT = psb("mixT", [128, 8, NT], BF16)
                xr = xt
                kcs = psb("kcs", [128, NB, 128])
                qcs = psb("qcs", [128, NT])
                epst = psb("epst", [128, NT])
                stat = psb("stat", [128, 64])
                junk = psb("junk", [128, 1024], BF16)
                dec = psb("dec", [128, 2, NCH])
                TFl = [psb("TF%d" % i, [128, NT]) for i in range(10)]
                TBl = [psb("TB%d" % i, [128, NT], BF16) for i in range(11)]
                PTb = [psb("PT%d" % i, [128, NT], BF16) for i in range(4)]
                TFg = Rot([(t, t.name) for t in TFl[0:5]])
                TFq = Rot([(t, t.name) for t in TFl[5:8]])
                TFk = Rot([(t, t.name) for t in TFl[8:10]])
                TBg = Rot([(t, t.name) for t in TBl[0:6]])
                TBq = Rot([(t, t.name) for t in TBl[6:11]])
                PT = Rot([(t, t.name) for t in PTb])
                lat2 = xt[:, 0:512].rearrange("p (b d) -> p b d", b=NB)
                kr2 = xt[:, 512:640].rearrange("p (b d) -> p b d", b=NB)
                kcs2 = xt[:, 640:896].rearrange("p (b d) -> p b d", b=NB)
                latT2 = hT[:, 0:2, :]
                BS0 = dict(lat=lat_tm[:, :, :], latk="lat_tm", kr=kr_tm[:, :, :], krk="kr_tm", kcs=kcs[:, :, :], kcsk="kcs", latT=latT[:, :, :], latTk="latT")
                BS1 = dict(lat=lat2, latk="xt_lat", kr=kr2, krk="xt_kr", kcs=kcs2, kcsk="xt_kcs", latT=latT2, latTk="hTa")
                XTK = ["xt", "xt_kcs"] + [("xt_lat", i) for i in range(NB)] + [("xt_kr", i) for i in range(NB)]


                WIN = ["win0", "win1"]

                def HK(hTk, kcs=range(8), bs=range(NB)):
                    return [(hTk, kc, b) for kc in kcs for b in bs]

                def BK(pref, idx=None, n=NB):
                    return [(pref, i) for i in (range(n) if idx is None else idx)]
                S.add("dve", OP("memset", Kr[:, :], 0.0), w=[("Kr", kt_) for kt_ in range(2048 // NT)])
                S.add("dve", OP("memset", qr[:, :, :, :], 0.0), w=[("qr", h) for h in range(4)])

                def load_g1bc(v):
                    for fh in range(2):
                        Pg, Pgk = gp()
                        S.add("pe", (OP("matmul", Pg[:, :], sel[:, v, :], gater[:, 0, fh * 512:(fh + 1) * 512], start=True, stop=True)),
                              r=["sel", "gater"], w=[Pgk])
                        S.add("act", (OP("activation", out=g1bc[:, fh * 512:(fh + 1) * 512], in_=Pg[:, :], func=AF.Copy)), r=[Pgk], w=["g1bc"])

                def front(xsrc, r0, segs, hTb=None, hTk="hT", TFp=None):
                    hTb = hT if hTb is None else hTb
                    TFp = TFg if TFp is None else TFp
                    for b in range(NB):
                        S.add("sp", (OP("dma_start", out=xt[:, :], in_=xsrc[r0 + b * 128:r0 + (b + 1) * 128, :])), w=XTK, dma="xt")
                        S.add("act", OP("activation", out=junk[:, :], in_=xt[:, :], func=AF.Square, accum_out=stat[:, 4:5]), r=["xt"], w=[("junk", 0), ("junk", 1), ("junk", 2), "st_x"])
                        S.add("act", act(stat[:, 5:6], stat[:, 4:5], AF.Ln, scale=1.0 / 1024, bias=EPS), r=["st_x"], w=["st_x"])
                        S.add("act", act(stat[:, 6:7], stat[:, 5:6], AF.Exp, scale=-0.5), r=["st_x"], w=["st_x"])
                        S.add("dve", OP("tensor_scalar", out=xt[:, :], in0=xt[:, :], scalar1=stat[:, 6:7], scalar2=None, op0=ALU.mult), r=["xt", "st_x"], w=["xt"])
                        yield
                        for k2 in range(2):
                            Pt, Ptk = gp()
                            S.add("pe", seq([(OP("transpose", Pt[:, kk * 128:(kk + 1) * 128], xt[:, (k2 * 4 + kk) * 128:(k2 * 4 + kk + 1) * 128], ident[:, :])) for kk in range(4)]),
                                  r=["xt", "ident"], w=[Ptk])
                            for kk in range(4):
                                kc = k2 * 4 + kk
                                for (c0, ncol, m) in segs:
                                    lo = max(c0, b * 128)
                                    hi = min(c0 + ncol, (b + 1) * 128)
                                    if lo >= hi:
                                        continue
                                    S.add("act", (OP("activation",
                                        out=hTb[:, kc, lo:hi], in_=Pt[:, kk * 128 + lo - b * 128:kk * 128 + hi - b * 128], func=AF.Identity,
                                        scale=gm1[:, kc, m:m + 1], bias=sh1[:, kc, m:m + 1])), r=[Ptk, "gm1", "sh1"], w=[(hTk, kc, b), "hTa"])
                        yield

                def kvproj(lat_dst, kr_dst, r0, hTb=None, hTk="hT", TFp=None):
                    hTb = hT if hTb is None else hTb
                    TFp = TFg if TFp is None else TFp
                    for b in range(NB):
                        Pq, Pqk = gp()
                        S.add("pe", seq([mm(Pq[:, 0:320], hTb[:, kc, b * 128:(b + 1) * 128], win[:, kc, CKV0:CKV0 + 320], start=(kc == 0), stop=(kc == 7)) for kc in range(8)]),
                              r=HK(hTk, bs=[b]) + WIN, w=[Pqk])
                        skv = ("st_kv", b)
                        S.add("act", (OP("activation", out=junk[:, b * 256:(b + 1) * 256], in_=Pq[:, 0:256], func=AF.Square, accum_out=stat[:, 8 + b:9 + b])), r=[Pqk], w=[("junk", b), skv])
                        S.add("act", (OP("activation", out=stat[:, 12 + b:13 + b], in_=stat[:, 8 + b:9 + b], func=AF.Ln, scale=1.0 / 256, bias=EPS)), r=[skv], w=[skv])
                        S.add("act", (OP("activation", out=stat[:, 16 + b:17 + b], in_=stat[:, 12 + b:13 + b], func=AF.Exp, scale=-0.5)), r=[skv], w=[skv])
                        S.add("dve", (OP("scalar_tensor_tensor", out=lat_tm[:, b, :], in0=Pq[:, 0:256], scalar=stat[:, 16 + b:17 + b], in1=gkv_bc[:, :], op0=ALU.mult, op1=ALU.mult)),
                              r=[Pqk, skv, "gkv_bc"], w=[("lat_tm", b)])
                        S.add("act", (OP("activation", out=kr_tm[:, b, :], in_=Pq[:, 256:320], func=AF.Copy)), r=[Pqk], w=[("kr_tm", b)])
                        yield
                    if lat_dst is not None:
                        S.add("pool", OP("dma_start", out=lat_dst[r0:r0 + NT, :].rearrange("(b p) d -> p b d", p=128), in_=lat_tm[:, :, :]), r=BK("lat_tm"), dma="lat_o")
                        S.add("pool", OP("dma_start", out=kr_dst[r0:r0 + NT, :].rearrange("(b p) d -> p b d", p=128), in_=kr_tm[:, :, :]), r=BK("kr_tm"), dma="kr_o")

                def lat_transpose(bs=None):
                    bs = BS0 if bs is None else bs
                    for lc in range(2):
                        Pt, Ptk = gp()
                        S.add("pe", seq([(OP("transpose", Pt[:, b * 128:(b + 1) * 128], bs["lat"][:, b, lc * 128:(lc + 1) * 128], ident[:, :])) for b in range(NB)]),
                              r=BK(bs["latk"]) + ["ident"], w=[Ptk])
                        S.add("dve", (OP("tensor_copy", out=bs["latT"][:, lc, :], in_=Pt[:, 0:NT])), r=[Ptk], w=[(bs["latTk"], lc)] + (["hTa"] if bs["latTk"] == "hTa" else []))
                        yield

                def kside(key0, cs_src, cs_b0, bs=None, TFp=None, load_cs=True):
                    kb0 = key0 // 128
                    bs = BS0 if bs is None else bs
                    TFp = TFg if TFp is None else TFp
                    latT_, latTk_, kr_, krk_, kcs_, kcsk_ = bs["latT"], bs["latTk"], bs["kr"], bs["krk"], bs["kcs"], bs["kcsk"]
                    if load_cs:
                        S.add("sp", OP("dma_start", out=kcs_, in_=cs_src[:, cs_b0:cs_b0 + NB, :]), w=[kcsk_], dma=kcsk_)
                    for h in range(4):
                        Pq, Pqk = gp()
                        S.add("pe", seq([mm(Pq[:, 0:NT], wukv[:, lc, h * 128:(h + 1) * 128], latT_[:, lc, :], start=(lc == 0), stop=(lc == 1)) for lc in range(2)]),
                              r=["wukv", (latTk_, 0), (latTk_, 1)], w=[Pqk])
                        S.add("dve", OP("tensor_copy", out=Kn[:, h, key0:key0 + NT], in_=Pq[:, 0:NT]), r=[Pqk], w=[("Kn", key0 // NT, h)])
                        yield
                    for b in range(NB):
                        Pq, Pqk = gp()
                        S.add("pe", seq([mm(Pq[:, :], latT_[:, lc, b * 128:(b + 1) * 128], wukv[:, lc, 0:512], start=(lc == 0), stop=(lc == 1)) for lc in range(2)]),
                              r=["wukv", (latTk_, 0), (latTk_, 1)], w=[Pqk])
                        sk = ("st_k", b)
                        for h in range(4):
                            S.add("act", (OP("activation", out=Pq[:, h * 128:(h + 1) * 128], in_=Pq[:, h * 128:(h + 1) * 128], func=AF.Square, accum_out=stat[:, 20 + b * 4 + h:21 + b * 4 + h])),
                                  r=[Pqk], w=[(Pqk, h), (sk, h)])
                        S.add("act", (OP("activation", out=junk[:, 512 + b * 64:512 + (b + 1) * 64], in_=kr_[:, b, :], func=AF.Square, accum_out=stat[:, 28 + b:29 + b])), r=[(krk_, b)], w=[("junk", 2), (sk, 4)])
                        S.add("dve", (OP("tensor_scalar", out=stat[:, 32 + b * 4:36 + b * 4], in0=stat[:, 20 + b * 4:24 + b * 4], scalar1=stat[:, 28 + b:29 + b], scalar2=None, op0=ALU.add)),
                              r=[(sk, 0), (sk, 1), (sk, 2), (sk, 3), (sk, 4)], w=[sk])
                        S.add("act", (OP("activation", out=stat[:, 40 + b * 4:44 + b * 4], in_=stat[:, 32 + b * 4:36 + b * 4], func=AF.Ln, scale=1.0 / 192, bias=EPS)), r=[sk], w=[sk])
                        S.add("act", (OP("activation", out=sclK[:, kb0 + b, :], in_=stat[:, 40 + b * 4:44 + b * 4], func=AF.Exp, scale=-0.5, bias=float(-0.5 * np.log(192.0)))),
                              r=[sk], w=[("sclK", kb0 + b)])
                        yield
                        Pv, Pvk = gp()
                        S.add("pe", seq([mm(Pv[:, :], latT_[:, lc, b * 128:(b + 1) * 128], wukv[:, lc, 512:1024], start=(lc == 0), stop=(lc == 1)) for lc in range(2)]),
                              r=["wukv", (latTk_, 0), (latTk_, 1)], w=[Pvk])
                        S.add("dve", (OP("tensor_copy", out=Vs[:, kb0 + b, :], in_=Pv[:, :])), r=[Pvk], w=[("Vs", kb0 + b)])
                        yield
                    t1, t1k = TFp.get()
                    t2, t2k = TFp.get()
                    t3, t3k = TFp.get()
                    v1 = t1[:, 0:NB * 64].rearrange("p (b d) -> p b d", b=NB)
                    v2 = t2[:, 0:NB * 64].rearrange("p (b d) -> p b d", b=NB)
                    v3 = t3[:, 0:NB * 64].rearrange("p (b d) -> p b d", b=NB)
                    S.add("dve", OP("tensor_tensor", out=v1, in0=kr_, in1=gkr_bc[:, :, :], op=ALU.mult), r=BK(krk_) + ["gkr_bc"], w=[t1k])
                    S.add("dve", OP("tensor_tensor", out=v2, in0=v1, in1=kcs_[:, :, 0:64], op=ALU.mult), r=[t1k, kcsk_], w=[t2k])
                    S.add("dve", OP("tensor_tensor", out=v3[:, :, 0:32], in0=v1[:, :, 32:64], in1=kcs_[:, :, 64:96], op=ALU.mult), r=[t1k, kcsk_], w=[t3k])
                    S.add("dve", OP("tensor_tensor", out=v3[:, :, 32:64], in0=v1[:, :, 0:32], in1=kcs_[:, :, 96:128], op=ALU.mult), r=[t1k, kcsk_], w=[t3k])
                    S.add("dve", OP("tensor_tensor", out=v2, in0=v2, in1=v3, op=ALU.add), r=[t2k, t3k], w=[t2k])
                    yield
                    Pt, Ptk = gp()
                    S.add("pe", seq([(OP("transpose", Pt[0:64, b * 128:(b + 1) * 128], v2[:, b, :], ident[:, :])) for b in range(NB)]),
                          r=[t2k, "ident"], w=[Ptk])
                    hb = 0 if key0 < 2048 else 64
                    kk0 = key0 % 2048
                    if hb == 0:
                        S.add("act", OP("activation", out=Kr[0:64, kk0:kk0 + NT], in_=Pt[0:64, 0:NT], func=AF.Copy), r=[Ptk], w=[("Kr", (key0 % 2048) // NT)])
                    else:
                        sh, shk = TFp.get()
                        S.add("act", OP("activation", out=sh[0:64, :], in_=Pt[0:64, 0:NT], func=AF.Copy), r=[Ptk], w=[shk])
                        P2, P2k = gp()
                        S.add("pe", OP("matmul", P2[:, 0:NT], comb[0:64, :], sh[0:64, :], start=True, stop=True), r=[shk, "comb"], w=[P2k])
                        S.add("act", OP("activation", out=Kr[64:128, kk0:kk0 + NT], in_=P2[64:128, 0:NT], func=AF.Copy), r=[P2k], w=[("Kr", (key0 % 2048) // NT)])
                    yield

                def gla_common(own, hTb=None, hTk="hT", TFp=None):
                    hTb = hT if hTb is None else hTb
                    TFp = TFg if TFp is None else TFp
                    Pl, Plk = gp()
                    S.add("pe", seq([mm(Pl[0:16, 0:NT], win[:, kc, GLR0:GLR0 + 16], hTb[:, kc, :], start=(kc == 0), stop=(kc == 7)) for kc in range(8)]),
                          r=HK(hTk) + WIN, w=[Plk])
                    S.add("act", OP("activation", out=glrT[0:16, :], in_=Pl[0:16, 0:NT], func=AF.Copy), r=[Plk], w=["glrT"])
                    yield
                    for b in range(NB):
                        Pk_, Pkk = gp()
                        S.add("pe", seq([mm(Pk_[:, 0:256], hTb[:, kc, b * 128:(b + 1) * 128], win[:, kc, GK0:GK0 + 256], start=(kc == 0), stop=(kc == 7)) for kc in range(8)]),
                              r=HK(hTk, bs=[b]) + WIN, w=[Pkk])
                        S.add("act", (OP("activation", out=gk_tm[:, b, :], in_=Pk_[:, 0:256], func=AF.Copy)), r=[Pkk], w=[("gk_tm", b)])
                        yield
                        Pv, Pvk = gp()
                        S.add("pe", seq([mm(Pv[:, :], hTb[:, kc, b * 128:(b + 1) * 128], win[:, kc, GV0:GV0 + 512], start=(kc == 0), stop=(kc == 7)) for kc in range(8)]),
                              r=HK(hTk, bs=[b]) + WIN, w=[Pvk])
                        S.add("dve", (OP("tensor_copy", out=gv_tm[:, b, :], in_=Pv[:, :])), r=[Pvk], w=[("gv_tm", b)])
                        yield
                        Pz, Pzk = gp()
                        S.add("pe", (OP("matmul", Pz[:, 0:256], glrT[:, b * 128:(b + 1) * 128], wgu[:, :], start=True, stop=True)), r=["glrT", "wgu"], w=[Pzk])
                        S.add("act", (OP("activation", out=l_tm[:, b, :], in_=Pz[:, 0:256], func=AF.Exp, scale=-1.0)), r=[Pzk], w=[("l_tm", b)])
                        S.add("act", (OP("activation", out=l_tm[:, b, :], in_=l_tm[:, b, :], func=AF.Ln, bias=1.0)), r=[("l_tm", b)], w=[("l_tm", b)])
                        yield
                        Pc, Pck = gp()
                        S.add("pe", (OP("matmul", Pc[:, 0:256], ut[:, :], l_tm[:, b, :], start=True, stop=True)), r=["ut", ("l_tm", b)], w=[Pck])
                        et, etk = TFp.get()
                        S.add("act", (OP("activation", out=et[:, :], in_=Pc[:, 0:256], func=AF.Exp)), r=[Pck], w=[etk])
                        S.add("dve", (OP("tensor_tensor", out=khat[:, b, :], in0=gk_tm[:, b, :], in1=et[:, :], op=ALU.mult)), r=[etk, ("gk_tm", b)], w=[("khat", b)])
                        yield
                    return None

                def bt_step(hp, TFp, need_e):
                    Pb_, Pbk_ = gp()
                    S.add("pe", seq([OP("matmul", Pb_[:, b * 128:(b + 1) * 128], l_tm[:, b, hp * 128:(hp + 1) * 128], tri[:, :], start=True, stop=True) for b in range(NB)]),
                          r=[("l_tm", b_) for b_ in range(NB)] + ["tri"], w=[Pbk_])
                    S.add("act", OP("activation", out=dec[:, hp, :], in_=Pb_[:, 63:NT:64], func=AF.Exp), r=[Pbk_], w=[("dec", hp)])
                    if not need_e:
                        return None
                    eb, ebk = TFp.get()
                    enb, enbk = TFp.get()
                    S.add("act", OP("activation", out=eb[:, :], in_=Pb_[:, 0:NT], func=AF.Exp), r=[Pbk_], w=[ebk])
                    S.add("act", OP("activation", out=enb[:, :], in_=Pb_[:, 0:NT], func=AF.Exp, scale=-1.0), r=[Pbk_], w=[enbk])
                    return eb, ebk, enb, enbk

                def state_update(hp, ch, Pu, Puk):
                    b, par = ch // 2, ch % 2
                    fns = []
                    for hh in range(2):
                        h = hp * 2 + hh
                        fns.append(mm(Pu[hh * 64:(hh + 1) * 64, 0:128], khat[par * 64:(par + 1) * 64, b, h * 64:(h + 1) * 64], gv_tm[par * 64:(par + 1) * 64, b, h * 128:(h + 1) * 128]))
                    S.add("pe", seq(fns), r=[("khat", b), ("gv_tm", b)], w=[Puk])

                def gla_prefix(hTb=None, hTk="hT", TFp=None):
                    yield from gla_common(False, hTb, hTk, TFp)
                    for hp in range(2):
                        bt_step(hp, TFp, False)
                        yield
                        for ch in range(NCH):
                            Pu, Puk = gp()
                            state_update(hp, ch, Pu, Puk)
                            S.add("dve", (OP("scalar_tensor_tensor", out=Sst[:, hp, :], in0=Sst[:, hp, :], scalar=dec[:, hp, ch:ch + 1], in1=Pu[:, 0:128], op0=ALU.mult, op1=ALU.add)),
                                  r=[Puk, ("dec", hp), ("Sst", hp)], w=[("Sst", hp)])
                            yield

                def gla_own(per_chunk_state, st_dst, TFp=None, TBp=None):
                    TFp = TFg if TFp is None else TFp
                    TBp = TBg if TBp is None else TBp
                    yield from gla_common(True, None, "hT", TFp)
                    for hp in range(2):
                        eb, ebk, enb, enbk = bt_step(hp, TFp, True)
                        yield
                        Pq, Pqk = gp()
                        S.add("pe", seq([mm(Pq[:, 0:NT], win[:, kc, GQ0 + hp * 128:GQ0 + (hp + 1) * 128], hT[:, kc, :], start=(kc == 0), stop=(kc == 7)) for kc in range(8)]),
                              r=HK("hT") + WIN, w=[Pqk])
                        qtls = []
                        for hh in range(2):
                            qtl, qtlk = TBp.get()
                            S.add("dve", OP("scalar_tensor_tensor", out=qtl[:, :], in0=Pq[:, 0:NT], scalar=hsel[:, hh:hh + 1], in1=eb[:, :], op0=ALU.mult, op1=ALU.mult),
                                  r=[Pqk, ebk, "hsel"], w=[qtlk])
                            qtls.append((qtl, qtlk))
                        yield
                        Pk2, Pk2k = gp()
                        S.add("pe", seq([mm(Pk2[:, 0:NT], win[:, kc, GK0 + hp * 128:GK0 + (hp + 1) * 128], hT[:, kc, :], start=(kc == 0), stop=(kc == 7)) for kc in range(8)]),
                              r=HK("hT") + WIN, w=[Pk2k])
                        ktl, ktlk = TBp.get()
                        S.add("dve", OP("tensor_tensor", out=ktl[:, :], in0=Pk2[:, 0:NT], in1=enb[:, :], op=ALU.mult), r=[Pk2k, enbk], w=[ktlk])
                        yield
                        Ps, Psk = gp()
                        fns = []
                        for hh in range(2):
                            for b in range(NB):
                                co = hh * NT + b * 128
                                fns.append(mm(Ps[:, co:co + 128], ktl[:, b * 128:(b + 1) * 128], qtls[hh][0][:, b * 128:(b + 1) * 128]))
                        S.add("pe", seq(fns), r=[ktlk, qtls[0][1], qtls[1][1]], w=[Psk])
                        mks = []
                        for hh in range(2):
                            msk, mskk = TBp.get()
                            S.add("dve", OP("tensor_tensor", out=msk[:, :], in0=Ps[:, hh * NT:(hh + 1) * NT], in1=maskr[:, :], op=ALU.mult), r=[Psk, "maskr"], w=[mskk])
                            mks.append((msk, mskk))
                        yield
                        Po, Pok = P[7], PK(7)
                        for ch in range(NCH):
                            b, par = ch // 2, ch % 2
                            sb_par = ch % 2
                            if per_chunk_state:
                                S.add("sp", OP("dma_start", out=Sst[:, hp, :], in_=stc[ch, hp, :, :]), w=[("Sst", hp)], dma=("Sst_in", hp))
                            if per_chunk_state or ch == 0:
                                S.add("act", OP("activation", out=Sbf[:, hp, sb_par, :], in_=Sst[:, hp, :], func=AF.Copy), r=[("Sst", hp)], w=[("Sbf", hp, sb_par)])
                            fns = []
                            for hh in range(2):
                                h = hp * 2 + hh
                                oc = hh * NT + ch * 64
                                if par == 0:
                                    ob = hh * NT + b * 128
                                    fns.append(OP("matmul", Po[:, ob:ob + 128], gv_tm[:, b, h * 128:(h + 1) * 128], mks[hh][0][:, b * 128:(b + 1) * 128], start=(ch == 0 and hh == 0), stop=False, skip_group_check=True))
                                fns.append(OP("matmul", Po[:, oc:oc + 64], Sbf[:, hp, sb_par, :], qtls[hh][0][:, ch * 64:(ch + 1) * 64], start=False, stop=(par == 1), skip_group_check=True))
                            S.add("pe", seq(fns), r=[("gv_tm", b), mks[0][1], mks[1][1], ("Sbf", hp, sb_par), qtls[0][1], qtls[1][1]], w=[Pok])
                            Pu, Puk = gp()
                            state_update(hp, ch, Pu, Puk)
                            S.add("dve", OP("scalar_tensor_tensor", out=Sst[:, hp, :], in0=Sst[:, hp, :], scalar=dec[:, hp, ch:ch + 1], in1=Pu[:, 0:128], op0=ALU.mult, op1=ALU.add),
                                  r=[Puk, ("dec", hp), ("Sst", hp)], w=[("Sst", hp)])
                            if per_chunk_state:
                                S.add("pool", OP("dma_start", out=st_dst[ch, hp, :, :], in_=Sst[:, hp, :]), r=[("Sst", hp)], dma=("Sst_out", hp))
                            elif ch + 1 < NCH:
                                np_ = (ch + 1) % 2
                                S.add("act", OP("activation", out=Sbf[:, hp, np_, :], in_=Sst[:, hp, :], func=AF.Copy), r=[("Sst", hp)], w=[("Sbf", hp, np_)])
                            yield
                        yield
                        for hh in range(2):
                            h = hp * 2 + hh
                            oc = hh * NT
                            sq, sqk = TBp.get()
                            S.add("act", (OP("activation", out=sq[:, :], in_=Po[:, oc:oc + NT], func=AF.Square)), r=[Pok], w=[sqk])
                            Pn, Pnk = gp()
                            S.add("pe", (OP("matmul", Pn[:, 0:NT], ones_bf[:, :], sq[:, :], start=True, stop=True)), r=[sqk, "ones_bf"], w=[Pnk])
                            rs, rsk = TFp.get()
                            S.add("act", (OP("activation", out=rs[:, :], in_=Pn[:, 0:NT], func=AF.Ln, scale=1.0 / 128, bias=EPS)), r=[Pnk], w=[rsk])
                            yield
                            S.add("act", (OP("activation", out=rs[:, :], in_=rs[:, :], func=AF.Exp, scale=-0.5)), r=[rsk], w=[rsk])
                            on, onk = TFp.get()
                            S.add("dve", (OP("tensor_tensor", out=on[:, :], in0=Po[:, oc:oc + NT], in1=rs[:, :], op=ALU.mult)), r=[Pok, rsk], w=[onk])
                            Pr, Prk = gp()
                            S.add("pe", seq([mm(Pr[:, 0:NT], win[:, kc, GR0 + h * 128:GR0 + (h + 1) * 128], hT[:, kc, :], start=(kc == 0), stop=(kc == 7)) for kc in range(8)]),
                                  r=HK("hT") + WIN, w=[Prk])
                            sg, sgk = TFp.get()
                            S.add("act", (OP("activation", out=sg[:, :], in_=Pr[:, 0:NT], func=AF.Exp, scale=-1.0)), r=[Prk], w=[sgk])
                            S.add("act", (OP("activation", out=sg[:, :], in_=sg[:, :], func=AF.Ln, bias=1.0)), r=[sgk], w=[sgk])
                            S.add("act", (OP("activation", out=sg[:, :], in_=sg[:, :], func=AF.Exp, scale=-1.0)), r=[sgk], w=[sgk])
                            S.add("dve", (OP("tensor_tensor", out=sg[:, :], in0=Pr[:, 0:NT], in1=sg[:, :], op=ALU.mult)), r=[Prk, sgk], w=[sgk])
                            S.add("dve", (OP("scalar_tensor_tensor", out=mixT[:, 4 + h, :], in0=on[:, :], scalar=ggo[:, h:h + 1], in1=sg[:, :], op0=ALU.mult, op1=ALU.mult)),
                                  r=[onk, sgk, "ggo"], w=[("mixT", 4 + h), "hTalt"])
                            yield

                def qpath(qcs_src, c0, TFp=None, TBp=None):
                    TFp = TFq if TFp is None else TFp
                    TBp = TBq if TBp is None else TBp
                    S.add("sp", OP("dma_start", out=qcs[:, :], in_=qcs_src[:, c0:c0 + NT]), w=["qcs"], dma="qcs")
                    sqs = []
                    for kc3 in range(3):
                        Pq, Pqk = gp()
                        S.add("pe", seq([mm(Pq[:, 0:NT], win[:, kc, CQ0 + kc3 * 128:CQ0 + (kc3 + 1) * 128], hT[:, kc, :], start=(kc == 0), stop=(kc == 7)) for kc in range(8)]),
                              r=HK("hT") + WIN, w=[Pqk])
                        S.add("act", (OP("activation", out=cqT[:, kc3, :], in_=Pq[:, 0:NT], func=AF.Copy)), r=[Pqk], w=[("cqT", kc3)])
                        sq, sqk = TBp.get()
                        S.add("act", (OP("activation", out=sq[:, :], in_=Pq[:, 0:NT], func=AF.Square)), r=[Pqk], w=[sqk])
                        sqs.append((sq, sqk))
                        yield
                    Pss, Pssk = gp()
                    S.add("pe", seq([mm(Pss[:, 0:NT], ones_bf[:, :], sqs[i][0][:, :], start=(i == 0), stop=(i == 2)) for i in range(3)]),
                          r=[s[1] for s in sqs] + ["ones_bf"], w=[Pssk])
                    epstk = "epst"
                    S.add("act", (OP("activation", out=epst[:, :], in_=Pss[:, 0:NT], func=AF.Identity, scale=EPS / 384.0, bias=EPS * EPS)), r=[Pssk], w=[epstk])
                    yield
                    for h in range(4):
                        Pn_, Pnk_ = gp()
                        S.add("pe", seq([mm(Pn_[:, 0:NT], wuq[:, kc3, h * 192:h * 192 + 128], cqT[:, kc3, :], start=(kc3 == 0), stop=(kc3 == 2)) for kc3 in range(3)]),
                              r=["wuq", ("cqT", 0), ("cqT", 1), ("cqT", 2)], w=[Pnk_])
                        Pab, Pabk = gp()
                        fns = [mm(Pab[0:64, 0:NT], wuq[:, kc3, h * 192 + 128:h * 192 + 192], cqT[:, kc3, :], start=(kc3 == 0), stop=(kc3 == 2)) for kc3 in range(3)]
                        fns += [mm(Pab[64:128, 0:NT], wuq[:, kc3, 768 + h * 64:768 + (h + 1) * 64], cqT[:, kc3, :], start=(kc3 == 0), stop=(kc3 == 2)) for kc3 in range(3)]
                        S.add("pe", seq(fns), r=["wuq", ("cqT", 0), ("cqT", 1), ("cqT", 2)], w=[Pabk])
                        s1, s1k = TBp.get()
                        s2, s2k = TBp.get()
                        S.add("act", (OP("activation", out=s1[:, :], in_=Pn_[:, 0:NT], func=AF.Square)), r=[Pnk_], w=[s1k])
                        S.add("act", (OP("activation", out=s2[0:64, :], in_=Pab[0:64, 0:NT], func=AF.Square)), r=[Pabk], w=[s2k])
                        Ph, Phk = gp()
                        S.add("pe", seq([mm(Ph[:, 0:NT], ones_bf[:, :], s1[:, :], start=True, stop=False), mm(Ph[:, 0:NT], ones_bf[0:64, :], s2[0:64, :], start=False, stop=True)]),
                              r=[s1k, s2k, "ones_bf"], w=[Phk])
                        rq, rqk = TFp.get()
                        S.add("dve", (OP("scalar_tensor_tensor", out=rq[:, :], in0=Ph[:, 0:NT], scalar=1.0 / 192, in1=epst[:, :], op0=ALU.mult, op1=ALU.add)), r=[Phk, epstk], w=[rqk])
                        S.add("act", (OP("activation", out=rq[:, :], in_=rq[:, :], func=AF.Ln)), r=[rqk], w=[rqk])
                        S.add("act", (OP("activation", out=rq[:, :], in_=rq[:, :], func=AF.Exp, scale=-0.5)), r=[rqk], w=[rqk])
                        S.add("dve", (OP("scalar_tensor_tensor", out=qn[:, h, :], in0=Pn_[:, 0:NT], scalar=gqk[:, 0:1], in1=rq[:, :], op0=ALU.mult, op1=ALU.mult)),
                              r=[Pnk_, rqk, "gqk"], w=[("qn", h)])
                        ab, abk = TFp.get()
                        S.add("dve", (OP("scalar_tensor_tensor", out=ab[:, :], in0=Pab[:, 0:NT], scalar=gqr_t[:, 0:1], in1=rq[:, :], op0=ALU.mult, op1=ALU.mult)),
                              r=[Pabk, rqk, "gqr_t"], w=[abk])
                        S.add("dve", (OP("tensor_tensor", out=ab[:, :], in0=ab[:, :], in1=qcs[:, :], op=ALU.mult)), r=[abk, "qcs"], w=[abk])
                        Pc_, Pck_ = gp()
                        S.add("pe", (OP("matmul", Pc_[:, 0:NT], comb[:, :], ab[:, :], start=True, stop=True)), r=[abk, "comb"], w=[Pck_])
                        S.add("act", (OP("activation", out=qr[0:64, h, 0, :], in_=Pc_[0:64, 0:NT], func=AF.Copy)), r=[Pck_], w=[("qr", h)])
                        S.add("act", (OP("activation", out=qr[64:128, h, 1, :], in_=Pc_[64:128, 0:NT], func=AF.Copy)), r=[Pck_], w=[("qr", h)])
                        yield

                rlbuf = psb("rlbuf", [128, NT])
                att = dict(cnt=0)

                def blk(h, kb, ncols, q0, first, last, bias_ap, zero_tri=False, zero_rows=None, finish=None):
                    return dict(h=h, kb=kb, ncols=ncols, q0=q0, first=first, last=last, bias=bias_ap, zero_tri=zero_tri, zero_rows=zero_rows, finish=finish)

                def emit_qk(B):
                    h, kb, ncols, q0 = B["h"], B["kb"], B["ncols"], B["q0"]
                    key0 = kb * 128
                    hb = 0 if key0 < 2048 else 64
                    kk0 = key0 % 2048
                    si = SCB[att["cnt"] % 3]
                    att["cnt"] += 1
                    Ps, Psk = P[si], PK(si)
                    B["Ps"], B["Psk"] = Ps, Psk
                    kt = key0 // NT
                    S.add("pe", seq([
                        mm(Ps[:, 0:ncols], Kn[:, h, key0:key0 + 128], qn[:, h, q0:q0 + ncols], start=True, stop=False),
                        mm(Ps[:, 0:ncols], Kr[:, kk0:kk0 + 128], qr[:, h, hb // 64, q0:q0 + ncols], start=False, stop=True)]),
                        r=[("Kn", kt, h), ("Kr", kk0 // NT), ("qn", h), ("qr", h)], w=[Psk])

                def emit_rest(B):
                    h, kb, ncols, q0 = B["h"], B["kb"], B["ncols"], B["q0"]
                    Ps, Psk = B["Ps"], B["Psk"]
                    kt = kb * 128 // NT
                    pt, ptk = PT.get()
                    if B["bias"] is None:
                        S.add("act", OP("activation", out=pt[:, 0:ncols], in_=Ps[:, 0:ncols], func=AF.Exp, scale=sclK[:, kb, h:h + 1]),
                              r=[Psk, ("sclK", kb)], w=[ptk])
                    else:
                        S.add("act", OP("activation", out=pt[:, 0:ncols], in_=Ps[:, 0:ncols], func=AF.Exp, scale=sclK[:, kb, h:h + 1], bias=B["bias"]),
                              r=[Psk, ("sclK", kb), "flg"], w=[ptk])
                    if B["zero_tri"]:
                        S.add("pool", OP("memset", pt[64:128, 0:64], 0.0), r=[ptk], w=[ptk])
                    if B["zero_rows"] is not None:
                        zr = B["zero_rows"]
                        S.add("pool", OP("memset", pt[zr[0]:zr[1], 0:ncols], 0.0), r=[ptk], w=[ptk])
                    ab_ = 2 + (h % 2)
                    Pa, Pak = P[ab_], PK(ab_)
                    S.add("pe", seq([
                        OP("matmul", Pa[:, q0:q0 + ncols], Vs[:, kb, h * 128:(h + 1) * 128], pt[:, 0:ncols], start=B["first"], stop=B["last"], skip_group_check=True),
                        OP("matmul", Pa[:, NT + q0:NT + q0 + ncols], ones_bf[:, :], pt[:, 0:ncols], start=False, stop=B["last"], skip_group_check=True)]),
                        r=[("Vs", kb), ptk, "ones_bf"], w=[Pak])
                    if B["finish"] is not None:
                        fh_, fq0, fn = B["finish"]
                        S.add("dve", OP("reciprocal", out=rlbuf[:, 0:fn], in_=Pa[:, NT + fq0:NT + fq0 + fn]), r=[Pak], w=["rlbuf"])
                        S.add("dve", OP("tensor_tensor", out=mixT[:, fh_, fq0:fq0 + fn], in0=Pa[:, fq0:fq0 + fn], in1=rlbuf[:, 0:fn], op=ALU.mult), r=[Pak, "rlbuf"], w=[("mixT", fh_), "hTalt"])

                def attn_run(blocks, hook=None):
                    set_gen(GEN_IN)
                    try:
                        _attn_run(blocks, hook)
                    finally:
                        set_gen(GEN_OUT)

                def _attn_run(blocks, hook=None):
                    n = len(blocks)
                    emit_qk(blocks[0])
                    if n > 1:
                        emit_qk(blocks[1])
                    for k in range(n):
                        if k + 2 < n:
                            emit_qk(blocks[k + 2])
                        emit_rest(blocks[k])
                        if hook is not None:
                            hook()

                def prompt_blocks(p):
                    out = []
                    for h in range(4):
                        nown = NB * p + NB
                        for kb in range(16):
                            out.append(blk(h, kb, NT, 0, kb == 0, False, flg[:, 1:2]))
                        for j in range(nown):
                            kb = 16 + j
                            dj = j - NB * p
                            if dj < 0:
                                out.append(blk(h, kb, NT, 0, False, False, None))
                            else:
                                lastb = (j == nown - 1)
                                out.append(blk(h, kb, NT - 128 * dj, 128 * dj, False, lastb, None, zero_tri=True, finish=((h, 0, NT) if lastb else None)))
                    return out

                def sample_blocks(i):
                    q0 = i * 64
                    par = i % 2
                    out = []
                    for h in range(4):
                        for kb in range(16):
                            out.append(blk(h, kb, 64, q0, kb == 0, False, None))
                        out.append(blk(h, 16 + i // 2, 64, q0, False, True, None, zero_rows=((1 - par) * 64, (1 - par) * 64 + 64), finish=(h, q0, 64)))
                    return out

                def back(xsrc, r0, x1row0, blocks_g1, TFp=None):
                    TFp = TFk if TFp is None else TFp
                    for b in range(NB):
                        if blocks_g1 is not None:
                            load_g1bc(blocks_g1[b])
                        S.add("sp", (OP("dma_start", out=xr[:, :], in_=xsrc[r0 + b * 128:r0 + (b + 1) * 128, :])), w=XTK, dma="xt")
                        for fh in range(2):
                            Po, Pok = gp()
                            S.add("pe", seq([mm(Po[:, :], mixT[:, k, b * 128:(b + 1) * 128], wout[:, k, fh * 512:(fh + 1) * 512], start=(k == 0), stop=(k == 7)) for k in range(8)]),
                                  r=[("mixT", k) for k in range(8)] + ["wout"], w=[Pok])
                            tt, ttk = TFp.get()
                            tt2, tt2k = TFp.get()
                            S.add("dve", (OP("tensor_tensor", out=tt[:, :], in0=Po[:, 0:256], in1=g1bc[:, fh * 512:fh * 512 + 256], op=ALU.mult)), r=[Pok, "g1bc"], w=[ttk])
                            S.add("dve", (OP("tensor_tensor", out=tt2[:, :], in0=Po[:, 256:512], in1=g1bc[:, fh * 512 + 256:fh * 512 + 512], op=ALU.mult)), r=[Pok, "g1bc"], w=[tt2k])
                            S.add("dve", (OP("tensor_tensor", out=xr[:, fh * 512:fh * 512 + 256], in0=xr[:, fh * 512:fh * 512 + 256], in1=tt[:, :], op=ALU.add)), r=[ttk, "xt"], w=["xt"])
                            S.add("dve", (OP("tensor_tensor", out=xr[:, fh * 512 + 256:fh * 512 + 512], in0=xr[:, fh * 512 + 256:fh * 512 + 512], in1=tt2[:, :], op=ALU.add)), r=[tt2k, "xt"], w=["xt"])
                            yield
                        S.add("pool", (OP("dma_start", out=x1s[x1row0 + b * 128:x1row0 + (b + 1) * 128, :], in_=xr[:, :])), r=["xt"], w=[("x1s", x1row0 // 128 + b)], dma="x1s_w")

                def run(g):
                    for _ in g:
                        pass

                def chain(*gens):
                    for g in gens:
                        yield from g

                def hook_of(g, every=1):
                    st = dict(n=0)

                    def hk():
                        st["n"] += 1
                        if st["n"] % every == 0:
                            next(g, None)
                    return hk

                def inter(gens):
                    active = dict(gens)
                    while active:
                        for name in list(active):
                            g = active.get(name)
                            if g is None:
                                continue
                            try:
                                tok = next(g)
                            except StopIteration:
                                del active[name]
                                continue
                            if isinstance(tok, str) and tok.startswith("need:"):
                                dep = tok[5:]
                                if dep in active:
                                    for _ in active[dep]:
                                        pass
                                    del active[dep]

                load_g1bc(0)
                NPRE = 2048 // NT
                hbufs = [(hT, "hT"), (mixT, "hTalt")]
                run(front(xpre, 0, [(0, NT, 0)], *hbufs[0]))
                for t in range(NPRE):
                    hb_, hk_ = hbufs[t % 2]
                    gens = dict(k=chain(kvproj(None, None, 0, hb_, hk_), lat_transpose(), kside(t * NT, kcs_pre, t * NB)),
                                g=gla_prefix(hb_, hk_, TFq))
                    if t + 1 < NPRE:
                        gens["f"] = front(xpre, (t + 1) * NT, [(0, NT, 0)], hbufs[(t + 1) % 2][0], hbufs[(t + 1) % 2][1], TFk)
                    inter(gens)
                    ck('pre%d' % t)
                for hp in range(2):
                    S.add("dve", (OP("tensor_scalar", out=Sst[:, hp, :], in0=Sst[:, hp, :], scalar1=flg[:, 0:1], scalar2=None, op0=ALU.mult)), r=[("Sst", hp), "flg"], w=[("Sst", hp)])
                NOWN = 2048 // NT

                def pre_own(p):
                    return chain(front(xown, p * NT, [(0, NT, 0)]), kvproj(lat_own, kr_own, p * NT), lat_transpose(), kside(2048 + p * NT, kcs_own, p * NB))
                run(pre_own(0))
                run(qpath(qcs_own, 0))
                for p in range(NOWN):
                    hk_chain = chain(gla_own(False, None), pre_own(p + 1) if p + 1 < NOWN else iter(()))
                    attn_run(prompt_blocks(p), hook_of(hk_chain, 1))
                    set_gen(GEN_IN)
                    run(hk_chain)
                    set_gen(GEN_OUT)
                    gens = dict(back=back(xown, p * NT, p * NT, None))
                    if p + 1 < NOWN:
                        gens["q"] = qpath(qcs_own, (p + 1) * NT)
                    inter(gens)
                    ck('own%d' % p)
                for hp in range(2):
                    S.add("pool", (OP("dma_start", out=st_p[hp, :, :], in_=Sst[:, hp, :])), r=[("Sst", hp)], dma=("st_p", hp))
                run(front(xsm, 0, [(i * 64, 64, 1 + i) for i in range(4)]))
                run(kvproj(lat_s, kr_s, 0))
                run(lat_transpose())
                run(kside(2048, kcs_sn, 0))
                inter(dict(q=qpath(qcs_s, 0), g=gla_own(True, st_s)))
                NPT = 2048 // NT
                BSS = [BS0, BS1]

                def stage_a(i, t, bs):
                    S.add("sp", OP("dma_start", out=bs["lat"], in_=latc[i, t * NT:(t + 1) * NT, :].rearrange("(b p) d -> p b d", p=128)), w=BK(bs["latk"]), dma=bs["latk"])
                    S.add("sp", OP("dma_start", out=bs["kr"], in_=krc[i, t * NT:(t + 1) * NT, :].rearrange("(b p) d -> p b d", p=128)), w=BK(bs["krk"]), dma=bs["krk"])
                    S.add("sp", OP("dma_start", out=bs["kcs"], in_=kcs_pre[:, t * NB:(t + 1) * NB, :]), w=[bs["kcsk"]], dma=bs["kcsk"])
                    yield
                    yield from lat_transpose(bs)

                tiles = [(i, t) for i in range(4) for t in range(NPT)]
                run(stage_a(0, 0, BSS[0]))
                for n, (i, t) in enumerate(tiles):
                    bs = BSS[n % 2]
                    nxtA = stage_a(tiles[n + 1][0], tiles[n + 1][1], BSS[(n + 1) % 2]) if n + 1 < len(tiles) else iter(())
                    if t < NPT - 1:
                        inter(dict(b=kside(t * NT, kcs_pre, t * NB, bs, None, False), a=nxtA))
                    else:
                        run(kside(t * NT, kcs_pre, t * NB, bs, None, False))
                        attn_run(sample_blocks(i), hook_of(nxtA, 4))
                        run(nxtA)
                run(back(xsm, 0, 2048, [1, 2]))
                ck('samp')
            S.barrier()

            with contextlib.ExitStack() as ph2:
                def msb(name, shape, dt=F32):
                    return ph2.enter_context(nc.sbuf_tensor(name, list(shape), dt))
                wup = msb("wup", [128, 8, 4096], BF16)
                wdn = msb("wdn", [128, 32, 1024], BF16)
                g2bc = msb("g2bc", [128, 1024])
                x1t = [msb("x1t%d" % i, [128, NB, 1024]) for i in range(2)]
                xw = msb("xw", [128, 1024])
                h2T = [msb("h2T%d" % i, [128, 8, NT], BF16) for i in range(2)]
                uT = msb("uT", [128, 32, NT], BF16)
                st2 = msb("st2", [128, 8])
                junk2 = msb("junk2", [128, 1024], BF16)
                RFb = [msb("RF%d" % i, [128, NT]) for i in range(6)]
                RFa = Rot([(t, t.name) for t in RFb[0:2]])
                RF = Rot([(t, t.name) for t in RFb[2:6]])
                gen8 = Rot(list(range(8)))

                def gp8():
                    i = gen8.get()
                    return P[i], PK(i)

                for jb in range(8):
                    S.add("pool", OP("dma_start", out=wup[:, :, jb * 512:(jb + 1) * 512], in_=w_up[:, jb * 512:(jb + 1) * 512].rearrange("(kc p) n -> p kc n", p=128)),
                          w=[("wup", jb)], dma=("wup", jb))
                for j4 in range(8):
                    S.add("pool", OP("dma_start", out=wdn[:, j4 * 4:(j4 + 1) * 4, :], in_=w_down[j4 * 512:(j4 + 1) * 512, :].rearrange("(j p) n -> p j n", p=128)),
                          w=[("wdn", j4)], dma=("wdn", j4))
                WDN = [("wdn", j4) for j4 in range(8)]

                def load_g2bc(v):
                    for fh in range(2):
                        Pg, Pgk = gp8()
                        S.add("pe", OP("matmul", Pg[:, :], sel[:, v, :], gater[:, 1, fh * 512:(fh + 1) * 512], start=True, stop=True),
                              r=["sel", "gater"], w=[Pgk])
                        S.add("act", OP("activation", out=g2bc[:, fh * 512:(fh + 1) * 512], in_=Pg[:, :], func=AF.Copy), r=[Pgk], w=["g2bc"])

                def mlp_front(ti, row0, segs):
                    xb, hb = x1t[ti % 2], h2T[ti % 2]
                    xk, hk = "x1t%d" % (ti % 2), "h2T%d" % (ti % 2)
                    for b in range(NB):
                        S.add("sp", OP("dma_start", out=xb[:, b, :], in_=x1s[row0 + b * 128:row0 + (b + 1) * 128, :]), r=[("x1s", row0 // 128 + b)], w=[(xk, b)], dma=(xk, b))
                        S.add("act", OP("activation", out=junk2[:, :], in_=xb[:, b, :], func=AF.Square, accum_out=st2[:, 4:5]), r=[(xk, b)], w=["junk2", "st2"])
                        S.add("act", act(st2[:, 5:6], st2[:, 4:5], AF.Ln, scale=1.0 / 1024, bias=EPS), r=["st2"], w=["st2"])
                        S.add("act", act(st2[:, 6:7], st2[:, 5:6], AF.Exp, scale=-0.5), r=["st2"], w=["st2"])
                        S.add("dve", OP("tensor_scalar", out=xw[:, :], in0=xb[:, b, :], scalar1=st2[:, 6:7], scalar2=None, op0=ALU.mult), r=[(xk, b), "st2"], w=["xw"])
                        yield
                        for k2 in range(2):
                            Pt, Ptk = gp8()
                            S.add("pe", seq([OP("transpose", Pt[:, kk * 128:(kk + 1) * 128], xw[:, (k2 * 4 + kk) * 128:(k2 * 4 + kk + 1) * 128], ident[:, :]) for kk in range(4)]),
                                  r=["xw", "ident"], w=[Ptk])
                            for kk in range(4):
                                kc = k2 * 4 + kk
                                for (c0, ncol, m) in segs:
                                    lo = max(c0, b * 128)
                                    hi = min(c0 + ncol, (b + 1) * 128)
                                    if lo >= hi:
                                        continue
                                    S.add("act", OP("activation", out=hb[:, kc, lo:hi], in_=Pt[:, kk * 128 + lo - b * 128:kk * 128 + hi - b * 128], func=AF.Identity,
                                                    scale=gm2[:, kc, m:m + 1], bias=sh2[:, kc, m:m + 1]), r=[Ptk, "gm2", "sh2"], w=[(hk, kc, b)])
                            yield

                def mlp_main(ti, ydst, yrow0, blocks_g2, hook):
                    xb, hb = x1t[ti % 2], h2T[ti % 2]
                    xk, hk = "x1t%d" % (ti % 2), "h2T%d" % (ti % 2)
                    for j in range(32):
                        Pu, Puk = gp8()
                        S.add("pe", seq([mm(Pu[:, 0:NT], wup[:, kc, j * 128:(j + 1) * 128], hb[:, kc, :], start=(kc == 0), stop=(kc == 7)) for kc in range(8)]),
                              r=[(hk, kc_, b_) for kc_ in range(8) for b_ in range(NB)] + [("wup", j // 4)], w=[Puk])
                        rt, rtk = RF.get()
                        S.add("act", OP("activation", out=rt[:, :], in_=Pu[:, 0:NT], func=AF.Relu), r=[Puk], w=[rtk])
                        S.add("dve", OP("tensor_tensor", out=uT[:, j, :], in0=Pu[:, 0:NT], in1=rt[:, :], op=ALU.mult), r=[Puk, rtk], w=[("uT", j)])
                        if hook is not None and j % 2 == 1:
                            hook()
                    UT = [("uT", j) for j in range(32)]
                    for b in range(NB):
                        if blocks_g2 is not None:
                            load_g2bc(blocks_g2[b])
                        for fh in range(2):
                            Po, Pok = gp8()
                            S.add("pe", seq([mm(Po[:, :], uT[:, j, b * 128:(b + 1) * 128], wdn[:, j, fh * 512:(fh + 1) * 512], start=(j == 0), stop=(j == 31)) for j in range(32)]),
                                  r=UT + WDN, w=[Pok])
                            for q2 in range(2):
                                tt, ttk = RF.get()
                                c0 = fh * 512 + q2 * 256
                                S.add("dve", OP("tensor_tensor", out=tt[:, :], in0=Po[:, q2 * 256:(q2 + 1) * 256], in1=g2bc[:, c0:c0 + 256], op=ALU.mult), r=[Pok, "g2bc"], w=[ttk])
                                S.add("dve", OP("tensor_tensor", out=xb[:, b, c0:c0 + 256], in0=xb[:, b, c0:c0 + 256], in1=tt[:, :], op=ALU.add), r=[ttk, (xk, b)], w=[(xk, b)])
                        S.add("pool", OP("dma_start", out=ydst[yrow0 + b * 128:yrow0 + (b + 1) * 128, :], in_=xb[:, b, :]), r=[(xk, b)], dma=("y_o", ti % 2, b))

                load_g2bc(0)
                NMT = 2048 // NT
                tiles2 = [(p * NT, y_own, p * NT, [(0, NT, 0)], None) for p in range(NMT)] + [(2048, y_s, 0, [(i * 64, 64, 1 + i) for i in range(4)], [1, 2])]
                for _ in mlp_front(0, tiles2[0][0], tiles2[0][3]):
                    pass
                for ti, (row0, ydst, yrow0, segs, bg2) in enumerate(tiles2):
                    if ti + 1 < len(tiles2):
                        nx = mlp_front(ti + 1, tiles2[ti + 1][0], tiles2[ti + 1][3])
                    else:
                        nx = iter(())
                    mlp_main(ti, ydst, yrow0, bg2, (lambda nx=nx: next(nx, None)))
                    for _ in nx:
                        pass

        except _Stop:
            pass
        S.emit()
    return nc


def _rope_tables(pos):
    half = 32
    inv = np.power(np.float32(10000.0), -np.arange(half, dtype=np.float32) / np.float32(half)).astype(np.float32)
    ang = pos.astype(np.float32)[:, None] * inv[None, :]
    return np.cos(ang).astype(np.float32), np.sin(ang).astype(np.float32)


def _kcs(pos):
    c, s = _rope_tables(pos)
    t = np.concatenate([c, c, -s, s], axis=1)
    n = pos.shape[0]
    return np.ascontiguousarray(t.reshape(n // 128, 128, 128).transpose(1, 0, 2))


def _qcs(pos):
    c, s = _rope_tables(pos)
    return np.ascontiguousarray(np.concatenate([c.T, c.T, s.T, s.T], axis=0))


_NC_CACHE = {}


def _prep(x_prompt, x_sample, cache_mla_latent, cache_mla_krope, state_gla, c_prompt, c_sample,
           w_ada, b_ada, g_norm1, w_in, g_q_lora, w_uq, g_kv_lora, w_ukv, g_q_head, g_k_head,
           w_gate_up, b_gate_up, g_gla_out, w_out, g_norm2, w_up, w_down):
    f = lambda a: np.ascontiguousarray(np.asarray(a, dtype=np.float32))
    x_prompt, x_sample = f(x_prompt), f(x_sample)
    latc_all, krc_all, st_all = f(cache_mla_latent)[0], f(cache_mla_krope)[0], f(state_gla)[0]
    c_prompt, c_sample = f(c_prompt), f(c_sample)
    w_ada, b_ada, g_norm1, w_in = f(w_ada)[0], f(b_ada)[0], f(g_norm1)[0], f(w_in)[0]
    g_q_lora, w_uq, g_kv_lora, w_ukv = f(g_q_lora)[0], f(w_uq)[0], f(g_kv_lora)[0], f(w_ukv)[0]
    g_q_head, g_k_head = f(g_q_head)[0], f(g_k_head)[0]
    w_gate_up, b_gate_up, g_gla_out = f(w_gate_up)[0], f(b_gate_up)[0], f(g_gla_out)[0]
    w_out, g_norm2, w_up, w_down = f(w_out)[0], f(g_norm2)[0], f(w_up)[0], f(w_down)[0]

    rot_cols = []
    for h in range(4):
        base = h * 192 + 128
        rot_cols += list(range(base + 32, base + 64)) + list(range(base, base + 32))
    w_uq_ext = np.ascontiguousarray(np.concatenate([w_uq, w_uq[:, rot_cols]], axis=1))
    kn_cols, v_cols = [], []
    for h in range(4):
        kn_cols += list(range(h * 256, h * 256 + 128))
        v_cols += list(range(h * 256 + 128, h * 256 + 256))
    w_ukv_p = np.ascontiguousarray(w_ukv[:, kn_cols + v_cols])
    w_gu_aug = np.zeros((32, 256), np.float32)
    w_gu_aug[0:16] = w_gate_up
    w_gu_aug[16] = b_gate_up
    colT = lambda v, n: np.ascontiguousarray(v.reshape(n, 128).T)
    gqr_col = np.concatenate([g_q_head[128:192], np.roll(g_q_head[128:192], -32)])[:, None]
    ident = np.eye(128, dtype=np.float32)
    ii = np.arange(128)
    same = (ii[:, None] // 64) == (ii[None, :] // 64)
    tri = np.where(same & (ii[:, None] <= ii[None, :]), np.float32(-1.0 / 16), np.float32(0)).astype(np.float32)
    utm = np.where(same & (ii[:, None] > ii[None, :]), np.float32(-1.0 / 16), np.float32(0)).astype(np.float32)
    mblk = (same & (ii[:, None] <= ii[None, :])).astype(np.float32)
    maskr = np.ascontiguousarray(np.tile(mblk, (1, NB)))
    hsel = np.zeros((128, 2), np.float32)
    hsel[0:64, 0] = 0.125
    hsel[64:128, 1] = 0.125
    sel = np.zeros((5, 3, 128), np.float32)
    sel[0, 0, :] = 1
    sel[1, 1, 0:64] = 1
    sel[2, 1, 64:128] = 1
    sel[3, 2, 0:64] = 1
    sel[4, 2, 64:128] = 1
    comb = ((ii[:, None] % 64) == (ii[None, :] % 64)).astype(np.float32)
    kcs_pre = _kcs(np.arange(2048))
    kcs_sn = _kcs(2048 + (np.arange(256) % 64))
    qcs_s = _qcs(2048 + (np.arange(256) % 64))
    shared = dict(
        w_ada=w_ada, b_ada_row=b_ada[None, :], b_adaT=colT(b_ada, 48),
        g1T=colT(g_norm1, 8), g2T=colT(g_norm2, 8), gqlT=colT(g_q_lora, 3),
        gqn=np.ascontiguousarray(g_q_head[0:128, None]), gkn=np.ascontiguousarray(g_k_head[0:128, None]),
        gqr=np.ascontiguousarray(gqr_col), gkr_row=np.ascontiguousarray(g_k_head[None, 128:192]),
        gkv_row=g_kv_lora[None, :], ggoT=colT(g_gla_out, 4),
        w_in=w_in, w_uq_ext=w_uq_ext, w_ukv_p=w_ukv_p, w_gu_aug=w_gu_aug, w_out=w_out, w_up=w_up, w_down=w_down,
        kcs_pre=kcs_pre, kcs_sn=kcs_sn, qcs_s=qcs_s,
        c_ident=ident, c_tri=tri, c_ut=utm, c_mask=maskr, c_sel=sel, c_comb=comb, c_hsel=hsel,
    )
    in_maps = []
    for c in range(8):
        pb, half = c // 2, c % 2
        pos_own = half * 2048 + np.arange(2048)
        flag = np.zeros((128, 2), np.float32)
        flag[:, 0] = float(half)
        flag[:, 1] = 0.0 if half == 1 else NEG
        cvec = np.concatenate([c_prompt[pb:pb + 1], c_sample[4 * c:4 * c + 4]], axis=0)
        m = dict(shared)
        m.update(
            xpre=np.ascontiguousarray(x_prompt[pb, 0:2048]),
            xown=np.ascontiguousarray(x_prompt[pb, half * 2048:(half + 1) * 2048]),
            xsm=np.ascontiguousarray(x_sample[4 * c:4 * c + 4].reshape(256, 1024)),
            latc=np.ascontiguousarray(latc_all[4 * c:4 * c + 4]),
            krc=np.ascontiguousarray(krc_all[4 * c:4 * c + 4]),
            stc=np.ascontiguousarray(st_all[4 * c:4 * c + 4].reshape(4, 2, 128, 128)),
            cT=np.ascontiguousarray(cvec.T), flag=flag,
            kcs_own=_kcs(pos_own), qcs_own=_qcs(pos_own),
        )
        in_maps.append(m)

    return in_maps


def kernel(**inputs):
    in_maps = _prep(**inputs)
    if "nc" not in _NC_CACHE:
        _NC_CACHE["nc"] = build()
    res = run_bass_kernel_spmd(_NC_CACHE["nc"], in_maps, core_ids=list(range(8)))
    return _assemble(res.results)


def _assemble(R):
    y_p = np.zeros((4, 4096, 1024), np.float32)
    lat_p = np.zeros((1, 4, 4096, 256), np.float32)
    kr_p = np.zeros((1, 4, 4096, 64), np.float32)
    st_pp = np.zeros((1, 4, 4, 64, 128), np.float32)
    y_s = np.zeros((32, 64, 1024), np.float32)
    lat_s = np.zeros((1, 32, 64, 256), np.float32)
    kr_s = np.zeros((1, 32, 64, 64), np.float32)
    st_s = np.zeros((1, 32, 4, 64, 128), np.float32)
    for c in range(8):
        pb, half = c // 2, c % 2
        sl = slice(half * 2048, (half + 1) * 2048)
        y_p[pb, sl] = R[c]["y_own"]
        lat_p[0, pb, sl] = R[c]["lat_own"]
        kr_p[0, pb, sl] = R[c]["kr_own"]
        if half == 1:
            st_pp[0, pb] = R[c]["st_p"].reshape(4, 64, 128)
        y_s[4 * c:4 * c + 4] = R[c]["y_s"].reshape(4, 64, 1024)
        lat_s[0, 4 * c:4 * c + 4] = R[c]["lat_s"].reshape(4, 64, 256)
        kr_s[0, 4 * c:4 * c + 4] = R[c]["kr_s"].reshape(4, 64, 64)
        st_s[0, 4 * c:4 * c + 4] = R[c]["st_s"].reshape(4, 4, 64, 128)
    return (y_p, y_s, lat_p, kr_p, st_pp, lat_s, kr_s, st_s)
```

```python
import contextlib
import numpy as np
import concourse.bass as bass
import concourse.mybir as mybir
from concourse.bass_utils import run_bass_kernel_spmd

F32 = mybir.dt.float32
BF16 = mybir.dt.bfloat16
AF = mybir.ActivationFunctionType
ALU = mybir.AluOpType
EPS = 1e-6
NT = 256
NB = NT // 128
NCH = NT // 64
NEG = -30000.0


class Sched:
    ENGS = ("pe", "act", "dve", "pool", "sp")

    def __init__(self, nc):
        self.nc = nc
        self.ops = {e: [] for e in self.ENGS}
        self.last_w = {}
        self.readers = {}
        self.dma_cnt = {}
        self.pending = {e: None for e in self.ENGS}
        self.stopped = False
        self.defer_mode = False
        self.allow_defer = True
        self.defer_engs = ("act", "dve", "pool")
        self.deferred = []
        self.def_r = set()
        self.def_w = set()

    def barrier(self):
        if self.stopped:
            return
        self.flush()
        toks = []
        for e in self.ENGS:
            for i in range(len(self.ops[e]) - 1, -1, -1):
                if self.ops[e][i]["dma"] is None:
                    toks.append(("c", e, i))
                    break
        for k, c in self.dma_cnt.items():
            toks.append(("d", k, c))
        for e in self.ENGS:
            self.pending[e] = list(toks)

    def add(self, eng, fn, r=(), w=(), dma=None):
        if self.stopped:
            return None
        if self.defer_mode and self.allow_defer and dma is None and eng in self.defer_engs:
            self.deferred.append((eng, fn, tuple(r), tuple(w)))
            self.def_r.update(r)
            self.def_w.update(w)
            return None
        if self.deferred:
            if any(k in self.def_w for k in r) or any((k in self.def_w or k in self.def_r) for k in w):
                self.flush()
        return self._add(eng, fn, r, w, dma)

    def flush(self):
        d = self.deferred
        self.deferred = []
        self.def_r = set()
        self.def_w = set()
        for (eng, fn, r, w) in d:
            self._add(eng, fn, r, w, None)

    def _add(self, eng, fn, r=(), w=(), dma=None):
        ops = self.ops[eng]
        idx = len(ops)
        deps = set()
        for k in r:
            t = self.last_w.get(k)
            if t is not None:
                deps.add(t)
        for k in w:
            t = self.last_w.get(k)
            if t is not None:
                deps.add(t)
            for t in self.readers.get(k, ()):
                deps.add(t)
        if self.pending[eng] is not None:
            deps.update(self.pending[eng])
            self.pending[eng] = None
        if dma is not None:
            c = self.dma_cnt.get(dma, 0) + 16
            self.dma_cnt[dma] = c
            tok = ("d", dma, c)
        else:
            tok = ("c", eng, idx)
        fdeps = []
        for t in deps:
            if t[0] == "c":
                if t[1] == eng and eng == "pe":
                    continue
                if t[1] == eng and t[2] == idx:
                    continue
                self.ops[t[1]][t[2]]["sig"] = True
            fdeps.append(t)
        ops.append(dict(fn=fn, deps=fdeps, sig=False, dma=dma))
        for k in w:
            self.last_w[k] = tok
            self.readers[k] = []
        for k in r:
            self.readers.setdefault(k, []).append(tok)
        return tok

    def emit(self, final_eng="sp"):
        self.flush()
        nc = self.nc
        with contextlib.ExitStack() as es:
            esem = {e: es.enter_context(nc.semaphore("s_" + e)) for e in self.ENGS}
            dsem = {}
            for i, k in enumerate(self.dma_cnt):
                dsem[k] = es.enter_context(nc.semaphore("d%d" % i))
            sigidx = {}
            for e in self.ENGS:
                c = 0
                arr = []
                for op in self.ops[e]:
                    if op["sig"]:
                        c += 1
                    arr.append(c)
                sigidx[e] = arr
            block = es.enter_context(nc.Block())

            def run(e, h):
                waited = {}
                for op in self.ops[e]:
                    for t in op["deps"]:
                        if t[0] == "c":
                            key = ("c", t[1])
                            val = sigidx[t[1]][t[2]]
                            sem = esem[t[1]]
                        else:
                            key = ("d", t[1])
                            val = t[2]
                            sem = dsem[t[1]]
                        if waited.get(key, 0) >= val:
                            continue
                        waited[key] = val
                        h.wait_ge(sem, val)
                    ins = op["fn"](h)
                    if op["dma"] is not None:
                        ins.then_inc(dsem[op["dma"]], 16)
                    elif op["sig"]:
                        ins.then_inc(esem[e], 1)
                if e == final_eng:
                    for k, c in self.dma_cnt.items():
                        if waited.get(("d", k), 0) < c:
                            h.wait_ge(dsem[k], c)

            @block.tensor
            def _(h):
                run("pe", h)

            @block.scalar
            def _(h):
                run("act", h)

            @block.vector
            def _(h):
                run("dve", h)

            @block.gpsimd
            def _(h):
                run("pool", h)

            @block.sync
            def _(h):
                run("sp", h)


class Rot:
    def __init__(self, items):
        self.items = items
        self.i = 0

    def get(self):
        it = self.items[self.i % len(self.items)]
        self.i += 1
        return it


def OP(meth, *a, **kw):
    return lambda e: getattr(e, meth)(*a, **kw)


def seq(fns):
    def f(e):
        ins = None
        for g in fns:
            ins = g(e)
        return ins
    return f


def mm(out, lhsT, rhs, start=True, stop=True):
    return OP("matmul", out, lhsT, rhs, start=start, stop=stop)


def act(out, in_, func, **kw):
    return OP("activation", out=out, in_=in_, func=func, **kw)


CQ0, CKV0, KR0, GQ0, GK0, GV0, GLR0, GR0 = 0, 384, 640, 704, 960, 1216, 1728, 1744


class _Stop(Exception):
    pass


def build(dbg=False, stop=None):
    def ck(tag):
        if stop == tag:
            S.stopped = True
    nc = bass.Bass("TRN2", target_bir_lowering=False)
    S = Sched(nc)

    def din(name, shape):
        return nc.dram_tensor(name, list(shape), F32, kind="ExternalInput").ap()

    def dout(name, shape):
        return nc.dram_tensor(name, list(shape), F32, kind="ExternalOutput").ap()

    xpre = din("xpre", [2048, 1024])
    xown = din("xown", [2048, 1024])
    xsm = din("xsm", [256, 1024])
    latc = din("latc", [4, 2048, 256])
    krc = din("krc", [4, 2048, 64])
    stc = din("stc", [4, 2, 128, 128])
    cT = din("cT", [1024, 5])
    flag = din("flag", [128, 2])
    w_ada = din("w_ada", [1024, 6144])
    b_ada_row = din("b_ada_row", [1, 6144])
    b_adaT = din("b_adaT", [128, 48])
    g1T = din("g1T", [128, 8])
    g2T = din("g2T", [128, 8])
    gqlT = din("gqlT", [128, 3])
    gqn = din("gqn", [128, 1])
    gkn = din("gkn", [128, 1])
    gqr = din("gqr", [128, 1])
    gkr_row = din("gkr_row", [1, 64])
    gkv_row = din("gkv_row", [1, 256])
    ggoT = din("ggoT", [128, 4])
    w_in = din("w_in", [1024, 2256])
    w_uq_ext = din("w_uq_ext", [384, 1024])
    w_ukv_p = din("w_ukv_p", [256, 1024])
    w_gu_aug = din("w_gu_aug", [32, 256])
    w_out = din("w_out", [1024, 1024])
    w_up = din("w_up", [1024, 4096])
    w_down = din("w_down", [4096, 1024])
    kcs_pre = din("kcs_pre", [128, 16, 128])
    kcs_own = din("kcs_own", [128, 16, 128])
    kcs_sn = din("kcs_sn", [128, 2, 128])
    qcs_own = din("qcs_own", [128, 2048])
    qcs_s = din("qcs_s", [128, 256])
    c_ident = din("c_ident", [128, 128])
    c_tri = din("c_tri", [128, 128])
    c_ut = din("c_ut", [128, 128])
    c_mask = din("c_mask", [128, 256])
    c_sel = din("c_sel", [5, 3, 128])
    c_comb = din("c_comb", [128, 128])
    c_hsel = din("c_hsel", [128, 2])

    y_own = dout("y_own", [2048, 1024])
    y_s = dout("y_s", [256, 1024])
    lat_own = dout("lat_own", [2048, 256])
    kr_own = dout("kr_own", [2048, 64])
    st_p = dout("st_p", [2, 128, 128])
    lat_s = dout("lat_s", [256, 256])
    kr_s = dout("kr_s", [256, 64])
    st_s = dout("st_s", [4, 2, 128, 128])
    x1s = nc.dram_tensor("x1s", [2304, 1024], F32).ap()

    P = [nc.alloc_psum_tensor("P%d" % i, [128, 512], F32) for i in range(8)]

    def PK(i):
        return "P%d" % i

    with contextlib.ExitStack() as glob:
        def gsb(name, shape, dt=F32):
            return glob.enter_context(nc.sbuf_tensor(name, list(shape), dt))

        ident = gsb("ident", [128, 128])
        tri = gsb("tri", [128, 128])
        ut = gsb("ut", [128, 128])
        maskr = gsb("maskr", [128, 256])
        sel = gsb("sel", [5, 3, 128])
        comb = gsb("comb", [128, 128])
        flg = gsb("flg", [128, 2])
        hsel = gsb("hsel", [128, 2])
        ones_bf = gsb("ones_bf", [128, 128], BF16)
        ones_f = gsb("ones_f", [1, 128])
        gm1 = gsb("gm1", [128, 8, 5])
        sh1 = gsb("sh1", [128, 8, 5])
        gm2 = gsb("gm2", [128, 8, 5])
        sh2 = gsb("sh2", [128, 8, 5])
        gater = gsb("gater", [5, 2, 1024])
        g2c = gsb("g2c", [128, 8])
        for (t, d) in ((ident, c_ident), (tri, c_tri), (ut, c_ut), (maskr, c_mask), (comb, c_comb), (flg, flag), (g2c, g2T), (hsel, c_hsel)):
            S.add("sp", (OP("dma_start", out=t[:, :], in_=d[:, :])), w=[t.name], dma=t.name)
        S.add("sp", OP("dma_start", out=sel[:, :, :], in_=c_sel[:, :, :]), w=["sel"], dma="sel")
        S.add("dve", OP("memset", ones_bf[:, :], 1.0), w=["ones_bf"])
        S.add("dve", OP("memset", ones_f[:, :], 1.0), w=["ones_f"])

        GEN_OUT = [5, 6, 0, 1, 4]
        GEN_IN = [5, 6]
        gen4 = Rot(list(GEN_OUT))
        SCB = [0, 1, 4]

        def set_gen(lst):
            gen4.items = list(lst)
            gen4.i = 0

        def gp():
            i = gen4.get()
            return P[i], PK(i)

        try:
            with contextlib.ExitStack() as ph1:
                def psb(name, shape, dt=F32):
                    return ph1.enter_context(nc.sbuf_tensor(name, list(shape), dt))

                win = psb("win", [128, 8, 2256], BF16)
                wuq = psb("wuq", [128, 3, 1024], BF16)
                wukv = psb("wukv", [128, 2, 1024], BF16)
                wout = psb("wout", [128, 8, 1024], BF16)
                wgu = psb("wgu", [32, 256], BF16)
                g1c = psb("g1c", [128, 8])
                gqk = psb("gqk", [128, 1])
                gkn_t = psb("gkn_t", [128, 1])
                gqr_t = psb("gqr_t", [128, 1])
                ggo = psb("ggo", [128, 4])
                gkr_bc = psb("gkr_bc", [128, NB, 64])
                gkv_bc = psb("gkv_bc", [128, 256])
                g1bc = psb("g1bc", [128, 1024])
                zcol = psb("zcol", [128, 1])
                glrT = psb("glrT", [32, NT], BF16)
                Sst = psb("Sst", [128, 2, 128])
                S.add("dve", OP("memset", zcol[:, :], 0.0), w=["zcol"])
                S.add("dve", OP("memset", glrT[:, :], 1.0), w=["glrT"])
                S.add("dve", OP("memset", Sst[:, :, :], 0.0), w=[("Sst", 0), ("Sst", 1)])
                for (t, d) in ((g1c, g1T), (gkn_t, gkn), (gqr_t, gqr), (ggo, ggoT)):
                    S.add("sp", (OP("dma_start", out=t[:, :], in_=d[:, :])), w=[t.name], dma=t.name)
                S.add("sp", OP("dma_start", out=gqk[:, :], in_=gqn[:, :]), w=["gqk"], dma="gqk")
                S.add("dve", OP("tensor_tensor", out=gqk[:, :], in0=gqk[:, :], in1=gkn_t[:, :], op=ALU.mult), r=["gkn_t", "gqk"], w=["gqk"])

                with contextlib.ExitStack() as ph0:
                    def zsb(name, shape, dt=F32):
                        return ph0.enter_context(nc.sbuf_tensor(name, list(shape), dt))
                    wa = [zsb("wa%d" % i, [128, 8, 512], BF16) for i in range(2)]
                    wuq_f = zsb("wuq_f", [128, 3, 1024])
                    ctile = zsb("ctile", [128, 8, 5])
                    c_e = zsb("c_e", [128, 8, 5])
                    siluT = zsb("siluT", [128, 8, 5], BF16)
                    badT = zsb("badT", [128, 48])
                    badr = zsb("badr", [1, 6144])
                    badr_bf = zsb("badr_bf", [1, 6144], BF16)
                    ones5 = zsb("ones5", [1, 8], BF16)
                    gql = zsb("gql", [128, 3])
                    rowt = zsb("rowt", [1, 320])
                    modT = zsb("modT", [128, 32, 5])

                    S.add("sp", OP("dma_start", out=ctile[:, :, :], in_=cT.rearrange("(kc p) b -> p kc b", p=128)), w=["ctile"], dma="ctile")
                    S.add("sp", OP("dma_start", out=badT[:, :], in_=b_adaT[:, :]), w=["badT"], dma="badT")
                    S.add("sp", OP("dma_start", out=badr[:, :], in_=b_ada_row[:, :]), w=["badr"], dma="badr")
                    S.add("sp", OP("dma_start", out=gql[:, :], in_=gqlT[:, :]), w=["gql"], dma="gql")
                    S.add("sp", OP("dma_start", out=rowt[:, 0:256], in_=gkv_row[:, :]), w=["rowt0"], dma="rowt0")
                    S.add("sp", OP("dma_start", out=rowt[:, 256:320], in_=gkr_row[:, :]), w=["rowt1"], dma="rowt1")
                    S.add("sp", OP("dma_start", out=wuq_f[:, :, :], in_=w_uq_ext.rearrange("(kc p) n -> p kc n", p=128)), w=["wuq_f"], dma="wuq_f")
                    S.add("dve", OP("memset", ones5[:, :], 1.0), w=["ones5"])
                    S.add("dve", OP("tensor_copy", out=badr_bf[:, :], in_=badr[:, :]), r=["badr"], w=["badr_bf"])
                    S.add("act", act(c_e[:, :, :], ctile[:, :, :], AF.Exp, scale=-1.0), r=["ctile"], w=["c_e"])
                    S.add("dve", OP("tensor_scalar", out=c_e[:, :, :], in0=c_e[:, :, :], scalar1=1.0, scalar2=None, op0=ALU.add), r=["c_e"], w=["c_e"])
                    S.add("dve", OP("reciprocal", out=c_e[:, :, :], in_=c_e[:, :, :]), r=["c_e"], w=["c_e"])
                    S.add("dve", OP("tensor_tensor", out=siluT[:, :, :], in0=ctile[:, :, :], in1=c_e[:, :, :], op=ALU.mult), r=["c_e", "ctile"], w=["siluT"])

                    def wa_load(ct, slot):
                        S.add("pool", OP("dma_start", out=wa[slot][:, :, :], in_=w_ada[:, ct * 512:(ct + 1) * 512].rearrange("(kc p) n -> p kc n", p=128)),
                              w=["wa%d" % slot], dma="wa%d" % slot)
                    order = [0, 1, 2, 3, 4, 5, 6, 7, 8, 9, 10, 11]
                    wa_load(order[0], 0)
                    fidx = {}
                    fi = 0
                    for ct in order:
                        if ct in (0, 1, 2, 3, 6, 7, 8, 9):
                            for cc in range(4):
                                fidx[(ct, cc)] = fi
                                fi += 1
                    PF, PFk = P[4], PK(4)
                    for n, ct in enumerate(order):
                        slot = n % 2
                        if n + 1 < len(order):
                            wa_load(order[n + 1], (n + 1) % 2)
                        if ct in (0, 1, 2, 3, 6, 7, 8, 9):
                            fns = []
                            for cc in range(4):
                                f = fidx[(ct, cc)]
                                for kc in range(8):
                                    fns.append(mm(PF[:, f * 8:f * 8 + 5], wa[slot][:, kc, cc * 128:(cc + 1) * 128], siluT[:, kc, :], start=(kc == 0), stop=(kc == 7)))
                            S.add("pe", seq(fns), r=["wa%d" % slot, "siluT"], w=[PFk])
                        else:
                            gi = 0 if ct in (4, 5) else 1
                            hf = ct % 2
                            Pr, Prk = P[5 + (n % 2)], PK(5 + (n % 2))
                            fns = [mm(Pr[0:5, :], siluT[:, kc, :], wa[slot][:, kc, :], start=(kc == 0), stop=False) for kc in range(8)]
                            fns.append(mm(Pr[0:5, :], ones5[0:1, 0:5], badr_bf[0:1, ct * 512:(ct + 1) * 512], start=False, stop=True))
                            S.add("pe", seq(fns), r=["wa%d" % slot, "siluT", "ones5", "badr_bf"], w=[Prk])
                            S.add("act", act(gater[:, gi, hf * 512:(hf + 1) * 512], Pr[0:5, :], AF.Copy), r=[Prk], w=["gater"])
                    for (ct, cc), f in fidx.items():
                        chunk = ct * 4 + cc
                        S.add("dve", (OP("tensor_scalar", out=modT[:, f, :], in0=PF[:, f * 8:f * 8 + 5], scalar1=badT[:, chunk:chunk + 1], scalar2=None, op0=ALU.add)),
                              r=[PFk, "badT"], w=["modT"])
                    for kc in range(8):
                        S.add("dve", (OP("tensor_scalar", out=gm1[:, kc, :], in0=modT[:, 8 + kc, :], scalar1=1.0, scalar2=g1c[:, kc:kc + 1], op0=ALU.add, op1=ALU.mult)),
                              r=["modT", "g1c"], w=["gm1"])
                        S.add("dve", (OP("tensor_scalar", out=gm2[:, kc, :], in0=modT[:, 24 + kc, :], scalar1=1.0, scalar2=g2c[:, kc:kc + 1], op0=ALU.add, op1=ALU.mult)),
                              r=["modT", "g2c"], w=["gm2"])
                    S.add("dve", OP("tensor_copy", out=sh1[:, :, :], in_=modT[:, 0:8, :]), r=["modT"], w=["sh1"])
                    S.add("dve", OP("tensor_copy", out=sh2[:, :, :], in_=modT[:, 16:24, :]), r=["modT"], w=["sh2"])

                    for hf in range(2):
                        S.add("pool", (OP("dma_start", out=win[:, :, hf * 1128:(hf + 1) * 1128], in_=w_in[:, hf * 1128:(hf + 1) * 1128].rearrange("(kc p) n -> p kc n", p=128))),
                              w=["win%d" % hf], dma="win%d" % hf)
                    S.add("pool", OP("dma_start", out=wukv[:, :, :], in_=w_ukv_p.rearrange("(kc p) n -> p kc n", p=128)), w=["wukv"], dma="wukv")
                    S.add("pool", OP("dma_start", out=wgu[:, :], in_=w_gu_aug[:, :]), w=["wgu"], dma="wgu")
                    S.add("pool", OP("dma_start", out=wout[:, :, :], in_=w_out.rearrange("(kc p) n -> p kc n", p=128)), w=["wout"], dma="wout")
                    for kc in range(3):
                        S.add("dve", (OP("tensor_scalar", out=wuq[:, kc, :], in0=wuq_f[:, kc, :], scalar1=gql[:, kc:kc + 1], scalar2=None, op0=ALU.mult)),
                              r=["wuq_f", "gql"], w=["wuq"])
                        for h in range(4):
                            c0 = 768 + h * 64
                            S.add("dve", (OP("tensor_scalar", out=wuq[:, kc, c0:c0 + 32], in0=wuq[:, kc, c0:c0 + 32], scalar1=-1.0, scalar2=None, op0=ALU.mult)),
                                  r=["wuq"], w=["wuq"])
                    Pb, Pbk = P[7], PK(7)
                    S.add("pe", mm(Pb[:, 0:320], ones_f[0:1, :], rowt[0:1, 0:320]), r=["ones_f", "rowt0", "rowt1"], w=[Pbk])
                    S.add("act", act(gkv_bc[:, :], Pb[:, 0:256], AF.Copy), r=[Pbk], w=["gkv_bc"])
                    for b in range(NB):
                        S.add("act", (OP("activation", out=gkr_bc[:, b, :], in_=Pb[:, 256:320], func=AF.Copy)), r=[Pbk], w=["gkr_bc"])
                S.barrier()
                ck('p0')

                Kn = psb("Kn", [128, 4, 4096], BF16)
                Kr = psb("Kr", [128, 2048], BF16)
                Vs = psb("Vs", [128, 32, 512], BF16)
                sclK = psb("sclK", [128, 32, 4])
                Sbf = psb("Sbf", [128, 2, 2, 128], BF16)
                xt = psb("xt", [128, 1024])
                hT = psb("hT", [128, 8, NT], BF16)
                lat_tm = psb("lat_tm", [128, NB, 256])
                kr_tm = psb("kr_tm", [128, NB, 64])
                latT = psb("latT", [128, 2, NT], BF16)
                gk_tm = psb("gk_tm", [128, NB, 256])
                gv_tm = psb("gv_tm", [128, NB, 512], BF16)
                l_tm = psb("l_tm", [128, NB, 256])
                khat = psb("khat", [128, NB, 256], BF16)
                cqT = psb("cqT", [128, 3, NT], BF16)
                qn = psb("qn", [128, 4, NT], BF16)
                qr = psb("qr", [128, 4, 2, NT], BF16)
                mixT = psb("mixT", [128, 8, NT], BF16)
                xr = xt
                kcs = psb("kcs", [128, NB, 128])
                qcs = psb("qcs", [128, NT])
                epst = psb("epst", [128, NT])
                stat = psb("stat", [128, 64])
                junk = psb("junk", [128, 1024], BF16)
                dec = psb("dec", [128, 2, NCH])
                TFl = [psb("TF%d" % i, [128, NT]) for i in range(10)]
                TBl = [psb("TB%d" % i, [128, NT], BF16) for i in range(11)]
                PTb = [psb("PT%d" % i, [128, NT], BF16) for i in range(4)]
                TFg = Rot([(t, t.name) for t in TFl[0:5]])
                TFq = Rot([(t, t.name) for t in TFl[5:8]])
                TFk = Rot([(t, t.name) for t in TFl[8:10]])
                TBg = Rot([(t, t.name) for t in TBl[0:6]])
                TBq = Rot([(t, t.name) for t in TBl[6:11]])
                PT = Rot([(t, t.name) for t in PTb])
                lat2 = xt[:, 0:512].rearrange("p (b d) -> p b d", b=NB)
                kr2 = xt[:, 512:640].rearrange("p (b d) -> p b d", b=NB)
                kcs2 = xt[:, 640:896].rearrange("p (b d) -> p b d", b=NB)
                latT2 = hT[:, 0:2, :]
                BS0 = dict(lat=lat_tm[:, :, :], latk="lat_tm", kr=kr_tm[:, :, :], krk="kr_tm", kcs=kcs[:, :, :], kcsk="kcs", latT=latT[:, :, :], latTk="latT")
                BS1 = dict(lat=lat2, latk="xt_lat", kr=kr2, krk="xt_kr", kcs=kcs2, kcsk="xt_kcs", latT=latT2, latTk="hTa")
                XTK = ["xt", "xt_kcs"] + [("xt_lat", i) for i in range(NB)] + [("xt_kr", i) for i in range(NB)]


                WIN = ["win0", "win1"]

                def HK(hTk, kcs=range(8), bs=range(NB)):
                    return [(hTk, kc, b) for kc in kcs for b in bs]

                def BK(pref, idx=None, n=NB):
                    return [(pref, i) for i in (range(n) if idx is None else idx)]
                S.add("dve", OP("memset", Kr[:, :], 0.0), w=[("Kr", kt_) for kt_ in range(2048 // NT)])
                S.add("dve", OP("memset", qr[:, :, :, :], 0.0), w=[("qr", h) for h in range(4)])

                def load_g1bc(v):
                    for fh in range(2):
                        Pg, Pgk = gp()
                        S.add("pe", (OP("matmul", Pg[:, :], sel[:, v, :], gater[:, 0, fh * 512:(fh + 1) * 512], start=True, stop=True)),
                              r=["sel", "gater"], w=[Pgk])
                        S.add("act", (OP("activation", out=g1bc[:, fh * 512:(fh + 1) * 512], in_=Pg[:, :], func=AF.Copy)), r=[Pgk], w=["g1bc"])

                def front(xsrc, r0, segs, hTb=None, hTk="hT", TFp=None):
                    hTb = hT if hTb is None else hTb
                    TFp = TFg if TFp is None else TFp
                    for b in range(NB):
                        S.add("sp", (OP("dma_start", out=xt[:, :], in_=xsrc[r0 + b * 128:r0 + (b + 1) * 128, :])), w=XTK, dma="xt")
                        S.add("act", OP("activation", out=junk[:, :], in_=xt[:, :], func=AF.Square, accum_out=stat[:, 4:5]), r=["xt"], w=[("junk", 0), ("junk", 1), ("junk", 2), "st_x"])
                        S.add("act", act(stat[:, 5:6], stat[:, 4:5], AF.Ln, scale=1.0 / 1024, bias=EPS), r=["st_x"], w=["st_x"])
                        S.add("act", act(stat[:, 6:7], stat[:, 5:6], AF.Exp, scale=-0.5), r=["st_x"], w=["st_x"])
                        S.add("dve", OP("tensor_scalar", out=xt[:, :], in0=xt[:, :], scalar1=stat[:, 6:7], scalar2=None, op0=ALU.mult), r=["xt", "st_x"], w=["xt"])
                        yield
                        for k2 in range(2):
                            Pt, Ptk = gp()
                            S.add("pe", seq([(OP("transpose", Pt[:, kk * 128:(kk + 1) * 128], xt[:, (k2 * 4 + kk) * 128:(k2 * 4 + kk + 1) * 128], ident[:, :])) for kk in range(4)]),
                                  r=["xt", "ident"], w=[Ptk])
                            for kk in range(4):
                                kc = k2 * 4 + kk
                                for (c0, ncol, m) in segs:
                                    lo = max(c0, b * 128)
                                    hi = min(c0 + ncol, (b + 1) * 128)
                                    if lo >= hi:
                                        continue
                                    S.add("act", (OP("activation",
                                        out=hTb[:, kc, lo:hi], in_=Pt[:, kk * 128 + lo - b * 128:kk * 128 + hi - b * 128], func=AF.Identity,
                                        scale=gm1[:, kc, m:m + 1], bias=sh1[:, kc, m:m + 1])), r=[Ptk, "gm1", "sh1"], w=[(hTk, kc, b), "hTa"])
                        yield

                def kvproj(lat_dst, kr_dst, r0, hTb=None, hTk="hT", TFp=None):
                    hTb = hT if hTb is None else hTb
                    TFp = TFg if TFp is None else TFp
                    for b in range(NB):
                        Pq, Pqk = gp()
                        S.add("pe", seq([mm(Pq[:, 0:320], hTb[:, kc, b * 128:(b + 1) * 128], win[:, kc, CKV0:CKV0 + 320], start=(kc == 0), stop=(kc == 7)) for kc in range(8)]),
                              r=HK(hTk, bs=[b]) + WIN, w=[Pqk])
                        skv = ("st_kv", b)
                        S.add("act", (OP("activation", out=junk[:, b * 256:(b + 1) * 256], in_=Pq[:, 0:256], func=AF.Square, accum_out=stat[:, 8 + b:9 + b])), r=[Pqk], w=[("junk", b), skv])
                        S.add("act", (OP("activation", out=stat[:, 12 + b:13 + b], in_=stat[:, 8 + b:9 + b], func=AF.Ln, scale=1.0 / 256, bias=EPS)), r=[skv], w=[skv])
                        S.add("act", (OP("activation", out=stat[:, 16 + b:17 + b], in_=stat[:, 12 + b:13 + b], func=AF.Exp, scale=-0.5)), r=[skv], w=[skv])
                        S.add("dve", (OP("scalar_tensor_tensor", out=lat_tm[:, b, :], in0=Pq[:, 0:256], scalar=stat[:, 16 + b:17 + b], in1=gkv_bc[:, :], op0=ALU.mult, op1=ALU.mult)),
                              r=[Pqk, skv, "gkv_bc"], w=[("lat_tm", b)])
                        S.add("act", (OP("activation", out=kr_tm[:, b, :], in_=Pq[:, 256:320], func=AF.Copy)), r=[Pqk], w=[("kr_tm", b)])
                        yield
                    if lat_dst is not None:
                        S.add("pool", OP("dma_start", out=lat_dst[r0:r0 + NT, :].rearrange("(b p) d -> p b d", p=128), in_=lat_tm[:, :, :]), r=BK("lat_tm"), dma="lat_o")
                        S.add("pool", OP("dma_start", out=kr_dst[r0:r0 + NT, :].rearrange("(b p) d -> p b d", p=128), in_=kr_tm[:, :, :]), r=BK("kr_tm"), dma="kr_o")

                def lat_transpose(bs=None):
                    bs = BS0 if bs is None else bs
                    for lc in range(2):
                        Pt, Ptk = gp()
                        S.add("pe", seq([(OP("transpose", Pt[:, b * 128:(b + 1) * 128], bs["lat"][:, b, lc * 128:(lc + 1) * 128], ident[:, :])) for b in range(NB)]),
                              r=BK(bs["latk"]) + ["ident"], w=[Ptk])
                        S.add("dve", (OP("tensor_copy", out=bs["latT"][:, lc, :], in_=Pt[:, 0:NT])), r=[Ptk], w=[(bs["latTk"], lc)] + (["hTa"] if bs["latTk"] == "hTa" else []))
                        yield

                def kside(key0, cs_src, cs_b0, bs=None, TFp=None, load_cs=True):
                    kb0 = key0 // 128
                    bs = BS0 if bs is None else bs
                    TFp = TFg if TFp is None else TFp
                    latT_, latTk_, kr_, krk_, kcs_, kcsk_ = bs["latT"], bs["latTk"], bs["kr"], bs["krk"], bs["kcs"], bs["kcsk"]
                    if load_cs:
                        S.add("sp", OP("dma_start", out=kcs_, in_=cs_src[:, cs_b0:cs_b0 + NB, :]), w=[kcsk_], dma=kcsk_)
                    for h in range(4):
                        Pq, Pqk = gp()
                        S.add("pe", seq([mm(Pq[:, 0:NT], wukv[:, lc, h * 128:(h + 1) * 128], latT_[:, lc, :], start=(lc == 0), stop=(lc == 1)) for lc in range(2)]),
                              r=["wukv", (latTk_, 0), (latTk_, 1)], w=[Pqk])
                        S.add("dve", OP("tensor_copy", out=Kn[:, h, key0:key0 + NT], in_=Pq[:, 0:NT]), r=[Pqk], w=[("Kn", key0 // NT, h)])
                        yield
                    for b in range(NB):
                        Pq, Pqk = gp()
                        S.add("pe", seq([mm(Pq[:, :], latT_[:, lc, b * 128:(b + 1) * 128], wukv[:, lc, 0:512], start=(lc == 0), stop=(lc == 1)) for lc in range(2)]),
                              r=["wukv", (latTk_, 0), (latTk_, 1)], w=[Pqk])
                        sk = ("st_k", b)
                        for h in range(4):
                            S.add("act", (OP("activation", out=Pq[:, h * 128:(h + 1) * 128], in_=Pq[:, h * 128:(h + 1) * 128], func=AF.Square, accum_out=stat[:, 20 + b * 4 + h:21 + b * 4 + h])),
                                  r=[Pqk], w=[(Pqk, h), (sk, h)])
                        S.add("act", (OP("activation", out=junk[:, 512 + b * 64:512 + (b + 1) * 64], in_=kr_[:, b, :], func=AF.Square, accum_out=stat[:, 28 + b:29 + b])), r=[(krk_, b)], w=[("junk", 2), (sk, 4)])
                        S.add("dve", (OP("tensor_scalar", out=stat[:, 32 + b * 4:36 + b * 4], in0=stat[:, 20 + b * 4:24 + b * 4], scalar1=stat[:, 28 + b:29 + b], scalar2=None, op0=ALU.add)),
                              r=[(sk, 0), (sk, 1), (sk, 2), (sk, 3), (sk, 4)], w=[sk])
                        S.add("act", (OP("activation", out=stat[:, 40 + b * 4:44 + b * 4], in_=stat[:, 32 + b * 4:36 + b * 4], func=AF.Ln, scale=1.0 / 192, bias=EPS)), r=[sk], w=[sk])
                        S.add("act", (OP("activation", out=sclK[:, kb0 + b, :], in_=stat[:, 40 + b * 4:44 + b * 4], func=AF.Exp, scale=-0.5, bias=float(-0.5 * np.log(192.0)))),
                              r=[sk], w=[("sclK", kb0 + b)])
                        yield
                        Pv, Pvk = gp()
                        S.add("pe", seq([mm(Pv[:, :], latT_[:, lc, b * 128:(b + 1) * 128], wukv[:, lc, 512:1024], start=(lc == 0), stop=(lc == 1)) for lc in range(2)]),
                              r=["wukv", (latTk_, 0), (latTk_, 1)], w=[Pvk])
                        S.add("dve", (OP("tensor_copy", out=Vs[:, kb0 + b, :], in_=Pv[:, :])), r=[Pvk], w=[("Vs", kb0 + b)])
                        yield
                    t1, t1k = TFp.get()
                    t2, t2k = TFp.get()
                    t3, t3k = TFp.get()
                    v1 = t1[:, 0:NB * 64].rearrange("p (b d) -> p b d", b=NB)
                    v2 = t2[:, 0:NB * 64].rearrange("p (b d) -> p b d", b=NB)
                    v3 = t3[:, 0:NB * 64].rearrange("p (b d) -> p b d", b=NB)
                    S.add("dve", OP("tensor_tensor", out=v1, in0=kr_, in1=gkr_bc[:, :, :], op=ALU.mult), r=BK(krk_) + ["gkr_bc"], w=[t1k])
                    S.add("dve", OP("tensor_tensor", out=v2, in0=v1, in1=kcs_[:, :, 0:64], op=ALU.mult), r=[t1k, kcsk_], w=[t2k])
                    S.add("dve", OP("tensor_tensor", out=v3[:, :, 0:32], in0=v1[:, :, 32:64], in1=kcs_[:, :, 64:96], op=ALU.mult), r=[t1k, kcsk_], w=[t3k])
                    S.add("dve", OP("tensor_tensor", out=v3[:, :, 32:64], in0=v1[:, :, 0:32], in1=kcs_[:, :, 96:128], op=ALU.mult), r=[t1k, kcsk_], w=[t3k])
                    S.add("dve", OP("tensor_tensor", out=v2, in0=v2, in1=v3, op=ALU.add), r=[t2k, t3k], w=[t2k])
                    yield
                    Pt, Ptk = gp()
                    S.add("pe", seq([(OP("transpose", Pt[0:64, b * 128:(b + 1) * 128], v2[:, b, :], ident[:, :])) for b in range(NB)]),
                          r=[t2k, "ident"], w=[Ptk])
                    hb = 0 if key0 < 2048 else 64
                    kk0 = key0 % 2048
                    if hb == 0:
                        S.add("act", OP("activation", out=Kr[0:64, kk0:kk0 + NT], in_=Pt[0:64, 0:NT], func=AF.Copy), r=[Ptk], w=[("Kr", (key0 % 2048) // NT)])
                    else:
                        sh, shk = TFp.get()
                        S.add("act", OP("activation", out=sh[0:64, :], in_=Pt[0:64, 0:NT], func=AF.Copy), r=[Ptk], w=[shk])
                        P2, P2k = gp()
                        S.add("pe", OP("matmul", P2[:, 0:NT], comb[0:64, :], sh[0:64, :], start=True, stop=True), r=[shk, "comb"], w=[P2k])
                        S.add("act", OP("activation", out=Kr[64:128, kk0:kk0 + NT], in_=P2[64:128, 0:NT], func=AF.Copy), r=[P2k], w=[("Kr", (key0 % 2048) // NT)])
                    yield

                def gla_common(own, hTb=None, hTk="hT", TFp=None):
                    hTb = hT if hTb is None else hTb
                    TFp = TFg if TFp is None else TFp
                    Pl, Plk = gp()
                    S.add("pe", seq([mm(Pl[0:16, 0:NT], win[:, kc, GLR0:GLR0 + 16], hTb[:, kc, :], start=(kc == 0), stop=(kc == 7)) for kc in range(8)]),
                          r=HK(hTk) + WIN, w=[Plk])
                    S.add("act", OP("activation", out=glrT[0:16, :], in_=Pl[0:16, 0:NT], func=AF.Copy), r=[Plk], w=["glrT"])
                    yield
                    for b in range(NB):
                        Pk_, Pkk = gp()
                        S.add("pe", seq([mm(Pk_[:, 0:256], hTb[:, kc, b * 128:(b + 1) * 128], win[:, kc, GK0:GK0 + 256], start=(kc == 0), stop=(kc == 7)) for kc in range(8)]),
                              r=HK(hTk, bs=[b]) + WIN, w=[Pkk])
                        S.add("act", (OP("activation", out=gk_tm[:, b, :], in_=Pk_[:, 0:256], func=AF.Copy)), r=[Pkk], w=[("gk_tm", b)])
                        yield
                        Pv, Pvk = gp()
                        S.add("pe", seq([mm(Pv[:, :], hTb[:, kc, b * 128:(b + 1) * 128], win[:, kc, GV0:GV0 + 512], start=(kc == 0), stop=(kc == 7)) for kc in range(8)]),
                              r=HK(hTk, bs=[b]) + WIN, w=[Pvk])
                        S.add("dve", (OP("tensor_copy", out=gv_tm[:, b, :], in_=Pv[:, :])), r=[Pvk], w=[("gv_tm", b)])
                        yield
                        Pz, Pzk = gp()
                        S.add("pe", (OP("matmul", Pz[:, 0:256], glrT[:, b * 128:(b + 1) * 128], wgu[:, :], start=True, stop=True)), r=["glrT", "wgu"], w=[Pzk])
                        S.add("act", (OP("activation", out=l_tm[:, b, :], in_=Pz[:, 0:256], func=AF.Exp, scale=-1.0)), r=[Pzk], w=[("l_tm", b)])
                        S.add("act", (OP("activation", out=l_tm[:, b, :], in_=l_tm[:, b, :], func=AF.Ln, bias=1.0)), r=[("l_tm", b)], w=[("l_tm", b)])
                        yield
                        Pc, Pck = gp()
                        S.add("pe", (OP("matmul", Pc[:, 0:256], ut[:, :], l_tm[:, b, :], start=True, stop=True)), r=["ut", ("l_tm", b)], w=[Pck])
                        et, etk = TFp.get()
                        S.add("act", (OP("activation", out=et[:, :], in_=Pc[:, 0:256], func=AF.Exp)), r=[Pck], w=[etk])
                        S.add("dve", (OP("tensor_tensor", out=khat[:, b, :], in0=gk_tm[:, b, :], in1=et[:, :], op=ALU.mult)), r=[etk, ("gk_tm", b)], w=[("khat", b)])
                        yield
                    return None

                def bt_step(hp, TFp, need_e):
                    Pb_, Pbk_ = gp()
                    S.add("pe", seq([OP("matmul", Pb_[:, b * 128:(b + 1) * 128], l_tm[:, b, hp * 128:(hp + 1) * 128], tri[:, :], start=True, stop=True) for b in range(NB)]),
                          r=[("l_tm", b_) for b_ in range(NB)] + ["tri"], w=[Pbk_])
                    S.add("act", OP("activation", out=dec[:, hp, :], in_=Pb_[:, 63:NT:64], func=AF.Exp), r=[Pbk_], w=[("dec", hp)])
                    if not need_e:
                        return None
                    eb, ebk = TFp.get()
                    enb, enbk = TFp.get()
                    S.add("act", OP("activation", out=eb[:, :], in_=Pb_[:, 0:NT], func=AF.Exp), r=[Pbk_], w=[ebk])
                    S.add("act", OP("activation", out=enb[:, :], in_=Pb_[:, 0:NT], func=AF.Exp, scale=-1.0), r=[Pbk_], w=[enbk])
                    return eb, ebk, enb, enbk

                def state_update(hp, ch, Pu, Puk):
                    b, par = ch // 2, ch % 2
                    fns = []
                    for hh in range(2):
                        h = hp * 2 + hh
                        fns.append(mm(Pu[hh * 64:(hh + 1) * 64, 0:128], khat[par * 64:(par + 1) * 64, b, h * 64:(h + 1) * 64], gv_tm[par * 64:(par + 1) * 64, b, h * 128:(h + 1) * 128]))
                    S.add("pe", seq(fns), r=[("khat", b), ("gv_tm", b)], w=[Puk])

                def gla_prefix(hTb=None, hTk="hT", TFp=None):
                    yield from gla_common(False, hTb, hTk, TFp)
                    for hp in range(2):
                        bt_step(hp, TFp, False)
                        yield
                        for ch in range(NCH):
                            Pu, Puk = gp()
                            state_update(hp, ch, Pu, Puk)
                            S.add("dve", (OP("scalar_tensor_tensor", out=Sst[:, hp, :], in0=Sst[:, hp, :], scalar=dec[:, hp, ch:ch + 1], in1=Pu[:, 0:128], op0=ALU.mult, op1=ALU.add)),
                                  r=[Puk, ("dec", hp), ("Sst", hp)], w=[("Sst", hp)])
                            yield

                def gla_own(per_chunk_state, st_dst, TFp=None, TBp=None):
                    TFp = TFg if TFp is None else TFp
                    TBp = TBg if TBp is None else TBp
                    yield from gla_common(True, None, "hT", TFp)
                    for hp in range(2):
                        eb, ebk, enb, enbk = bt_step(hp, TFp, True)
                        yield
                        Pq, Pqk = gp()
                        S.add("pe", seq([mm(Pq[:, 0:NT], win[:, kc, GQ0 + hp * 128:GQ0 + (hp + 1) * 128], hT[:, kc, :], start=(kc == 0), stop=(kc == 7)) for kc in range(8)]),
                              r=HK("hT") + WIN, w=[Pqk])
                        qtls = []
                        for hh in range(2):
                            qtl, qtlk = TBp.get()
                            S.add("dve", OP("scalar_tensor_tensor", out=qtl[:, :], in0=Pq[:, 0:NT], scalar=hsel[:, hh:hh + 1], in1=eb[:, :], op0=ALU.mult, op1=ALU.mult),
                                  r=[Pqk, ebk, "hsel"], w=[qtlk])
                            qtls.append((qtl, qtlk))
                        yield
                        Pk2, Pk2k = gp()
                        S.add("pe", seq([mm(Pk2[:, 0:NT], win[:, kc, GK0 + hp * 128:GK0 + (hp + 1) * 128], hT[:, kc, :], start=(kc == 0), stop=(kc == 7)) for kc in range(8)]),
                              r=HK("hT") + WIN, w=[Pk2k])
                        ktl, ktlk = TBp.get()
                        S.add("dve", OP("tensor_tensor", out=ktl[:, :], in0=Pk2[:, 0:NT], in1=enb[:, :], op=ALU.mult), r=[Pk2k, enbk], w=[ktlk])
                        yield
                        Ps, Psk = gp()
                        fns = []
                        for hh in range(2):
                            for b in range(NB):
                                co = hh * NT + b * 128
                                fns.append(mm(Ps[:, co:co + 128], ktl[:, b * 128:(b + 1) * 128], qtls[hh][0][:, b * 128:(b + 1) * 128]))
                        S.add("pe", seq(fns), r=[ktlk, qtls[0][1], qtls[1][1]], w=[Psk])
                        mks = []
                        for hh in range(2):
                            msk, mskk = TBp.get()
                            S.add("dve", OP("tensor_tensor", out=msk[:, :], in0=Ps[:, hh * NT:(hh + 1) * NT], in1=maskr[:, :], op=ALU.mult), r=[Psk, "maskr"], w=[mskk])
                            mks.append((msk, mskk))
                        yield
                        Po, Pok = P[7], PK(7)
                        for ch in range(NCH):
                            b, par = ch // 2, ch % 2
                            sb_par = ch % 2
                            if per_chunk_state:
                                S.add("sp", OP("dma_start", out=Sst[:, hp, :], in_=stc[ch, hp, :, :]), w=[("Sst", hp)], dma=("Sst_in", hp))
                            if per_chunk_state or ch == 0:
                                S.add("act", OP("activation", out=Sbf[:, hp, sb_par, :], in_=Sst[:, hp, :], func=AF.Copy), r=[("Sst", hp)], w=[("Sbf", hp, sb_par)])
                            fns = []
                            for hh in range(2):
                                h = hp * 2 + hh
                                oc = hh * NT + ch * 64
                                if par == 0:
                                    ob = hh * NT + b * 128
                                    fns.append(OP("matmul", Po[:, ob:ob + 128], gv_tm[:, b, h * 128:(h + 1) * 128], mks[hh][0][:, b * 128:(b + 1) * 128], start=(ch == 0 and hh == 0), stop=False, skip_group_check=True))
                                fns.append(OP("matmul", Po[:, oc:oc + 64], Sbf[:, hp, sb_par, :], qtls[hh][0][:, ch * 64:(ch + 1) * 64], start=False, stop=(par == 1), skip_group_check=True))
                            S.add("pe", seq(fns), r=[("gv_tm", b), mks[0][1], mks[1][1], ("Sbf", hp, sb_par), qtls[0][1], qtls[1][1]], w=[Pok])
                            Pu, Puk = gp()
                            state_update(hp, ch, Pu, Puk)
                            S.add("dve", OP("scalar_tensor_tensor", out=Sst[:, hp, :], in0=Sst[:, hp, :], scalar=dec[:, hp, ch:ch + 1], in1=Pu[:, 0:128], op0=ALU.mult, op1=ALU.add),
                                  r=[Puk, ("dec", hp), ("Sst", hp)], w=[("Sst", hp)])
                            if per_chunk_state:
                                S.add("pool", OP("dma_start", out=st_dst[ch, hp, :, :], in_=Sst[:, hp, :]), r=[("Sst", hp)], dma=("Sst_out", hp))
                            elif ch + 1 < NCH:
                                np_ = (ch + 1) % 2
                                S.add("act", OP("activation", out=Sbf[:, hp, np_, :], in_=Sst[:, hp, :], func=AF.Copy), r=[("Sst", hp)], w=[("Sbf", hp, np_)])
                            yield
                        yield
                        for hh in range(2):
                            h = hp * 2 + hh
                            oc = hh * NT
                            sq, sqk = TBp.get()
                            S.add("act", (OP("activation", out=sq[:, :], in_=Po[:, oc:oc + NT], func=AF.Square)), r=[Pok], w=[sqk])
                            Pn, Pnk = gp()
                            S.add("pe", (OP("matmul", Pn[:, 0:NT], ones_bf[:, :], sq[:, :], start=True, stop=True)), r=[sqk, "ones_bf"], w=[Pnk])
                            rs, rsk = TFp.get()
                            S.add("act", (OP("activation", out=rs[:, :], in_=Pn[:, 0:NT], func=AF.Ln, scale=1.0 / 128, bias=EPS)), r=[Pnk], w=[rsk])
                            yield
                            S.add("act", (OP("activation", out=rs[:, :], in_=rs[:, :], func=AF.Exp, scale=-0.5)), r=[rsk], w=[rsk])
                            on, onk = TFp.get()
                            S.add("dve", (OP("tensor_tensor", out=on[:, :], in0=Po[:, oc:oc + NT], in1=rs[:, :], op=ALU.mult)), r=[Pok, rsk], w=[onk])
                            Pr, Prk = gp()
                            S.add("pe", seq([mm(Pr[:, 0:NT], win[:, kc, GR0 + h * 128:GR0 + (h + 1) * 128], hT[:, kc, :], start=(kc == 0), stop=(kc == 7)) for kc in range(8)]),
                                  r=HK("hT") + WIN, w=[Prk])
                            sg, sgk = TFp.get()
                            S.add("act", (OP("activation", out=sg[:, :], in_=Pr[:, 0:NT], func=AF.Exp, scale=-1.0)), r=[Prk], w=[sgk])
                            S.add("act", (OP("activation", out=sg[:, :], in_=sg[:, :], func=AF.Ln, bias=1.0)), r=[sgk], w=[sgk])
                            S.add("act", (OP("activation", out=sg[:, :], in_=sg[:, :], func=AF.Exp, scale=-1.0)), r=[sgk], w=[sgk])
                            S.add("dve", (OP("tensor_tensor", out=sg[:, :], in0=Pr[:, 0:NT], in1=sg[:, :], op=ALU.mult)), r=[Prk, sgk], w=[sgk])
                            S.add("dve", (OP("scalar_tensor_tensor", out=mixT[:, 4 + h, :], in0=on[:, :], scalar=ggo[:, h:h + 1], in1=sg[:, :], op0=ALU.mult, op1=ALU.mult)),
                                  r=[onk, sgk, "ggo"], w=[("mixT", 4 + h), "hTalt"])
                            yield

                def qpath(qcs_src, c0, TFp=None, TBp=None):
                    TFp = TFq if TFp is None else TFp
                    TBp = TBq if TBp is None else TBp
                    S.add("sp", OP("dma_start", out=qcs[:, :], in_=qcs_src[:, c0:c0 + NT]), w=["qcs"], dma="qcs")
                    sqs = []
                    for kc3 in range(3):
                        Pq, Pqk = gp()
                        S.add("pe", seq([mm(Pq[:, 0:NT], win[:, kc, CQ0 + kc3 * 128:CQ0 + (kc3 + 1) * 128], hT[:, kc, :], start=(kc == 0), stop=(kc == 7)) for kc in range(8)]),
                              r=HK("hT") + WIN, w=[Pqk])
                        S.add("act", (OP("activation", out=cqT[:, kc3, :], in_=Pq[:, 0:NT], func=AF.Copy)), r=[Pqk], w=[("cqT", kc3)])
                        sq, sqk = TBp.get()
                        S.add("act", (OP("activation", out=sq[:, :], in_=Pq[:, 0:NT], func=AF.Square)), r=[Pqk], w=[sqk])
                        sqs.append((sq, sqk))
                        yield
                    Pss, Pssk = gp()
                    S.add("pe", seq([mm(Pss[:, 0:NT], ones_bf[:, :], sqs[i][0][:, :], start=(i == 0), stop=(i == 2)) for i in range(3)]),
                          r=[s[1] for s in sqs] + ["ones_bf"], w=[Pssk])
                    epstk = "epst"
                    S.add("act", (OP("activation", out=epst[:, :], in_=Pss[:, 0:NT], func=AF.Identity, scale=EPS / 384.0, bias=EPS * EPS)), r=[Pssk], w=[epstk])
                    yield
                    for h in range(4):
                        Pn_, Pnk_ = gp()
                        S.add("pe", seq([mm(Pn_[:, 0:NT], wuq[:, kc3, h * 192:h * 192 + 128], cqT[:, kc3, :], start=(kc3 == 0), stop=(kc3 == 2)) for kc3 in range(3)]),
                              r=["wuq", ("cqT", 0), ("cqT", 1), ("cqT", 2)], w=[Pnk_])
                        Pab, Pabk = gp()
                        fns = [mm(Pab[0:64, 0:NT], wuq[:, kc3, h * 192 + 128:h * 192 + 192], cqT[:, kc3, :], start=(kc3 == 0), stop=(kc3 == 2)) for kc3 in range(3)]
                        fns += [mm(Pab[64:128, 0:NT], wuq[:, kc3, 768 + h * 64:768 + (h + 1) * 64], cqT[:, kc3, :], start=(kc3 == 0), stop=(kc3 == 2)) for kc3 in range(3)]
                        S.add("pe", seq(fns), r=["wuq", ("cqT", 0), ("cqT", 1), ("cqT", 2)], w=[Pabk])
                        s1, s1k = TBp.get()
                        s2, s2k = TBp.get()
                        S.add("act", (OP("activation", out=s1[:, :], in_=Pn_[:, 0:NT], func=AF.Square)), r=[Pnk_], w=[s1k])
                        S.add("act", (OP("activation", out=s2[0:64, :], in_=Pab[0:64, 0:NT], func=AF.Square)), r=[Pabk], w=[s2k])
                        Ph, Phk = gp()
                        S.add("pe", seq([mm(Ph[:, 0:NT], ones_bf[:, :], s1[:, :], start=True, stop=False), mm(Ph[:, 0:NT], ones_bf[0:64, :], s2[0:64, :], start=False, stop=True)]),
                              r=[s1k, s2k, "ones_bf"], w=[Phk])
                        rq, rqk = TFp.get()
                        S.add("dve", (OP("scalar_tensor_tensor", out=rq[:, :], in0=Ph[:, 0:NT], scalar=1.0 / 192, in1=epst[:, :], op0=ALU.mult, op1=ALU.add)), r=[Phk, epstk], w=[rqk])
                        S.add("act", (OP("activation", out=rq[:, :], in_=rq[:, :], func=AF.Ln)), r=[rqk], w=[rqk])
                        S.add("act", (OP("activation", out=rq[:, :], in_=rq[:, :], func=AF.Exp, scale=-0.5)), r=[rqk], w=[rqk])
                        S.add("dve", (OP("scalar_tensor_tensor", out=qn[:, h, :], in0=Pn_[:, 0:NT], scalar=gqk[:, 0:1], in1=rq[:, :], op0=ALU.mult, op1=ALU.mult)),
                              r=[Pnk_, rqk, "gqk"], w=[("qn", h)])
                        ab, abk = TFp.get()
                        S.add("dve", (OP("scalar_tensor_tensor", out=ab[:, :], in0=Pab[:, 0:NT], scalar=gqr_t[:, 0:1], in1=rq[:, :], op0=ALU.mult, op1=ALU.mult)),
                              r=[Pabk, rqk, "gqr_t"], w=[abk])
                        S.add("dve", (OP("tensor_tensor", out=ab[:, :], in0=ab[:, :], in1=qcs[:, :], op=ALU.mult)), r=[abk, "qcs"], w=[abk])
                        Pc_, Pck_ = gp()
                        S.add("pe", (OP("matmul", Pc_[:, 0:NT], comb[:, :], ab[:, :], start=True, stop=True)), r=[abk, "comb"], w=[Pck_])
                        S.add("act", (OP("activation", out=qr[0:64, h, 0, :], in_=Pc_[0:64, 0:NT], func=AF.Copy)), r=[Pck_], w=[("qr", h)])
                        S.add("act", (OP("activation", out=qr[64:128, h, 1, :], in_=Pc_[64:128, 0:NT], func=AF.Copy)), r=[Pck_], w=[("qr", h)])
                        yield

                rlbuf = psb("rlbuf", [128, NT])
                att = dict(cnt=0)

                def blk(h, kb, ncols, q0, first, last, bias_ap, zero_tri=False, zero_rows=None, finish=None):
                    return dict(h=h, kb=kb, ncols=ncols, q0=q0, first=first, last=last, bias=bias_ap, zero_tri=zero_tri, zero_rows=zero_rows, finish=finish)

                def emit_qk(B):
                    h, kb, ncols, q0 = B["h"], B["kb"], B["ncols"], B["q0"]
                    key0 = kb * 128
                    hb = 0 if key0 < 2048 else 64
                    kk0 = key0 % 2048
                    si = SCB[att["cnt"] % 3]
                    att["cnt"] += 1
                    Ps, Psk = P[si], PK(si)
                    B["Ps"], B["Psk"] = Ps, Psk
                    kt = key0 // NT
                    S.add("pe", seq([
                        mm(Ps[:, 0:ncols], Kn[:, h, key0:key0 + 128], qn[:, h, q0:q0 + ncols], start=True, stop=False),
                        mm(Ps[:, 0:ncols], Kr[:, kk0:kk0 + 128], qr[:, h, hb // 64, q0:q0 + ncols], start=False, stop=True)]),
                        r=[("Kn", kt, h), ("Kr", kk0 // NT), ("qn", h), ("qr", h)], w=[Psk])

                def emit_rest(B):
                    h, kb, ncols, q0 = B["h"], B["kb"], B["ncols"], B["q0"]
                    Ps, Psk = B["Ps"], B["Psk"]
                    kt = kb * 128 // NT
                    pt, ptk = PT.get()
                    if B["bias"] is None:
                        S.add("act", OP("activation", out=pt[:, 0:ncols], in_=Ps[:, 0:ncols], func=AF.Exp, scale=sclK[:, kb, h:h + 1]),
                              r=[Psk, ("sclK", kb)], w=[ptk])
                    else:
                        S.add("act", OP("activation", out=pt[:, 0:ncols], in_=Ps[:, 0:ncols], func=AF.Exp, scale=sclK[:, kb, h:h + 1], bias=B["bias"]),
                              r=[Psk, ("sclK", kb), "flg"], w=[ptk])
                    if B["zero_tri"]:
                        S.add("pool", OP("memset", pt[64:128, 0:64], 0.0), r=[ptk], w=[ptk])
                    if B["zero_rows"] is not None:
                        zr = B["zero_rows"]
                        S.add("pool", OP("memset", pt[zr[0]:zr[1], 0:ncols], 0.0), r=[ptk], w=[ptk])
                    ab_ = 2 + (h % 2)
                    Pa, Pak = P[ab_], PK(ab_)
                    S.add("pe", seq([
                        OP("matmul", Pa[:, q0:q0 + ncols], Vs[:, kb, h * 128:(h + 1) * 128], pt[:, 0:ncols], start=B["first"], stop=B["last"], skip_group_check=True),
                        OP("matmul", Pa[:, NT + q0:NT + q0 + ncols], ones_bf[:, :], pt[:, 0:ncols], start=False, stop=B["last"], skip_group_check=True)]),
                        r=[("Vs", kb), ptk, "ones_bf"], w=[Pak])
                    if B["finish"] is not None:
                        fh_, fq0, fn = B["finish"]
                        S.add("dve", OP("reciprocal", out=rlbuf[:, 0:fn], in_=Pa[:, NT + fq0:NT + fq0 + fn]), r=[Pak], w=["rlbuf"])
                        S.add("dve", OP("tensor_tensor", out=mixT[:, fh_, fq0:fq0 + fn], in0=Pa[:, fq0:fq0 + fn], in1=rlbuf[:, 0:fn], op=ALU.mult), r=[Pak, "rlbuf"], w=[("mixT", fh_), "hTalt"])

                def attn_run(blocks, hook=None):
                    set_gen(GEN_IN)
                    try:
                        _attn_run(blocks, hook)
                    finally:
                        S.flush()
                        set_gen(GEN_OUT)

                def _attn_run(blocks, hook=None):
                    n = len(blocks)
                    emit_qk(blocks[0])
                    if n > 1:
                        emit_qk(blocks[1])
                    for k in range(n):
                        if k + 2 < n:
                            emit_qk(blocks[k + 2])
                        emit_rest(blocks[k])
                        if hook is not None:
                            hook()

                def prompt_blocks(p):
                    out = []
                    for h in range(4):
                        nown = NB * p + NB
                        for kb in range(16):
                            out.append(blk(h, kb, NT, 0, kb == 0, False, flg[:, 1:2]))
                        for j in range(nown):
                            kb = 16 + j
                            dj = j - NB * p
                            if dj < 0:
                                out.append(blk(h, kb, NT, 0, False, False, None))
                            else:
                                lastb = (j == nown - 1)
                                out.append(blk(h, kb, NT - 128 * dj, 128 * dj, False, lastb, None, zero_tri=True, finish=((h, 0, NT) if lastb else None)))
                    return out

                def sample_blocks(i):
                    q0 = i * 64
                    par = i % 2
                    out = []
                    for h in range(4):
                        for kb in range(16):
                            out.append(blk(h, kb, 64, q0, kb == 0, False, None))
                        out.append(blk(h, 16 + i // 2, 64, q0, False, True, None, zero_rows=((1 - par) * 64, (1 - par) * 64 + 64), finish=(h, q0, 64)))
                    return out

                def back(xsrc, r0, x1row0, blocks_g1, TFp=None):
                    TFp = TFk if TFp is None else TFp
                    for b in range(NB):
                        if blocks_g1 is not None:
                            load_g1bc(blocks_g1[b])
                        S.add("sp", (OP("dma_start", out=xr[:, :], in_=xsrc[r0 + b * 128:r0 + (b + 1) * 128, :])), w=XTK, dma="xt")
                        for fh in range(2):
                            Po, Pok = gp()
                            S.add("pe", seq([mm(Po[:, :], mixT[:, k, b * 128:(b + 1) * 128], wout[:, k, fh * 512:(fh + 1) * 512], start=(k == 0), stop=(k == 7)) for k in range(8)]),
                                  r=[("mixT", k) for k in range(8)] + ["wout"], w=[Pok])
                            tt, ttk = TFp.get()
                            tt2, tt2k = TFp.get()
                            S.add("dve", (OP("tensor_tensor", out=tt[:, :], in0=Po[:, 0:256], in1=g1bc[:, fh * 512:fh * 512 + 256], op=ALU.mult)), r=[Pok, "g1bc"], w=[ttk])
                            S.add("dve", (OP("tensor_tensor", out=tt2[:, :], in0=Po[:, 256:512], in1=g1bc[:, fh * 512 + 256:fh * 512 + 512], op=ALU.mult)), r=[Pok, "g1bc"], w=[tt2k])
                            S.add("dve", (OP("tensor_tensor", out=xr[:, fh * 512:fh * 512 + 256], in0=xr[:, fh * 512:fh * 512 + 256], in1=tt[:, :], op=ALU.add)), r=[ttk, "xt"], w=["xt"])
                            S.add("dve", (OP("tensor_tensor", out=xr[:, fh * 512 + 256:fh * 512 + 512], in0=xr[:, fh * 512 + 256:fh * 512 + 512], in1=tt2[:, :], op=ALU.add)), r=[tt2k, "xt"], w=["xt"])
                            yield
                        S.add("pool", (OP("dma_start", out=x1s[x1row0 + b * 128:x1row0 + (b + 1) * 128, :], in_=xr[:, :])), r=["xt"], w=[("x1s", x1row0 // 128 + b)], dma="x1s_w")

                def run(g):
                    for _ in g:
                        pass

                def chain(*gens):
                    for g in gens:
                        yield from g

                def hook_of(g, every=1):
                    st = dict(n=0)

                    def hk():
                        st["n"] += 1
                        if st["n"] % every == 0:
                            S.flush()
                            S.defer_mode = True
                            try:
                                next(g, None)
                            finally:
                                S.defer_mode = False
                    return hk

                def inter(gens):
                    active = dict(gens)
                    while active:
                        for name in list(active):
                            g = active.get(name)
                            if g is None:
                                continue
                            try:
                                tok = next(g)
                            except StopIteration:
                                del active[name]
                                continue
                            if isinstance(tok, str) and tok.startswith("need:"):
                                dep = tok[5:]
                                if dep in active:
                                    for _ in active[dep]:
                                        pass
                                    del active[dep]

                load_g1bc(0)
                NPRE = 2048 // NT
                hbufs = [(hT, "hT"), (mixT, "hTalt")]
                run(front(xpre, 0, [(0, NT, 0)], *hbufs[0]))
                for t in range(NPRE):
                    hb_, hk_ = hbufs[t % 2]
                    gens = dict(k=chain(kvproj(None, None, 0, hb_, hk_), lat_transpose(), kside(t * NT, kcs_pre, t * NB)),
                                g=gla_prefix(hb_, hk_, TFq))
                    if t + 1 < NPRE:
                        gens["f"] = front(xpre, (t + 1) * NT, [(0, NT, 0)], hbufs[(t + 1) % 2][0], hbufs[(t + 1) % 2][1], TFk)
                    inter(gens)
                    ck('pre%d' % t)
                for hp in range(2):
                    S.add("dve", (OP("tensor_scalar", out=Sst[:, hp, :], in0=Sst[:, hp, :], scalar1=flg[:, 0:1], scalar2=None, op0=ALU.mult)), r=[("Sst", hp), "flg"], w=[("Sst", hp)])
                NOWN = 2048 // NT

                def pre_own(p):
                    return chain(front(xown, p * NT, [(0, NT, 0)]), kvproj(lat_own, kr_own, p * NT), lat_transpose(), kside(2048 + p * NT, kcs_own, p * NB))
                run(pre_own(0))
                run(qpath(qcs_own, 0))
                for p in range(NOWN):
                    _dm = "pre"

                    def chain2(items):
                        for g_, allow in items:
                            S.allow_defer = allow
                            for x in g_:
                                yield x
                        S.allow_defer = True
                    hk_chain = chain2([(gla_own(False, None), _dm in ("all", "gla")), (pre_own(p + 1) if p + 1 < NOWN else iter(()), _dm in ("all", "pre"))])
                    attn_run(prompt_blocks(p), hook_of(hk_chain, 1))
                    set_gen(GEN_IN)
                    run(hk_chain)
                    set_gen(GEN_OUT)
                    gens = dict(back=back(xown, p * NT, p * NT, None))
                    if p + 1 < NOWN:
                        gens["q"] = qpath(qcs_own, (p + 1) * NT)
                    inter(gens)
                    ck('own%d' % p)
                for hp in range(2):
                    S.add("pool", (OP("dma_start", out=st_p[hp, :, :], in_=Sst[:, hp, :])), r=[("Sst", hp)], dma=("st_p", hp))
                run(front(xsm, 0, [(i * 64, 64, 1 + i) for i in range(4)]))
                run(kvproj(lat_s, kr_s, 0))
                run(lat_transpose())
                run(kside(2048, kcs_sn, 0))
                inter(dict(q=qpath(qcs_s, 0), g=gla_own(True, st_s)))
                NPT = 2048 // NT
                BSS = [BS0, BS1]

                def stage_a(i, t, bs):
                    S.add("sp", OP("dma_start", out=bs["lat"], in_=latc[i, t * NT:(t + 1) * NT, :].rearrange("(b p) d -> p b d", p=128)), w=BK(bs["latk"]), dma=bs["latk"])
                    S.add("sp", OP("dma_start", out=bs["kr"], in_=krc[i, t * NT:(t + 1) * NT, :].rearrange("(b p) d -> p b d", p=128)), w=BK(bs["krk"]), dma=bs["krk"])
                    S.add("sp", OP("dma_start", out=bs["kcs"], in_=kcs_pre[:, t * NB:(t + 1) * NB, :]), w=[bs["kcsk"]], dma=bs["kcsk"])
                    yield
                    yield from lat_transpose(bs)

                tiles = [(i, t) for i in range(4) for t in range(NPT)]
                run(stage_a(0, 0, BSS[0]))
                for n, (i, t) in enumerate(tiles):
                    bs = BSS[n % 2]
                    nxtA = stage_a(tiles[n + 1][0], tiles[n + 1][1], BSS[(n + 1) % 2]) if n + 1 < len(tiles) else iter(())
                    if t < NPT - 1:
                        inter(dict(b=kside(t * NT, kcs_pre, t * NB, bs, None, False), a=nxtA))
                    else:
                        run(kside(t * NT, kcs_pre, t * NB, bs, None, False))
                        attn_run(sample_blocks(i), hook_of(nxtA, 4))
                        run(nxtA)
                run(back(xsm, 0, 2048, [1, 2]))
                ck('samp')
            S.barrier()

            with contextlib.ExitStack() as ph2:
                def msb(name, shape, dt=F32):
                    return ph2.enter_context(nc.sbuf_tensor(name, list(shape), dt))
                wup = msb("wup", [128, 8, 4096], BF16)
                wdn = msb("wdn", [128, 32, 1024], BF16)
                g2bc = msb("g2bc", [128, 1024])
                x1t = [msb("x1t%d" % i, [128, NB, 1024]) for i in range(2)]
                xw = msb("xw", [128, 1024])
                h2T = [msb("h2T%d" % i, [128, 8, NT], BF16) for i in range(2)]
                uT = msb("uT", [128, 32, NT], BF16)
                st2 = msb("st2", [128, 8])
                junk2 = msb("junk2", [128, 1024], BF16)
                RFb = [msb("RF%d" % i, [128, NT]) for i in range(6)]
                RFa = Rot([(t, t.name) for t in RFb[0:2]])
                RF = Rot([(t, t.name) for t in RFb[2:6]])
                gen8 = Rot(list(range(8)))

                def gp8():
                    i = gen8.get()
                    return P[i], PK(i)

                for jb in range(8):
                    S.add("pool", OP("dma_start", out=wup[:, :, jb * 512:(jb + 1) * 512], in_=w_up[:, jb * 512:(jb + 1) * 512].rearrange("(kc p) n -> p kc n", p=128)),
                          w=[("wup", jb)], dma=("wup", jb))
                for j4 in range(8):
                    S.add("pool", OP("dma_start", out=wdn[:, j4 * 4:(j4 + 1) * 4, :], in_=w_down[j4 * 512:(j4 + 1) * 512, :].rearrange("(j p) n -> p j n", p=128)),
                          w=[("wdn", j4)], dma=("wdn", j4))
                WDN = [("wdn", j4) for j4 in range(8)]

                def load_g2bc(v):
                    for fh in range(2):
                        Pg, Pgk = gp8()
                        S.add("pe", OP("matmul", Pg[:, :], sel[:, v, :], gater[:, 1, fh * 512:(fh + 1) * 512], start=True, stop=True),
                              r=["sel", "gater"], w=[Pgk])
                        S.add("act", OP("activation", out=g2bc[:, fh * 512:(fh + 1) * 512], in_=Pg[:, :], func=AF.Copy), r=[Pgk], w=["g2bc"])

                def mlp_front(ti, row0, segs):
                    xb, hb = x1t[ti % 2], h2T[ti % 2]
                    xk, hk = "x1t%d" % (ti % 2), "h2T%d" % (ti % 2)
                    for b in range(NB):
                        S.add("sp", OP("dma_start", out=xb[:, b, :], in_=x1s[row0 + b * 128:row0 + (b + 1) * 128, :]), r=[("x1s", row0 // 128 + b)], w=[(xk, b)], dma=(xk, b))
                        S.add("act", OP("activation", out=junk2[:, :], in_=xb[:, b, :], func=AF.Square, accum_out=st2[:, 4:5]), r=[(xk, b)], w=["junk2", "st2"])
                        S.add("act", act(st2[:, 5:6], st2[:, 4:5], AF.Ln, scale=1.0 / 1024, bias=EPS), r=["st2"], w=["st2"])
                        S.add("act", act(st2[:, 6:7], st2[:, 5:6], AF.Exp, scale=-0.5), r=["st2"], w=["st2"])
                        S.add("dve", OP("tensor_scalar", out=xw[:, :], in0=xb[:, b, :], scalar1=st2[:, 6:7], scalar2=None, op0=ALU.mult), r=[(xk, b), "st2"], w=["xw"])
                        yield
                        for k2 in range(2):
                            Pt, Ptk = gp8()
                            S.add("pe", seq([OP("transpose", Pt[:, kk * 128:(kk + 1) * 128], xw[:, (k2 * 4 + kk) * 128:(k2 * 4 + kk + 1) * 128], ident[:, :]) for kk in range(4)]),
                                  r=["xw", "ident"], w=[Ptk])
                            for kk in range(4):
                                kc = k2 * 4 + kk
                                for (c0, ncol, m) in segs:
                                    lo = max(c0, b * 128)
                                    hi = min(c0 + ncol, (b + 1) * 128)
                                    if lo >= hi:
                                        continue
                                    S.add("act", OP("activation", out=hb[:, kc, lo:hi], in_=Pt[:, kk * 128 + lo - b * 128:kk * 128 + hi - b * 128], func=AF.Identity,
                                                    scale=gm2[:, kc, m:m + 1], bias=sh2[:, kc, m:m + 1]), r=[Ptk, "gm2", "sh2"], w=[(hk, kc, b)])
                            yield

                def mlp_main(ti, ydst, yrow0, blocks_g2, hook):
                    xb, hb = x1t[ti % 2], h2T[ti % 2]
                    xk, hk = "x1t%d" % (ti % 2), "h2T%d" % (ti % 2)
                    for j in range(32):
                        Pu, Puk = gp8()
                        S.add("pe", seq([mm(Pu[:, 0:NT], wup[:, kc, j * 128:(j + 1) * 128], hb[:, kc, :], start=(kc == 0), stop=(kc == 7)) for kc in range(8)]),
                              r=[(hk, kc_, b_) for kc_ in range(8) for b_ in range(NB)] + [("wup", j // 4)], w=[Puk])
                        rt, rtk = RF.get()
                        S.add("act", OP("activation", out=rt[:, :], in_=Pu[:, 0:NT], func=AF.Relu), r=[Puk], w=[rtk])
                        S.add("dve", OP("tensor_tensor", out=uT[:, j, :], in0=Pu[:, 0:NT], in1=rt[:, :], op=ALU.mult), r=[Puk, rtk], w=[("uT", j)])
                        if hook is not None and j % 2 == 1:
                            hook()
                    UT = [("uT", j) for j in range(32)]
                    for b in range(NB):
                        if blocks_g2 is not None:
                            load_g2bc(blocks_g2[b])
                        for fh in range(2):
                            Po, Pok = gp8()
                            S.add("pe", seq([mm(Po[:, :], uT[:, j, b * 128:(b + 1) * 128], wdn[:, j, fh * 512:(fh + 1) * 512], start=(j == 0), stop=(j == 31)) for j in range(32)]),
                                  r=UT + WDN, w=[Pok])
                            for q2 in range(2):
                                tt, ttk = RF.get()
                                c0 = fh * 512 + q2 * 256
                                S.add("dve", OP("tensor_tensor", out=tt[:, :], in0=Po[:, q2 * 256:(q2 + 1) * 256], in1=g2bc[:, c0:c0 + 256], op=ALU.mult), r=[Pok, "g2bc"], w=[ttk])
                                S.add("dve", OP("tensor_tensor", out=xb[:, b, c0:c0 + 256], in0=xb[:, b, c0:c0 + 256], in1=tt[:, :], op=ALU.add), r=[ttk, (xk, b)], w=[(xk, b)])
                        S.add("pool", OP("dma_start", out=ydst[yrow0 + b * 128:yrow0 + (b + 1) * 128, :], in_=xb[:, b, :]), r=[(xk, b)], dma=("y_o", ti % 2, b))

                load_g2bc(0)
                NMT = 2048 // NT
                tiles2 = [(p * NT, y_own, p * NT, [(0, NT, 0)], None) for p in range(NMT)] + [(2048, y_s, 0, [(i * 64, 64, 1 + i) for i in range(4)], [1, 2])]
                for _ in mlp_front(0, tiles2[0][0], tiles2[0][3]):
                    pass
                for ti, (row0, ydst, yrow0, segs, bg2) in enumerate(tiles2):
                    if ti + 1 < len(tiles2):
                        nx = mlp_front(ti + 1, tiles2[ti + 1][0], tiles2[ti + 1][3])
                    else:
                        nx = iter(())
                    mlp_main(ti, ydst, yrow0, bg2, (lambda nx=nx: next(nx, None)))
                    for _ in nx:
                        pass

        except _Stop:
            pass
        S.emit()
    return nc


def _rope_tables(pos):
    half = 32
    inv = np.power(np.float32(10000.0), -np.arange(half, dtype=np.float32) / np.float32(half)).astype(np.float32)
    ang = pos.astype(np.float32)[:, None] * inv[None, :]
    return np.cos(ang).astype(np.float32), np.sin(ang).astype(np.float32)


def _kcs(pos):
    c, s = _rope_tables(pos)
    t = np.concatenate([c, c, -s, s], axis=1)
    n = pos.shape[0]
    return np.ascontiguousarray(t.reshape(n // 128, 128, 128).transpose(1, 0, 2))


def _qcs(pos):
    c, s = _rope_tables(pos)
    return np.ascontiguousarray(np.concatenate([c.T, c.T, s.T, s.T], axis=0))


_NC_CACHE = {}


def _prep(x_prompt, x_sample, cache_mla_latent, cache_mla_krope, state_gla, c_prompt, c_sample,
           w_ada, b_ada, g_norm1, w_in, g_q_lora, w_uq, g_kv_lora, w_ukv, g_q_head, g_k_head,
           w_gate_up, b_gate_up, g_gla_out, w_out, g_norm2, w_up, w_down):
    f = lambda a: np.ascontiguousarray(np.asarray(a, dtype=np.float32))
    x_prompt, x_sample = f(x_prompt), f(x_sample)
    latc_all, krc_all, st_all = f(cache_mla_latent)[0], f(cache_mla_krope)[0], f(state_gla)[0]
    c_prompt, c_sample = f(c_prompt), f(c_sample)
    w_ada, b_ada, g_norm1, w_in = f(w_ada)[0], f(b_ada)[0], f(g_norm1)[0], f(w_in)[0]
    g_q_lora, w_uq, g_kv_lora, w_ukv = f(g_q_lora)[0], f(w_uq)[0], f(g_kv_lora)[0], f(w_ukv)[0]
    g_q_head, g_k_head = f(g_q_head)[0], f(g_k_head)[0]
    w_gate_up, b_gate_up, g_gla_out = f(w_gate_up)[0], f(b_gate_up)[0], f(g_gla_out)[0]
    w_out, g_norm2, w_up, w_down = f(w_out)[0], f(g_norm2)[0], f(w_up)[0], f(w_down)[0]

    rot_cols = []
    for h in range(4):
        base = h * 192 + 128
        rot_cols += list(range(base + 32, base + 64)) + list(range(base, base + 32))
    w_uq_ext = np.ascontiguousarray(np.concatenate([w_uq, w_uq[:, rot_cols]], axis=1))
    kn_cols, v_cols = [], []
    for h in range(4):
        kn_cols += list(range(h * 256, h * 256 + 128))
        v_cols += list(range(h * 256 + 128, h * 256 + 256))
    w_ukv_p = np.ascontiguousarray(w_ukv[:, kn_cols + v_cols])
    w_gu_aug = np.zeros((32, 256), np.float32)
    w_gu_aug[0:16] = w_gate_up
    w_gu_aug[16] = b_gate_up
    colT = lambda v, n: np.ascontiguousarray(v.reshape(n, 128).T)
    gqr_col = np.concatenate([g_q_head[128:192], np.roll(g_q_head[128:192], -32)])[:, None]
    ident = np.eye(128, dtype=np.float32)
    ii = np.arange(128)
    same = (ii[:, None] // 64) == (ii[None, :] // 64)
    tri = np.where(same & (ii[:, None] <= ii[None, :]), np.float32(-1.0 / 16), np.float32(0)).astype(np.float32)
    utm = np.where(same & (ii[:, None] > ii[None, :]), np.float32(-1.0 / 16), np.float32(0)).astype(np.float32)
    mblk = (same & (ii[:, None] <= ii[None, :])).astype(np.float32)
    maskr = np.ascontiguousarray(np.tile(mblk, (1, NB)))
    hsel = np.zeros((128, 2), np.float32)
    hsel[0:64, 0] = 0.125
    hsel[64:128, 1] = 0.125
    sel = np.zeros((5, 3, 128), np.float32)
    sel[0, 0, :] = 1
    sel[1, 1, 0:64] = 1
    sel[2, 1, 64:128] = 1
    sel[3, 2, 0:64] = 1
    sel[4, 2, 64:128] = 1
    comb = ((ii[:, None] % 64) == (ii[None, :] % 64)).astype(np.float32)
    kcs_pre = _kcs(np.arange(2048))
    kcs_sn = _kcs(2048 + (np.arange(256) % 64))
    qcs_s = _qcs(2048 + (np.arange(256) % 64))
    shared = dict(
        w_ada=w_ada, b_ada_row=b_ada[None, :], b_adaT=colT(b_ada, 48),
        g1T=colT(g_norm1, 8), g2T=colT(g_norm2, 8), gqlT=colT(g_q_lora, 3),
        gqn=np.ascontiguousarray(g_q_head[0:128, None]), gkn=np.ascontiguousarray(g_k_head[0:128, None]),
        gqr=np.ascontiguousarray(gqr_col), gkr_row=np.ascontiguousarray(g_k_head[None, 128:192]),
        gkv_row=g_kv_lora[None, :], ggoT=colT(g_gla_out, 4),
        w_in=w_in, w_uq_ext=w_uq_ext, w_ukv_p=w_ukv_p, w_gu_aug=w_gu_aug, w_out=w_out, w_up=w_up, w_down=w_down,
        kcs_pre=kcs_pre, kcs_sn=kcs_sn, qcs_s=qcs_s,
        c_ident=ident, c_tri=tri, c_ut=utm, c_mask=maskr, c_sel=sel, c_comb=comb, c_hsel=hsel,
    )
    in_maps = []
    for c in range(8):
        pb, half = c // 2, c % 2
        pos_own = half * 2048 + np.arange(2048)
        flag = np.zeros((128, 2), np.float32)
        flag[:, 0] = float(half)
        flag[:, 1] = 0.0 if half == 1 else NEG
        cvec = np.concatenate([c_prompt[pb:pb + 1], c_sample[4 * c:4 * c + 4]], axis=0)
        m = dict(shared)
        m.update(
            xpre=np.ascontiguousarray(x_prompt[pb, 0:2048]),
            xown=np.ascontiguousarray(x_prompt[pb, half * 2048:(half + 1) * 2048]),
            xsm=np.ascontiguousarray(x_sample[4 * c:4 * c + 4].reshape(256, 1024)),
            latc=np.ascontiguousarray(latc_all[4 * c:4 * c + 4]),
            krc=np.ascontiguousarray(krc_all[4 * c:4 * c + 4]),
            stc=np.ascontiguousarray(st_all[4 * c:4 * c + 4].reshape(4, 2, 128, 128)),
            cT=np.ascontiguousarray(cvec.T), flag=flag,
            kcs_own=_kcs(pos_own), qcs_own=_qcs(pos_own),
        )
        in_maps.append(m)

    return in_maps


def kernel(**inputs):
    in_maps = _prep(**inputs)
    if "nc" not in _NC_CACHE:
        _NC_CACHE["nc"] = build()
    res = run_bass_kernel_spmd(_NC_CACHE["nc"], in_maps, core_ids=list(range(8)))
    return _assemble(res.results)


def _assemble(R):
    y_p = np.zeros((4, 4096, 1024), np.float32)
    lat_p = np.zeros((1, 4, 4096, 256), np.float32)
    kr_p = np.zeros((1, 4, 4096, 64), np.float32)
    st_pp = np.zeros((1, 4, 4, 64, 128), np.float32)
    y_s = np.zeros((32, 64, 1024), np.float32)
    lat_s = np.zeros((1, 32, 64, 256), np.float32)
    kr_s = np.zeros((1, 32, 64, 64), np.float32)
    st_s = np.zeros((1, 32, 4, 64, 128), np.float32)
    for c in range(8):
        pb, half = c // 2, c % 2
        sl = slice(half * 2048, (half + 1) * 2048)
        y_p[pb, sl] = R[c]["y_own"]
        lat_p[0, pb, sl] = R[c]["lat_own"]
        kr_p[0, pb, sl] = R[c]["kr_own"]
        if half == 1:
            st_pp[0, pb] = R[c]["st_p"].reshape(4, 64, 128)
        y_s[4 * c:4 * c + 4] = R[c]["y_s"].reshape(4, 64, 1024)
        lat_s[0, 4 * c:4 * c + 4] = R[c]["lat_s"].reshape(4, 64, 256)
        kr_s[0, 4 * c:4 * c + 4] = R[c]["kr_s"].reshape(4, 64, 64)
        st_s[0, 4 * c:4 * c + 4] = R[c]["st_s"].reshape(4, 4, 64, 128)
    return (y_p, y_s, lat_p, kr_p, st_pp, lat_s, kr_s, st_s)
```

```python
import contextlib
import numpy as np
import concourse.bass as bass
import concourse.mybir as mybir
from concourse.bass_utils import run_bass_kernel_spmd

F32 = mybir.dt.float32
BF16 = mybir.dt.bfloat16
AF = mybir.ActivationFunctionType
ALU = mybir.AluOpType
EPS = 1e-6
NT = 256
NB = NT // 128
NCH = NT // 64
NEG = -30000.0


class Sched:
    ENGS = ("pe", "act", "dve", "pool", "sp")

    def __init__(self, nc):
        self.nc = nc
        self.ops = {e: [] for e in self.ENGS}
        self.last_w = {}
        self.readers = {}
        self.dma_cnt = {}
        self.pending = {e: None for e in self.ENGS}
        self.stopped = False

    def barrier(self):
        if self.stopped:
            return
        toks = []
        for e in self.ENGS:
            for i in range(len(self.ops[e]) - 1, -1, -1):
                if self.ops[e][i]["dma"] is None:
                    toks.append(("c", e, i))
                    break
        for k, c in self.dma_cnt.items():
            toks.append(("d", k, c))
        for e in self.ENGS:
            self.pending[e] = list(toks)

    def add(self, eng, fn, r=(), w=(), dma=None):
        if self.stopped:
            return None
        ops = self.ops[eng]
        idx = len(ops)
        deps = set()
        for k in r:
            t = self.last_w.get(k)
            if t is not None:
                deps.add(t)
        for k in w:
            t = self.last_w.get(k)
            if t is not None:
                deps.add(t)
            for t in self.readers.get(k, ()):
                deps.add(t)
        if self.pending[eng] is not None:
            deps.update(self.pending[eng])
            self.pending[eng] = None
        if dma is not None:
            c = self.dma_cnt.get(dma, 0) + 16
            self.dma_cnt[dma] = c
            tok = ("d", dma, c)
        else:
            tok = ("c", eng, idx)
        fdeps = []
        for t in deps:
            if t[0] == "c":
                if t[1] == eng and eng == "pe":
                    continue
                if t[1] == eng and t[2] == idx:
                    continue
                self.ops[t[1]][t[2]]["sig"] = True
            fdeps.append(t)
        ops.append(dict(fn=fn, deps=fdeps, sig=False, dma=dma))
        for k in w:
            self.last_w[k] = tok
            self.readers[k] = []
        for k in r:
            self.readers.setdefault(k, []).append(tok)
        return tok

    def emit(self, final_eng="sp"):
        nc = self.nc
        with contextlib.ExitStack() as es:
            esem = {e: es.enter_context(nc.semaphore("s_" + e)) for e in self.ENGS}
            dsem = {}
            for i, k in enumerate(self.dma_cnt):
                dsem[k] = es.enter_context(nc.semaphore("d%d" % i))
            sigidx = {}
            for e in self.ENGS:
                c = 0
                arr = []
                for op in self.ops[e]:
                    if op["sig"]:
                        c += 1
                    arr.append(c)
                sigidx[e] = arr
            block = es.enter_context(nc.Block())

            def run(e, h):
                waited = {}
                for op in self.ops[e]:
                    for t in op["deps"]:
                        if t[0] == "c":
                            key = ("c", t[1])
                            val = sigidx[t[1]][t[2]]
                            sem = esem[t[1]]
                        else:
                            key = ("d", t[1])
                            val = t[2]
                            sem = dsem[t[1]]
                        if waited.get(key, 0) >= val:
                            continue
                        waited[key] = val
                        h.wait_ge(sem, val)
                    ins = op["fn"](h)
                    if op["dma"] is not None:
                        ins.then_inc(dsem[op["dma"]], 16)
                    elif op["sig"]:
                        ins.then_inc(esem[e], 1)
                if e == final_eng:
                    for k, c in self.dma_cnt.items():
                        if waited.get(("d", k), 0) < c:
                            h.wait_ge(dsem[k], c)

            @block.tensor
            def _(h):
                run("pe", h)

            @block.scalar
            def _(h):
                run("act", h)

            @block.vector
            def _(h):
                run("dve", h)

            @block.gpsimd
            def _(h):
                run("pool", h)

            @block.sync
            def _(h):
                run("sp", h)


class Rot:
    def __init__(self, items):
        self.items = items
        self.i = 0

    def get(self):
        it = self.items[self.i % len(self.items)]
        self.i += 1
        return it


def OP(meth, *a, **kw):
    return lambda e: getattr(e, meth)(*a, **kw)


def seq(fns):
    def f(e):
        ins = None
        for g in fns:
            ins = g(e)
        return ins
    return f


def mm(out, lhsT, rhs, start=True, stop=True):
    return OP("matmul", out, lhsT, rhs, start=start, stop=stop)


def act(out, in_, func, **kw):
    return OP("activation", out=out, in_=in_, func=func, **kw)


CQ0, CKV0, KR0, GQ0, GK0, GV0, GLR0, GR0 = 0, 384, 640, 704, 960, 1216, 1728, 1744


class _Stop(Exception):
    pass


def build(dbg=False, stop=None):
    def ck(tag):
        if stop == tag:
            S.stopped = True
    nc = bass.Bass("TRN2", target_bir_lowering=False)
    S = Sched(nc)

    def din(name, shape):
        return nc.dram_tensor(name, list(shape), F32, kind="ExternalInput").ap()

    def dout(name, shape):
        return nc.dram_tensor(name, list(shape), F32, kind="ExternalOutput").ap()

    xpre = din("xpre", [2048, 1024])
    xown = din("xown", [2048, 1024])
    xsm = din("xsm", [256, 1024])
    latc = din("latc", [4, 2048, 256])
    krc = din("krc", [4, 2048, 64])
    stc = din("stc", [4, 2, 128, 128])
    cT = din("cT", [1024, 5])
    flag = din("flag", [128, 2])
    w_ada = din("w_ada", [1024, 6144])
    b_ada_row = din("b_ada_row", [1, 6144])
    b_adaT = din("b_adaT", [128, 48])
    g1T = din("g1T", [128, 8])
    g2T = din("g2T", [128, 8])
    gqlT = din("gqlT", [128, 3])
    gqn = din("gqn", [128, 1])
    gkn = din("gkn", [128, 1])
    gqr = din("gqr", [128, 1])
    gkr_row = din("gkr_row", [1, 64])
    gkv_row = din("gkv_row", [1, 256])
    ggoT = din("ggoT", [128, 4])
    w_in = din("w_in", [1024, 2256])
    w_uq_ext = din("w_uq_ext", [384, 1024])
    w_ukv_p = din("w_ukv_p", [256, 1024])
    w_gu_aug = din("w_gu_aug", [32, 256])
    w_out = din("w_out", [1024, 1024])
    w_up = din("w_up", [1024, 4096])
    w_down = din("w_down", [4096, 1024])
    kcs_pre = din("kcs_pre", [128, 16, 128])
    kcs_own = din("kcs_own", [128, 16, 128])
    kcs_sn = din("kcs_sn", [128, 2, 128])
    qcs_own = din("qcs_own", [128, 2048])
    qcs_s = din("qcs_s", [128, 256])
    c_ident = din("c_ident", [128, 128])
    c_tri = din("c_tri", [128, 128])
    c_ut = din("c_ut", [128, 128])
    c_mask = din("c_mask", [128, 256])
    c_sel = din("c_sel", [5, 3, 128])
    c_comb = din("c_comb", [128, 128])
    c_hsel = din("c_hsel", [128, 2])

    y_own = dout("y_own", [2048, 1024])
    y_s = dout("y_s", [256, 1024])
    lat_own = dout("lat_own", [2048, 256])
    kr_own = dout("kr_own", [2048, 64])
    st_p = dout("st_p", [2, 128, 128])
    lat_s = dout("lat_s", [256, 256])
    kr_s = dout("kr_s", [256, 64])
    st_s = dout("st_s", [4, 2, 128, 128])
    x1s = nc.dram_tensor("x1s", [2304, 1024], F32).ap()

    P = [nc.alloc_psum_tensor("P%d" % i, [128, 512], F32) for i in range(8)]

    def PK(i):
        return "P%d" % i

    with contextlib.ExitStack() as glob:
        def gsb(name, shape, dt=F32):
            return glob.enter_context(nc.sbuf_tensor(name, list(shape), dt))

        ident = gsb("ident", [128, 128])
        tri = gsb("tri", [128, 128])
        ut = gsb("ut", [128, 128])
        maskr = gsb("maskr", [128, 256])
        sel = gsb("sel", [5, 3, 128])
        comb = gsb("comb", [128, 128])
        flg = gsb("flg", [128, 2])
        hsel = gsb("hsel", [128, 2])
        ones_bf = gsb("ones_bf", [128, 128], BF16)
        ones_f = gsb("ones_f", [1, 128])
        gm1 = gsb("gm1", [128, 8, 5])
        sh1 = gsb("sh1", [128, 8, 5])
        gm2 = gsb("gm2", [128, 8, 5])
        sh2 = gsb("sh2", [128, 8, 5])
        gater = gsb("gater", [5, 2, 1024])
        g2c = gsb("g2c", [128, 8])
        for (t, d) in ((ident, c_ident), (tri, c_tri), (ut, c_ut), (maskr, c_mask), (comb, c_comb), (flg, flag), (g2c, g2T), (hsel, c_hsel)):
            S.add("sp", (OP("dma_start", out=t[:, :], in_=d[:, :])), w=[t.name], dma=t.name)
        S.add("sp", OP("dma_start", out=sel[:, :, :], in_=c_sel[:, :, :]), w=["sel"], dma="sel")
        S.add("dve", OP("memset", ones_bf[:, :], 1.0), w=["ones_bf"])
        S.add("dve", OP("memset", ones_f[:, :], 1.0), w=["ones_f"])

        GEN_OUT = [5, 6, 0, 1, 4]
        GEN_IN = [5, 6]
        gen4 = Rot(list(GEN_OUT))
        SCB = [0, 1, 4]

        def set_gen(lst):
            gen4.items = list(lst)
            gen4.i = 0

        def gp():
            i = gen4.get()
            return P[i], PK(i)

        try:
            with contextlib.ExitStack() as ph1:
                def psb(name, shape, dt=F32):
                    return ph1.enter_context(nc.sbuf_tensor(name, list(shape), dt))

                win = psb("win", [128, 8, 2256], BF16)
                wuq = psb("wuq", [128, 3, 1024], BF16)
                wukv = psb("wukv", [128, 2, 1024], BF16)
                wout = psb("wout", [128, 8, 1024], BF16)
                wgu = psb("wgu", [32, 256], BF16)
                g1c = psb("g1c", [128, 8])
                gqk = psb("gqk", [128, 1])
                gkn_t = psb("gkn_t", [128, 1])
                gqr_t = psb("gqr_t", [128, 1])
                ggo = psb("ggo", [128, 4])
                gkr_bc = psb("gkr_bc", [128, NB, 64])
                gkv_bc = psb("gkv_bc", [128, 256])
                g1bc = psb("g1bc", [128, 1024])
                zcol = psb("zcol", [128, 1])
                glrT = psb("glrT", [32, NT], BF16)
                Sst = psb("Sst", [128, 2, 128])
                S.add("dve", OP("memset", zcol[:, :], 0.0), w=["zcol"])
                S.add("dve", OP("memset", glrT[:, :], 1.0), w=["glrT"])
                S.add("dve", OP("memset", Sst[:, :, :], 0.0), w=[("Sst", 0), ("Sst", 1)])
                for (t, d) in ((g1c, g1T), (gkn_t, gkn), (gqr_t, gqr), (ggo, ggoT)):
                    S.add("sp", (OP("dma_start", out=t[:, :], in_=d[:, :])), w=[t.name], dma=t.name)
                S.add("sp", OP("dma_start", out=gqk[:, :], in_=gqn[:, :]), w=["gqk"], dma="gqk")
                S.add("dve", OP("tensor_tensor", out=gqk[:, :], in0=gqk[:, :], in1=gkn_t[:, :], op=ALU.mult), r=["gkn_t", "gqk"], w=["gqk"])

                with contextlib.ExitStack() as ph0:
                    def zsb(name, shape, dt=F32):
                        return ph0.enter_context(nc.sbuf_tensor(name, list(shape), dt))
                    wa = [zsb("wa%d" % i, [128, 8, 512], BF16) for i in range(2)]
                    wuq_f = zsb("wuq_f", [128, 3, 1024])
                    ctile = zsb("ctile", [128, 8, 5])
                    c_e = zsb("c_e", [128, 8, 5])
                    siluT = zsb("siluT", [128, 8, 5], BF16)
                    badT = zsb("badT", [128, 48])
                    badr = zsb("badr", [1, 6144])
                    badr_bf = zsb("badr_bf", [1, 6144], BF16)
                    ones5 = zsb("ones5", [1, 8], BF16)
                    gql = zsb("gql", [128, 3])
                    rowt = zsb("rowt", [1, 320])
                    modT = zsb("modT", [128, 32, 5])

                    S.add("sp", OP("dma_start", out=ctile[:, :, :], in_=cT.rearrange("(kc p) b -> p kc b", p=128)), w=["ctile"], dma="ctile")
                    S.add("sp", OP("dma_start", out=badT[:, :], in_=b_adaT[:, :]), w=["badT"], dma="badT")
                    S.add("sp", OP("dma_start", out=badr[:, :], in_=b_ada_row[:, :]), w=["badr"], dma="badr")
                    S.add("sp", OP("dma_start", out=gql[:, :], in_=gqlT[:, :]), w=["gql"], dma="gql")
                    S.add("sp", OP("dma_start", out=rowt[:, 0:256], in_=gkv_row[:, :]), w=["rowt0"], dma="rowt0")
                    S.add("sp", OP("dma_start", out=rowt[:, 256:320], in_=gkr_row[:, :]), w=["rowt1"], dma="rowt1")
                    S.add("sp", OP("dma_start", out=wuq_f[:, :, :], in_=w_uq_ext.rearrange("(kc p) n -> p kc n", p=128)), w=["wuq_f"], dma="wuq_f")
                    S.add("dve", OP("memset", ones5[:, :], 1.0), w=["ones5"])
                    S.add("dve", OP("tensor_copy", out=badr_bf[:, :], in_=badr[:, :]), r=["badr"], w=["badr_bf"])
                    S.add("act", act(c_e[:, :, :], ctile[:, :, :], AF.Exp, scale=-1.0), r=["ctile"], w=["c_e"])
                    S.add("dve", OP("tensor_scalar", out=c_e[:, :, :], in0=c_e[:, :, :], scalar1=1.0, scalar2=None, op0=ALU.add), r=["c_e"], w=["c_e"])
                    S.add("dve", OP("reciprocal", out=c_e[:, :, :], in_=c_e[:, :, :]), r=["c_e"], w=["c_e"])
                    S.add("dve", OP("tensor_tensor", out=siluT[:, :, :], in0=ctile[:, :, :], in1=c_e[:, :, :], op=ALU.mult), r=["c_e", "ctile"], w=["siluT"])

                    def wa_load(ct, slot):
                        S.add("pool", OP("dma_start", out=wa[slot][:, :, :], in_=w_ada[:, ct * 512:(ct + 1) * 512].rearrange("(kc p) n -> p kc n", p=128)),
                              w=["wa%d" % slot], dma="wa%d" % slot)
                    order = [0, 1, 2, 3, 4, 5, 6, 7, 8, 9, 10, 11]
                    wa_load(order[0], 0)
                    fidx = {}
                    fi = 0
                    for ct in order:
                        if ct in (0, 1, 2, 3, 6, 7, 8, 9):
                            for cc in range(4):
                                fidx[(ct, cc)] = fi
                                fi += 1
                    PF, PFk = P[4], PK(4)
                    for n, ct in enumerate(order):
                        slot = n % 2
                        if n + 1 < len(order):
                            wa_load(order[n + 1], (n + 1) % 2)
                        if ct in (0, 1, 2, 3, 6, 7, 8, 9):
                            fns = []
                            for cc in range(4):
                                f = fidx[(ct, cc)]
                                for kc in range(8):
                                    fns.append(mm(PF[:, f * 8:f * 8 + 5], wa[slot][:, kc, cc * 128:(cc + 1) * 128], siluT[:, kc, :], start=(kc == 0), stop=(kc == 7)))
                            S.add("pe", seq(fns), r=["wa%d" % slot, "siluT"], w=[PFk])
                        else:
                            gi = 0 if ct in (4, 5) else 1
                            hf = ct % 2
                            Pr, Prk = P[5 + (n % 2)], PK(5 + (n % 2))
                            fns = [mm(Pr[0:5, :], siluT[:, kc, :], wa[slot][:, kc, :], start=(kc == 0), stop=False) for kc in range(8)]
                            fns.append(mm(Pr[0:5, :], ones5[0:1, 0:5], badr_bf[0:1, ct * 512:(ct + 1) * 512], start=False, stop=True))
                            S.add("pe", seq(fns), r=["wa%d" % slot, "siluT", "ones5", "badr_bf"], w=[Prk])
                            S.add("act", act(gater[:, gi, hf * 512:(hf + 1) * 512], Pr[0:5, :], AF.Copy), r=[Prk], w=["gater"])
                    for (ct, cc), f in fidx.items():
                        chunk = ct * 4 + cc
                        S.add("dve", (OP("tensor_scalar", out=modT[:, f, :], in0=PF[:, f * 8:f * 8 + 5], scalar1=badT[:, chunk:chunk + 1], scalar2=None, op0=ALU.add)),
                              r=[PFk, "badT"], w=["modT"])
                    for kc in range(8):
                        S.add("dve", (OP("tensor_scalar", out=gm1[:, kc, :], in0=modT[:, 8 + kc, :], scalar1=1.0, scalar2=g1c[:, kc:kc + 1], op0=ALU.add, op1=ALU.mult)),
                              r=["modT", "g1c"], w=["gm1"])
                        S.add("dve", (OP("tensor_scalar", out=gm2[:, kc, :], in0=modT[:, 24 + kc, :], scalar1=1.0, scalar2=g2c[:, kc:kc + 1], op0=ALU.add, op1=ALU.mult)),
                              r=["modT", "g2c"], w=["gm2"])
                    S.add("dve", OP("tensor_copy", out=sh1[:, :, :], in_=modT[:, 0:8, :]), r=["modT"], w=["sh1"])
                    S.add("dve", OP("tensor_copy", out=sh2[:, :, :], in_=modT[:, 16:24, :]), r=["modT"], w=["sh2"])

                    for hf in range(2):
                        S.add("pool", (OP("dma_start", out=win[:, :, hf * 1128:(hf + 1) * 1128], in_=w_in[:, hf * 1128:(hf + 1) * 1128].rearrange("(kc p) n -> p kc n", p=128))),
                              w=["win%d" % hf], dma="win%d" % hf)
                    S.add("pool", OP("dma_start", out=wukv[:, :, :], in_=w_ukv_p.rearrange("(kc p) n -> p kc n", p=128)), w=["wukv"], dma="wukv")
                    S.add("pool", OP("dma_start", out=wgu[:, :], in_=w_gu_aug[:, :]), w=["wgu"], dma="wgu")
                    S.add("pool", OP("dma_start", out=wout[:, :, :], in_=w_out.rearrange("(kc p) n -> p kc n", p=128)), w=["wout"], dma="wout")
                    for kc in range(3):
                        S.add("dve", (OP("tensor_scalar", out=wuq[:, kc, :], in0=wuq_f[:, kc, :], scalar1=gql[:, kc:kc + 1], scalar2=None, op0=ALU.mult)),
                              r=["wuq_f", "gql"], w=["wuq"])
                        for h in range(4):
                            c0 = 768 + h * 64
                            S.add("dve", (OP("tensor_scalar", out=wuq[:, kc, c0:c0 + 32], in0=wuq[:, kc, c0:c0 + 32], scalar1=-1.0, scalar2=None, op0=ALU.mult)),
                                  r=["wuq"], w=["wuq"])
                    Pb, Pbk = P[7], PK(7)
                    S.add("pe", mm(Pb[:, 0:320], ones_f[0:1, :], rowt[0:1, 0:320]), r=["ones_f", "rowt0", "rowt1"], w=[Pbk])
                    S.add("act", act(gkv_bc[:, :], Pb[:, 0:256], AF.Copy), r=[Pbk], w=["gkv_bc"])
                    for b in range(NB):
                        S.add("act", (OP("activation", out=gkr_bc[:, b, :], in_=Pb[:, 256:320], func=AF.Copy)), r=[Pbk], w=["gkr_bc"])
                S.barrier()
                ck('p0')

                Kn = psb("Kn", [128, 4, 4096], BF16)
                Kr = psb("Kr", [128, 2048], BF16)
                Vs = psb("Vs", [128, 32, 512], BF16)
                sclK = psb("sclK", [128, 32, 4])
                Sbf = psb("Sbf", [128, 2, 2, 128], BF16)
                xt = psb("xt", [128, 1024])
                hT = psb("hT", [128, 8, NT], BF16)
                lat_tm = psb("lat_tm", [128, NB, 256])
                kr_tm = psb("kr_tm", [128, NB, 64])
                latT = psb("latT", [128, 2, NT], BF16)
                gk_tm = psb("gk_tm", [128, NB, 256])
                gv_tm = psb("gv_tm", [128, NB, 512], BF16)
                l_tm = psb("l_tm", [128, NB, 256])
                khat = psb("khat", [128, NB, 256], BF16)
                cqT = psb("cqT", [128, 3, NT], BF16)
                qn = psb("qn", [128, 4, NT], BF16)
                qr = psb("qr", [128, 4, 2, NT], BF16)
                mixT = psb("mixT", [128, 8, NT], BF16)
                xr = xt
                kcs = psb("kcs", [128, NB, 128])
                qcs = psb("qcs", [128, NT])
                epst = psb("epst", [128, NT])
                stat = psb("stat", [128, 64])
                junk = psb("junk", [128, 1024], BF16)
                dec = psb("dec", [128, 2, NCH])
                TFl = [psb("TF%d" % i, [128, NT]) for i in range(10)]
                TBl = [psb("TB%d" % i, [128, NT], BF16) for i in range(11)]
                PTb = [psb("PT%d" % i, [128, NT], BF16) for i in range(4)]
                TFg = Rot([(t, t.name) for t in TFl[0:5]])
                TFq = Rot([(t, t.name) for t in TFl[5:8]])
                TFk = Rot([(t, t.name) for t in TFl[8:10]])
                TBg = Rot([(t, t.name) for t in TBl[0:6]])
                TBq = Rot([(t, t.name) for t in TBl[6:11]])
                PT = Rot([(t, t.name) for t in PTb])
                lat2 = xt[:, 0:512].rearrange("p (b d) -> p b d", b=NB)
                kr2 = xt[:, 512:640].rearrange("p (b d) -> p b d", b=NB)
                kcs2 = xt[:, 640:896].rearrange("p (b d) -> p b d", b=NB)
                latT2 = hT[:, 0:2, :]
                BS0 = dict(lat=lat_tm[:, :, :], latk="lat_tm", kr=kr_tm[:, :, :], krk="kr_tm", kcs=kcs[:, :, :], kcsk="kcs", latT=latT[:, :, :], latTk="latT")
                BS1 = dict(lat=lat2, latk="xt_lat", kr=kr2, krk="xt_kr", kcs=kcs2, kcsk="xt_kcs", latT=latT2, latTk="hTa")
                XTK = ["xt", "xt_kcs"] + [("xt_lat", i) for i in range(NB)] + [("xt_kr", i) for i in range(NB)]


                WIN = ["win0", "win1"]

                def HK(hTk, kcs=range(8), bs=range(NB)):
                    return [(hTk, kc, b) for kc in kcs for b in bs]

                def BK(pref, idx=None, n=NB):
                    return [(pref, i) for i in (range(n) if idx is None else idx)]
                S.add("dve", OP("memset", Kr[:, :], 0.0), w=[("Kr", kt_) for kt_ in range(2048 // NT)])
                S.add("dve", OP("memset", qr[:, :, :, :], 0.0), w=[("qr", h) for h in range(4)])

                def load_g1bc(v):
                    for fh in range(2):
                        Pg, Pgk = gp()
                        S.add("pe", (OP("matmul", Pg[:, :], sel[:, v, :], gater[:, 0, fh * 512:(fh + 1) * 512], start=True, stop=True)),
                              r=["sel", "gater"], w=[Pgk])
                        S.add("act", (OP("activation", out=g1bc[:, fh * 512:(fh + 1) * 512], in_=Pg[:, :], func=AF.Copy)), r=[Pgk], w=["g1bc"])

                def front(xsrc, r0, segs, hTb=None, hTk="hT", TFp=None):
                    hTb = hT if hTb is None else hTb
                    TFp = TFg if TFp is None else TFp
                    for b in range(NB):
                        S.add("sp", (OP("dma_start", out=xt[:, :], in_=xsrc[r0 + b * 128:r0 + (b + 1) * 128, :])), w=XTK, dma="xt")
                        S.add("act", OP("activation", out=junk[:, :], in_=xt[:, :], func=AF.Square, accum_out=stat[:, 4:5]), r=["xt"], w=[("junk", 0), ("junk", 1), ("junk", 2), "st_x"])
                        S.add("act", act(stat[:, 5:6], stat[:, 4:5], AF.Ln, scale=1.0 / 1024, bias=EPS), r=["st_x"], w=["st_x"])
                        S.add("act", act(stat[:, 6:7], stat[:, 5:6], AF.Exp, scale=-0.5), r=["st_x"], w=["st_x"])
                        S.add("dve", OP("tensor_scalar", out=xt[:, :], in0=xt[:, :], scalar1=stat[:, 6:7], scalar2=None, op0=ALU.mult), r=["xt", "st_x"], w=["xt"])
                        yield
                        for k2 in range(2):
                            Pt, Ptk = gp()
                            S.add("pe", seq([(OP("transpose", Pt[:, kk * 128:(kk + 1) * 128], xt[:, (k2 * 4 + kk) * 128:(k2 * 4 + kk + 1) * 128], ident[:, :])) for kk in range(4)]),
                                  r=["xt", "ident"], w=[Ptk])
                            for kk in range(4):
                                kc = k2 * 4 + kk
                                for (c0, ncol, m) in segs:
                                    lo = max(c0, b * 128)
                                    hi = min(c0 + ncol, (b + 1) * 128)
                                    if lo >= hi:
                                        continue
                                    S.add("dve", OP("tensor_scalar", out=hTb[:, kc, lo:hi], in0=Pt[:, kk * 128 + lo - b * 128:kk * 128 + hi - b * 128],
                                                    scalar1=gm1[:, kc, m:m + 1], scalar2=sh1[:, kc, m:m + 1], op0=ALU.mult, op1=ALU.add),
                          r=[Ptk, "gm1", "sh1"], w=[(hTk, kc, b), "hTa"])
                        yield

                def kvproj(lat_dst, kr_dst, r0, hTb=None, hTk="hT", TFp=None):
                    hTb = hT if hTb is None else hTb
                    TFp = TFg if TFp is None else TFp
                    for b in range(NB):
                        Pq, Pqk = gp()
                        S.add("pe", seq([mm(Pq[:, 0:320], hTb[:, kc, b * 128:(b + 1) * 128], win[:, kc, CKV0:CKV0 + 320], start=(kc == 0), stop=(kc == 7)) for kc in range(8)]),
                              r=HK(hTk, bs=[b]) + WIN, w=[Pqk])
                        skv = ("st_kv", b)
                        S.add("act", (OP("activation", out=junk[:, b * 256:(b + 1) * 256], in_=Pq[:, 0:256], func=AF.Square, accum_out=stat[:, 8 + b:9 + b])), r=[Pqk], w=[("junk", b), skv])
                        S.add("act", (OP("activation", out=stat[:, 12 + b:13 + b], in_=stat[:, 8 + b:9 + b], func=AF.Ln, scale=1.0 / 256, bias=EPS)), r=[skv], w=[skv])
                        S.add("act", (OP("activation", out=stat[:, 16 + b:17 + b], in_=stat[:, 12 + b:13 + b], func=AF.Exp, scale=-0.5)), r=[skv], w=[skv])
                        S.add("dve", (OP("scalar_tensor_tensor", out=lat_tm[:, b, :], in0=Pq[:, 0:256], scalar=stat[:, 16 + b:17 + b], in1=gkv_bc[:, :], op0=ALU.mult, op1=ALU.mult)),
                              r=[Pqk, skv, "gkv_bc"], w=[("lat_tm", b)])
                        S.add("dve", OP("tensor_copy", out=kr_tm[:, b, :], in_=Pq[:, 256:320]), r=[Pqk], w=[("kr_tm", b)])
                        yield
                    if lat_dst is not None:
                        S.add("pool", OP("dma_start", out=lat_dst[r0:r0 + NT, :].rearrange("(b p) d -> p b d", p=128), in_=lat_tm[:, :, :]), r=BK("lat_tm"), dma="lat_o")
                        S.add("pool", OP("dma_start", out=kr_dst[r0:r0 + NT, :].rearrange("(b p) d -> p b d", p=128), in_=kr_tm[:, :, :]), r=BK("kr_tm"), dma="kr_o")

                def lat_transpose(bs=None):
                    bs = BS0 if bs is None else bs
                    for lc in range(2):
                        Pt, Ptk = gp()
                        S.add("pe", seq([(OP("transpose", Pt[:, b * 128:(b + 1) * 128], bs["lat"][:, b, lc * 128:(lc + 1) * 128], ident[:, :])) for b in range(NB)]),
                              r=BK(bs["latk"]) + ["ident"], w=[Ptk])
                        S.add("dve", (OP("tensor_copy", out=bs["latT"][:, lc, :], in_=Pt[:, 0:NT])), r=[Ptk], w=[(bs["latTk"], lc)] + (["hTa"] if bs["latTk"] == "hTa" else []))
                        yield

                def kside(key0, cs_src, cs_b0, bs=None, TFp=None, load_cs=True):
                    kb0 = key0 // 128
                    bs = BS0 if bs is None else bs
                    TFp = TFg if TFp is None else TFp
                    latT_, latTk_, kr_, krk_, kcs_, kcsk_ = bs["latT"], bs["latTk"], bs["kr"], bs["krk"], bs["kcs"], bs["kcsk"]
                    if load_cs:
                        S.add("sp", OP("dma_start", out=kcs_, in_=cs_src[:, cs_b0:cs_b0 + NB, :]), w=[kcsk_], dma=kcsk_)
                    for h in range(4):
                        Pq, Pqk = gp()
                        S.add("pe", seq([mm(Pq[:, 0:NT], wukv[:, lc, h * 128:(h + 1) * 128], latT_[:, lc, :], start=(lc == 0), stop=(lc == 1)) for lc in range(2)]),
                              r=["wukv", (latTk_, 0), (latTk_, 1)], w=[Pqk])
                        S.add("dve", OP("tensor_copy", out=Kn[:, h, key0:key0 + NT], in_=Pq[:, 0:NT]), r=[Pqk], w=[("Kn", key0 // NT, h)])
                        yield
                    for b in range(NB):
                        Pq, Pqk = gp()
                        S.add("pe", seq([mm(Pq[:, :], latT_[:, lc, b * 128:(b + 1) * 128], wukv[:, lc, 0:512], start=(lc == 0), stop=(lc == 1)) for lc in range(2)]),
                              r=["wukv", (latTk_, 0), (latTk_, 1)], w=[Pqk])
                        sk = ("st_k", b)
                        for h in range(4):
                            S.add("act", (OP("activation", out=Pq[:, h * 128:(h + 1) * 128], in_=Pq[:, h * 128:(h + 1) * 128], func=AF.Square, accum_out=stat[:, 20 + b * 4 + h:21 + b * 4 + h])),
                                  r=[Pqk], w=[(Pqk, h), (sk, h)])
                        S.add("act", (OP("activation", out=junk[:, 512 + b * 64:512 + (b + 1) * 64], in_=kr_[:, b, :], func=AF.Square, accum_out=stat[:, 28 + b:29 + b])), r=[(krk_, b)], w=[("junk", 2), (sk, 4)])
                        S.add("dve", (OP("tensor_scalar", out=stat[:, 32 + b * 4:36 + b * 4], in0=stat[:, 20 + b * 4:24 + b * 4], scalar1=stat[:, 28 + b:29 + b], scalar2=None, op0=ALU.add)),
                              r=[(sk, 0), (sk, 1), (sk, 2), (sk, 3), (sk, 4)], w=[sk])
                        S.add("act", (OP("activation", out=stat[:, 40 + b * 4:44 + b * 4], in_=stat[:, 32 + b * 4:36 + b * 4], func=AF.Ln, scale=1.0 / 192, bias=EPS)), r=[sk], w=[sk])
                        S.add("act", (OP("activation", out=sclK[:, kb0 + b, :], in_=stat[:, 40 + b * 4:44 + b * 4], func=AF.Exp, scale=-0.5, bias=float(-0.5 * np.log(192.0)))),
                              r=[sk], w=[("sclK", kb0 + b)])
                        yield
                        Pv, Pvk = gp()
                        S.add("pe", seq([mm(Pv[:, :], latT_[:, lc, b * 128:(b + 1) * 128], wukv[:, lc, 512:1024], start=(lc == 0), stop=(lc == 1)) for lc in range(2)]),
                              r=["wukv", (latTk_, 0), (latTk_, 1)], w=[Pvk])
                        S.add("dve", (OP("tensor_copy", out=Vs[:, kb0 + b, :], in_=Pv[:, :])), r=[Pvk], w=[("Vs", kb0 + b)])
                        yield
                    t1, t1k = TFp.get()
                    t2, t2k = TFp.get()
                    t3, t3k = TFp.get()
                    v1 = t1[:, 0:NB * 64].rearrange("p (b d) -> p b d", b=NB)
                    v2 = t2[:, 0:NB * 64].rearrange("p (b d) -> p b d", b=NB)
                    v3 = t3[:, 0:NB * 64].rearrange("p (b d) -> p b d", b=NB)
                    S.add("dve", OP("tensor_tensor", out=v1, in0=kr_, in1=gkr_bc[:, :, :], op=ALU.mult), r=BK(krk_) + ["gkr_bc"], w=[t1k])
                    S.add("dve", OP("tensor_tensor", out=v2, in0=v1, in1=kcs_[:, :, 0:64], op=ALU.mult), r=[t1k, kcsk_], w=[t2k])
                    S.add("dve", OP("tensor_tensor", out=v3[:, :, 0:32], in0=v1[:, :, 32:64], in1=kcs_[:, :, 64:96], op=ALU.mult), r=[t1k, kcsk_], w=[t3k])
                    S.add("dve", OP("tensor_tensor", out=v3[:, :, 32:64], in0=v1[:, :, 0:32], in1=kcs_[:, :, 96:128], op=ALU.mult), r=[t1k, kcsk_], w=[t3k])
                    S.add("dve", OP("tensor_tensor", out=v2, in0=v2, in1=v3, op=ALU.add), r=[t2k, t3k], w=[t2k])
                    yield
                    Pt, Ptk = gp()
                    S.add("pe", seq([(OP("transpose", Pt[0:64, b * 128:(b + 1) * 128], v2[:, b, :], ident[:, :])) for b in range(NB)]),
                          r=[t2k, "ident"], w=[Ptk])
                    hb = 0 if key0 < 2048 else 64
                    kk0 = key0 % 2048
                    if hb == 0:
                        S.add("act", OP("activation", out=Kr[0:64, kk0:kk0 + NT], in_=Pt[0:64, 0:NT], func=AF.Copy), r=[Ptk], w=[("Kr", (key0 % 2048) // NT)])
                    else:
                        sh, shk = TFp.get()
                        S.add("act", OP("activation", out=sh[0:64, :], in_=Pt[0:64, 0:NT], func=AF.Copy), r=[Ptk], w=[shk])
                        P2, P2k = gp()
                        S.add("pe", OP("matmul", P2[:, 0:NT], comb[0:64, :], sh[0:64, :], start=True, stop=True), r=[shk, "comb"], w=[P2k])
                        S.add("act", OP("activation", out=Kr[64:128, kk0:kk0 + NT], in_=P2[64:128, 0:NT], func=AF.Copy), r=[P2k], w=[("Kr", (key0 % 2048) // NT)])
                    yield

                def gla_common(own, hTb=None, hTk="hT", TFp=None):
                    hTb = hT if hTb is None else hTb
                    TFp = TFg if TFp is None else TFp
                    Pl, Plk = gp()
                    S.add("pe", seq([mm(Pl[0:16, 0:NT], win[:, kc, GLR0:GLR0 + 16], hTb[:, kc, :], start=(kc == 0), stop=(kc == 7)) for kc in range(8)]),
                          r=HK(hTk) + WIN, w=[Plk])
                    S.add("dve", OP("tensor_copy", out=glrT[0:16, :], in_=Pl[0:16, 0:NT]), r=[Plk], w=["glrT"])
                    yield
                    for b in range(NB):
                        Pk_, Pkk = gp()
                        S.add("pe", seq([mm(Pk_[:, 0:256], hTb[:, kc, b * 128:(b + 1) * 128], win[:, kc, GK0:GK0 + 256], start=(kc == 0), stop=(kc == 7)) for kc in range(8)]),
                              r=HK(hTk, bs=[b]) + WIN, w=[Pkk])
                        S.add("dve", OP("tensor_copy", out=gk_tm[:, b, :], in_=Pk_[:, 0:256]), r=[Pkk], w=[("gk_tm", b)])
                        yield
                        Pv, Pvk = gp()
                        S.add("pe", seq([mm(Pv[:, :], hTb[:, kc, b * 128:(b + 1) * 128], win[:, kc, GV0:GV0 + 512], start=(kc == 0), stop=(kc == 7)) for kc in range(8)]),
                              r=HK(hTk, bs=[b]) + WIN, w=[Pvk])
                        S.add("dve", (OP("tensor_copy", out=gv_tm[:, b, :], in_=Pv[:, :])), r=[Pvk], w=[("gv_tm", b)])
                        yield
                        Pz, Pzk = gp()
                        S.add("pe", (OP("matmul", Pz[:, 0:256], glrT[:, b * 128:(b + 1) * 128], wgu[:, :], start=True, stop=True)), r=["glrT", "wgu"], w=[Pzk])
                        S.add("act", (OP("activation", out=l_tm[:, b, :], in_=Pz[:, 0:256], func=AF.Exp, scale=-1.0)), r=[Pzk], w=[("l_tm", b)])
                        S.add("act", (OP("activation", out=l_tm[:, b, :], in_=l_tm[:, b, :], func=AF.Ln, bias=1.0)), r=[("l_tm", b)], w=[("l_tm", b)])
                        yield
                        Pc, Pck = gp()
                        S.add("pe", (OP("matmul", Pc[:, 0:256], ut[:, :], l_tm[:, b, :], start=True, stop=True)), r=["ut", ("l_tm", b)], w=[Pck])
                        et, etk = TFp.get()
                        S.add("act", (OP("activation", out=et[:, :], in_=Pc[:, 0:256], func=AF.Exp)), r=[Pck], w=[etk])
                        S.add("dve", (OP("tensor_tensor", out=khat[:, b, :], in0=gk_tm[:, b, :], in1=et[:, :], op=ALU.mult)), r=[etk, ("gk_tm", b)], w=[("khat", b)])
                        yield
                    return None

                def bt_step(hp, TFp, need_e):
                    Pb_, Pbk_ = gp()
                    S.add("pe", seq([OP("matmul", Pb_[:, b * 128:(b + 1) * 128], l_tm[:, b, hp * 128:(hp + 1) * 128], tri[:, :], start=True, stop=True) for b in range(NB)]),
                          r=[("l_tm", b_) for b_ in range(NB)] + ["tri"], w=[Pbk_])
                    S.add("act", OP("activation", out=dec[:, hp, :], in_=Pb_[:, 63:NT:64], func=AF.Exp), r=[Pbk_], w=[("dec", hp)])
                    if not need_e:
                        return None
                    eb, ebk = TFp.get()
                    enb, enbk = TFp.get()
                    S.add("act", OP("activation", out=eb[:, :], in_=Pb_[:, 0:NT], func=AF.Exp), r=[Pbk_], w=[ebk])
                    S.add("act", OP("activation", out=enb[:, :], in_=Pb_[:, 0:NT], func=AF.Exp, scale=-1.0), r=[Pbk_], w=[enbk])
                    return eb, ebk, enb, enbk

                def state_update(hp, ch, Pu, Puk):
                    b, par = ch // 2, ch % 2
                    fns = []
                    for hh in range(2):
                        h = hp * 2 + hh
                        fns.append(mm(Pu[hh * 64:(hh + 1) * 64, 0:128], khat[par * 64:(par + 1) * 64, b, h * 64:(h + 1) * 64], gv_tm[par * 64:(par + 1) * 64, b, h * 128:(h + 1) * 128]))
                    S.add("pe", seq(fns), r=[("khat", b), ("gv_tm", b)], w=[Puk])

                def gla_prefix(hTb=None, hTk="hT", TFp=None):
                    yield from gla_common(False, hTb, hTk, TFp)
                    for hp in range(2):
                        bt_step(hp, TFp, False)
                        yield
                        for ch in range(NCH):
                            Pu, Puk = gp()
                            state_update(hp, ch, Pu, Puk)
                            S.add("dve", (OP("scalar_tensor_tensor", out=Sst[:, hp, :], in0=Sst[:, hp, :], scalar=dec[:, hp, ch:ch + 1], in1=Pu[:, 0:128], op0=ALU.mult, op1=ALU.add)),
                                  r=[Puk, ("dec", hp), ("Sst", hp)], w=[("Sst", hp)])
                            yield

                def gla_own(per_chunk_state, st_dst, TFp=None, TBp=None):
                    TFp = TFg if TFp is None else TFp
                    TBp = TBg if TBp is None else TBp
                    yield from gla_common(True, None, "hT", TFp)
                    for hp in range(2):
                        eb, ebk, enb, enbk = bt_step(hp, TFp, True)
                        yield
                        Pq, Pqk = gp()
                        S.add("pe", seq([mm(Pq[:, 0:NT], win[:, kc, GQ0 + hp * 128:GQ0 + (hp + 1) * 128], hT[:, kc, :], start=(kc == 0), stop=(kc == 7)) for kc in range(8)]),
                              r=HK("hT") + WIN, w=[Pqk])
                        qtls = []
                        for hh in range(2):
                            qtl, qtlk = TBp.get()
                            S.add("dve", OP("scalar_tensor_tensor", out=qtl[:, :], in0=Pq[:, 0:NT], scalar=hsel[:, hh:hh + 1], in1=eb[:, :], op0=ALU.mult, op1=ALU.mult),
                                  r=[Pqk, ebk, "hsel"], w=[qtlk])
                            qtls.append((qtl, qtlk))
                        yield
                        Pk2, Pk2k = gp()
                        S.add("pe", seq([mm(Pk2[:, 0:NT], win[:, kc, GK0 + hp * 128:GK0 + (hp + 1) * 128], hT[:, kc, :], start=(kc == 0), stop=(kc == 7)) for kc in range(8)]),
                              r=HK("hT") + WIN, w=[Pk2k])
                        ktl, ktlk = TBp.get()
                        S.add("dve", OP("tensor_tensor", out=ktl[:, :], in0=Pk2[:, 0:NT], in1=enb[:, :], op=ALU.mult), r=[Pk2k, enbk], w=[ktlk])
                        yield
                        Ps, Psk = gp()
                        fns = []
                        for hh in range(2):
                            for b in range(NB):
                                co = hh * NT + b * 128
                                fns.append(mm(Ps[:, co:co + 128], ktl[:, b * 128:(b + 1) * 128], qtls[hh][0][:, b * 128:(b + 1) * 128]))
                        S.add("pe", seq(fns), r=[ktlk, qtls[0][1], qtls[1][1]], w=[Psk])
                        mks = []
                        for hh in range(2):
                            msk, mskk = TBp.get()
                            S.add("dve", OP("tensor_tensor", out=msk[:, :], in0=Ps[:, hh * NT:(hh + 1) * NT], in1=maskr[:, :], op=ALU.mult), r=[Psk, "maskr"], w=[mskk])
                            mks.append((msk, mskk))
                        yield
                        Po, Pok = P[7], PK(7)
                        for ch in range(NCH):
                            b, par = ch // 2, ch % 2
                            sb_par = ch % 2
                            if per_chunk_state:
                                S.add("sp", OP("dma_start", out=Sst[:, hp, :], in_=stc[ch, hp, :, :]), w=[("Sst", hp)], dma=("Sst_in", hp))
                            if per_chunk_state or ch == 0:
                                S.add("dve", OP("tensor_copy", out=Sbf[:, hp, sb_par, :], in_=Sst[:, hp, :]), r=[("Sst", hp)], w=[("Sbf", hp, sb_par)])
                            fns = []
                            for hh in range(2):
                                h = hp * 2 + hh
                                oc = hh * NT + ch * 64
                                if par == 0:
                                    ob = hh * NT + b * 128
                                    fns.append(OP("matmul", Po[:, ob:ob + 128], gv_tm[:, b, h * 128:(h + 1) * 128], mks[hh][0][:, b * 128:(b + 1) * 128], start=(ch == 0 and hh == 0), stop=False, skip_group_check=True))
                                fns.append(OP("matmul", Po[:, oc:oc + 64], Sbf[:, hp, sb_par, :], qtls[hh][0][:, ch * 64:(ch + 1) * 64], start=False, stop=(par == 1), skip_group_check=True))
                            S.add("pe", seq(fns), r=[("gv_tm", b), mks[0][1], mks[1][1], ("Sbf", hp, sb_par), qtls[0][1], qtls[1][1]], w=[Pok])
                            Pu, Puk = gp()
                            state_update(hp, ch, Pu, Puk)
                            S.add("dve", OP("scalar_tensor_tensor", out=Sst[:, hp, :], in0=Sst[:, hp, :], scalar=dec[:, hp, ch:ch + 1], in1=Pu[:, 0:128], op0=ALU.mult, op1=ALU.add),
                                  r=[Puk, ("dec", hp), ("Sst", hp)], w=[("Sst", hp)])
                            if per_chunk_state:
                                S.add("pool", OP("dma_start", out=st_dst[ch, hp, :, :], in_=Sst[:, hp, :]), r=[("Sst", hp)], dma=("Sst_out", hp))
                            elif ch + 1 < NCH:
                                np_ = (ch + 1) % 2
                                S.add("dve", OP("tensor_copy", out=Sbf[:, hp, np_, :], in_=Sst[:, hp, :]), r=[("Sst", hp)], w=[("Sbf", hp, np_)])
                            yield
                        yield
                        for hh in range(2):
                            h = hp * 2 + hh
                            oc = hh * NT
                            sq, sqk = TBp.get()
                            S.add("act", (OP("activation", out=sq[:, :], in_=Po[:, oc:oc + NT], func=AF.Square)), r=[Pok], w=[sqk])
                            Pn, Pnk = gp()
                            S.add("pe", (OP("matmul", Pn[:, 0:NT], ones_bf[:, :], sq[:, :], start=True, stop=True)), r=[sqk, "ones_bf"], w=[Pnk])
                            rs, rsk = TFp.get()
                            S.add("act", (OP("activation", out=rs[:, :], in_=Pn[:, 0:NT], func=AF.Ln, scale=1.0 / 128, bias=EPS)), r=[Pnk], w=[rsk])
                            yield
                            S.add("act", (OP("activation", out=rs[:, :], in_=rs[:, :], func=AF.Exp, scale=-0.5)), r=[rsk], w=[rsk])
                            on, onk = TFp.get()
                            S.add("dve", (OP("tensor_tensor", out=on[:, :], in0=Po[:, oc:oc + NT], in1=rs[:, :], op=ALU.mult)), r=[Pok, rsk], w=[onk])
                            Pr, Prk = gp()
                            S.add("pe", seq([mm(Pr[:, 0:NT], win[:, kc, GR0 + h * 128:GR0 + (h + 1) * 128], hT[:, kc, :], start=(kc == 0), stop=(kc == 7)) for kc in range(8)]),
                                  r=HK("hT") + WIN, w=[Prk])
                            sg, sgk = TFp.get()
                            S.add("act", (OP("activation", out=sg[:, :], in_=Pr[:, 0:NT], func=AF.Exp, scale=-1.0)), r=[Prk], w=[sgk])
                            S.add("act", (OP("activation", out=sg[:, :], in_=sg[:, :], func=AF.Ln, bias=1.0)), r=[sgk], w=[sgk])
                            S.add("act", (OP("activation", out=sg[:, :], in_=sg[:, :], func=AF.Exp, scale=-1.0)), r=[sgk], w=[sgk])
                            S.add("dve", (OP("tensor_tensor", out=sg[:, :], in0=Pr[:, 0:NT], in1=sg[:, :], op=ALU.mult)), r=[Prk, sgk], w=[sgk])
                            S.add("dve", (OP("scalar_tensor_tensor", out=mixT[:, 4 + h, :], in0=on[:, :], scalar=ggo[:, h:h + 1], in1=sg[:, :], op0=ALU.mult, op1=ALU.mult)),
                                  r=[onk, sgk, "ggo"], w=[("mixT", 4 + h), "hTalt"])
                            yield

                def qpath(qcs_src, c0, TFp=None, TBp=None):
                    TFp = TFq if TFp is None else TFp
                    TBp = TBq if TBp is None else TBp
                    S.add("sp", OP("dma_start", out=qcs[:, :], in_=qcs_src[:, c0:c0 + NT]), w=["qcs"], dma="qcs")
                    sqs = []
                    for kc3 in range(3):
                        Pq, Pqk = gp()
                        S.add("pe", seq([mm(Pq[:, 0:NT], win[:, kc, CQ0 + kc3 * 128:CQ0 + (kc3 + 1) * 128], hT[:, kc, :], start=(kc == 0), stop=(kc == 7)) for kc in range(8)]),
                              r=HK("hT") + WIN, w=[Pqk])
                        S.add("act", (OP("activation", out=cqT[:, kc3, :], in_=Pq[:, 0:NT], func=AF.Copy)), r=[Pqk], w=[("cqT", kc3)])
                        sq, sqk = TBp.get()
                        S.add("act", (OP("activation", out=sq[:, :], in_=Pq[:, 0:NT], func=AF.Square)), r=[Pqk], w=[sqk])
                        sqs.append((sq, sqk))
                        yield
                    Pss, Pssk = gp()
                    S.add("pe", seq([mm(Pss[:, 0:NT], ones_bf[:, :], sqs[i][0][:, :], start=(i == 0), stop=(i == 2)) for i in range(3)]),
                          r=[s[1] for s in sqs] + ["ones_bf"], w=[Pssk])
                    epstk = "epst"
                    S.add("act", (OP("activation", out=epst[:, :], in_=Pss[:, 0:NT], func=AF.Identity, scale=EPS / 384.0, bias=EPS * EPS)), r=[Pssk], w=[epstk])
                    yield
                    for h in range(4):
                        Pn_, Pnk_ = gp()
                        S.add("pe", seq([mm(Pn_[:, 0:NT], wuq[:, kc3, h * 192:h * 192 + 128], cqT[:, kc3, :], start=(kc3 == 0), stop=(kc3 == 2)) for kc3 in range(3)]),
                              r=["wuq", ("cqT", 0), ("cqT", 1), ("cqT", 2)], w=[Pnk_])
                        Pab, Pabk = gp()
                        fns = [mm(Pab[0:64, 0:NT], wuq[:, kc3, h * 192 + 128:h * 192 + 192], cqT[:, kc3, :], start=(kc3 == 0), stop=(kc3 == 2)) for kc3 in range(3)]
                        fns += [mm(Pab[64:128, 0:NT], wuq[:, kc3, 768 + h * 64:768 + (h + 1) * 64], cqT[:, kc3, :], start=(kc3 == 0), stop=(kc3 == 2)) for kc3 in range(3)]
                        S.add("pe", seq(fns), r=["wuq", ("cqT", 0), ("cqT", 1), ("cqT", 2)], w=[Pabk])
                        s1, s1k = TBp.get()
                        s2, s2k = TBp.get()
                        S.add("act", (OP("activation", out=s1[:, :], in_=Pn_[:, 0:NT], func=AF.Square)), r=[Pnk_], w=[s1k])
                        S.add("act", (OP("activation", out=s2[0:64, :], in_=Pab[0:64, 0:NT], func=AF.Square)), r=[Pabk], w=[s2k])
                        Ph, Phk = gp()
                        S.add("pe", seq([mm(Ph[:, 0:NT], ones_bf[:, :], s1[:, :], start=True, stop=False), mm(Ph[:, 0:NT], ones_bf[0:64, :], s2[0:64, :], start=False, stop=True)]),
                              r=[s1k, s2k, "ones_bf"], w=[Phk])
                        rq, rqk = TFp.get()
                        S.add("dve", (OP("scalar_tensor_tensor", out=rq[:, :], in0=Ph[:, 0:NT], scalar=1.0 / 192, in1=epst[:, :], op0=ALU.mult, op1=ALU.add)), r=[Phk, epstk], w=[rqk])
                        S.add("act", (OP("activation", out=rq[:, :], in_=rq[:, :], func=AF.Ln)), r=[rqk], w=[rqk])
                        S.add("act", (OP("activation", out=rq[:, :], in_=rq[:, :], func=AF.Exp, scale=-0.5)), r=[rqk], w=[rqk])
                        S.add("dve", (OP("scalar_tensor_tensor", out=qn[:, h, :], in0=Pn_[:, 0:NT], scalar=gqk[:, 0:1], in1=rq[:, :], op0=ALU.mult, op1=ALU.mult)),
                              r=[Pnk_, rqk, "gqk"], w=[("qn", h)])
                        ab, abk = TFp.get()
                        S.add("dve", (OP("scalar_tensor_tensor", out=ab[:, :], in0=Pab[:, 0:NT], scalar=gqr_t[:, 0:1], in1=rq[:, :], op0=ALU.mult, op1=ALU.mult)),
                              r=[Pabk, rqk, "gqr_t"], w=[abk])
                        S.add("dve", (OP("tensor_tensor", out=ab[:, :], in0=ab[:, :], in1=qcs[:, :], op=ALU.mult)), r=[abk, "qcs"], w=[abk])
                        Pc_, Pck_ = gp()
                        S.add("pe", (OP("matmul", Pc_[:, 0:NT], comb[:, :], ab[:, :], start=True, stop=True)), r=[abk, "comb"], w=[Pck_])
                        S.add("act", (OP("activation", out=qr[0:64, h, 0, :], in_=Pc_[0:64, 0:NT], func=AF.Copy)), r=[Pck_], w=[("qr", h)])
                        S.add("act", (OP("activation", out=qr[64:128, h, 1, :], in_=Pc_[64:128, 0:NT], func=AF.Copy)), r=[Pck_], w=[("qr", h)])
                        yield

                rlbuf = psb("rlbuf", [128, NT])
                att = dict(cnt=0)

                def blk(h, kb, ncols, q0, first, last, bias_ap, zero_tri=False, zero_rows=None, finish=None):
                    return dict(h=h, kb=kb, ncols=ncols, q0=q0, first=first, last=last, bias=bias_ap, zero_tri=zero_tri, zero_rows=zero_rows, finish=finish)

                def emit_qk(B):
                    h, kb, ncols, q0 = B["h"], B["kb"], B["ncols"], B["q0"]
                    key0 = kb * 128
                    hb = 0 if key0 < 2048 else 64
                    kk0 = key0 % 2048
                    si = SCB[att["cnt"] % 3]
                    att["cnt"] += 1
                    Ps, Psk = P[si], PK(si)
                    B["Ps"], B["Psk"] = Ps, Psk
                    kt = key0 // NT
                    S.add("pe", seq([
                        mm(Ps[:, 0:ncols], Kn[:, h, key0:key0 + 128], qn[:, h, q0:q0 + ncols], start=True, stop=False),
                        mm(Ps[:, 0:ncols], Kr[:, kk0:kk0 + 128], qr[:, h, hb // 64, q0:q0 + ncols], start=False, stop=True)]),
                        r=[("Kn", kt, h), ("Kr", kk0 // NT), ("qn", h), ("qr", h)], w=[Psk])

                def emit_rest(B):
                    h, kb, ncols, q0 = B["h"], B["kb"], B["ncols"], B["q0"]
                    Ps, Psk = B["Ps"], B["Psk"]
                    kt = kb * 128 // NT
                    pt, ptk = PT.get()
                    if B["bias"] is None:
                        S.add("act", OP("activation", out=pt[:, 0:ncols], in_=Ps[:, 0:ncols], func=AF.Exp, scale=sclK[:, kb, h:h + 1]),
                              r=[Psk, ("sclK", kb)], w=[ptk])
                    else:
                        S.add("act", OP("activation", out=pt[:, 0:ncols], in_=Ps[:, 0:ncols], func=AF.Exp, scale=sclK[:, kb, h:h + 1], bias=B["bias"]),
                              r=[Psk, ("sclK", kb), "flg"], w=[ptk])
                    if B["zero_tri"]:
                        S.add("pool", OP("memset", pt[64:128, 0:64], 0.0), r=[ptk], w=[ptk])
                    if B["zero_rows"] is not None:
                        zr = B["zero_rows"]
                        S.add("pool", OP("memset", pt[zr[0]:zr[1], 0:ncols], 0.0), r=[ptk], w=[ptk])
                    ab_ = 2 + (h % 2)
                    Pa, Pak = P[ab_], PK(ab_)
                    S.add("pe", seq([
                        OP("matmul", Pa[:, q0:q0 + ncols], Vs[:, kb, h * 128:(h + 1) * 128], pt[:, 0:ncols], start=B["first"], stop=B["last"], skip_group_check=True),
                        OP("matmul", Pa[:, NT + q0:NT + q0 + ncols], ones_bf[:, :], pt[:, 0:ncols], start=False, stop=B["last"], skip_group_check=True)]),
                        r=[("Vs", kb), ptk, "ones_bf"], w=[Pak])
                    if B["finish"] is not None:
                        fh_, fq0, fn = B["finish"]
                        S.add("dve", OP("reciprocal", out=rlbuf[:, 0:fn], in_=Pa[:, NT + fq0:NT + fq0 + fn]), r=[Pak], w=["rlbuf"])
                        S.add("dve", OP("tensor_tensor", out=mixT[:, fh_, fq0:fq0 + fn], in0=Pa[:, fq0:fq0 + fn], in1=rlbuf[:, 0:fn], op=ALU.mult), r=[Pak, "rlbuf"], w=[("mixT", fh_), "hTalt"])

                def attn_run(blocks, hook=None):
                    set_gen(GEN_IN)
                    try:
                        _attn_run(blocks, hook)
                    finally:
                        set_gen(GEN_OUT)

                def _attn_run(blocks, hook=None):
                    n = len(blocks)
                    emit_qk(blocks[0])
                    if n > 1:
                        emit_qk(blocks[1])
                    for k in range(n):
                        if k + 2 < n:
                            emit_qk(blocks[k + 2])
                        emit_rest(blocks[k])
                        if hook is not None:
                            hook()

                def prompt_blocks(p):
                    out = []
                    for h in range(4):
                        nown = NB * p + NB
                        for kb in range(16):
                            out.append(blk(h, kb, NT, 0, kb == 0, False, flg[:, 1:2]))
                        for j in range(nown):
                            kb = 16 + j
                            dj = j - NB * p
                            if dj < 0:
                                out.append(blk(h, kb, NT, 0, False, False, None))
                            else:
                                lastb = (j == nown - 1)
                                out.append(blk(h, kb, NT - 128 * dj, 128 * dj, False, lastb, None, zero_tri=True, finish=((h, 0, NT) if lastb else None)))
                    return out

                def sample_blocks(i):
                    q0 = i * 64
                    par = i % 2
                    out = []
                    for h in range(4):
                        for kb in range(16):
                            out.append(blk(h, kb, 64, q0, kb == 0, False, None))
                        out.append(blk(h, 16 + i // 2, 64, q0, False, True, None, zero_rows=((1 - par) * 64, (1 - par) * 64 + 64), finish=(h, q0, 64)))
                    return out

                def back(xsrc, r0, x1row0, blocks_g1, TFp=None):
                    TFp = TFk if TFp is None else TFp
                    for b in range(NB):
                        if blocks_g1 is not None:
                            load_g1bc(blocks_g1[b])
                        S.add("sp", (OP("dma_start", out=xr[:, :], in_=xsrc[r0 + b * 128:r0 + (b + 1) * 128, :])), w=XTK, dma="xt")
                        for fh in range(2):
                            Po, Pok = gp()
                            S.add("pe", seq([mm(Po[:, :], mixT[:, k, b * 128:(b + 1) * 128], wout[:, k, fh * 512:(fh + 1) * 512], start=(k == 0), stop=(k == 7)) for k in range(8)]),
                                  r=[("mixT", k) for k in range(8)] + ["wout"], w=[Pok])
                            tt, ttk = TFp.get()
                            tt2, tt2k = TFp.get()
                            S.add("dve", (OP("tensor_tensor", out=tt[:, :], in0=Po[:, 0:256], in1=g1bc[:, fh * 512:fh * 512 + 256], op=ALU.mult)), r=[Pok, "g1bc"], w=[ttk])
                            S.add("dve", (OP("tensor_tensor", out=tt2[:, :], in0=Po[:, 256:512], in1=g1bc[:, fh * 512 + 256:fh * 512 + 512], op=ALU.mult)), r=[Pok, "g1bc"], w=[tt2k])
                            S.add("dve", (OP("tensor_tensor", out=xr[:, fh * 512:fh * 512 + 256], in0=xr[:, fh * 512:fh * 512 + 256], in1=tt[:, :], op=ALU.add)), r=[ttk, "xt"], w=["xt"])
                            S.add("dve", (OP("tensor_tensor", out=xr[:, fh * 512 + 256:fh * 512 + 512], in0=xr[:, fh * 512 + 256:fh * 512 + 512], in1=tt2[:, :], op=ALU.add)), r=[tt2k, "xt"], w=["xt"])
                            yield
                        S.add("pool", (OP("dma_start", out=x1s[x1row0 + b * 128:x1row0 + (b + 1) * 128, :], in_=xr[:, :])), r=["xt"], w=[("x1s", x1row0 // 128 + b)], dma="x1s_w")

                def run(g):
                    for _ in g:
                        pass

                def chain(*gens):
                    for g in gens:
                        yield from g

                def hook_of(g, every=1):
                    st = dict(n=0)

                    def hk():
                        st["n"] += 1
                        if st["n"] % every == 0:
                            next(g, None)
                    return hk

                def inter(gens):
                    active = dict(gens)
                    while active:
                        for name in list(active):
                            g = active.get(name)
                            if g is None:
                                continue
                            try:
                                tok = next(g)
                            except StopIteration:
                                del active[name]
                                continue
                            if isinstance(tok, str) and tok.startswith("need:"):
                                dep = tok[5:]
                                if dep in active:
                                    for _ in active[dep]:
                                        pass
                                    del active[dep]

                load_g1bc(0)
                NPRE = 2048 // NT
                hbufs = [(hT, "hT"), (mixT, "hTalt")]
                run(front(xpre, 0, [(0, NT, 0)], *hbufs[0]))
                for t in range(NPRE):
                    hb_, hk_ = hbufs[t % 2]
                    gens = dict(k=chain(kvproj(None, None, 0, hb_, hk_), lat_transpose(), kside(t * NT, kcs_pre, t * NB)),
                                g=gla_prefix(hb_, hk_, TFq))
                    if t + 1 < NPRE:
                        gens["f"] = front(xpre, (t + 1) * NT, [(0, NT, 0)], hbufs[(t + 1) % 2][0], hbufs[(t + 1) % 2][1], TFk)
                    inter(gens)
                    ck('pre%d' % t)
                for hp in range(2):
                    S.add("dve", (OP("tensor_scalar", out=Sst[:, hp, :], in0=Sst[:, hp, :], scalar1=flg[:, 0:1], scalar2=None, op0=ALU.mult)), r=[("Sst", hp), "flg"], w=[("Sst", hp)])
                NOWN = 2048 // NT

                def pre_own(p):
                    return chain(front(xown, p * NT, [(0, NT, 0)]), kvproj(lat_own, kr_own, p * NT), lat_transpose(), kside(2048 + p * NT, kcs_own, p * NB))
                run(pre_own(0))
                run(qpath(qcs_own, 0))
                for p in range(NOWN):
                    hk_chain = chain(gla_own(False, None), pre_own(p + 1) if p + 1 < NOWN else iter(()))
                    attn_run(prompt_blocks(p), hook_of(hk_chain, 1))
                    set_gen(GEN_IN)
                    run(hk_chain)
                    set_gen(GEN_OUT)
                    gens = dict(back=back(xown, p * NT, p * NT, None))
                    if p + 1 < NOWN:
                        gens["q"] = qpath(qcs_own, (p + 1) * NT)
                    inter(gens)
                    ck('own%d' % p)
                for hp in range(2):
                    S.add("pool", (OP("dma_start", out=st_p[hp, :, :], in_=Sst[:, hp, :])), r=[("Sst", hp)], dma=("st_p", hp))
                run(front(xsm, 0, [(i * 64, 64, 1 + i) for i in range(4)]))
                run(kvproj(lat_s, kr_s, 0))
                run(lat_transpose())
                run(kside(2048, kcs_sn, 0))
                inter(dict(q=qpath(qcs_s, 0), g=gla_own(True, st_s)))
                NPT = 2048 // NT
                BSS = [BS0, BS1]

                def stage_a(i, t, bs):
                    S.add("sp", OP("dma_start", out=bs["lat"], in_=latc[i, t * NT:(t + 1) * NT, :].rearrange("(b p) d -> p b d", p=128)), w=BK(bs["latk"]), dma=bs["latk"])
                    S.add("sp", OP("dma_start", out=bs["kr"], in_=krc[i, t * NT:(t + 1) * NT, :].rearrange("(b p) d -> p b d", p=128)), w=BK(bs["krk"]), dma=bs["krk"])
                    S.add("sp", OP("dma_start", out=bs["kcs"], in_=kcs_pre[:, t * NB:(t + 1) * NB, :]), w=[bs["kcsk"]], dma=bs["kcsk"])
                    yield
                    yield from lat_transpose(bs)

                tiles = [(i, t) for i in range(4) for t in range(NPT)]
                run(stage_a(0, 0, BSS[0]))
                for n, (i, t) in enumerate(tiles):
                    bs = BSS[n % 2]
                    nxtA = stage_a(tiles[n + 1][0], tiles[n + 1][1], BSS[(n + 1) % 2]) if n + 1 < len(tiles) else iter(())
                    if t < NPT - 1:
                        inter(dict(b=kside(t * NT, kcs_pre, t * NB, bs, None, False), a=nxtA))
                    else:
                        run(kside(t * NT, kcs_pre, t * NB, bs, None, False))
                        attn_run(sample_blocks(i), hook_of(nxtA, 4))
                        run(nxtA)
                run(back(xsm, 0, 2048, [1, 2]))
                ck('samp')
            S.barrier()

            with contextlib.ExitStack() as ph2:
                def msb(name, shape, dt=F32):
                    return ph2.enter_context(nc.sbuf_tensor(name, list(shape), dt))
                wup = msb("wup", [128, 8, 4096], BF16)
                wdn = msb("wdn", [128, 32, 1024], BF16)
                g2bc = msb("g2bc", [128, 1024])
                x1t = [msb("x1t%d" % i, [128, NB, 1024]) for i in range(2)]
                xw = msb("xw", [128, 1024])
                h2T = [msb("h2T%d" % i, [128, 8, NT], BF16) for i in range(2)]
                uT = msb("uT", [128, 32, NT], BF16)
                st2 = msb("st2", [128, 8])
                junk2 = msb("junk2", [128, 1024], BF16)
                RFb = [msb("RF%d" % i, [128, NT]) for i in range(6)]
                RFa = Rot([(t, t.name) for t in RFb[0:2]])
                RF = Rot([(t, t.name) for t in RFb[2:6]])
                gen8 = Rot(list(range(8)))

                def gp8():
                    i = gen8.get()
                    return P[i], PK(i)

                for jb in range(8):
                    S.add("pool", OP("dma_start", out=wup[:, :, jb * 512:(jb + 1) * 512], in_=w_up[:, jb * 512:(jb + 1) * 512].rearrange("(kc p) n -> p kc n", p=128)),
                          w=[("wup", jb)], dma=("wup", jb))
                for j4 in range(8):
                    S.add("pool", OP("dma_start", out=wdn[:, j4 * 4:(j4 + 1) * 4, :], in_=w_down[j4 * 512:(j4 + 1) * 512, :].rearrange("(j p) n -> p j n", p=128)),
                          w=[("wdn", j4)], dma=("wdn", j4))
                WDN = [("wdn", j4) for j4 in range(8)]

                def load_g2bc(v):
                    for fh in range(2):
                        Pg, Pgk = gp8()
                        S.add("pe", OP("matmul", Pg[:, :], sel[:, v, :], gater[:, 1, fh * 512:(fh + 1) * 512], start=True, stop=True),
                              r=["sel", "gater"], w=[Pgk])
                        S.add("act", OP("activation", out=g2bc[:, fh * 512:(fh + 1) * 512], in_=Pg[:, :], func=AF.Copy), r=[Pgk], w=["g2bc"])

                def mlp_front(ti, row0, segs):
                    xb, hb = x1t[ti % 2], h2T[ti % 2]
                    xk, hk = "x1t%d" % (ti % 2), "h2T%d" % (ti % 2)
                    for b in range(NB):
                        S.add("sp", OP("dma_start", out=xb[:, b, :], in_=x1s[row0 + b * 128:row0 + (b + 1) * 128, :]), r=[("x1s", row0 // 128 + b)], w=[(xk, b)], dma=(xk, b))
                        S.add("act", OP("activation", out=junk2[:, :], in_=xb[:, b, :], func=AF.Square, accum_out=st2[:, 4:5]), r=[(xk, b)], w=["junk2", "st2"])
                        S.add("act", act(st2[:, 5:6], st2[:, 4:5], AF.Ln, scale=1.0 / 1024, bias=EPS), r=["st2"], w=["st2"])
                        S.add("act", act(st2[:, 6:7], st2[:, 5:6], AF.Exp, scale=-0.5), r=["st2"], w=["st2"])
                        S.add("dve", OP("tensor_scalar", out=xw[:, :], in0=xb[:, b, :], scalar1=st2[:, 6:7], scalar2=None, op0=ALU.mult), r=[(xk, b), "st2"], w=["xw"])
                        yield
                        for k2 in range(2):
                            Pt, Ptk = gp8()
                            S.add("pe", seq([OP("transpose", Pt[:, kk * 128:(kk + 1) * 128], xw[:, (k2 * 4 + kk) * 128:(k2 * 4 + kk + 1) * 128], ident[:, :]) for kk in range(4)]),
                                  r=["xw", "ident"], w=[Ptk])
                            for kk in range(4):
                                kc = k2 * 4 + kk
                                for (c0, ncol, m) in segs:
                                    lo = max(c0, b * 128)
                                    hi = min(c0 + ncol, (b + 1) * 128)
                                    if lo >= hi:
                                        continue
                                    S.add("act", OP("activation", out=hb[:, kc, lo:hi], in_=Pt[:, kk * 128 + lo - b * 128:kk * 128 + hi - b * 128], func=AF.Identity,
                                                    scale=gm2[:, kc, m:m + 1], bias=sh2[:, kc, m:m + 1]), r=[Ptk, "gm2", "sh2"], w=[(hk, kc, b)])
                            yield

                def mlp_main(ti, ydst, yrow0, blocks_g2, hook):
                    xb, hb = x1t[ti % 2], h2T[ti % 2]
                    xk, hk = "x1t%d" % (ti % 2), "h2T%d" % (ti % 2)
                    for j in range(32):
                        Pu, Puk = gp8()
                        S.add("pe", seq([mm(Pu[:, 0:NT], wup[:, kc, j * 128:(j + 1) * 128], hb[:, kc, :], start=(kc == 0), stop=(kc == 7)) for kc in range(8)]),
                              r=[(hk, kc_, b_) for kc_ in range(8) for b_ in range(NB)] + [("wup", j // 4)], w=[Puk])
                        rt, rtk = RF.get()
                        S.add("act", OP("activation", out=rt[:, :], in_=Pu[:, 0:NT], func=AF.Relu), r=[Puk], w=[rtk])
                        S.add("dve", OP("tensor_tensor", out=uT[:, j, :], in0=Pu[:, 0:NT], in1=rt[:, :], op=ALU.mult), r=[Puk, rtk], w=[("uT", j)])
                        if hook is not None and j % 2 == 1:
                            hook()
                    UT = [("uT", j) for j in range(32)]
                    for b in range(NB):
                        if blocks_g2 is not None:
                            load_g2bc(blocks_g2[b])
                        for fh in range(2):
                            Po, Pok = gp8()
                            S.add("pe", seq([mm(Po[:, :], uT[:, j, b * 128:(b + 1) * 128], wdn[:, j, fh * 512:(fh + 1) * 512], start=(j == 0), stop=(j == 31)) for j in range(32)]),
                                  r=UT + WDN, w=[Pok])
                            for q2 in range(2):
                                tt, ttk = RF.get()
                                c0 = fh * 512 + q2 * 256
                                S.add("dve", OP("tensor_tensor", out=tt[:, :], in0=Po[:, q2 * 256:(q2 + 1) * 256], in1=g2bc[:, c0:c0 + 256], op=ALU.mult), r=[Pok, "g2bc"], w=[ttk])
                                S.add("dve", OP("tensor_tensor", out=xb[:, b, c0:c0 + 256], in0=xb[:, b, c0:c0 + 256], in1=tt[:, :], op=ALU.add), r=[ttk, (xk, b)], w=[(xk, b)])
                        S.add("pool", OP("dma_start", out=ydst[yrow0 + b * 128:yrow0 + (b + 1) * 128, :], in_=xb[:, b, :]), r=[(xk, b)], dma=("y_o", ti % 2, b))

                load_g2bc(0)
                NMT = 2048 // NT
                tiles2 = [(p * NT, y_own, p * NT, [(0, NT, 0)], None) for p in range(NMT)] + [(2048, y_s, 0, [(i * 64, 64, 1 + i) for i in range(4)], [1, 2])]
                for _ in mlp_front(0, tiles2[0][0], tiles2[0][3]):
                    pass
                for ti, (row0, ydst, yrow0, segs, bg2) in enumerate(tiles2):
                    if ti + 1 < len(tiles2):
                        nx = mlp_front(ti + 1, tiles2[ti + 1][0], tiles2[ti + 1][3])
                    else:
                        nx = iter(())
                    mlp_main(ti, ydst, yrow0, bg2, (lambda nx=nx: next(nx, None)))
                    for _ in nx:
                        pass

        except _Stop:
            pass
        S.emit()
    return nc


def _rope_tables(pos):
    half = 32
    inv = np.power(np.float32(10000.0), -np.arange(half, dtype=np.float32) / np.float32(half)).astype(np.float32)
    ang = pos.astype(np.float32)[:, None] * inv[None, :]
    return np.cos(ang).astype(np.float32), np.sin(ang).astype(np.float32)


def _kcs(pos):
    c, s = _rope_tables(pos)
    t = np.concatenate([c, c, -s, s], axis=1)
    n = pos.shape[0]
    return np.ascontiguousarray(t.reshape(n // 128, 128, 128).transpose(1, 0, 2))


def _qcs(pos):
    c, s = _rope_tables(pos)
    return np.ascontiguousarray(np.concatenate([c.T, c.T, s.T, s.T], axis=0))


_NC_CACHE = {}


def _prep(x_prompt, x_sample, cache_mla_latent, cache_mla_krope, state_gla, c_prompt, c_sample,
           w_ada, b_ada, g_norm1, w_in, g_q_lora, w_uq, g_kv_lora, w_ukv, g_q_head, g_k_head,
           w_gate_up, b_gate_up, g_gla_out, w_out, g_norm2, w_up, w_down):
    f = lambda a: np.ascontiguousarray(np.asarray(a, dtype=np.float32))
    x_prompt, x_sample = f(x_prompt), f(x_sample)
    latc_all, krc_all, st_all = f(cache_mla_latent)[0], f(cache_mla_krope)[0], f(state_gla)[0]
    c_prompt, c_sample = f(c_prompt), f(c_sample)
    w_ada, b_ada, g_norm1, w_in = f(w_ada)[0], f(b_ada)[0], f(g_norm1)[0], f(w_in)[0]
    g_q_lora, w_uq, g_kv_lora, w_ukv = f(g_q_lora)[0], f(w_uq)[0], f(g_kv_lora)[0], f(w_ukv)[0]
    g_q_head, g_k_head = f(g_q_head)[0], f(g_k_head)[0]
    w_gate_up, b_gate_up, g_gla_out = f(w_gate_up)[0], f(b_gate_up)[0], f(g_gla_out)[0]
    w_out, g_norm2, w_up, w_down = f(w_out)[0], f(g_norm2)[0], f(w_up)[0], f(w_down)[0]

    rot_cols = []
    for h in range(4):
        base = h * 192 + 128
        rot_cols += list(range(base + 32, base + 64)) + list(range(base, base + 32))
    w_uq_ext = np.ascontiguousarray(np.concatenate([w_uq, w_uq[:, rot_cols]], axis=1))
    kn_cols, v_cols = [], []
    for h in range(4):
        kn_cols += list(range(h * 256, h * 256 + 128))
        v_cols += list(range(h * 256 + 128, h * 256 + 256))
    w_ukv_p = np.ascontiguousarray(w_ukv[:, kn_cols + v_cols])
    w_gu_aug = np.zeros((32, 256), np.float32)
    w_gu_aug[0:16] = w_gate_up
    w_gu_aug[16] = b_gate_up
    colT = lambda v, n: np.ascontiguousarray(v.reshape(n, 128).T)
    gqr_col = np.concatenate([g_q_head[128:192], np.roll(g_q_head[128:192], -32)])[:, None]
    ident = np.eye(128, dtype=np.float32)
    ii = np.arange(128)
    same = (ii[:, None] // 64) == (ii[None, :] // 64)
    tri = np.where(same & (ii[:, None] <= ii[None, :]), np.float32(-1.0 / 16), np.float32(0)).astype(np.float32)
    utm = np.where(same & (ii[:, None] > ii[None, :]), np.float32(-1.0 / 16), np.float32(0)).astype(np.float32)
    mblk = (same & (ii[:, None] <= ii[None, :])).astype(np.float32)
    maskr = np.ascontiguousarray(np.tile(mblk, (1, NB)))
    hsel = np.zeros((128, 2), np.float32)
    hsel[0:64, 0] = 0.125
    hsel[64:128, 1] = 0.125
    sel = np.zeros((5, 3, 128), np.float32)
    sel[0, 0, :] = 1
    sel[1, 1, 0:64] = 1
    sel[2, 1, 64:128] = 1
    sel[3, 2, 0:64] = 1
    sel[4, 2, 64:128] = 1
    comb = ((ii[:, None] % 64) == (ii[None, :] % 64)).astype(np.float32)
    kcs_pre = _kcs(np.arange(2048))
    kcs_sn = _kcs(2048 + (np.arange(256) % 64))
    qcs_s = _qcs(2048 + (np.arange(256) % 64))
    shared = dict(
        w_ada=w_ada, b_ada_row=b_ada[None, :], b_adaT=colT(b_ada, 48),
        g1T=colT(g_norm1, 8), g2T=colT(g_norm2, 8), gqlT=colT(g_q_lora, 3),
        gqn=np.ascontiguousarray(g_q_head[0:128, None]), gkn=np.ascontiguousarray(g_k_head[0:128, None]),
        gqr=np.ascontiguousarray(gqr_col), gkr_row=np.ascontiguousarray(g_k_head[None, 128:192]),
        gkv_row=g_kv_lora[None, :], ggoT=colT(g_gla_out, 4),
        w_in=w_in, w_uq_ext=w_uq_ext, w_ukv_p=w_ukv_p, w_gu_aug=w_gu_aug, w_out=w_out, w_up=w_up, w_down=w_down,
        kcs_pre=kcs_pre, kcs_sn=kcs_sn, qcs_s=qcs_s,
        c_ident=ident, c_tri=tri, c_ut=utm, c_mask=maskr, c_sel=sel, c_comb=comb, c_hsel=hsel,
    )
    in_maps = []
    for c in range(8):
        pb, half = c // 2, c % 2
        pos_own = half * 2048 + np.arange(2048)
        flag = np.zeros((128, 2), np.float32)
        flag[:, 0] = float(half)
        flag[:, 1] = 0.0 if half == 1 else NEG
        cvec = np.concatenate([c_prompt[pb:pb + 1], c_sample[4 * c:4 * c + 4]], axis=0)
        m = dict(shared)
        m.update(
            xpre=np.ascontiguousarray(x_prompt[pb, 0:2048]),
            xown=np.ascontiguousarray(x_prompt[pb, half * 2048:(half + 1) * 2048]),
            xsm=np.ascontiguousarray(x_sample[4 * c:4 * c + 4].reshape(256, 1024)),
            latc=np.ascontiguousarray(latc_all[4 * c:4 * c + 4]),
            krc=np.ascontiguousarray(krc_all[4 * c:4 * c + 4]),
            stc=np.ascontiguousarray(st_all[4 * c:4 * c + 4].reshape(4, 2, 128, 128)),
            cT=np.ascontiguousarray(cvec.T), flag=flag,
            kcs_own=_kcs(pos_own), qcs_own=_qcs(pos_own),
        )
        in_maps.append(m)

    return in_maps


def kernel(**inputs):
    in_maps = _prep(**inputs)
    if "nc" not in _NC_CACHE:
        _NC_CACHE["nc"] = build()
    res = run_bass_kernel_spmd(_NC_CACHE["nc"], in_maps, core_ids=list(range(8)))
    return _assemble(res.results)


def _assemble(R):
    y_p = np.zeros((4, 4096, 1024), np.float32)
    lat_p = np.zeros((1, 4, 4096, 256), np.float32)
    kr_p = np.zeros((1, 4, 4096, 64), np.float32)
    st_pp = np.zeros((1, 4, 4, 64, 128), np.float32)
    y_s = np.zeros((32, 64, 1024), np.float32)
    lat_s = np.zeros((1, 32, 64, 256), np.float32)
    kr_s = np.zeros((1, 32, 64, 64), np.float32)
    st_s = np.zeros((1, 32, 4, 64, 128), np.float32)
    for c in range(8):
        pb, half = c // 2, c % 2
        sl = slice(half * 2048, (half + 1) * 2048)
        y_p[pb, sl] = R[c]["y_own"]
        lat_p[0, pb, sl] = R[c]["lat_own"]
        kr_p[0, pb, sl] = R[c]["kr_own"]
        if half == 1:
            st_pp[0, pb] = R[c]["st_p"].reshape(4, 64, 128)
        y_s[4 * c:4 * c + 4] = R[c]["y_s"].reshape(4, 64, 1024)
        lat_s[0, 4 * c:4 * c + 4] = R[c]["lat_s"].reshape(4, 64, 256)
        kr_s[0, 4 * c:4 * c + 4] = R[c]["kr_s"].reshape(4, 64, 64)
        st_s[0, 4 * c:4 * c + 4] = R[c]["st_s"].reshape(4, 4, 64, 128)
    return (y_p, y_s, lat_p, kr_p, st_pp, lat_s, kr_s, st_s)
```

```python
import contextlib
import numpy as np
import concourse.bass as bass
import concourse.mybir as mybir
from concourse.bass_utils import run_bass_kernel_spmd

F32 = mybir.dt.float32
BF16 = mybir.dt.bfloat16
AF = mybir.ActivationFunctionType
ALU = mybir.AluOpType
EPS = 1e-6
NT = 256
NB = NT // 128
NCH = NT // 64
NEG = -30000.0


class Sched:
    ENGS = ("pe", "act", "dve", "pool", "sp")

    def __init__(self, nc):
        self.nc = nc
        self.ops = {e: [] for e in self.ENGS}
        self.last_w = {}
        self.readers = {}
        self.dma_cnt = {}
        self.pending = {e: None for e in self.ENGS}
        self.stopped = False

    def barrier(self):
        if self.stopped:
            return
        toks = []
        for e in self.ENGS:
            for i in range(len(self.ops[e]) - 1, -1, -1):
                if self.ops[e][i]["dma"] is None:
                    toks.append(("c", e, i))
                    break
        for k, c in self.dma_cnt.items():
            toks.append(("d", k, c))
        for e in self.ENGS:
            self.pending[e] = list(toks)

    def add(self, eng, fn, r=(), w=(), dma=None):
        if self.stopped:
            return None
        ops = self.ops[eng]
        idx = len(ops)
        deps = set()
        for k in r:
            t = self.last_w.get(k)
            if t is not None:
                deps.add(t)
        for k in w:
            t = self.last_w.get(k)
            if t is not None:
                deps.add(t)
            for t in self.readers.get(k, ()):
                deps.add(t)
        if self.pending[eng] is not None:
            deps.update(self.pending[eng])
            self.pending[eng] = None
        if dma is not None:
            c = self.dma_cnt.get(dma, 0) + 16
            self.dma_cnt[dma] = c
            tok = ("d", dma, c)
        else:
            tok = ("c", eng, idx)
        fdeps = []
        for t in deps:
            if t[0] == "c":
                if t[1] == eng and eng == "pe":
                    continue
                if t[1] == eng and t[2] == idx:
                    continue
                self.ops[t[1]][t[2]]["sig"] = True
            fdeps.append(t)
        ops.append(dict(fn=fn, deps=fdeps, sig=False, dma=dma))
        for k in w:
            self.last_w[k] = tok
            self.readers[k] = []
        for k in r:
            self.readers.setdefault(k, []).append(tok)
        return tok

    def emit(self, final_eng="sp"):
        nc = self.nc
        with contextlib.ExitStack() as es:
            esem = {e: es.enter_context(nc.semaphore("s_" + e)) for e in self.ENGS}
            dsem = {}
            for i, k in enumerate(self.dma_cnt):
                dsem[k] = es.enter_context(nc.semaphore("d%d" % i))
            sigidx = {}
            for e in self.ENGS:
                c = 0
                arr = []
                for op in self.ops[e]:
                    if op["sig"]:
                        c += 1
                    arr.append(c)
                sigidx[e] = arr
            block = es.enter_context(nc.Block())

            def run(e, h):
                waited = {}
                for op in self.ops[e]:
                    for t in op["deps"]:
                        if t[0] == "c":
                            key = ("c", t[1])
                            val = sigidx[t[1]][t[2]]
                            sem = esem[t[1]]
                        else:
                            key = ("d", t[1])
                            val = t[2]
                            sem = dsem[t[1]]
                        if waited.get(key, 0) >= val:
                            continue
                        waited[key] = val
                        h.wait_ge(sem, val)
                    ins = op["fn"](h)
                    if op["dma"] is not None:
                        ins.then_inc(dsem[op["dma"]], 16)
                    elif op["sig"]:
                        ins.then_inc(esem[e], 1)
                if e == final_eng:
                    for k, c in self.dma_cnt.items():
                        if waited.get(("d", k), 0) < c:
                            h.wait_ge(dsem[k], c)

            @block.tensor
            def _(h):
                run("pe", h)

            @block.scalar
            def _(h):
                run("act", h)

            @block.vector
            def _(h):
                run("dve", h)

            @block.gpsimd
            def _(h):
                run("pool", h)

            @block.sync
            def _(h):
                run("sp", h)


class Rot:
    def __init__(self, items):
        self.items = items
        self.i = 0

    def get(self):
        it = self.items[self.i % len(self.items)]
        self.i += 1
        return it


def OP(meth, *a, **kw):
    return lambda e: getattr(e, meth)(*a, **kw)


def seq(fns):
    def f(e):
        ins = None
        for g in fns:
            ins = g(e)
        return ins
    return f


def mm(out, lhsT, rhs, start=True, stop=True):
    return OP("matmul", out, lhsT, rhs, start=start, stop=stop)


def act(out, in_, func, **kw):
    return OP("activation", out=out, in_=in_, func=func, **kw)


CQ0, CKV0, KR0, GQ0, GK0, GV0, GLR0, GR0 = 0, 384, 640, 704, 960, 1216, 1728, 1744


class _Stop(Exception):
    pass


def build(dbg=False, stop=None):
    def ck(tag):
        if stop == tag:
            S.stopped = True
    nc = bass.Bass("TRN2", target_bir_lowering=False)
    S = Sched(nc)

    def din(name, shape):
        return nc.dram_tensor(name, list(shape), F32, kind="ExternalInput").ap()

    def dout(name, shape):
        return nc.dram_tensor(name, list(shape), F32, kind="ExternalOutput").ap()

    xpre = din("xpre", [2048, 1024])
    xown = din("xown", [2048, 1024])
    xsm = din("xsm", [256, 1024])
    latc = din("latc", [4, 2048, 256])
    krc = din("krc", [4, 2048, 64])
    stc = din("stc", [4, 2, 128, 128])
    cT = din("cT", [1024, 5])
    flag = din("flag", [128, 2])
    w_ada = din("w_ada", [1024, 6144])
    b_ada_row = din("b_ada_row", [1, 6144])
    b_adaT = din("b_adaT", [128, 48])
    g1T = din("g1T", [128, 8])
    g2T = din("g2T", [128, 8])
    gqlT = din("gqlT", [128, 3])
    gqn = din("gqn", [128, 1])
    gkn = din("gkn", [128, 1])
    gqr = din("gqr", [128, 1])
    gkr_row = din("gkr_row", [1, 64])
    gkv_row = din("gkv_row", [1, 256])
    ggoT = din("ggoT", [128, 4])
    w_in = din("w_in", [1024, 2256])
    w_uq_ext = din("w_uq_ext", [384, 1024])
    w_ukv_p = din("w_ukv_p", [256, 1024])
    w_gu_aug = din("w_gu_aug", [32, 256])
    w_out = din("w_out", [1024, 1024])
    w_up = din("w_up", [1024, 4096])
    w_down = din("w_down", [4096, 1024])
    kcs_pre = din("kcs_pre", [128, 16, 128])
    kcs_own = din("kcs_own", [128, 16, 128])
    kcs_sn = din("kcs_sn", [128, 2, 128])
    qcs_own = din("qcs_own", [128, 2048])
    qcs_s = din("qcs_s", [128, 256])
    c_ident = din("c_ident", [128, 128])
    c_tri = din("c_tri", [128, 128])
    c_ut = din("c_ut", [128, 128])
    c_mask = din("c_mask", [128, 256])
    c_sel = din("c_sel", [5, 3, 128])
    c_comb = din("c_comb", [128, 128])
    c_hsel = din("c_hsel", [128, 2])

    y_own = dout("y_own", [2048, 1024])
    y_s = dout("y_s", [256, 1024])
    lat_own = dout("lat_own", [2048, 256])
    kr_own = dout("kr_own", [2048, 64])
    st_p = dout("st_p", [2, 128, 128])
    lat_s = dout("lat_s", [256, 256])
    kr_s = dout("kr_s", [256, 64])
    st_s = dout("st_s", [4, 2, 128, 128])
    x1s = nc.dram_tensor("x1s", [2304, 1024], F32).ap()

    P = [nc.alloc_psum_tensor("P%d" % i, [128, 512], F32) for i in range(8)]

    def PK(i):
        return "P%d" % i

    with contextlib.ExitStack() as glob:
        def gsb(name, shape, dt=F32):
            return glob.enter_context(nc.sbuf_tensor(name, list(shape), dt))

        ident = gsb("ident", [128, 128])
        tri = gsb("tri", [128, 128])
        ut = gsb("ut", [128, 128])
        maskr = gsb("maskr", [128, 256])
        sel = gsb("sel", [5, 3, 128])
        comb = gsb("comb", [128, 128])
        flg = gsb("flg", [128, 2])
        hsel = gsb("hsel", [128, 2])
        ones_bf = gsb("ones_bf", [128, 128], BF16)
        ones_f = gsb("ones_f", [1, 128])
        gm1 = gsb("gm1", [128, 8, 5])
        sh1 = gsb("sh1", [128, 8, 5])
        gm2 = gsb("gm2", [128, 8, 5])
        sh2 = gsb("sh2", [128, 8, 5])
        gater = gsb("gater", [5, 2, 1024])
        g2c = gsb("g2c", [128, 8])
        for (t, d) in ((ident, c_ident), (tri, c_tri), (ut, c_ut), (maskr, c_mask), (comb, c_comb), (flg, flag), (g2c, g2T), (hsel, c_hsel)):
            S.add("sp", (OP("dma_start", out=t[:, :], in_=d[:, :])), w=[t.name], dma=t.name)
        S.add("sp", OP("dma_start", out=sel[:, :, :], in_=c_sel[:, :, :]), w=["sel"], dma="sel")
        S.add("dve", OP("memset", ones_bf[:, :], 1.0), w=["ones_bf"])
        S.add("dve", OP("memset", ones_f[:, :], 1.0), w=["ones_f"])

        GEN_OUT = [5, 6, 0, 1, 4]
        GEN_IN = [5, 6]
        gen4 = Rot(list(GEN_OUT))
        SCB = [0, 1, 4]

        def set_gen(lst):
            gen4.items = list(lst)
            gen4.i = 0

        def gp():
            i = gen4.get()
            return P[i], PK(i)

        try:
            with contextlib.ExitStack() as ph1:
                def psb(name, shape, dt=F32):
                    return ph1.enter_context(nc.sbuf_tensor(name, list(shape), dt))

                win = psb("win", [128, 8, 2256], BF16)
                wuq = psb("wuq", [128, 3, 1024], BF16)
                wukv = psb("wukv", [128, 2, 1024], BF16)
                wout = psb("wout", [128, 8, 1024], BF16)
                wgu = psb("wgu", [32, 256], BF16)
                g1c = psb("g1c", [128, 8])
                gqk = psb("gqk", [128, 1])
                gkn_t = psb("gkn_t", [128, 1])
                gqr_t = psb("gqr_t", [128, 1])
                ggo = psb("ggo", [128, 4])
                gkr_bc = psb("gkr_bc", [128, NB, 64])
                gkv_bc = psb("gkv_bc", [128, 256])
                g1bc = psb("g1bc", [128, 1024])
                zcol = psb("zcol", [128, 1])
                glrT = psb("glrT", [32, NT], BF16)
                Sst = psb("Sst", [128, 2, 128])
                S.add("dve", OP("memset", zcol[:, :], 0.0), w=["zcol"])
                S.add("dve", OP("memset", glrT[:, :], 1.0), w=["glrT"])
                S.add("dve", OP("memset", Sst[:, :, :], 0.0), w=[("Sst", 0), ("Sst", 1)])
                for (t, d) in ((g1c, g1T), (gkn_t, gkn), (gqr_t, gqr), (ggo, ggoT)):
                    S.add("sp", (OP("dma_start", out=t[:, :], in_=d[:, :])), w=[t.name], dma=t.name)
                S.add("sp", OP("dma_start", out=gqk[:, :], in_=gqn[:, :]), w=["gqk"], dma="gqk")
                S.add("dve", OP("tensor_tensor", out=gqk[:, :], in0=gqk[:, :], in1=gkn_t[:, :], op=ALU.mult), r=["gkn_t", "gqk"], w=["gqk"])

                with contextlib.ExitStack() as ph0:
                    def zsb(name, shape, dt=F32):
                        return ph0.enter_context(nc.sbuf_tensor(name, list(shape), dt))
                    wa = [zsb("wa%d" % i, [128, 8, 512], BF16) for i in range(2)]
                    wuq_f = zsb("wuq_f", [128, 3, 1024])
                    ctile = zsb("ctile", [128, 8, 5])
                    c_e = zsb("c_e", [128, 8, 5])
                    siluT = zsb("siluT", [128, 8, 5], BF16)
                    badT = zsb("badT", [128, 48])
                    badr = zsb("badr", [1, 6144])
                    badr_bf = zsb("badr_bf", [1, 6144], BF16)
                    ones5 = zsb("ones5", [1, 8], BF16)
                    gql = zsb("gql", [128, 3])
                    rowt = zsb("rowt", [1, 320])
                    modT = zsb("modT", [128, 32, 5])

                    S.add("sp", OP("dma_start", out=ctile[:, :, :], in_=cT.rearrange("(kc p) b -> p kc b", p=128)), w=["ctile"], dma="ctile")
                    S.add("sp", OP("dma_start", out=badT[:, :], in_=b_adaT[:, :]), w=["badT"], dma="badT")
                    S.add("sp", OP("dma_start", out=badr[:, :], in_=b_ada_row[:, :]), w=["badr"], dma="badr")
                    S.add("sp", OP("dma_start", out=gql[:, :], in_=gqlT[:, :]), w=["gql"], dma="gql")
                    S.add("sp", OP("dma_start", out=rowt[:, 0:256], in_=gkv_row[:, :]), w=["rowt0"], dma="rowt0")
                    S.add("sp", OP("dma_start", out=rowt[:, 256:320], in_=gkr_row[:, :]), w=["rowt1"], dma="rowt1")
                    S.add("sp", OP("dma_start", out=wuq_f[:, :, :], in_=w_uq_ext.rearrange("(kc p) n -> p kc n", p=128)), w=["wuq_f"], dma="wuq_f")
                    S.add("dve", OP("memset", ones5[:, :], 1.0), w=["ones5"])
                    S.add("dve", OP("tensor_copy", out=badr_bf[:, :], in_=badr[:, :]), r=["badr"], w=["badr_bf"])
                    S.add("act", act(c_e[:, :, :], ctile[:, :, :], AF.Exp, scale=-1.0), r=["ctile"], w=["c_e"])
                    S.add("dve", OP("tensor_scalar", out=c_e[:, :, :], in0=c_e[:, :, :], scalar1=1.0, scalar2=None, op0=ALU.add), r=["c_e"], w=["c_e"])
                    S.add("dve", OP("reciprocal", out=c_e[:, :, :], in_=c_e[:, :, :]), r=["c_e"], w=["c_e"])
                    S.add("dve", OP("tensor_tensor", out=siluT[:, :, :], in0=ctile[:, :, :], in1=c_e[:, :, :], op=ALU.mult), r=["c_e", "ctile"], w=["siluT"])

                    def wa_load(ct, slot):
                        S.add("pool", OP("dma_start", out=wa[slot][:, :, :], in_=w_ada[:, ct * 512:(ct + 1) * 512].rearrange("(kc p) n -> p kc n", p=128)),
                              w=["wa%d" % slot], dma="wa%d" % slot)
                    order = [0, 1, 2, 3, 4, 5, 6, 7, 8, 9, 10, 11]
                    wa_load(order[0], 0)
                    fidx = {}
                    fi = 0
                    for ct in order:
                        if ct in (0, 1, 2, 3, 6, 7, 8, 9):
                            for cc in range(4):
                                fidx[(ct, cc)] = fi
                                fi += 1
                    PF, PFk = P[4], PK(4)
                    for n, ct in enumerate(order):
                        slot = n % 2
                        if n + 1 < len(order):
                            wa_load(order[n + 1], (n + 1) % 2)
                        if ct in (0, 1, 2, 3, 6, 7, 8, 9):
                            fns = []
                            for cc in range(4):
                                f = fidx[(ct, cc)]
                                for kc in range(8):
                                    fns.append(mm(PF[:, f * 8:f * 8 + 5], wa[slot][:, kc, cc * 128:(cc + 1) * 128], siluT[:, kc, :], start=(kc == 0), stop=(kc == 7)))
                            S.add("pe", seq(fns), r=["wa%d" % slot, "siluT"], w=[PFk])
                        else:
                            gi = 0 if ct in (4, 5) else 1
                            hf = ct % 2
                            Pr, Prk = P[5 + (n % 2)], PK(5 + (n % 2))
                            fns = [mm(Pr[0:5, :], siluT[:, kc, :], wa[slot][:, kc, :], start=(kc == 0), stop=False) for kc in range(8)]
                            fns.append(mm(Pr[0:5, :], ones5[0:1, 0:5], badr_bf[0:1, ct * 512:(ct + 1) * 512], start=False, stop=True))
                            S.add("pe", seq(fns), r=["wa%d" % slot, "siluT", "ones5", "badr_bf"], w=[Prk])
                            S.add("act", act(gater[:, gi, hf * 512:(hf + 1) * 512], Pr[0:5, :], AF.Copy), r=[Prk], w=["gater"])
                    for (ct, cc), f in fidx.items():
                        chunk = ct * 4 + cc
                        S.add("dve", (OP("tensor_scalar", out=modT[:, f, :], in0=PF[:, f * 8:f * 8 + 5], scalar1=badT[:, chunk:chunk + 1], scalar2=None, op0=ALU.add)),
                              r=[PFk, "badT"], w=["modT"])
                    for kc in range(8):
                        S.add("dve", (OP("tensor_scalar", out=gm1[:, kc, :], in0=modT[:, 8 + kc, :], scalar1=1.0, scalar2=g1c[:, kc:kc + 1], op0=ALU.add, op1=ALU.mult)),
                              r=["modT", "g1c"], w=["gm1"])
                        S.add("dve", (OP("tensor_scalar", out=gm2[:, kc, :], in0=modT[:, 24 + kc, :], scalar1=1.0, scalar2=g2c[:, kc:kc + 1], op0=ALU.add, op1=ALU.mult)),
                              r=["modT", "g2c"], w=["gm2"])
                    S.add("dve", OP("tensor_copy", out=sh1[:, :, :], in_=modT[:, 0:8, :]), r=["modT"], w=["sh1"])
                    S.add("dve", OP("tensor_copy", out=sh2[:, :, :], in_=modT[:, 16:24, :]), r=["modT"], w=["sh2"])

                    for hf in range(2):
                        S.add("pool", (OP("dma_start", out=win[:, :, hf * 1128:(hf + 1) * 1128], in_=w_in[:, hf * 1128:(hf + 1) * 1128].rearrange("(kc p) n -> p kc n", p=128))),
                              w=["win%d" % hf], dma="win%d" % hf)
                    S.add("pool", OP("dma_start", out=wukv[:, :, :], in_=w_ukv_p.rearrange("(kc p) n -> p kc n", p=128)), w=["wukv"], dma="wukv")
                    S.add("pool", OP("dma_start", out=wgu[:, :], in_=w_gu_aug[:, :]), w=["wgu"], dma="wgu")
                    for kc in range(3):
                        S.add("dve", (OP("tensor_scalar", out=wuq[:, kc, :], in0=wuq_f[:, kc, :], scalar1=gql[:, kc:kc + 1], scalar2=None, op0=ALU.mult)),
                              r=["wuq_f", "gql"], w=["wuq"])
                        for h in range(4):
                            c0 = 768 + h * 64
                            S.add("dve", (OP("tensor_scalar", out=wuq[:, kc, c0:c0 + 32], in0=wuq[:, kc, c0:c0 + 32], scalar1=-1.0, scalar2=None, op0=ALU.mult)),
                                  r=["wuq"], w=["wuq"])
                    Pb, Pbk = P[7], PK(7)
                    S.add("pe", mm(Pb[:, 0:320], ones_f[0:1, :], rowt[0:1, 0:320]), r=["ones_f", "rowt0", "rowt1"], w=[Pbk])
                    S.add("act", act(gkv_bc[:, :], Pb[:, 0:256], AF.Copy), r=[Pbk], w=["gkv_bc"])
                    for b in range(NB):
                        S.add("act", (OP("activation", out=gkr_bc[:, b, :], in_=Pb[:, 256:320], func=AF.Copy)), r=[Pbk], w=["gkr_bc"])
                S.barrier()
                ck('p0')
                S.add("pool", OP("dma_start", out=wout[:, :, :], in_=w_out.rearrange("(kc p) n -> p kc n", p=128)), w=["wout"], dma="wout")

                Kn = psb("Kn", [128, 4, 4096], BF16)
                Kr = psb("Kr", [128, 2048], BF16)
                Vs = psb("Vs", [128, 32, 512], BF16)
                sclK = psb("sclK", [128, 32, 4])
                Sbf = psb("Sbf", [128, 2, 2, 128], BF16)
                xt = psb("xt", [128, 1024])
                hT = psb("hT", [128, 8, NT], BF16)
                lat_tm = psb("lat_tm", [128, NB, 256])
                kr_tm = psb("kr_tm", [128, NB, 64])
                latT = psb("latT", [128, 2, NT], BF16)
                gk_tm = psb("gk_tm", [128, NB, 256])
                gv_tm = psb("gv_tm", [128, NB, 512], BF16)
                l_tm = psb("l_tm", [128, NB, 256])
                khat = psb("khat", [128, NB, 256], BF16)
                cqT = psb("cqT", [128, 3, NT], BF16)
                qn = psb("qn", [128, 4, NT], BF16)
                qr = psb("qr", [128, 4, 2, NT], BF16)
                mixT = psb("mixT", [128, 8, NT], BF16)
                xr = xt
                kcs = psb("kcs", [128, NB, 128])
                qcs = psb("qcs", [128, NT])
                epst = psb("epst", [128, NT])
                stat = psb("stat", [128, 64])
                junk = psb("junk", [128, 1024], BF16)
                dec = psb("dec", [128, 2, NCH])
                TFl = [psb("TF%d" % i, [128, NT]) for i in range(10)]
                TBl = [psb("TB%d" % i, [128, NT], BF16) for i in range(11)]
                PTb = [psb("PT%d" % i, [128, NT], BF16) for i in range(4)]
                TFg = Rot([(t, t.name) for t in TFl[0:5]])
                TFq = Rot([(t, t.name) for t in TFl[5:8]])
                TFk = Rot([(t, t.name) for t in TFl[8:10]])
                TBg = Rot([(t, t.name) for t in TBl[0:6]])
                TBq = Rot([(t, t.name) for t in TBl[6:11]])
                PT = Rot([(t, t.name) for t in PTb])
                lat2 = xt[:, 0:512].rearrange("p (b d) -> p b d", b=NB)
                kr2 = xt[:, 512:640].rearrange("p (b d) -> p b d", b=NB)
                kcs2 = xt[:, 640:896].rearrange("p (b d) -> p b d", b=NB)
                latT2 = hT[:, 0:2, :]
                BS0 = dict(lat=lat_tm[:, :, :], latk="lat_tm", kr=kr_tm[:, :, :], krk="kr_tm", kcs=kcs[:, :, :], kcsk="kcs", latT=latT[:, :, :], latTk="latT")
                BS1 = dict(lat=lat2, latk="xt_lat", kr=kr2, krk="xt_kr", kcs=kcs2, kcsk="xt_kcs", latT=latT2, latTk="hTa")
                XTK = ["xt", "xt_kcs"] + [("xt_lat", i) for i in range(NB)] + [("xt_kr", i) for i in range(NB)]


                WIN = ["win0", "win1"]

                def HK(hTk, kcs=range(8), bs=range(NB)):
                    return [(hTk, kc, b) for kc in kcs for b in bs]

                def BK(pref, idx=None, n=NB):
                    return [(pref, i) for i in (range(n) if idx is None else idx)]
                S.add("dve", OP("memset", Kr[:, :], 0.0), w=[("Kr", kt_) for kt_ in range(2048 // NT)])
                S.add("dve", OP("memset", qr[:, :, :, :], 0.0), w=[("qr", h) for h in range(4)])

                def load_g1bc(v):
                    for fh in range(2):
                        Pg, Pgk = gp()
                        S.add("pe", (OP("matmul", Pg[:, :], sel[:, v, :], gater[:, 0, fh * 512:(fh + 1) * 512], start=True, stop=True)),
                              r=["sel", "gater"], w=[Pgk])
                        S.add("act", (OP("activation", out=g1bc[:, fh * 512:(fh + 1) * 512], in_=Pg[:, :], func=AF.Copy)), r=[Pgk], w=["g1bc"])

                def front(xsrc, r0, segs, hTb=None, hTk="hT", TFp=None):
                    hTb = hT if hTb is None else hTb
                    TFp = TFg if TFp is None else TFp
                    for b in range(NB):
                        S.add("sp", (OP("dma_start", out=xt[:, :], in_=xsrc[r0 + b * 128:r0 + (b + 1) * 128, :])), w=XTK, dma="xt")
                        S.add("act", OP("activation", out=junk[:, :], in_=xt[:, :], func=AF.Square, accum_out=stat[:, 4:5]), r=["xt"], w=[("junk", 0), ("junk", 1), ("junk", 2), "st_x"])
                        S.add("act", act(stat[:, 5:6], stat[:, 4:5], AF.Ln, scale=1.0 / 1024, bias=EPS), r=["st_x"], w=["st_x"])
                        S.add("act", act(stat[:, 6:7], stat[:, 5:6], AF.Exp, scale=-0.5), r=["st_x"], w=["st_x"])
                        S.add("dve", OP("tensor_scalar", out=xt[:, :], in0=xt[:, :], scalar1=stat[:, 6:7], scalar2=None, op0=ALU.mult), r=["xt", "st_x"], w=["xt"])
                        yield
                        for k2 in range(2):
                            Pt, Ptk = gp()
                            S.add("pe", seq([(OP("transpose", Pt[:, kk * 128:(kk + 1) * 128], xt[:, (k2 * 4 + kk) * 128:(k2 * 4 + kk + 1) * 128], ident[:, :])) for kk in range(4)]),
                                  r=["xt", "ident"], w=[Ptk])
                            for kk in range(4):
                                kc = k2 * 4 + kk
                                for (c0, ncol, m) in segs:
                                    lo = max(c0, b * 128)
                                    hi = min(c0 + ncol, (b + 1) * 128)
                                    if lo >= hi:
                                        continue
                                    S.add("dve", OP("tensor_scalar", out=hTb[:, kc, lo:hi], in0=Pt[:, kk * 128 + lo - b * 128:kk * 128 + hi - b * 128],
                                                    scalar1=gm1[:, kc, m:m + 1], scalar2=sh1[:, kc, m:m + 1], op0=ALU.mult, op1=ALU.add),
                          r=[Ptk, "gm1", "sh1"], w=[(hTk, kc, b), "hTa"])
                        yield

                def kvproj(lat_dst, kr_dst, r0, hTb=None, hTk="hT", TFp=None):
                    hTb = hT if hTb is None else hTb
                    TFp = TFg if TFp is None else TFp
                    for b in range(NB):
                        Pq, Pqk = gp()
                        S.add("pe", seq([mm(Pq[:, 0:320], hTb[:, kc, b * 128:(b + 1) * 128], win[:, kc, CKV0:CKV0 + 320], start=(kc == 0), stop=(kc == 7)) for kc in range(8)]),
                              r=HK(hTk, bs=[b]) + WIN, w=[Pqk])
                        skv = ("st_kv", b)
                        S.add("act", (OP("activation", out=junk[:, b * 256:(b + 1) * 256], in_=Pq[:, 0:256], func=AF.Square, accum_out=stat[:, 8 + b:9 + b])), r=[Pqk], w=[("junk", b), skv])
                        S.add("act", (OP("activation", out=stat[:, 12 + b:13 + b], in_=stat[:, 8 + b:9 + b], func=AF.Ln, scale=1.0 / 256, bias=EPS)), r=[skv], w=[skv])
                        S.add("act", (OP("activation", out=stat[:, 16 + b:17 + b], in_=stat[:, 12 + b:13 + b], func=AF.Exp, scale=-0.5)), r=[skv], w=[skv])
                        S.add("dve", (OP("scalar_tensor_tensor", out=lat_tm[:, b, :], in0=Pq[:, 0:256], scalar=stat[:, 16 + b:17 + b], in1=gkv_bc[:, :], op0=ALU.mult, op1=ALU.mult)),
                              r=[Pqk, skv, "gkv_bc"], w=[("lat_tm", b)])
                        S.add("dve", OP("tensor_copy", out=kr_tm[:, b, :], in_=Pq[:, 256:320]), r=[Pqk], w=[("kr_tm", b)])
                        yield
                    if lat_dst is not None:
                        S.add("pool", OP("dma_start", out=lat_dst[r0:r0 + NT, :].rearrange("(b p) d -> p b d", p=128), in_=lat_tm[:, :, :]), r=BK("lat_tm"), dma="lat_o")
                        S.add("pool", OP("dma_start", out=kr_dst[r0:r0 + NT, :].rearrange("(b p) d -> p b d", p=128), in_=kr_tm[:, :, :]), r=BK("kr_tm"), dma="kr_o")

                def lat_transpose(bs=None):
                    bs = BS0 if bs is None else bs
                    for lc in range(2):
                        Pt, Ptk = gp()
                        S.add("pe", seq([(OP("transpose", Pt[:, b * 128:(b + 1) * 128], bs["lat"][:, b, lc * 128:(lc + 1) * 128], ident[:, :])) for b in range(NB)]),
                              r=BK(bs["latk"]) + ["ident"], w=[Ptk])
                        S.add("dve", (OP("tensor_copy", out=bs["latT"][:, lc, :], in_=Pt[:, 0:NT])), r=[Ptk], w=[(bs["latTk"], lc)] + (["hTa"] if bs["latTk"] == "hTa" else []))
                        yield

                def kside(key0, cs_src, cs_b0, bs=None, TFp=None, load_cs=True):
                    kb0 = key0 // 128
                    bs = BS0 if bs is None else bs
                    TFp = TFg if TFp is None else TFp
                    latT_, latTk_, kr_, krk_, kcs_, kcsk_ = bs["latT"], bs["latTk"], bs["kr"], bs["krk"], bs["kcs"], bs["kcsk"]
                    if load_cs:
                        S.add("sp", OP("dma_start", out=kcs_, in_=cs_src[:, cs_b0:cs_b0 + NB, :]), w=[kcsk_], dma=kcsk_)
                    for h in range(4):
                        Pq, Pqk = gp()
                        S.add("pe", seq([mm(Pq[:, 0:NT], wukv[:, lc, h * 128:(h + 1) * 128], latT_[:, lc, :], start=(lc == 0), stop=(lc == 1)) for lc in range(2)]),
                              r=["wukv", (latTk_, 0), (latTk_, 1)], w=[Pqk])
                        S.add("dve", OP("tensor_copy", out=Kn[:, h, key0:key0 + NT], in_=Pq[:, 0:NT]), r=[Pqk], w=[("Kn", key0 // NT, h)])
                        yield
                    for b in range(NB):
                        Pq, Pqk = gp()
                        S.add("pe", seq([mm(Pq[:, :], latT_[:, lc, b * 128:(b + 1) * 128], wukv[:, lc, 0:512], start=(lc == 0), stop=(lc == 1)) for lc in range(2)]),
                              r=["wukv", (latTk_, 0), (latTk_, 1)], w=[Pqk])
                        sk = ("st_k", b)
                        for h in range(4):
                            S.add("act", (OP("activation", out=Pq[:, h * 128:(h + 1) * 128], in_=Pq[:, h * 128:(h + 1) * 128], func=AF.Square, accum_out=stat[:, 20 + b * 4 + h:21 + b * 4 + h])),
                                  r=[Pqk], w=[(Pqk, h), (sk, h)])
                        S.add("act", (OP("activation", out=junk[:, 512 + b * 64:512 + (b + 1) * 64], in_=kr_[:, b, :], func=AF.Square, accum_out=stat[:, 28 + b:29 + b])), r=[(krk_, b)], w=[("junk", 2), (sk, 4)])
                        S.add("dve", (OP("tensor_scalar", out=stat[:, 32 + b * 4:36 + b * 4], in0=stat[:, 20 + b * 4:24 + b * 4], scalar1=stat[:, 28 + b:29 + b], scalar2=None, op0=ALU.add)),
                              r=[(sk, 0), (sk, 1), (sk, 2), (sk, 3), (sk, 4)], w=[sk])
                        S.add("act", (OP("activation", out=stat[:, 40 + b * 4:44 + b * 4], in_=stat[:, 32 + b * 4:36 + b * 4], func=AF.Ln, scale=1.0 / 192, bias=EPS)), r=[sk], w=[sk])
                        S.add("act", (OP("activation", out=sclK[:, kb0 + b, :], in_=stat[:, 40 + b * 4:44 + b * 4], func=AF.Exp, scale=-0.5, bias=float(-0.5 * np.log(192.0)))),
                              r=[sk], w=[("sclK", kb0 + b)])
                        yield
                        Pv, Pvk = gp()
                        S.add("pe", seq([mm(Pv[:, :], latT_[:, lc, b * 128:(b + 1) * 128], wukv[:, lc, 512:1024], start=(lc == 0), stop=(lc == 1)) for lc in range(2)]),
                              r=["wukv", (latTk_, 0), (latTk_, 1)], w=[Pvk])
                        S.add("dve", (OP("tensor_copy", out=Vs[:, kb0 + b, :], in_=Pv[:, :])), r=[Pvk], w=[("Vs", kb0 + b)])
                        yield
                    t1, t1k = TFp.get()
                    t2, t2k = TFp.get()
                    t3, t3k = TFp.get()
                    v1 = t1[:, 0:NB * 64].rearrange("p (b d) -> p b d", b=NB)
                    v2 = t2[:, 0:NB * 64].rearrange("p (b d) -> p b d", b=NB)
                    v3 = t3[:, 0:NB * 64].rearrange("p (b d) -> p b d", b=NB)
                    S.add("dve", OP("tensor_tensor", out=v1, in0=kr_, in1=gkr_bc[:, :, :], op=ALU.mult), r=BK(krk_) + ["gkr_bc"], w=[t1k])
                    S.add("dve", OP("tensor_tensor", out=v2, in0=v1, in1=kcs_[:, :, 0:64], op=ALU.mult), r=[t1k, kcsk_], w=[t2k])
                    S.add("dve", OP("tensor_tensor", out=v3[:, :, 0:32], in0=v1[:, :, 32:64], in1=kcs_[:, :, 64:96], op=ALU.mult), r=[t1k, kcsk_], w=[t3k])
                    S.add("dve", OP("tensor_tensor", out=v3[:, :, 32:64], in0=v1[:, :, 0:32], in1=kcs_[:, :, 96:128], op=ALU.mult), r=[t1k, kcsk_], w=[t3k])
                    S.add("dve", OP("tensor_tensor", out=v2, in0=v2, in1=v3, op=ALU.add), r=[t2k, t3k], w=[t2k])
                    yield
                    Pt, Ptk = gp()
                    S.add("pe", seq([(OP("transpose", Pt[0:64, b * 128:(b + 1) * 128], v2[:, b, :], ident[:, :])) for b in range(NB)]),
                          r=[t2k, "ident"], w=[Ptk])
                    hb = 0 if key0 < 2048 else 64
                    kk0 = key0 % 2048
                    if hb == 0:
                        S.add("act", OP("activation", out=Kr[0:64, kk0:kk0 + NT], in_=Pt[0:64, 0:NT], func=AF.Copy), r=[Ptk], w=[("Kr", (key0 % 2048) // NT)])
                    else:
                        sh, shk = TFp.get()
                        S.add("act", OP("activation", out=sh[0:64, :], in_=Pt[0:64, 0:NT], func=AF.Copy), r=[Ptk], w=[shk])
                        P2, P2k = gp()
                        S.add("pe", OP("matmul", P2[:, 0:NT], comb[0:64, :], sh[0:64, :], start=True, stop=True), r=[shk, "comb"], w=[P2k])
                        S.add("act", OP("activation", out=Kr[64:128, kk0:kk0 + NT], in_=P2[64:128, 0:NT], func=AF.Copy), r=[P2k], w=[("Kr", (key0 % 2048) // NT)])
                    yield

                def gla_common(own, hTb=None, hTk="hT", TFp=None):
                    hTb = hT if hTb is None else hTb
                    TFp = TFg if TFp is None else TFp
                    Pl, Plk = gp()
                    S.add("pe", seq([mm(Pl[0:16, 0:NT], win[:, kc, GLR0:GLR0 + 16], hTb[:, kc, :], start=(kc == 0), stop=(kc == 7)) for kc in range(8)]),
                          r=HK(hTk) + WIN, w=[Plk])
                    S.add("dve", OP("tensor_copy", out=glrT[0:16, :], in_=Pl[0:16, 0:NT]), r=[Plk], w=["glrT"])
                    yield
                    for b in range(NB):
                        Pk_, Pkk = gp()
                        S.add("pe", seq([mm(Pk_[:, 0:256], hTb[:, kc, b * 128:(b + 1) * 128], win[:, kc, GK0:GK0 + 256], start=(kc == 0), stop=(kc == 7)) for kc in range(8)]),
                              r=HK(hTk, bs=[b]) + WIN, w=[Pkk])
                        S.add("dve", OP("tensor_copy", out=gk_tm[:, b, :], in_=Pk_[:, 0:256]), r=[Pkk], w=[("gk_tm", b)])
                        yield
                        Pv, Pvk = gp()
                        S.add("pe", seq([mm(Pv[:, :], hTb[:, kc, b * 128:(b + 1) * 128], win[:, kc, GV0:GV0 + 512], start=(kc == 0), stop=(kc == 7)) for kc in range(8)]),
                              r=HK(hTk, bs=[b]) + WIN, w=[Pvk])
                        S.add("dve", (OP("tensor_copy", out=gv_tm[:, b, :], in_=Pv[:, :])), r=[Pvk], w=[("gv_tm", b)])
                        yield
                        Pz, Pzk = gp()
                        S.add("pe", (OP("matmul", Pz[:, 0:256], glrT[:, b * 128:(b + 1) * 128], wgu[:, :], start=True, stop=True)), r=["glrT", "wgu"], w=[Pzk])
                        S.add("act", (OP("activation", out=l_tm[:, b, :], in_=Pz[:, 0:256], func=AF.Exp, scale=-1.0)), r=[Pzk], w=[("l_tm", b)])
                        S.add("act", (OP("activation", out=l_tm[:, b, :], in_=l_tm[:, b, :], func=AF.Ln, bias=1.0)), r=[("l_tm", b)], w=[("l_tm", b)])
                        yield
                        Pc, Pck = gp()
                        S.add("pe", (OP("matmul", Pc[:, 0:256], ut[:, :], l_tm[:, b, :], start=True, stop=True)), r=["ut", ("l_tm", b)], w=[Pck])
                        et, etk = TFp.get()
                        S.add("act", (OP("activation", out=et[:, :], in_=Pc[:, 0:256], func=AF.Exp)), r=[Pck], w=[etk])
                        S.add("dve", (OP("tensor_tensor", out=khat[:, b, :], in0=gk_tm[:, b, :], in1=et[:, :], op=ALU.mult)), r=[etk, ("gk_tm", b)], w=[("khat", b)])
                        yield
                    return None

                def bt_step(hp, TFp, need_e):
                    Pb_, Pbk_ = gp()
                    S.add("pe", seq([OP("matmul", Pb_[:, b * 128:(b + 1) * 128], l_tm[:, b, hp * 128:(hp + 1) * 128], tri[:, :], start=True, stop=True) for b in range(NB)]),
                          r=[("l_tm", b_) for b_ in range(NB)] + ["tri"], w=[Pbk_])
                    S.add("act", OP("activation", out=dec[:, hp, :], in_=Pb_[:, 63:NT:64], func=AF.Exp), r=[Pbk_], w=[("dec", hp)])
                    if not need_e:
                        return None
                    eb, ebk = TFp.get()
                    enb, enbk = TFp.get()
                    S.add("act", OP("activation", out=eb[:, :], in_=Pb_[:, 0:NT], func=AF.Exp), r=[Pbk_], w=[ebk])
                    S.add("act", OP("activation", out=enb[:, :], in_=Pb_[:, 0:NT], func=AF.Exp, scale=-1.0), r=[Pbk_], w=[enbk])
                    return eb, ebk, enb, enbk

                def state_update(hp, ch, Pu, Puk):
                    b, par = ch // 2, ch % 2
                    fns = []
                    for hh in range(2):
                        h = hp * 2 + hh
                        fns.append(mm(Pu[hh * 64:(hh + 1) * 64, 0:128], khat[par * 64:(par + 1) * 64, b, h * 64:(h + 1) * 64], gv_tm[par * 64:(par + 1) * 64, b, h * 128:(h + 1) * 128]))
                    S.add("pe", seq(fns), r=[("khat", b), ("gv_tm", b)], w=[Puk])

                def gla_prefix(hTb=None, hTk="hT", TFp=None):
                    yield from gla_common(False, hTb, hTk, TFp)
                    for hp in range(2):
                        bt_step(hp, TFp, False)
                        yield
                        for ch in range(NCH):
                            Pu, Puk = gp()
                            state_update(hp, ch, Pu, Puk)
                            S.add("dve", (OP("scalar_tensor_tensor", out=Sst[:, hp, :], in0=Sst[:, hp, :], scalar=dec[:, hp, ch:ch + 1], in1=Pu[:, 0:128], op0=ALU.mult, op1=ALU.add)),
                                  r=[Puk, ("dec", hp), ("Sst", hp)], w=[("Sst", hp)])
                            yield

                def gla_own(per_chunk_state, st_dst, TFp=None, TBp=None):
                    TFp = TFg if TFp is None else TFp
                    TBp = TBg if TBp is None else TBp
                    yield from gla_common(True, None, "hT", TFp)
                    for hp in range(2):
                        eb, ebk, enb, enbk = bt_step(hp, TFp, True)
                        yield
                        Pq, Pqk = gp()
                        S.add("pe", seq([mm(Pq[:, 0:NT], win[:, kc, GQ0 + hp * 128:GQ0 + (hp + 1) * 128], hT[:, kc, :], start=(kc == 0), stop=(kc == 7)) for kc in range(8)]),
                              r=HK("hT") + WIN, w=[Pqk])
                        qtls = []
                        for hh in range(2):
                            qtl, qtlk = TBp.get()
                            S.add("dve", OP("scalar_tensor_tensor", out=qtl[:, :], in0=Pq[:, 0:NT], scalar=hsel[:, hh:hh + 1], in1=eb[:, :], op0=ALU.mult, op1=ALU.mult),
                                  r=[Pqk, ebk, "hsel"], w=[qtlk])
                            qtls.append((qtl, qtlk))
                        yield
                        Pk2, Pk2k = gp()
                        S.add("pe", seq([mm(Pk2[:, 0:NT], win[:, kc, GK0 + hp * 128:GK0 + (hp + 1) * 128], hT[:, kc, :], start=(kc == 0), stop=(kc == 7)) for kc in range(8)]),
                              r=HK("hT") + WIN, w=[Pk2k])
                        ktl, ktlk = TBp.get()
                        S.add("dve", OP("tensor_tensor", out=ktl[:, :], in0=Pk2[:, 0:NT], in1=enb[:, :], op=ALU.mult), r=[Pk2k, enbk], w=[ktlk])
                        yield
                        Ps, Psk = gp()
                        fns = []
                        for hh in range(2):
                            for b in range(NB):
                                co = hh * NT + b * 128
                                fns.append(mm(Ps[:, co:co + 128], ktl[:, b * 128:(b + 1) * 128], qtls[hh][0][:, b * 128:(b + 1) * 128]))
                        S.add("pe", seq(fns), r=[ktlk, qtls[0][1], qtls[1][1]], w=[Psk])
                        mks = []
                        for hh in range(2):
                            msk, mskk = TBp.get()
                            S.add("dve", OP("tensor_tensor", out=msk[:, :], in0=Ps[:, hh * NT:(hh + 1) * NT], in1=maskr[:, :], op=ALU.mult), r=[Psk, "maskr"], w=[mskk])
                            mks.append((msk, mskk))
                        yield
                        Po, Pok = P[7], PK(7)
                        for ch in range(NCH):
                            b, par = ch // 2, ch % 2
                            sb_par = ch % 2
                            if per_chunk_state:
                                S.add("sp", OP("dma_start", out=Sst[:, hp, :], in_=stc[ch, hp, :, :]), w=[("Sst", hp)], dma=("Sst_in", hp))
                            if per_chunk_state or ch == 0:
                                S.add("dve", OP("tensor_copy", out=Sbf[:, hp, sb_par, :], in_=Sst[:, hp, :]), r=[("Sst", hp)], w=[("Sbf", hp, sb_par)])
                            fns = []
                            for hh in range(2):
                                h = hp * 2 + hh
                                oc = hh * NT + ch * 64
                                if par == 0:
                                    ob = hh * NT + b * 128
                                    fns.append(OP("matmul", Po[:, ob:ob + 128], gv_tm[:, b, h * 128:(h + 1) * 128], mks[hh][0][:, b * 128:(b + 1) * 128], start=(ch == 0 and hh == 0), stop=False, skip_group_check=True))
                                fns.append(OP("matmul", Po[:, oc:oc + 64], Sbf[:, hp, sb_par, :], qtls[hh][0][:, ch * 64:(ch + 1) * 64], start=False, stop=(par == 1), skip_group_check=True))
                            S.add("pe", seq(fns), r=[("gv_tm", b), mks[0][1], mks[1][1], ("Sbf", hp, sb_par), qtls[0][1], qtls[1][1]], w=[Pok])
                            Pu, Puk = gp()
                            state_update(hp, ch, Pu, Puk)
                            S.add("dve", OP("scalar_tensor_tensor", out=Sst[:, hp, :], in0=Sst[:, hp, :], scalar=dec[:, hp, ch:ch + 1], in1=Pu[:, 0:128], op0=ALU.mult, op1=ALU.add),
                                  r=[Puk, ("dec", hp), ("Sst", hp)], w=[("Sst", hp)])
                            if per_chunk_state:
                                S.add("pool", OP("dma_start", out=st_dst[ch, hp, :, :], in_=Sst[:, hp, :]), r=[("Sst", hp)], dma=("Sst_out", hp))
                            elif ch + 1 < NCH:
                                np_ = (ch + 1) % 2
                                S.add("dve", OP("tensor_copy", out=Sbf[:, hp, np_, :], in_=Sst[:, hp, :]), r=[("Sst", hp)], w=[("Sbf", hp, np_)])
                            yield
                        yield
                        for hh in range(2):
                            h = hp * 2 + hh
                            oc = hh * NT
                            sq, sqk = TBp.get()
                            S.add("act", (OP("activation", out=sq[:, :], in_=Po[:, oc:oc + NT], func=AF.Square)), r=[Pok], w=[sqk])
                            Pn, Pnk = gp()
                            S.add("pe", (OP("matmul", Pn[:, 0:NT], ones_bf[:, :], sq[:, :], start=True, stop=True)), r=[sqk, "ones_bf"], w=[Pnk])
                            rs, rsk = TFp.get()
                            S.add("act", (OP("activation", out=rs[:, :], in_=Pn[:, 0:NT], func=AF.Ln, scale=1.0 / 128, bias=EPS)), r=[Pnk], w=[rsk])
                            yield
                            S.add("act", (OP("activation", out=rs[:, :], in_=rs[:, :], func=AF.Exp, scale=-0.5)), r=[rsk], w=[rsk])
                            on, onk = TFp.get()
                            S.add("dve", (OP("tensor_tensor", out=on[:, :], in0=Po[:, oc:oc + NT], in1=rs[:, :], op=ALU.mult)), r=[Pok, rsk], w=[onk])
                            Pr, Prk = gp()
                            S.add("pe", seq([mm(Pr[:, 0:NT], win[:, kc, GR0 + h * 128:GR0 + (h + 1) * 128], hT[:, kc, :], start=(kc == 0), stop=(kc == 7)) for kc in range(8)]),
                                  r=HK("hT") + WIN, w=[Prk])
                            sg, sgk = TFp.get()
                            S.add("act", (OP("activation", out=sg[:, :], in_=Pr[:, 0:NT], func=AF.Exp, scale=-1.0)), r=[Prk], w=[sgk])
                            S.add("act", (OP("activation", out=sg[:, :], in_=sg[:, :], func=AF.Ln, bias=1.0)), r=[sgk], w=[sgk])
                            S.add("act", (OP("activation", out=sg[:, :], in_=sg[:, :], func=AF.Exp, scale=-1.0)), r=[sgk], w=[sgk])
                            S.add("dve", (OP("tensor_tensor", out=sg[:, :], in0=Pr[:, 0:NT], in1=sg[:, :], op=ALU.mult)), r=[Prk, sgk], w=[sgk])
                            S.add("dve", (OP("scalar_tensor_tensor", out=mixT[:, 4 + h, :], in0=on[:, :], scalar=ggo[:, h:h + 1], in1=sg[:, :], op0=ALU.mult, op1=ALU.mult)),
                                  r=[onk, sgk, "ggo"], w=[("mixT", 4 + h), "hTalt"])
                            yield

                def qpath(qcs_src, c0, TFp=None, TBp=None):
                    TFp = TFq if TFp is None else TFp
                    TBp = TBq if TBp is None else TBp
                    S.add("sp", OP("dma_start", out=qcs[:, :], in_=qcs_src[:, c0:c0 + NT]), w=["qcs"], dma="qcs")
                    sqs = []
                    for kc3 in range(3):
                        Pq, Pqk = gp()
                        S.add("pe", seq([mm(Pq[:, 0:NT], win[:, kc, CQ0 + kc3 * 128:CQ0 + (kc3 + 1) * 128], hT[:, kc, :], start=(kc == 0), stop=(kc == 7)) for kc in range(8)]),
                              r=HK("hT") + WIN, w=[Pqk])
                        S.add("act", (OP("activation", out=cqT[:, kc3, :], in_=Pq[:, 0:NT], func=AF.Copy)), r=[Pqk], w=[("cqT", kc3)])
                        sq, sqk = TBp.get()
                        S.add("act", (OP("activation", out=sq[:, :], in_=Pq[:, 0:NT], func=AF.Square)), r=[Pqk], w=[sqk])
                        sqs.append((sq, sqk))
                        yield
                    Pss, Pssk = gp()
                    S.add("pe", seq([mm(Pss[:, 0:NT], ones_bf[:, :], sqs[i][0][:, :], start=(i == 0), stop=(i == 2)) for i in range(3)]),
                          r=[s[1] for s in sqs] + ["ones_bf"], w=[Pssk])
                    epstk = "epst"
                    S.add("act", (OP("activation", out=epst[:, :], in_=Pss[:, 0:NT], func=AF.Identity, scale=EPS / 384.0, bias=EPS * EPS)), r=[Pssk], w=[epstk])
                    yield
                    for h in range(4):
                        Pn_, Pnk_ = gp()
                        S.add("pe", seq([mm(Pn_[:, 0:NT], wuq[:, kc3, h * 192:h * 192 + 128], cqT[:, kc3, :], start=(kc3 == 0), stop=(kc3 == 2)) for kc3 in range(3)]),
                              r=["wuq", ("cqT", 0), ("cqT", 1), ("cqT", 2)], w=[Pnk_])
                        Pab, Pabk = gp()
                        fns = [mm(Pab[0:64, 0:NT], wuq[:, kc3, h * 192 + 128:h * 192 + 192], cqT[:, kc3, :], start=(kc3 == 0), stop=(kc3 == 2)) for kc3 in range(3)]
                        fns += [mm(Pab[64:128, 0:NT], wuq[:, kc3, 768 + h * 64:768 + (h + 1) * 64], cqT[:, kc3, :], start=(kc3 == 0), stop=(kc3 == 2)) for kc3 in range(3)]
                        S.add("pe", seq(fns), r=["wuq", ("cqT", 0), ("cqT", 1), ("cqT", 2)], w=[Pabk])
                        s1, s1k = TBp.get()
                        s2, s2k = TBp.get()
                        S.add("act", (OP("activation", out=s1[:, :], in_=Pn_[:, 0:NT], func=AF.Square)), r=[Pnk_], w=[s1k])
                        S.add("act", (OP("activation", out=s2[0:64, :], in_=Pab[0:64, 0:NT], func=AF.Square)), r=[Pabk], w=[s2k])
                        Ph, Phk = gp()
                        S.add("pe", seq([mm(Ph[:, 0:NT], ones_bf[:, :], s1[:, :], start=True, stop=False), mm(Ph[:, 0:NT], ones_bf[0:64, :], s2[0:64, :], start=False, stop=True)]),
                              r=[s1k, s2k, "ones_bf"], w=[Phk])
                        rq, rqk = TFp.get()
                        S.add("dve", (OP("scalar_tensor_tensor", out=rq[:, :], in0=Ph[:, 0:NT], scalar=1.0 / 192, in1=epst[:, :], op0=ALU.mult, op1=ALU.add)), r=[Phk, epstk], w=[rqk])
                        S.add("act", (OP("activation", out=rq[:, :], in_=rq[:, :], func=AF.Ln)), r=[rqk], w=[rqk])
                        S.add("act", (OP("activation", out=rq[:, :], in_=rq[:, :], func=AF.Exp, scale=-0.5)), r=[rqk], w=[rqk])
                        S.add("dve", (OP("scalar_tensor_tensor", out=qn[:, h, :], in0=Pn_[:, 0:NT], scalar=gqk[:, 0:1], in1=rq[:, :], op0=ALU.mult, op1=ALU.mult)),
                              r=[Pnk_, rqk, "gqk"], w=[("qn", h)])
                        ab, abk = TFp.get()
                        S.add("dve", (OP("scalar_tensor_tensor", out=ab[:, :], in0=Pab[:, 0:NT], scalar=gqr_t[:, 0:1], in1=rq[:, :], op0=ALU.mult, op1=ALU.mult)),
                              r=[Pabk, rqk, "gqr_t"], w=[abk])
                        S.add("dve", (OP("tensor_tensor", out=ab[:, :], in0=ab[:, :], in1=qcs[:, :], op=ALU.mult)), r=[abk, "qcs"], w=[abk])
                        Pc_, Pck_ = gp()
                        S.add("pe", (OP("matmul", Pc_[:, 0:NT], comb[:, :], ab[:, :], start=True, stop=True)), r=[abk, "comb"], w=[Pck_])
                        S.add("act", (OP("activation", out=qr[0:64, h, 0, :], in_=Pc_[0:64, 0:NT], func=AF.Copy)), r=[Pck_], w=[("qr", h)])
                        S.add("act", (OP("activation", out=qr[64:128, h, 1, :], in_=Pc_[64:128, 0:NT], func=AF.Copy)), r=[Pck_], w=[("qr", h)])
                        yield

                rlbuf = psb("rlbuf", [128, NT])
                att = dict(cnt=0)

                def blk(h, kb, ncols, q0, first, last, bias_ap, zero_tri=False, zero_rows=None, finish=None):
                    return dict(h=h, kb=kb, ncols=ncols, q0=q0, first=first, last=last, bias=bias_ap, zero_tri=zero_tri, zero_rows=zero_rows, finish=finish)

                def emit_qk(B):
                    h, kb, ncols, q0 = B["h"], B["kb"], B["ncols"], B["q0"]
                    key0 = kb * 128
                    hb = 0 if key0 < 2048 else 64
                    kk0 = key0 % 2048
                    si = SCB[att["cnt"] % 3]
                    att["cnt"] += 1
                    Ps, Psk = P[si], PK(si)
                    B["Ps"], B["Psk"] = Ps, Psk
                    kt = key0 // NT
                    S.add("pe", seq([
                        mm(Ps[:, 0:ncols], Kn[:, h, key0:key0 + 128], qn[:, h, q0:q0 + ncols], start=True, stop=False),
                        mm(Ps[:, 0:ncols], Kr[:, kk0:kk0 + 128], qr[:, h, hb // 64, q0:q0 + ncols], start=False, stop=True)]),
                        r=[("Kn", kt, h), ("Kr", kk0 // NT), ("qn", h), ("qr", h)], w=[Psk])

                def emit_rest(B):
                    h, kb, ncols, q0 = B["h"], B["kb"], B["ncols"], B["q0"]
                    Ps, Psk = B["Ps"], B["Psk"]
                    kt = kb * 128 // NT
                    pt, ptk = PT.get()
                    if B["bias"] is None:
                        S.add("act", OP("activation", out=pt[:, 0:ncols], in_=Ps[:, 0:ncols], func=AF.Exp, scale=sclK[:, kb, h:h + 1]),
                              r=[Psk, ("sclK", kb)], w=[ptk])
                    else:
                        S.add("act", OP("activation", out=pt[:, 0:ncols], in_=Ps[:, 0:ncols], func=AF.Exp, scale=sclK[:, kb, h:h + 1], bias=B["bias"]),
                              r=[Psk, ("sclK", kb), "flg"], w=[ptk])
                    if B["zero_tri"]:
                        S.add("pool", OP("memset", pt[64:128, 0:64], 0.0), r=[ptk], w=[ptk])
                    if B["zero_rows"] is not None:
                        zr = B["zero_rows"]
                        S.add("pool", OP("memset", pt[zr[0]:zr[1], 0:ncols], 0.0), r=[ptk], w=[ptk])
                    ab_ = 2 + (h % 2)
                    Pa, Pak = P[ab_], PK(ab_)
                    S.add("pe", seq([
                        OP("matmul", Pa[:, q0:q0 + ncols], Vs[:, kb, h * 128:(h + 1) * 128], pt[:, 0:ncols], start=B["first"], stop=B["last"], skip_group_check=True),
                        OP("matmul", Pa[:, NT + q0:NT + q0 + ncols], ones_bf[:, :], pt[:, 0:ncols], start=False, stop=B["last"], skip_group_check=True)]),
                        r=[("Vs", kb), ptk, "ones_bf"], w=[Pak])
                    if B["finish"] is not None:
                        fh_, fq0, fn = B["finish"]
                        S.add("dve", OP("reciprocal", out=rlbuf[:, 0:fn], in_=Pa[:, NT + fq0:NT + fq0 + fn]), r=[Pak], w=["rlbuf"])
                        S.add("dve", OP("tensor_tensor", out=mixT[:, fh_, fq0:fq0 + fn], in0=Pa[:, fq0:fq0 + fn], in1=rlbuf[:, 0:fn], op=ALU.mult), r=[Pak, "rlbuf"], w=[("mixT", fh_), "hTalt"])

                def attn_run(blocks, hook=None):
                    set_gen(GEN_IN)
                    try:
                        _attn_run(blocks, hook)
                    finally:
                        set_gen(GEN_OUT)

                def _attn_run(blocks, hook=None):
                    n = len(blocks)
                    emit_qk(blocks[0])
                    if n > 1:
                        emit_qk(blocks[1])
                    for k in range(n):
                        if k + 2 < n:
                            emit_qk(blocks[k + 2])
                        emit_rest(blocks[k])
                        if hook is not None:
                            hook()

                def prompt_blocks(p):
                    out = []
                    for h in range(4):
                        nown = NB * p + NB
                        for kb in range(16):
                            out.append(blk(h, kb, NT, 0, kb == 0, False, flg[:, 1:2]))
                        for j in range(nown):
                            kb = 16 + j
                            dj = j - NB * p
                            if dj < 0:
                                out.append(blk(h, kb, NT, 0, False, False, None))
                            else:
                                lastb = (j == nown - 1)
                                out.append(blk(h, kb, NT - 128 * dj, 128 * dj, False, lastb, None, zero_tri=True, finish=((h, 0, NT) if lastb else None)))
                    return out

                def sample_blocks(i):
                    q0 = i * 64
                    par = i % 2
                    out = []
                    for h in range(4):
                        for kb in range(16):
                            out.append(blk(h, kb, 64, q0, kb == 0, False, None))
                        out.append(blk(h, 16 + i // 2, 64, q0, False, True, None, zero_rows=((1 - par) * 64, (1 - par) * 64 + 64), finish=(h, q0, 64)))
                    return out

                def back(xsrc, r0, x1row0, blocks_g1, TFp=None):
                    TFp = TFk if TFp is None else TFp
                    for b in range(NB):
                        if blocks_g1 is not None:
                            load_g1bc(blocks_g1[b])
                        S.add("sp", (OP("dma_start", out=xr[:, :], in_=xsrc[r0 + b * 128:r0 + (b + 1) * 128, :])), w=XTK, dma="xt")
                        for fh in range(2):
                            Po, Pok = gp()
                            S.add("pe", seq([mm(Po[:, :], mixT[:, k, b * 128:(b + 1) * 128], wout[:, k, fh * 512:(fh + 1) * 512], start=(k == 0), stop=(k == 7)) for k in range(8)]),
                                  r=[("mixT", k) for k in range(8)] + ["wout"], w=[Pok])
                            tt, ttk = TFp.get()
                            tt2, tt2k = TFp.get()
                            S.add("dve", (OP("tensor_tensor", out=tt[:, :], in0=Po[:, 0:256], in1=g1bc[:, fh * 512:fh * 512 + 256], op=ALU.mult)), r=[Pok, "g1bc"], w=[ttk])
                            S.add("dve", (OP("tensor_tensor", out=tt2[:, :], in0=Po[:, 256:512], in1=g1bc[:, fh * 512 + 256:fh * 512 + 512], op=ALU.mult)), r=[Pok, "g1bc"], w=[tt2k])
                            S.add("dve", (OP("tensor_tensor", out=xr[:, fh * 512:fh * 512 + 256], in0=xr[:, fh * 512:fh * 512 + 256], in1=tt[:, :], op=ALU.add)), r=[ttk, "xt"], w=["xt"])
                            S.add("dve", (OP("tensor_tensor", out=xr[:, fh * 512 + 256:fh * 512 + 512], in0=xr[:, fh * 512 + 256:fh * 512 + 512], in1=tt2[:, :], op=ALU.add)), r=[tt2k, "xt"], w=["xt"])
                            yield
                        S.add("pool", (OP("dma_start", out=x1s[x1row0 + b * 128:x1row0 + (b + 1) * 128, :], in_=xr[:, :])), r=["xt"], w=[("x1s", x1row0 // 128 + b)], dma="x1s_w")

                def run(g):
                    for _ in g:
                        pass

                def chain(*gens):
                    for g in gens:
                        yield from g

                def hook_of(g, every=1):
                    st = dict(n=0)

                    def hk():
                        st["n"] += 1
                        if st["n"] % every == 0:
                            next(g, None)
                    return hk

                def inter(gens):
                    active = dict(gens)
                    while active:
                        for name in list(active):
                            g = active.get(name)
                            if g is None:
                                continue
                            try:
                                tok = next(g)
                            except StopIteration:
                                del active[name]
                                continue
                            if isinstance(tok, str) and tok.startswith("need:"):
                                dep = tok[5:]
                                if dep in active:
                                    for _ in active[dep]:
                                        pass
                                    del active[dep]

                load_g1bc(0)
                NPRE = 2048 // NT
                hbufs = [(hT, "hT"), (mixT, "hTalt")]
                run(front(xpre, 0, [(0, NT, 0)], *hbufs[0]))
                for t in range(NPRE):
                    hb_, hk_ = hbufs[t % 2]
                    gens = dict(k=chain(kvproj(None, None, 0, hb_, hk_), lat_transpose(), kside(t * NT, kcs_pre, t * NB)),
                                g=gla_prefix(hb_, hk_, TFq))
                    if t + 1 < NPRE:
                        gens["f"] = front(xpre, (t + 1) * NT, [(0, NT, 0)], hbufs[(t + 1) % 2][0], hbufs[(t + 1) % 2][1], TFk)
                    inter(gens)
                    ck('pre%d' % t)
                for hp in range(2):
                    S.add("dve", (OP("tensor_scalar", out=Sst[:, hp, :], in0=Sst[:, hp, :], scalar1=flg[:, 0:1], scalar2=None, op0=ALU.mult)), r=[("Sst", hp), "flg"], w=[("Sst", hp)])
                NOWN = 2048 // NT

                def pre_own(p):
                    return chain(front(xown, p * NT, [(0, NT, 0)]), kvproj(lat_own, kr_own, p * NT), lat_transpose(), kside(2048 + p * NT, kcs_own, p * NB))
                run(pre_own(0))
                run(qpath(qcs_own, 0))
                for p in range(NOWN):
                    hk_chain = chain(gla_own(False, None), pre_own(p + 1) if p + 1 < NOWN else iter(()))
                    attn_run(prompt_blocks(p), hook_of(hk_chain, 1))
                    set_gen(GEN_IN)
                    run(hk_chain)
                    set_gen(GEN_OUT)
                    gens = dict(back=back(xown, p * NT, p * NT, None))
                    if p + 1 < NOWN:
                        gens["q"] = qpath(qcs_own, (p + 1) * NT)
                    inter(gens)
                    ck('own%d' % p)
                for hp in range(2):
                    S.add("pool", (OP("dma_start", out=st_p[hp, :, :], in_=Sst[:, hp, :])), r=[("Sst", hp)], dma=("st_p", hp))
                run(front(xsm, 0, [(i * 64, 64, 1 + i) for i in range(4)]))
                run(kvproj(lat_s, kr_s, 0))
                run(lat_transpose())
                run(kside(2048, kcs_sn, 0))
                inter(dict(q=qpath(qcs_s, 0), g=gla_own(True, st_s)))
                NPT = 2048 // NT
                BSS = [BS0, BS1]

                def stage_a(i, t, bs):
                    S.add("sp", OP("dma_start", out=bs["lat"], in_=latc[i, t * NT:(t + 1) * NT, :].rearrange("(b p) d -> p b d", p=128)), w=BK(bs["latk"]), dma=bs["latk"])
                    S.add("sp", OP("dma_start", out=bs["kr"], in_=krc[i, t * NT:(t + 1) * NT, :].rearrange("(b p) d -> p b d", p=128)), w=BK(bs["krk"]), dma=bs["krk"])
                    S.add("sp", OP("dma_start", out=bs["kcs"], in_=kcs_pre[:, t * NB:(t + 1) * NB, :]), w=[bs["kcsk"]], dma=bs["kcsk"])
                    yield
                    yield from lat_transpose(bs)

                tiles = [(i, t) for i in range(4) for t in range(NPT)]
                run(stage_a(0, 0, BSS[0]))
                for n, (i, t) in enumerate(tiles):
                    bs = BSS[n % 2]
                    nxtA = stage_a(tiles[n + 1][0], tiles[n + 1][1], BSS[(n + 1) % 2]) if n + 1 < len(tiles) else iter(())
                    if t < NPT - 1:
                        inter(dict(b=kside(t * NT, kcs_pre, t * NB, bs, None, False), a=nxtA))
                    else:
                        run(kside(t * NT, kcs_pre, t * NB, bs, None, False))
                        attn_run(sample_blocks(i), hook_of(nxtA, 4))
                        run(nxtA)
                run(back(xsm, 0, 2048, [1, 2]))
                ck('samp')
            S.barrier()

            with contextlib.ExitStack() as ph2:
                def msb(name, shape, dt=F32):
                    return ph2.enter_context(nc.sbuf_tensor(name, list(shape), dt))
                wup = msb("wup", [128, 8, 4096], BF16)
                wdn = msb("wdn", [128, 32, 1024], BF16)
                g2bc = msb("g2bc", [128, 1024])
                x1t = [msb("x1t%d" % i, [128, NB, 1024]) for i in range(2)]
                xw = msb("xw", [128, 1024])
                h2T = [msb("h2T%d" % i, [128, 8, NT], BF16) for i in range(2)]
                uT = msb("uT", [128, 32, NT], BF16)
                st2 = msb("st2", [128, 8])
                junk2 = msb("junk2", [128, 1024], BF16)
                RFb = [msb("RF%d" % i, [128, NT]) for i in range(6)]
                RFa = Rot([(t, t.name) for t in RFb[0:2]])
                RF = Rot([(t, t.name) for t in RFb[2:6]])
                gen8 = Rot(list(range(8)))

                def gp8():
                    i = gen8.get()
                    return P[i], PK(i)

                for jb in range(8):
                    S.add("pool", OP("dma_start", out=wup[:, :, jb * 512:(jb + 1) * 512], in_=w_up[:, jb * 512:(jb + 1) * 512].rearrange("(kc p) n -> p kc n", p=128)),
                          w=[("wup", jb)], dma=("wup", jb))
                for j4 in range(8):
                    S.add("pool", OP("dma_start", out=wdn[:, j4 * 4:(j4 + 1) * 4, :], in_=w_down[j4 * 512:(j4 + 1) * 512, :].rearrange("(j p) n -> p j n", p=128)),
                          w=[("wdn", j4)], dma=("wdn", j4))
                WDN = [("wdn", j4) for j4 in range(8)]

                def load_g2bc(v):
                    for fh in range(2):
                        Pg, Pgk = gp8()
                        S.add("pe", OP("matmul", Pg[:, :], sel[:, v, :], gater[:, 1, fh * 512:(fh + 1) * 512], start=True, stop=True),
                              r=["sel", "gater"], w=[Pgk])
                        S.add("act", OP("activation", out=g2bc[:, fh * 512:(fh + 1) * 512], in_=Pg[:, :], func=AF.Copy), r=[Pgk], w=["g2bc"])

                def mlp_front(ti, row0, segs):
                    xb, hb = x1t[ti % 2], h2T[ti % 2]
                    xk, hk = "x1t%d" % (ti % 2), "h2T%d" % (ti % 2)
                    for b in range(NB):
                        S.add("sp", OP("dma_start", out=xb[:, b, :], in_=x1s[row0 + b * 128:row0 + (b + 1) * 128, :]), r=[("x1s", row0 // 128 + b)], w=[(xk, b)], dma=(xk, b))
                        S.add("act", OP("activation", out=junk2[:, :], in_=xb[:, b, :], func=AF.Square, accum_out=st2[:, 4:5]), r=[(xk, b)], w=["junk2", "st2"])
                        S.add("act", act(st2[:, 5:6], st2[:, 4:5], AF.Ln, scale=1.0 / 1024, bias=EPS), r=["st2"], w=["st2"])
                        S.add("act", act(st2[:, 6:7], st2[:, 5:6], AF.Exp, scale=-0.5), r=["st2"], w=["st2"])
                        S.add("dve", OP("tensor_scalar", out=xw[:, :], in0=xb[:, b, :], scalar1=st2[:, 6:7], scalar2=None, op0=ALU.mult), r=[(xk, b), "st2"], w=["xw"])
                        yield
                        for k2 in range(2):
                            Pt, Ptk = gp8()
                            S.add("pe", seq([OP("transpose", Pt[:, kk * 128:(kk + 1) * 128], xw[:, (k2 * 4 + kk) * 128:(k2 * 4 + kk + 1) * 128], ident[:, :]) for kk in range(4)]),
                                  r=["xw", "ident"], w=[Ptk])
                            for kk in range(4):
                                kc = k2 * 4 + kk
                                for (c0, ncol, m) in segs:
                                    lo = max(c0, b * 128)
                                    hi = min(c0 + ncol, (b + 1) * 128)
                                    if lo >= hi:
                                        continue
                                    S.add("act", OP("activation", out=hb[:, kc, lo:hi], in_=Pt[:, kk * 128 + lo - b * 128:kk * 128 + hi - b * 128], func=AF.Identity,
                                                    scale=gm2[:, kc, m:m + 1], bias=sh2[:, kc, m:m + 1]), r=[Ptk, "gm2", "sh2"], w=[(hk, kc, b)])
                            yield

                def mlp_main(ti, ydst, yrow0, blocks_g2, hook):
                    xb, hb = x1t[ti % 2], h2T[ti % 2]
                    xk, hk = "x1t%d" % (ti % 2), "h2T%d" % (ti % 2)
                    for j in range(32):
                        Pu, Puk = gp8()
                        S.add("pe", seq([mm(Pu[:, 0:NT], wup[:, kc, j * 128:(j + 1) * 128], hb[:, kc, :], start=(kc == 0), stop=(kc == 7)) for kc in range(8)]),
                              r=[(hk, kc_, b_) for kc_ in range(8) for b_ in range(NB)] + [("wup", j // 4)], w=[Puk])
                        rt, rtk = RF.get()
                        S.add("act", OP("activation", out=rt[:, :], in_=Pu[:, 0:NT], func=AF.Relu), r=[Puk], w=[rtk])
                        S.add("dve", OP("tensor_tensor", out=uT[:, j, :], in0=Pu[:, 0:NT], in1=rt[:, :], op=ALU.mult), r=[Puk, rtk], w=[("uT", j)])
                        if hook is not None and j % 2 == 1:
                            hook()
                    UT = [("uT", j) for j in range(32)]
                    for b in range(NB):
                        if blocks_g2 is not None:
                            load_g2bc(blocks_g2[b])
                        for fh in range(2):
                            Po, Pok = gp8()
                            S.add("pe", seq([mm(Po[:, :], uT[:, j, b * 128:(b + 1) * 128], wdn[:, j, fh * 512:(fh + 1) * 512], start=(j == 0), stop=(j == 31)) for j in range(32)]),
                                  r=UT + WDN, w=[Pok])
                            for q2 in range(2):
                                tt, ttk = RF.get()
                                c0 = fh * 512 + q2 * 256
                                S.add("dve", OP("tensor_tensor", out=tt[:, :], in0=Po[:, q2 * 256:(q2 + 1) * 256], in1=g2bc[:, c0:c0 + 256], op=ALU.mult), r=[Pok, "g2bc"], w=[ttk])
                                S.add("dve", OP("tensor_tensor", out=xb[:, b, c0:c0 + 256], in0=xb[:, b, c0:c0 + 256], in1=tt[:, :], op=ALU.add), r=[ttk, (xk, b)], w=[(xk, b)])
                        S.add("pool", OP("dma_start", out=ydst[yrow0 + b * 128:yrow0 + (b + 1) * 128, :], in_=xb[:, b, :]), r=[(xk, b)], dma=("y_o", ti % 2, b))

                load_g2bc(0)
                NMT = 2048 // NT
                tiles2 = [(p * NT, y_own, p * NT, [(0, NT, 0)], None) for p in range(NMT)] + [(2048, y_s, 0, [(i * 64, 64, 1 + i) for i in range(4)], [1, 2])]
                for _ in mlp_front(0, tiles2[0][0], tiles2[0][3]):
                    pass
                for ti, (row0, ydst, yrow0, segs, bg2) in enumerate(tiles2):
                    if ti + 1 < len(tiles2):
                        nx = mlp_front(ti + 1, tiles2[ti + 1][0], tiles2[ti + 1][3])
                    else:
                        nx = iter(())
                    mlp_main(ti, ydst, yrow0, bg2, (lambda nx=nx: next(nx, None)))
                    for _ in nx:
                        pass

        except _Stop:
            pass
        S.emit()
    return nc


def _rope_tables(pos):
    half = 32
    inv = np.power(np.float32(10000.0), -np.arange(half, dtype=np.float32) / np.float32(half)).astype(np.float32)
    ang = pos.astype(np.float32)[:, None] * inv[None, :]
    return np.cos(ang).astype(np.float32), np.sin(ang).astype(np.float32)


def _kcs(pos):
    c, s = _rope_tables(pos)
    t = np.concatenate([c, c, -s, s], axis=1)
    n = pos.shape[0]
    return np.ascontiguousarray(t.reshape(n // 128, 128, 128).transpose(1, 0, 2))


def _qcs(pos):
    c, s = _rope_tables(pos)
    return np.ascontiguousarray(np.concatenate([c.T, c.T, s.T, s.T], axis=0))


_NC_CACHE = {}


def _prep(x_prompt, x_sample, cache_mla_latent, cache_mla_krope, state_gla, c_prompt, c_sample,
           w_ada, b_ada, g_norm1, w_in, g_q_lora, w_uq, g_kv_lora, w_ukv, g_q_head, g_k_head,
           w_gate_up, b_gate_up, g_gla_out, w_out, g_norm2, w_up, w_down):
    f = lambda a: np.ascontiguousarray(np.asarray(a, dtype=np.float32))
    x_prompt, x_sample = f(x_prompt), f(x_sample)
    latc_all, krc_all, st_all = f(cache_mla_latent)[0], f(cache_mla_krope)[0], f(state_gla)[0]
    c_prompt, c_sample = f(c_prompt), f(c_sample)
    w_ada, b_ada, g_norm1, w_in = f(w_ada)[0], f(b_ada)[0], f(g_norm1)[0], f(w_in)[0]
    g_q_lora, w_uq, g_kv_lora, w_ukv = f(g_q_lora)[0], f(w_uq)[0], f(g_kv_lora)[0], f(w_ukv)[0]
    g_q_head, g_k_head = f(g_q_head)[0], f(g_k_head)[0]
    w_gate_up, b_gate_up, g_gla_out = f(w_gate_up)[0], f(b_gate_up)[0], f(g_gla_out)[0]
    w_out, g_norm2, w_up, w_down = f(w_out)[0], f(g_norm2)[0], f(w_up)[0], f(w_down)[0]

    rot_cols = []
    for h in range(4):
        base = h * 192 + 128
        rot_cols += list(range(base + 32, base + 64)) + list(range(base, base + 32))
    w_uq_ext = np.ascontiguousarray(np.concatenate([w_uq, w_uq[:, rot_cols]], axis=1))
    kn_cols, v_cols = [], []
    for h in range(4):
        kn_cols += list(range(h * 256, h * 256 + 128))
        v_cols += list(range(h * 256 + 128, h * 256 + 256))
    w_ukv_p = np.ascontiguousarray(w_ukv[:, kn_cols + v_cols])
    w_gu_aug = np.zeros((32, 256), np.float32)
    w_gu_aug[0:16] = w_gate_up
    w_gu_aug[16] = b_gate_up
    colT = lambda v, n: np.ascontiguousarray(v.reshape(n, 128).T)
    gqr_col = np.concatenate([g_q_head[128:192], np.roll(g_q_head[128:192], -32)])[:, None]
    ident = np.eye(128, dtype=np.float32)
    ii = np.arange(128)
    same = (ii[:, None] // 64) == (ii[None, :] // 64)
    tri = np.where(same & (ii[:, None] <= ii[None, :]), np.float32(-1.0 / 16), np.float32(0)).astype(np.float32)
    utm = np.where(same & (ii[:, None] > ii[None, :]), np.float32(-1.0 / 16), np.float32(0)).astype(np.float32)
    mblk = (same & (ii[:, None] <= ii[None, :])).astype(np.float32)
    maskr = np.ascontiguousarray(np.tile(mblk, (1, NB)))
    hsel = np.zeros((128, 2), np.float32)
    hsel[0:64, 0] = 0.125
    hsel[64:128, 1] = 0.125
    sel = np.zeros((5, 3, 128), np.float32)
    sel[0, 0, :] = 1
    sel[1, 1, 0:64] = 1
    sel[2, 1, 64:128] = 1
    sel[3, 2, 0:64] = 1
    sel[4, 2, 64:128] = 1
    comb = ((ii[:, None] % 64) == (ii[None, :] % 64)).astype(np.float32)
    kcs_pre = _kcs(np.arange(2048))
    kcs_sn = _kcs(2048 + (np.arange(256) % 64))
    qcs_s = _qcs(2048 + (np.arange(256) % 64))
    shared = dict(
        w_ada=w_ada, b_ada_row=b_ada[None, :], b_adaT=colT(b_ada, 48),
        g1T=colT(g_norm1, 8), g2T=colT(g_norm2, 8), gqlT=colT(g_q_lora, 3),
        gqn=np.ascontiguousarray(g_q_head[0:128, None]), gkn=np.ascontiguousarray(g_k_head[0:128, None]),
        gqr=np.ascontiguousarray(gqr_col), gkr_row=np.ascontiguousarray(g_k_head[None, 128:192]),
        gkv_row=g_kv_lora[None, :], ggoT=colT(g_gla_out, 4),
        w_in=w_in, w_uq_ext=w_uq_ext, w_ukv_p=w_ukv_p, w_gu_aug=w_gu_aug, w_out=w_out, w_up=w_up, w_down=w_down,
        kcs_pre=kcs_pre, kcs_sn=kcs_sn, qcs_s=qcs_s,
        c_ident=ident, c_tri=tri, c_ut=utm, c_mask=maskr, c_sel=sel, c_comb=comb, c_hsel=hsel,
    )
    in_maps = []
    for c in range(8):
        pb, half = c // 2, c % 2
        pos_own = half * 2048 + np.arange(2048)
        flag = np.zeros((128, 2), np.float32)
        flag[:, 0] = float(half)
        flag[:, 1] = 0.0 if half == 1 else NEG
        cvec = np.concatenate([c_prompt[pb:pb + 1], c_sample[4 * c:4 * c + 4]], axis=0)
        m = dict(shared)
        m.update(
            xpre=np.ascontiguousarray(x_prompt[pb, 0:2048]),
            xown=np.ascontiguousarray(x_prompt[pb, half * 2048:(half + 1) * 2048]),
            xsm=np.ascontiguousarray(x_sample[4 * c:4 * c + 4].reshape(256, 1024)),
            latc=np.ascontiguousarray(latc_all[4 * c:4 * c + 4]),
            krc=np.ascontiguousarray(krc_all[4 * c:4 * c + 4]),
            stc=np.ascontiguousarray(st_all[4 * c:4 * c + 4].reshape(4, 2, 128, 128)),
            cT=np.ascontiguousarray(cvec.T), flag=flag,
            kcs_own=_kcs(pos_own), qcs_own=_qcs(pos_own),
        )
        in_maps.append(m)

    return in_maps


def kernel(**inputs):
    in_maps = _prep(**inputs)
    if "nc" not in _NC_CACHE:
        _NC_CACHE["nc"] = build()
    res = run_bass_kernel_spmd(_NC_CACHE["nc"], in_maps, core_ids=list(range(8)))
    return _assemble(res.results)


def _assemble(R):
    y_p = np.zeros((4, 4096, 1024), np.float32)
    lat_p = np.zeros((1, 4, 4096, 256), np.float32)
    kr_p = np.zeros((1, 4, 4096, 64), np.float32)
    st_pp = np.zeros((1, 4, 4, 64, 128), np.float32)
    y_s = np.zeros((32, 64, 1024), np.float32)
    lat_s = np.zeros((1, 32, 64, 256), np.float32)
    kr_s = np.zeros((1, 32, 64, 64), np.float32)
    st_s = np.zeros((1, 32, 4, 64, 128), np.float32)
    for c in range(8):
        pb, half = c // 2, c % 2
        sl = slice(half * 2048, (half + 1) * 2048)
        y_p[pb, sl] = R[c]["y_own"]
        lat_p[0, pb, sl] = R[c]["lat_own"]
        kr_p[0, pb, sl] = R[c]["kr_own"]
        if half == 1:
            st_pp[0, pb] = R[c]["st_p"].reshape(4, 64, 128)
        y_s[4 * c:4 * c + 4] = R[c]["y_s"].reshape(4, 64, 1024)
        lat_s[0, 4 * c:4 * c + 4] = R[c]["lat_s"].reshape(4, 64, 256)
        kr_s[0, 4 * c:4 * c + 4] = R[c]["kr_s"].reshape(4, 64, 64)
        st_s[0, 4 * c:4 * c + 4] = R[c]["st_s"].reshape(4, 4, 64, 128)
    return (y_p, y_s, lat_p, kr_p, st_pp, lat_s, kr_s, st_s)
```
